# Optimizing a Trainium2 kernel written in Bass

```python
import jax, jax.numpy as jnp
from jax import lax
import numpy as np

D_MODEL = 1024
BATCH = 32
SEQ = 2048
DEPTH = 1

D_MIX = D_MODEL
MLA_HEADS = 4
MLA_NOPE = 128
MLA_ROPE = 64
MLA_V = 128
MLA_WIDTH = MLA_HEADS * MLA_V
Q_LORA = 256
KV_LORA = 128
ROPE_THETA = 10000.0
Q_BLOCK = 128
RW_HEAD = 64
RW_WIDTH = D_MIX - MLA_WIDTH
RW_HEADS = RW_WIDTH // RW_HEAD
W_LORA = 64
A_LORA = 64
RW_GN_EPS = 64e-5
NORM_EPS = 1e-6
MLA_COLS = Q_LORA + KV_LORA + MLA_ROPE
RW_SHIFT_COLS = 3 * RW_WIDTH + W_LORA + A_LORA
GATE_COLS = D_MIX
D_IN = MLA_COLS + RW_SHIFT_COLS + GATE_COLS

kernel_name = 'hymba_mla_rwkv7_sandwich'


def rmsnorm(x, g):
    xf = x.astype(jnp.float32)
    y = xf * lax.rsqrt(jnp.mean(xf * xf, axis=-1, keepdims=True) + NORM_EPS)
    return (y * g.astype(jnp.float32)).astype(x.dtype)


def rope_tables(positions):
    inv_freq = ROPE_THETA ** (-jnp.arange(0, MLA_ROPE, 2, dtype=jnp.float32) / MLA_ROPE)
    ang = positions.astype(jnp.float32)[..., None] * inv_freq
    ang = jnp.concatenate([ang, ang], axis=-1)
    return jnp.cos(ang), jnp.sin(ang)


def apply_rope(x, cos, sin):
    x1, x2 = jnp.split(x, 2, axis=-1)
    rot = jnp.concatenate([-x2, x1], axis=-1)
    return (x.astype(jnp.float32) * cos + rot.astype(jnp.float32) * sin).astype(x.dtype)


def token_shift(p):
    return jnp.pad(p[:, :-1], ((0, 0), (1, 0), (0, 0)))


def mla_attention(q_nope, q_rope, k_nope, k_rope, v):
    T = q_nope.shape[1]
    scale = (MLA_NOPE + MLA_ROPE) ** -0.5
    outs = []
    for i in range(T // Q_BLOCK):
        s, e = i * Q_BLOCK, (i + 1) * Q_BLOCK
        scores = (jnp.einsum('bqhd,bkhd->bhqk', q_nope[:, s:e], k_nope[:, :e])
                  + jnp.einsum('bqhr,bkr->bhqk', q_rope[:, s:e], k_rope[:, :e])).astype(jnp.float32) * scale
        mask = (s + jnp.arange(Q_BLOCK))[:, None] >= jnp.arange(e)[None, :]
        scores = jnp.where(mask, scores, -jnp.inf)
        probs = jax.nn.softmax(scores, axis=-1).astype(v.dtype)
        outs.append(jnp.einsum('bhqk,bkhd->bqhd', probs, v[:, :e]))
    return jnp.concatenate(outs, axis=1)


def wkv7_scan(r, w, k, v, kk, a):
    B, T, H, N = r.shape

    def step(S, inp):
        r_t, w_t, k_t, v_t, kk_t, a_t = inp
        sa = jnp.einsum('bhvk,bhk->bhv', S, -kk_t)
        S = (S * w_t[:, :, None, :] + sa[..., None] * (kk_t * a_t)[:, :, None, :]
             + v_t[..., None] * k_t[:, :, None, :])
        return S, jnp.einsum('bhvk,bhk->bhv', S, r_t)

    xs = tuple(jnp.moveaxis(t.astype(jnp.float32), 1, 0) for t in (r, w, k, v, kk, a))
    S0 = jnp.zeros((B, H, N, N), jnp.float32)
    _, ys = lax.scan(step, S0, xs)
    return jnp.moveaxis(ys, 0, 1)


def setup_inputs(seed: int = 0) -> dict:
    key = jax.random.key(seed)
    ks = jax.random.split(key, 24)
    L = DEPTH
    f32 = jnp.float32

    def nrm(k, shape, scale):
        return jax.random.normal(k, shape, f32) * scale

    x = nrm(ks[0], (BATCH, SEQ, D_MODEL), 1.0)
    offset = jax.random.randint(ks[1], (BATCH, 1), 0, 4096, dtype=jnp.int32)
    positions = offset + jnp.arange(SEQ, dtype=jnp.int32)[None, :]
    return {
        'x': x,
        'positions': positions,
        'norm_pre_g': 1.0 + nrm(ks[2], (L, D_MODEL), 0.02),
        'w_in': nrm(ks[3], (L, D_MODEL, D_IN), D_MODEL ** -0.5),
        'mla_q_norm_g': 1.0 + nrm(ks[4], (L, Q_LORA), 0.02),
        'mla_w_uq': nrm(ks[5], (L, Q_LORA, MLA_HEADS * (MLA_NOPE + MLA_ROPE)), Q_LORA ** -0.5),
        'mla_kv_norm_g': 1.0 + nrm(ks[6], (L, KV_LORA), 0.02),
        'mla_w_ukv': nrm(ks[7], (L, KV_LORA, MLA_HEADS * (MLA_NOPE + MLA_V)), KV_LORA ** -0.5),
        'rw_mu': jax.random.uniform(ks[8], (L, RW_SHIFT_COLS), f32),
        'rw_w0': -2.5 + nrm(ks[9], (L, RW_WIDTH), 0.5),
        'rw_w2': nrm(ks[10], (L, W_LORA, RW_WIDTH), 0.5 * W_LORA ** -0.5),
        'rw_a0': nrm(ks[11], (L, RW_WIDTH), 0.1),
        'rw_a2': nrm(ks[12], (L, A_LORA, RW_WIDTH), 0.5 * A_LORA ** -0.5),
        'rw_k_k': 0.85 + nrm(ks[13], (L, RW_WIDTH), 0.05),
        'rw_k_a': 1.0 + nrm(ks[14], (L, RW_WIDTH), 0.05),
        'rw_r_k': nrm(ks[15], (L, RW_HEADS, RW_HEAD), 0.1),
        'rw_ln_g': 1.0 + nrm(ks[16], (L, RW_WIDTH), 0.02),
        'rw_ln_b': nrm(ks[17], (L, RW_WIDTH), 0.02),
        'w_out': nrm(ks[18], (L, D_MIX, D_MODEL), D_MIX ** -0.5),
        'norm_post_g': 1.0 + nrm(ks[19], (L, D_MODEL), 0.02),
    }


def reference(x, positions, norm_pre_g, w_in, mla_q_norm_g, mla_w_uq, mla_kv_norm_g, mla_w_ukv,
              rw_mu, rw_w0, rw_w2, rw_a0, rw_a2, rw_k_k, rw_k_a, rw_r_k, rw_ln_g, rw_ln_b,
              w_out, norm_post_g):
    B, T, _ = x.shape
    f32 = jnp.float32
    cos, sin = rope_tables(positions)
    h = x
    for l in range(DEPTH):
        u = rmsnorm(h, norm_pre_g[l])
        p = u @ w_in[l]
        p_mla, p_rw, z = jnp.split(p, [MLA_COLS, MLA_COLS + RW_SHIFT_COLS], axis=-1)

        c_q, c_kv, k_r = jnp.split(p_mla, [Q_LORA, Q_LORA + KV_LORA], axis=-1)
        q = (rmsnorm(c_q, mla_q_norm_g[l]) @ mla_w_uq[l]).reshape(B, T, MLA_HEADS, MLA_NOPE + MLA_ROPE)
        q_nope, q_rope = jnp.split(q, [MLA_NOPE], axis=-1)
        kv = (rmsnorm(c_kv, mla_kv_norm_g[l]) @ mla_w_ukv[l]).reshape(B, T, MLA_HEADS, MLA_NOPE + MLA_V)
        k_nope, v_mla = jnp.split(kv, [MLA_NOPE], axis=-1)
        q_rope = apply_rope(q_rope, cos[:, :, None, :], sin[:, :, None, :])
        k_r = apply_rope(k_r, cos, sin)
        y_mla = mla_attention(q_nope, q_rope, k_nope, k_r, v_mla).reshape(B, T, MLA_WIDTH)

        ps = p_rw + (token_shift(p_rw) - p_rw) * rw_mu[l]
        r, k, v, xw, xa = jnp.split(
            ps, [RW_WIDTH, 2 * RW_WIDTH, 3 * RW_WIDTH, 3 * RW_WIDTH + W_LORA], axis=-1)
        w_log = -jax.nn.softplus(-(rw_w0[l] + jnp.tanh(xw) @ rw_w2[l]).astype(f32)) - 0.5
        decay = jnp.exp(-jnp.exp(w_log))
        a = jax.nn.sigmoid((rw_a0[l] + xa @ rw_a2[l]).astype(f32))
        kk = (k * rw_k_k[l]).astype(f32).reshape(B, T, RW_HEADS, RW_HEAD)
        kk = kk / jnp.maximum(jnp.linalg.norm(kk, axis=-1, keepdims=True), 1e-12)
        k = k.astype(f32) * (1.0 + (a - 1.0) * rw_k_a[l].astype(f32))
        heads = lambda t: t.reshape(B, T, RW_HEADS, RW_HEAD)
        r_h, k_h, v_h = heads(r.astype(f32)), heads(k), heads(v.astype(f32))
        y = wkv7_scan(r_h, heads(decay), k_h, v_h, kk, heads(a))
        mean = jnp.mean(y, axis=-1, keepdims=True)
        var = jnp.mean(jnp.square(y - mean), axis=-1, keepdims=True)
        y = ((y - mean) * lax.rsqrt(var + RW_GN_EPS)).reshape(B, T, RW_WIDTH)
        y = y * rw_ln_g[l].astype(f32) + rw_ln_b[l].astype(f32)
        bonus = jnp.sum(r_h * k_h * rw_r_k[l].astype(f32), axis=-1, keepdims=True) * v_h
        y_rw = (y + bonus.reshape(B, T, RW_WIDTH)).astype(x.dtype)

        y_cat = jnp.concatenate([y_mla, y_rw], axis=-1) * jax.nn.silu(z)
        out = y_cat @ w_out[l]
        h = h + rmsnorm(out, norm_post_g[l])
    return h
```

```python
import math
import os
import numpy as np
import concourse.bass as bass
import concourse.mybir as mybir
from concourse.bass_utils import run_bass_kernel_spmd
from contextlib import ExitStack

F32 = mybir.dt.float32
BF16 = mybir.dt.bfloat16
I32 = mybir.dt.int32
AF = mybir.ActivationFunctionType
ALU = mybir.AluOpType
AX = mybir.AxisListType

SAME_ENGINE_SYNC = True
N_CORES = 8
T_FULL = 2048
D = 1024


class TB:
    __slots__ = ("name", "last_w", "readers", "dma_sem", "dma_cnt")

    def __init__(self, name):
        self.name = name
        self.last_w = []
        self.readers = []
        self.dma_sem = None
        self.dma_cnt = 0


class V:
    __slots__ = ("ap", "tb")

    def __init__(self, ap, tb):
        self.ap = ap
        self.tb = tb

    def __getitem__(self, idx):
        return V(self.ap[idx], self.tb)

    def bc(self, shape):
        return V(self.ap.to_broadcast(list(shape)), self.tb)

    def re(self, s, **kw):
        return V(self.ap.rearrange(s, **kw), self.tb)


class Prog:
    ENGS = ("pe", "act", "dve", "pool", "sp")

    def __init__(self, nc, es):
        self.nc = nc
        self.es = es
        self.ops = []
        self.signal = set()

    def sb(self, name, shape, dt=F32):
        t = self.es.enter_context(self.nc.sbuf_tensor("s_" + name, list(shape), dt))
        return V(t[:], TB(name))

    def ps(self, name, shape, dt=F32):
        t = self.es.enter_context(self.nc.psum_tensor("p_" + name, list(shape), dt))
        return V(t[:], TB(name))

    def add(self, eng, emit, reads=(), writes=(), dma_tb=None):
        idx = len(self.ops)
        deps = []
        rt = [r.tb if isinstance(r, V) else r for r in reads]
        wt = [w.tb if isinstance(w, V) else w for w in writes]
        for tb in rt:
            deps.extend(tb.last_w)
        for tb in wt:
            deps.extend(tb.last_w)
            deps.extend(tb.readers)
        if dma_tb is not None:
            dma_tb.dma_cnt += 1
            me = ("d", dma_tb, 16 * dma_tb.dma_cnt)
        else:
            me = ("e", eng, idx)
        for tb in rt:
            tb.readers.append(me)
        for tb in wt:
            tb.last_w = [me]
            tb.readers = []
        seen = set()
        d2 = []
        for d in deps:
            k = (d[0], id(d[1]) if d[0] == "d" else d[1], d[2])
            if k in seen:
                continue
            seen.add(k)
            d2.append(d)
            if d[0] == "e":
                self.signal.add(d[2])
        self.ops.append((eng, emit, d2, dma_tb))
        return idx

    def dma(self, out, in_, eng="sp", reads=(), writes=()):
        sbv = out if isinstance(out, V) else in_
        o = out.ap if isinstance(out, V) else out
        i = in_.ap if isinstance(in_, V) else in_
        r = list(reads) + ([in_] if isinstance(in_, V) else [])
        w = list(writes) + ([out] if isinstance(out, V) else [])
        return self.add(eng, lambda e: e.dma_start(out=o, in_=i), r, w, dma_tb=sbv.tb)

    def mm(self, out, lhsT, rhs, start=True, stop=True):
        return self.add("pe", lambda e: e.matmul(out.ap, lhsT.ap, rhs.ap, start=start, stop=stop),
                        [lhsT, rhs], [out])

    def tr(self, out, in_, ident):
        return self.add("pe", lambda e: e.transpose(out.ap, in_.ap, ident.ap), [in_, ident], [out])

    def act(self, out, in_, func, bias=None, scale=1.0, accum_out=None):
        reads = [in_]
        kw = {}
        if isinstance(bias, V):
            reads.append(bias)
            kw["bias"] = bias.ap
        elif bias is not None:
            kw["bias"] = bias
        if isinstance(scale, V):
            reads.append(scale)
            kw["scale"] = scale.ap
        else:
            kw["scale"] = scale
        writes = [out]
        if accum_out is not None:
            writes.append(accum_out)
            kw["accum_out"] = accum_out.ap
        return self.add("act", lambda e: e.activation(out.ap, in_.ap, func, **kw), reads, writes)

    def tt(self, out, a, b, op, eng="dve"):
        return self.add(eng, lambda e: e.tensor_tensor(out.ap, a.ap, b.ap, op), [a, b], [out])

    def ts(self, out, a, s1, op0, s2=None, op1=None, eng="dve"):
        reads = [a]
        x1, x2 = s1, s2
        if isinstance(s1, V):
            reads.append(s1)
            x1 = s1.ap
        if isinstance(s2, V):
            reads.append(s2)
            x2 = s2.ap
        kw = {}
        if op1 is not None:
            kw["op1"] = op1
        return self.add(eng, lambda e: e.tensor_scalar(out.ap, a.ap, x1, x2, op0, **kw), reads, [out])

    def stt(self, out, a, s, b, op0, op1, eng="dve"):
        reads = [a, b]
        x = s
        if isinstance(s, V):
            reads.append(s)
            x = s.ap
        return self.add(eng, lambda e: e.scalar_tensor_tensor(out.ap, a.ap, x, b.ap, op0, op1), reads, [out])

    def copy(self, out, in_, eng="dve"):
        if eng == "act":
            return self.add("act", lambda e: e.copy(out.ap, in_.ap), [in_], [out])
        return self.add(eng, lambda e: e.tensor_copy(out.ap, in_.ap), [in_], [out])

    def memset(self, out, val, eng="dve"):
        return self.add(eng, lambda e: e.memset(out.ap, val), [], [out])

    def recip(self, out, in_):
        return self.add("dve", lambda e: e.reciprocal(out.ap, in_.ap), [in_], [out])

    def rsum(self, out, in_, eng="dve"):
        return self.add(eng, lambda e: e.tensor_reduce(out.ap, in_.ap, AX.X, ALU.add), [in_], [out])

    def scan(self, out, d0, d1, init, op0, op1):
        return self.add("dve", lambda e: e.tensor_tensor_scan(out.ap, d0.ap, d1.ap, init, op0, op1), [d0, d1], [out])

    def aselect(self, out, in_, pattern, cmp, fill, base, cm):
        return self.add("pool", lambda e: e.affine_select(out.ap, in_.ap, pattern, cmp, fill, base=base,
                                                          channel_multiplier=cm), [in_], [out])

    def emit(self, final_wait=()):
        nc, es = self.nc, self.es
        ordn = {}
        cnt = {e: 0 for e in self.ENGS}
        for i, (eng, _, _, dma_tb) in enumerate(self.ops):
            if dma_tb is None and i in self.signal:
                cnt[eng] += 1
                ordn[i] = cnt[eng]
        esem = {e: es.enter_context(nc.semaphore("sem_" + e)) for e in self.ENGS}
        for (eng, _, _, dma_tb) in self.ops:
            if dma_tb is not None and dma_tb.dma_sem is None:
                dma_tb.dma_sem = es.enter_context(nc.semaphore("dsem_%s" % dma_tb.name))
        per_eng = {e: [] for e in self.ENGS}
        for i, op in enumerate(self.ops):
            per_eng[op[0]].append(i)
        self.stats = {e: [len(per_eng[e]), cnt[e], 0] for e in self.ENGS}
        block = es.enter_context(nc.Block())
        ops, stats = self.ops, self.stats

        def run(engname, e):
            waited = {}
            for i in per_eng[engname]:
                _, emit, deps, dma_tb = ops[i]
                need = {}
                for d in deps:
                    if d[0] == "e":
                        if d[1] == engname and (engname == "pe" or not SAME_ENGINE_SYNC):
                            continue
                        sem, val, key = esem[d[1]], ordn[d[2]], "e" + d[1]
                    else:
                        sem, val, key = d[1].dma_sem, d[2], id(d[1])
                    if waited.get(key, 0) >= val:
                        continue
                    if key not in need or need[key][1] < val:
                        need[key] = (sem, val)
                for key, (sem, val) in need.items():
                    e.wait_ge(sem, val)
                    waited[key] = val
                    stats[engname][2] += 1
                ins = emit(e)
                if dma_tb is not None:
                    ins.then_inc(dma_tb.dma_sem, 16)
                elif i in ordn:
                    ins.then_inc(esem[engname], 1)
            if engname == "sp":
                for tb in final_wait:
                    if tb.dma_sem is not None:
                        e.wait_ge(tb.dma_sem, 16 * tb.dma_cnt)

        @block.tensor
        def _(e):
            run("pe", e)

        @block.scalar
        def _(e):
            run("act", e)

        @block.vector
        def _(e):
            run("dve", e)

        @block.gpsimd
        def _(e):
            run("pool", e)

        @block.sync
        def _(e):
            run("sp", e)


WC = 3136 + 64 + 512
C_CQ, C_CKV, C_KR, C_R, C_K, C_V, C_XW, C_Z = 0, 256, 384, 448, 960, 1472, 1984, 2112
C_KRROT, C_V2 = 3136, 3200
PP_GPRE, PP_GQ, PP_GKV, PP_W0, PP_A0, PP_KK, PP_KA, PP_RK, PP_INVF, PP_MU = 0, 8, 10, 11, 15, 19, 23, 27, 31, 32
NPP = 41
NROWS = 2560
GN_EPS = 64e-5
NORM_EPS = 1e-6
SM_SCALE = 192.0 ** -0.5


def build(NBC, NT, dbg_names=(), stop=None):
    nc = bass.Bass("TRN2", target_bir_lowering=False)

    def din(name, shape, dt=F32):
        return nc.dram_tensor(name, list(shape), dt, kind="ExternalInput").ap()

    T = NT * 128
    x_d = din("x", [NBC * T, D])
    pos_d = din("pos", [NBC, T], I32)
    win_d = din("win", [D, 3200])
    wuq_d = din("wuq", [256, 1024])
    wukv_d = din("wukv", [128, 1024])
    w2a_d = din("w2a", [128, 512])
    wout_d = din("wout", [D, D])
    pp_d = din("pp", [128, NPP])
    rows_d = din("rows", [1, NROWS])
    out_d = nc.dram_tensor("out", [NBC * T, D], F32, kind="ExternalOutput").ap()
    dbg_d = {}
    for (nm, shape) in dbg_names:
        dbg_d[nm] = nc.dram_tensor("dbg_" + nm, list(shape), F32, kind="ExternalOutput").ap()

    with ExitStack() as es:
        P = Prog(nc, es)
        ident = P.sb("ident", [128, 128])
        identb = P.sb("identb", [128, 128], BF16)
        onesf = P.sb("onesf", [128, 128])
        onesb = P.sb("onesb", [128, 128], BF16)
        blockones = P.sb("blockones", [128, 128])
        headind = P.sb("headind", [128, 2], BF16)
        m_su = P.sb("m_su", [128, 128], BF16)
        m_iu = P.sb("m_iu", [128, 128], BF16)
        m_sl = P.sb("m_sl", [128, 128], BF16)
        mask4 = P.sb("mask4", [128, 4, 128], BF16)
        P.memset(ident, 0.0, eng="pool")
        P.aselect(ident, ident, [[-1, 128]], ALU.not_equal, 1.0, 0, 1)
        P.copy(identb, ident)
        P.memset(onesf, 1.0)
        P.memset(onesb, 1.0)
        P.memset(blockones, 0.0)
        P.memset(blockones[0:64, 0:64], 1.0)
        P.memset(blockones[64:128, 64:128], 1.0)
        P.memset(headind, 0.0)
        P.memset(headind[0:64, 0:1], 1.0)
        P.memset(headind[64:128, 1:2], 1.0)
        for m in (m_su, m_iu, m_sl):
            P.memset(m, 1.0, eng="pool")
        P.aselect(m_su, m_su, [[1, 128]], ALU.is_gt, 0.0, 0, -1)
        P.aselect(m_iu, m_iu, [[1, 128]], ALU.is_ge, 0.0, 0, -1)
        P.aselect(m_sl, m_sl, [[-1, 128]], ALU.is_gt, 0.0, 0, 1)
        P.copy(mask4[:, 0, :], m_su)
        P.copy(mask4[:, 1, :], m_iu)
        P.copy(mask4[:, 2, :], m_su)
        P.copy(mask4[:, 3, :], m_iu)
        m4 = mask4

        pp = P.sb("pp", [128, NPP])
        P.dma(pp, pp_d)
        rows = P.sb("rows", [128, 2048])
        P.dma(rows, rows_d[:, 512:2560].broadcast_to([128, 2048]))
        der = P.sb("der", [128, 32])
        gneg = der[:, 0:8]
        gqneg = der[:, 8:10]
        omka = der[:, 10:14]
        P.ts(gneg, pp[:, PP_GPRE:PP_GPRE + 8], -1.0, ALU.mult)
        P.ts(gqneg, pp[:, PP_GQ:PP_GQ + 2], -1.0, ALU.mult)
        P.ts(omka, pp[:, PP_KA:PP_KA + 4], -1.0, ALU.mult, 1.0, ALU.add)
        Gt = es.enter_context(nc.sbuf_tensor("s_G", [128, 10, 512], F32))
        g = [V(Gt[:][:, i, :], TB("G%d" % i)) for i in range(10)]

        def g3(i, a, parts=128):
            return V(g[i].ap[0:parts, :].rearrange("p (a t) -> p a t", a=a), g[i].tb)

        muv, omuv = g[8], g[9]
        P.dma(muv, rows_d[:, 0:512].broadcast_to([128, 512]))
        P.ts(omuv, muv, -1.0, ALU.mult, 1.0, ALU.add)

        W = P.sb("W", [128, 8, WC], BF16)
        stg = [P.sb("stg0", [128, 3200]), P.sb("stg1", [128, 3200])]
        for c in range(8):
            s = stg[c % 2]
            P.dma(s, win_d[c * 128:(c + 1) * 128, :])
            gg = pp[:, PP_GPRE + c:PP_GPRE + c + 1]
            gn = gneg[:, c:c + 1]
            e1 = "dve" if c % 2 == 0 else "pool"
            e2_ = "pool" if c % 2 == 0 else "dve"
            P.ts(W[:, c, 0:C_V], s[:, 0:C_V], gg, ALU.mult, eng=e1)
            P.ts(W[:, c, C_XW:3136], s[:, C_XW:3136], gg, ALU.mult, eng=e2_)
            P.ts(W[:, c, C_KRROT:C_KRROT + 32], s[:, 3136:3168], gn, ALU.mult, eng=e1)
            P.ts(W[:, c, C_KRROT + 32:C_KRROT + 64], s[:, 3168:3200], gg, ALU.mult, eng=e1)
            P.stt(W[:, c, C_V:C_V + 512], s[:, C_V:C_V + 512], gg, omuv, ALU.mult, ALU.mult)
            P.stt(W[:, c, C_V2:C_V2 + 512], s[:, C_V:C_V + 512], gg, muv, ALU.mult, ALU.mult)
        Wq = P.sb("Wq", [128, 2, 1024], BF16)
        for c in range(2):
            s = stg[c % 2]
            P.dma(s[:, 0:1024], wuq_d[c * 128:(c + 1) * 128, :])
            gg = pp[:, PP_GQ + c:PP_GQ + c + 1]
            gn = gqneg[:, c:c + 1]
            P.ts(Wq[:, c, 0:768], s[:, 0:768], gg, ALU.mult)
            rot_o = Wq[:, c, 768:1024].re("p (h r) -> p h r", h=4)
            rot_in = s[:, 768:1024].re("p (h r) -> p h r", h=4)
            P.ts(rot_o[:, :, 0:32], rot_in[:, :, 0:32], gn, ALU.mult)
            P.ts(rot_o[:, :, 32:64], rot_in[:, :, 32:64], gg, ALU.mult)
        Wkv = P.sb("Wkv", [128, 1024], BF16)
        s = stg[0]
        P.dma(s[:, 0:1024], wukv_d)
        P.ts(Wkv, s[:, 0:1024], pp[:, PP_GKV:PP_GKV + 1], ALU.mult)
        W2A = P.sb("W2A", [128, 512], BF16)
        s = stg[1]
        P.dma(s[:, 0:512], w2a_d)
        P.copy(W2A, s[:, 0:512])
        Wout = P.sb("Wout", [128, 8, 1024], BF16)
        for c in range(8):
            s = stg[c % 2]
            P.dma(s[:, 0:1024], wout_d[c * 128:(c + 1) * 128, :])
            P.copy(Wout[:, c, :], s[:, 0:1024], eng=("dve" if c % 2 == 0 else "pool"))

        carve_off = [0, 0]

        def carve(si, name, shape, dt):
            esz = 4 if dt in (F32, I32) else 2
            n = 1
            for d_ in shape[1:]:
                n *= d_
            nb = n * esz
            c0 = carve_off[si] // 4
            carve_off[si] += nb
            assert carve_off[si] <= 12800, (name, carve_off)
            ap = stg[si].ap[0:shape[0], c0:c0 + nb // 4]
            if dt != F32:
                ap = ap.bitcast(dt)
            if len(shape) == 3:
                ap = ap.rearrange("p (a b) -> p a b", a=shape[1])
            elif len(shape) == 4:
                ap = ap.rearrange("p (a b c) -> p a b c", a=shape[1], b=shape[2])
            tb = TB(name)
            tb.last_w = list(stg[si].tb.last_w)
            tb.readers = list(stg[si].tb.readers)
            return V(ap, tb)

        AR = carve(0, "AR", [128, 4, 2, 128], BF16)
        Bt = carve(0, "Bt", [128, 4, 128], BF16)
        Kt = carve(0, "Kt", [128, 4, 128], BF16)
        bhat = carve(0, "bhat", [128, 4, 128], BF16)
        khat = carve(0, "khat", [128, 4, 128], BF16)
        BKtok = carve(0, "BKtok", [128, 8, 128], BF16)
        NK = carve(0, "NK", [128, 4, 4, 128], BF16)
        Am = P.sb("Am", [128, 4, 128], BF16) if os.environ.get("AMSB") else carve(1, "Am", [128, 4, 128], BF16)
        Qa = carve(1, "Qa", [128, 4, 128], BF16)
        QTa = carve(1, "QTa", [128, 4, 128], BF16)
        Mt = carve(1, "Mt", [128, 4, 128], BF16)
        Xb = carve(1, "Xb", [128, 4, 64], BF16)
        Ub = carve(1, "Ub", [128, 4, 64], BF16)
        yrb = carve(1, "yrb", [128, 512], BF16)
        prk = carve(1, "prk", [128, 4, 128], BF16)
        lor = carve(1, "lor", [128, 128], BF16)
        vtokb = carve(1, "vtokb", [128, 512], BF16)
        PT = [carve(1, "PT0", [128, 4, 128], BF16), carve(1, "PT1", [128, 4, 128], BF16)]
        cqn = carve(1, "cqn", [128, 3, 128], BF16)
        qnT = carve(1, "qnT", [128, 4, 128], BF16)

        rot_banks = [P.ps("bank%d" % i, [128, 512]) for i in range(4)]
        bank_y = P.ps("bank_y", [128, 512])
        bank_s = P.ps("bank_s", [128, 512])
        bank_x = P.ps("bank_x", [128, 512])
        bank_t = P.ps("bank_t", [128, 1024], BF16)
        rot_i = [0]

        def bank():
            bkk = rot_banks[rot_i[0] % 4]
            rot_i[0] += 1
            return bkk

        KnT = P.sb("KnT", [128, 4, T], BF16)
        krT = P.sb("krT", [64, T], BF16)
        Vm = P.sb("Vm", [128, NT, 512], BF16)
        uT = P.sb("uT", [128, 8, 128], BF16)
        uTs = P.sb("uTs", [128, 8, 128], BF16)
        xt = [P.sb("xt0", [128, D]), P.sb("xt1", [128, D])]
        sc = P.sb("sc", [128, 16])
        Praw = P.sb("Praw", [128, 9, 129])
        H32 = P.sb("H32", [128, 4, 64])
        Hbf = P.sb("Hbf", [128, 4, 64], BF16)
        qrT = P.sb("qrT", [64, 4, 128], BF16)
        zsT = P.sb("zsT", [128, 8, 128], BF16)
        ycatT = P.sb("ycatT", [128, 8, 128], BF16)
        mixed = P.sb("mixed", [128, 9, 128])
        vtokf = P.sb("vtokf", [128, 512])
        dbgst = P.sb("dbgst", [128, 1024]) if dbg_d else None

        def dump(nm, v, ncols):
            if nm not in dbg_d:
                return
            P.copy(dbgst[:, 0:ncols], v)
            P.dma(dbg_d[nm], dbgst[:, 0:ncols])

        def rsqrt(out, in_, scale, eps):
            P.act(out, in_, AF.Ln, bias=eps, scale=scale)
            P.act(out, out, AF.Exp, scale=-0.5)

        def bmid(v, shape):
            return V(v.ap.unsqueeze(1).to_broadcast(list(shape)), v.tb)

        def blast(v, shape):
            return V(v.ap.unsqueeze(2).to_broadcast(list(shape)), v.tb)

        ti_glob = 0
        for b in range(NBC):
            P.memset(uT[:, :, 127:128], 0.0)
            P.memset(Praw[:, :, 0:1], 0.0)
            P.memset(H32, 0.0)
            P.memset(Hbf, 0.0)
            for it in range(NT):
                r0 = b * T + it * 128
                tsl = slice(it * 128, (it + 1) * 128)
                xtile = xt[ti_glob % 2]
                ti_glob += 1
                P.dma(xtile, x_d[r0:r0 + 128, :])
                ss = sc[:, 0:1]
                rstd = sc[:, 1:2]
                P.memset(sc[:, 0:4], 0.0, eng="pool")
                xs = V(g[7].ap.bitcast(BF16), g[7].tb)
                sink = V(g[9].ap.bitcast(BF16), g[9].tb)
                P.act(sink, xtile, AF.Square, accum_out=ss)
                rsqrt(rstd, ss, 1.0 / D, NORM_EPS)
                P.ts(xs, xtile, rstd, ALU.mult)
                for c in range(8):
                    P.tr(bank_t[:, c * 128:(c + 1) * 128], xs[:, c * 128:(c + 1) * 128], identb)
                P.copy(uTs[:, :, 0:1], uT[:, :, 127:128], eng="pool")
                P.copy(uT, bank_t.re("p (c t) -> p c t", c=8), eng="act")
                P.copy(uTs[:, :, 1:128], uT[:, :, 0:127], eng="pool")

                if stop == 'a':
                    P.dma(out_d[r0:r0 + 128, :], xtile)
                    continue
                rope = g3(5, 4, 64)
                ropei = V(g[6].ap[0:64, 0:256].bitcast(I32).rearrange("p (a t) -> p a t", a=2), g[6].tb)
                turns, rtmp, sinT, cosT = (rope[:, i, :] for i in range(4))
                P.dma(ropei[:, 0, :], pos_d[b:b + 1, tsl].broadcast_to([64, 128]))
                P.copy(rtmp, ropei[:, 0, :])
                P.ts(turns, rtmp, pp[0:64, PP_INVF:PP_INVF + 1], ALU.mult)
                P.copy(ropei[:, 1, :], turns)
                P.copy(rtmp, ropei[:, 1, :])
                P.tt(rtmp, turns, rtmp, ALU.subtract)
                P.act(sinT, rtmp, AF.Sin, scale=2.0 * math.pi)
                P.ts(turns, turns, 0.25, ALU.add)
                P.copy(ropei[:, 1, :], turns)
                P.copy(rtmp, ropei[:, 1, :])
                P.tt(rtmp, turns, rtmp, ALU.subtract)
                P.act(cosT, rtmp, AF.Sin, scale=2.0 * math.pi)

                if stop == 'rope':
                    P.dma(out_d[r0:r0 + 128, :], xtile)
                    continue
                def proj(outv, col0, ncols, shift=False):
                    for c in range(8):
                        rhs = uTs[:, c, :] if shift else uT[:, c, :]
                        P.mm(outv, W[:, c, col0:col0 + ncols], rhs, start=(c == 0), stop=(c == 7))

                cq = g3(0, 4)[:, 0:3, :]
                sq3 = g3(1, 4)[:, 0:3, :]
                rs3 = g3(2, 4)[:, 0:3, :]
                qtmp = g3(3, 4, 64)
                qtmp2 = g3(4, 4, 64)
                sg = g3(7, 4)
                bk = bank()
                for j in range(3):
                    proj(bk[:, j * 128:(j + 1) * 128], C_CQ + j * 128, 128)
                P.copy(cq, bk[:, 0:384].re("p (j t) -> p j t", j=3), eng="act")
                if stop == 'b1':
                    P.dma(out_d[r0:r0 + 128, :], xtile)
                    continue
                bk = bank()
                proj(bk[0:64, 0:128], C_KR, 64)
                proj(bk[0:64, 128:256], C_KRROT, 64)
                P.tt(qtmp[:, 0, :], bk[0:64, 0:128], cosT, ALU.mult)
                P.tt(qtmp[:, 1, :], bk[0:64, 128:256], sinT, ALU.mult)
                P.tt(krT[:, tsl], qtmp[:, 0, :], qtmp[:, 1, :], ALU.add, eng="pool")
                if stop == 'b2':
                    P.dma(out_d[r0:r0 + 128, :], xtile)
                    continue
                for half in range(2):
                    bk = bank()
                    for j in range(4):
                        proj(bk[:, j * 128:(j + 1) * 128], C_Z + (half * 4 + j) * 128, 128)
                    bk3 = bk.re("p (j t) -> p j t", j=4)
                    P.act(sg, bk3, AF.Sigmoid)
                    P.tt(zsT[:, half * 4:(half + 1) * 4, :], bk3, sg, ALU.mult)
                if stop == 'b3':
                    P.dma(out_d[r0:r0 + 128, :], xtile)
                    continue
                for q_, col0 in ((0, C_R), (1, C_K)):
                    bk = bank()
                    for j in range(4):
                        proj(bk[:, j * 128:(j + 1) * 128], col0 + j * 128, 128)
                    P.copy(Praw[:, q_ * 4:(q_ + 1) * 4, 1:129], bk.re("p (j t) -> p j t", j=4),
                           eng=("act" if q_ == 0 else "dve"))
                bk = bank()
                proj(bk[:, 0:128], C_XW, 128)
                P.copy(Praw[:, 8, 1:129], bk[:, 0:128], eng="act")
                if stop == 'b4':
                    P.dma(out_d[r0:r0 + 128, :], xtile)
                    continue
                bk = bank()
                for c in range(8):
                    P.mm(bk, uT[:, c, :], W[:, c, C_V:C_V + 512], start=(c == 0), stop=False)
                for c in range(8):
                    P.mm(bk, uTs[:, c, :], W[:, c, C_V2:C_V2 + 512], start=False, stop=(c == 7))
                if stop == 'b5':
                    P.dma(out_d[r0:r0 + 128, :], xtile)
                    continue
                P.copy(vtokf, bk, eng="act")
                if stop == 'b6':
                    P.dma(out_d[r0:r0 + 128, :], xtile)
                    continue
                P.copy(vtokb, vtokf, eng="pool")

                if stop == 'b':
                    P.dma(out_d[r0:r0 + 128, :], xtile)
                    continue
                P.tt(sq3, cq, cq, ALU.mult, eng="pool")
                bk = bank()
                P.mm(bk[:, 0:128], onesf, sq3[:, 0, :], start=True, stop=False)
                P.mm(bk[:, 0:128], onesf, sq3[:, 1, :], start=False, stop=True)
                P.mm(bk[:, 128:256], onesf, sq3[:, 2, :], start=True, stop=True)
                rsqrt(rs3[:, 0, :], bk[:, 0:128], 1.0 / 256, NORM_EPS)
                rsqrt(rs3[:, 2, :], bk[:, 128:256], 1.0 / 128, NORM_EPS)
                P.tt(cqn[:, 0:2, :], cq[:, 0:2, :], bmid(rs3[:, 0, :], [128, 2, 128]), ALU.mult)
                P.tt(cqn[:, 2, :], cq[:, 2, :], rs3[:, 2, :], ALU.mult)
                bk = bank()
                for h in range(4):
                    for c in range(2):
                        P.mm(bk[:, h * 128:(h + 1) * 128], Wq[:, c, h * 128:(h + 1) * 128], cqn[:, c, :],
                             start=(c == 0), stop=(c == 1))
                P.copy(qnT, bk.re("p (h t) -> p h t", h=4), eng="act")
                bk = bank()
                bk2 = bank()
                for h in range(4):
                    for c in range(2):
                        P.mm(bk[0:64, h * 128:(h + 1) * 128], Wq[:, c, 512 + h * 64:512 + (h + 1) * 64], cqn[:, c, :],
                             start=(c == 0), stop=(c == 1))
                    for c in range(2):
                        P.mm(bk2[0:64, h * 128:(h + 1) * 128], Wq[:, c, 768 + h * 64:768 + (h + 1) * 64], cqn[:, c, :],
                             start=(c == 0), stop=(c == 1))
                P.tt(qtmp, bk[0:64, :].re("p (h t) -> p h t", h=4), bmid(cosT, [64, 4, 128]), ALU.mult)
                P.tt(qtmp2, bk2[0:64, :].re("p (h t) -> p h t", h=4), bmid(sinT, [64, 4, 128]), ALU.mult)
                P.tt(qrT, qtmp, qtmp2, ALU.add, eng="pool")
                bk = bank()
                for h in range(4):
                    P.mm(bk[:, h * 128:(h + 1) * 128], Wkv[:, h * 128:(h + 1) * 128], cqn[:, 2, :])
                P.copy(KnT[:, :, tsl], bk.re("p (h t) -> p h t", h=4), eng="act")
                bk = bank()
                P.mm(bk, cqn[:, 2, :], Wkv[:, 512:1024])
                P.copy(Vm[:, it, :], bk)

                if stop == 'c':
                    P.dma(out_d[r0:r0 + 128, :], xtile)
                    continue
                rcp, ynm = g[8], g[9]
                nj = it + 1
                pti = 0
                for h in range(4):
                    for jb in range(0, nj, 4):
                        njj = min(4, nj - jb)
                        bk = bank()
                        pt = PT[pti % 2]
                        pti += 1
                        for jj in range(njj):
                            j = jb + jj
                            P.mm(bk[:, jj * 128:(jj + 1) * 128], KnT[:, h, j * 128:(j + 1) * 128], qnT[:, h, :],
                                 start=True, stop=False)
                            P.mm(bk[:, jj * 128:(jj + 1) * 128], krT[:, j * 128:(j + 1) * 128], qrT[:, h, :],
                                 start=False, stop=True)
                        P.act(pt[:, 0:njj, :], bk[:, 0:njj * 128].re("p (j t) -> p j t", j=njj), AF.Exp, scale=SM_SCALE)
                        if jb + njj == nj:
                            P.tt(pt[:, njj - 1, :], pt[:, njj - 1, :], m_iu, ALU.mult, eng="pool")
                        for jj in range(njj):
                            j = jb + jj
                            P.mm(bank_y[:, h * 128:(h + 1) * 128], Vm[:, j, h * 128:(h + 1) * 128], pt[:, jj, :],
                                 start=(j == 0), stop=(j == nj - 1))
                            P.mm(bank_s[:, h * 128:(h + 1) * 128], onesb, pt[:, jj, :],
                                 start=(j == 0), stop=(j == nj - 1))
                P.recip(rcp, bank_s)
                P.tt(ynm, bank_y, rcp, ALU.mult)
                P.tt(ycatT[:, 0:4, :], ynm.re("p (h t) -> p h t", h=4), zsT[:, 0:4, :], ALU.mult, eng="pool")

                if stop == 'd':
                    P.dma(out_d[r0:r0 + 128, :], xtile)
                    continue
                e2, av, kk, kmod, bb, tm1, tm2, Lc, Lx, EL = (g3(i, 4) for i in range(10))
                mu_bc = blast(pp[:, PP_MU:PP_MU + 9], [128, 9, 128])
                P.tt(mixed, Praw[:, :, 0:128], Praw[:, :, 1:129], ALU.subtract, eng="pool")
                P.tt(mixed, mixed, mu_bc, ALU.mult, eng="pool")
                P.tt(mixed, mixed, Praw[:, :, 1:129], ALU.add, eng="pool")
                P.copy(Praw[:, :, 0:1], Praw[:, :, 128:129], eng="pool")
                rm = mixed[:, 0:4, :]
                km = mixed[:, 4:8, :]
                P.act(lor[0:64, :], mixed[0:64, 8, :], AF.Tanh)
                P.copy(lor[64:128, :], mixed[64:128, 8, :], eng="pool")
                if stop == 'e1':
                    P.dma(out_d[r0:r0 + 128, :], xtile)
                    continue
                bkw = bank()
                bka = bank()
                for hp in range(4):
                    P.mm(bkw[:, hp * 128:(hp + 1) * 128], W2A[0:64, hp * 128:(hp + 1) * 128], lor[0:64, :])
                    P.mm(bka[:, hp * 128:(hp + 1) * 128], W2A[64:128, hp * 128:(hp + 1) * 128], lor[64:128, :])

                def pbc(col):
                    return blast(pp[:, col:col + 4], [128, 4, 128])

                P.tt(e2, bkw.re("p (h t) -> p h t", h=4), pbc(PP_W0), ALU.add)
                P.tt(av, bka.re("p (h t) -> p h t", h=4), pbc(PP_A0), ALU.add)
                P.act(e2, e2, AF.Sigmoid)
                P.act(av, av, AF.Sigmoid)
                P.ts(e2, e2, math.exp(-0.5), ALU.mult, eng="pool")
                if stop == 'e2':
                    P.dma(out_d[r0:r0 + 128, :], xtile)
                    continue
                P.tt(kk, km, pbc(PP_KK), ALU.mult, eng="pool")
                P.tt(tm1, kk, kk, ALU.mult, eng="pool")
                bk = bank()
                for hp in range(4):
                    P.mm(bk[:, hp * 128:(hp + 1) * 128], blockones, tm1[:, hp, :])
                rsqrt(tm2, bk.re("p (h t) -> p h t", h=4), 1.0, 1e-24)
                P.tt(kk, kk, tm2, ALU.mult)
                P.tt(tm1, av, pbc(PP_KA), ALU.mult, eng="pool")
                P.tt(tm1, tm1, blast(omka, [128, 4, 128]), ALU.add, eng="pool")
                P.tt(kmod, km, tm1, ALU.mult, eng="pool")
                P.tt(bb, kk, av, ALU.mult, eng="pool")
                P.tt(tm1, rm, kmod, ALU.mult, eng="pool")
                P.tt(prk, tm1, pbc(PP_RK), ALU.mult, eng="pool")
                if stop == 'e3':
                    P.dma(out_d[r0:r0 + 128, :], xtile)
                    continue
                for hp in range(4):
                    P.scan(Lc[:, hp, :], onesf, e2[:, hp, :], 0.0, ALU.mult, ALU.subtract)
                P.tt(Lx, Lc, e2, ALU.add, eng="pool")
                P.act(EL, Lc, AF.Exp)
                P.act(Lx, Lx, AF.Exp)
                P.act(Lc, Lc, AF.Exp, scale=-1.0)
                gC = EL[:, :, 127:128].bc([128, 4, 128])
                P.tt(AR[:, :, 1, :], rm, EL, ALU.mult)
                P.stt(AR[:, :, 0, :], kk, -1.0, Lx, ALU.mult, ALU.mult)
                P.tt(tm1, bb, Lc, ALU.mult, eng="pool")
                P.tt(tm2, kmod, Lc, ALU.mult, eng="pool")
                P.copy(Bt, tm1, eng="pool")
                P.copy(Kt, tm2, eng="pool")
                P.tt(bhat, tm1, gC, ALU.mult)
                P.tt(khat, tm2, gC, ALU.mult)
                if stop == 'e4':
                    P.dma(out_d[r0:r0 + 128, :], xtile)
                    continue
                for hp in range(4):
                    P.tr(bank_t[:, hp * 128:(hp + 1) * 128], bhat[:, hp, :], identb)
                    P.tr(bank_t[:, (4 + hp) * 128:(5 + hp) * 128], khat[:, hp, :], identb)
                P.copy(BKtok, bank_t.re("p (c t) -> p c t", c=8), eng="act")
                if stop == 'e5':
                    P.dma(out_d[r0:r0 + 128, :], xtile)
                    continue
                msl = bmid(m_sl, [128, 4, 128])
                idb = bmid(identb, [128, 4, 128])
                cut = False
                for hg in range(2):
                    for hl in range(4):
                        h = hg * 4 + hl
                        hp, pb = h // 2, (h % 2) * 64
                        bk = bank()
                        rhs = AR[pb:pb + 64, hp, :, :]
                        P.mm(bk[:, 0:256], Bt[pb:pb + 64, hp, :], rhs)
                        P.mm(bk[:, 256:512], Kt[pb:pb + 64, hp, :], rhs)
                        P.tt(NK[:, hl, :, :], bk.re("p (a t) -> p a t", a=4), m4, ALU.mult)
                    if stop == 'h0' + ('@1' if hg else ''):
                        cut = True
                        break
                    bke = bank()
                    bko = bank()
                    for hl in range(4):
                        h = hg * 4 + hl
                        hp, pb = h // 2, (h % 2) * 64
                        bsel = bke if hl % 2 == 0 else bko
                        P.mm(bsel[:, (hl // 2) * 128:(hl // 2 + 1) * 128], AR[pb:pb + 64, hp, 0, :], Bt[pb:pb + 64, hp, :])
                    msl2 = bmid(m_sl, [128, 2, 128])
                    Am4 = Am.re("p (a b) t -> p a b t", b=2)
                    P.tt(Am4[:, :, 0, :], bke[:, 0:256].re("p (h t) -> p h t", h=2), msl2, ALU.mult)
                    P.tt(Am4[:, :, 1, :], bko[:, 0:256].re("p (h t) -> p h t", h=2), msl2, ALU.mult)
                    if stop == 'h1' + ('@1' if hg else ''):
                        cut = True
                        break
                    P.tt(Mt, NK[:, :, 0, :], idb, ALU.add, eng="pool")
                    qc, qtc = NK[:, :, 0, :], Am
                    for k in range(1, 7):
                        bq = bank() if k < 6 else None
                        bqt = bank()
                        for hl in range(4):
                            if bq is not None:
                                P.mm(bq[:, hl * 128:(hl + 1) * 128], qtc[:, hl, :], qc[:, hl, :])
                            P.mm(bqt[:, hl * 128:(hl + 1) * 128], qc[:, hl, :], qtc[:, hl, :])
                        if bq is not None:
                            P.copy(Qa, bq.re("p (h t) -> p h t", h=4), eng="act")
                        P.copy(QTa, bqt.re("p (h t) -> p h t", h=4))
                        bp = bank()
                        for hl in range(4):
                            P.mm(bp[:, hl * 128:(hl + 1) * 128], QTa[:, hl, :], Mt[:, hl, :])
                        P.tt(Mt, bp.re("p (h t) -> p h t", h=4), Mt, ALU.add)
                        qc, qtc = Qa, QTa
                    if stop == 'h2' + ('@1' if hg else ''):
                        cut = True
                        break
                    if b == 0 and it == min(1, NT - 1) and hg == 0:
                        dump("mt", Mt.re("p c t -> p (c t)"), 512)
                    bk = bank()
                    for hl in range(4):
                        h = hg * 4 + hl
                        hp, pb = h // 2, (h % 2) * 64
                        P.mm(bk[:, hl * 64:(hl + 1) * 64], AR[pb:pb + 64, hp, 0, :], Hbf[pb:pb + 64, hp, :], start=True, stop=False)
                        P.mm(bk[:, hl * 64:(hl + 1) * 64], NK[:, hl, 2, :], vtokb[:, h * 64:(h + 1) * 64], start=False, stop=True)
                    if stop == 'h3' + ('@1' if hg else ''):
                        cut = True
                        break
                    P.copy(Xb, bk[:, 0:256].re("p (h v) -> p h v", h=4), eng="act")
                    bk = bank()
                    for hl in range(4):
                        P.mm(bk[:, hl * 64:(hl + 1) * 64], Mt[:, hl, :], Xb[:, hl, :])
                    if stop == 'h4' + ('@1' if hg else ''):
                        cut = True
                        break
                    P.copy(Ub, bk[:, 0:256].re("p (h v) -> p h v", h=4))
                    for hl in range(4):
                        h = hg * 4 + hl
                        hp, pb = h // 2, (h % 2) * 64
                        P.mm(bank_x[:, h * 64:(h + 1) * 64], AR[pb:pb + 64, hp, 1, :], Hbf[pb:pb + 64, hp, :], start=True, stop=False)
                        P.mm(bank_x[:, h * 64:(h + 1) * 64], NK[:, hl, 1, :], Ub[:, hl, :], start=False, stop=False)
                        P.mm(bank_x[:, h * 64:(h + 1) * 64], NK[:, hl, 3, :], vtokb[:, h * 64:(h + 1) * 64], start=False, stop=True)
                    bk = bank()
                    for hl in range(4):
                        h = hg * 4 + hl
                        hp = h // 2
                        P.mm(bk[:, hl * 64:(hl + 1) * 64], BKtok[:, hp, :], Ub[:, hl, :], start=True, stop=False)
                        P.mm(bk[:, hl * 64:(hl + 1) * 64], BKtok[:, 4 + hp, :], vtokb[:, h * 64:(h + 1) * 64], start=False, stop=True)
                    if stop == 'h5' + ('@1' if hg else ''):
                        cut = True
                        break
                    Hs = H32[:, 2 * hg:2 * hg + 2, :]
                    P.tt(Hs, Hs, EL[:, 2 * hg:2 * hg + 2, 127:128].bc([128, 2, 64]), ALU.mult)
                    bk4 = bk[:, 0:256].re("p (a b v) -> p a b v", a=2, b=2)
                    P.tt(Hs[0:64], bk4[0:64, :, 0, :], Hs[0:64], ALU.add)
                    P.tt(Hs[64:128], bk4[64:128, :, 1, :], Hs[64:128], ALU.add)
                    P.copy(Hbf[:, 2 * hg:2 * hg + 2, :], Hs, eng="pool")
                    if stop == 'h6' + ('@1' if hg else ''):
                        cut = True
                        break
                if cut:
                    P.dma(out_d[r0:r0 + 128, :], xtile)
                    continue
                if stop == 'e6':
                    P.dma(out_d[r0:r0 + 128, :], xtile)
                    continue
                bkr = bank()
                for hp in range(4):
                    P.mm(bkr[:, hp * 2:hp * 2 + 2], prk[:, hp, :], headind)
                if stop == 'e7':
                    P.dma(out_d[r0:r0 + 128, :], xtile)
                    continue
                gst = sc[:, 8:16]
                yc = V(g[0].ap.rearrange("p (h v) -> p h v", h=8), g[0].tb)
                ysq = V(g[1].ap.rearrange("p (h v) -> p h v", h=8), g[1].tb)
                st1 = V(g[2].ap[:, 0:32].rearrange("p (a h) -> p a h", a=4), g[2].tb)
                y3 = bank_x.re("p (h v) -> p h v", h=8)
                P.rsum(st1[:, 0, :], y3)
                P.ts(st1[:, 0, :], st1[:, 0, :], 1.0 / 64, ALU.mult)
                P.tt(yc, y3, blast(st1[:, 0, :], [128, 8, 64]), ALU.subtract)
                P.tt(ysq, yc, yc, ALU.mult, eng="pool")
                P.rsum(st1[:, 1, :], ysq)
                rsqrt(st1[:, 2, :], st1[:, 1, :], 1.0 / 64, GN_EPS)
                P.copy(st1[:, 3, :], bkr[:, 0:8])
                P.tt(yc, yc, blast(st1[:, 2, :], [128, 8, 64]), ALU.mult)
                yc2 = yc.re("p h v -> p (h v)")
                P.tt(yc2, yc2, rows[:, 0:512], ALU.mult, eng="pool")
                P.tt(yc2, yc2, rows[:, 512:1024], ALU.add, eng="pool")
                P.tt(ysq, vtokf.re("p (h v) -> p h v", h=8), blast(st1[:, 3, :], [128, 8, 64]), ALU.mult, eng="pool")
                P.tt(yrb, yc2, ysq.re("p h v -> p (h v)"), ALU.add)
                for c in range(4):
                    P.tr(bank_t[:, c * 128:(c + 1) * 128], yrb[:, c * 128:(c + 1) * 128], identb)
                P.tt(ycatT[:, 4:8, :], bank_t[:, 0:512].re("p (c t) -> p c t", c=4), zsT[:, 4:8, :], ALU.mult)

                if stop == 'e':
                    P.dma(out_d[r0:r0 + 128, :], xtile)
                    continue
                bo = [bank(), bank()]
                for n in range(2):
                    for c in range(8):
                        P.mm(bo[n], ycatT[:, c, :], Wout[:, c, n * 512:(n + 1) * 512], start=(c == 0), stop=(c == 7))
                P.act(g[2], bo[0], AF.Square, accum_out=sc[:, 2:3])
                P.act(g[3], bo[1], AF.Square, accum_out=sc[:, 3:4])
                P.tt(sc[:, 4:5], sc[:, 2:3], sc[:, 3:4], ALU.add)
                rsqrt(sc[:, 5:6], sc[:, 4:5], 1.0 / D, NORM_EPS)
                for n in range(2):
                    P.stt(g[4 + n], bo[n], sc[:, 5:6], rows[:, 1024 + n * 512:1024 + (n + 1) * 512], ALU.mult, ALU.mult)
                    P.tt(xtile[:, n * 512:(n + 1) * 512], xtile[:, n * 512:(n + 1) * 512], g[4 + n], ALU.add, eng="pool")
                P.dma(out_d[r0:r0 + 128, :], xtile)

                if b == 0 and it == min(1, NT - 1):
                    dump("ycat", ycatT.re("p c t -> p (c t)"), 1024)
                    dump("yrw", yrb, 512)
                    dump("kk", kk.re("p c t -> p (c t)"), 512)
                    dump("e2", e2.re("p c t -> p (c t)"), 512)
                    dump("H", H32.re("p c t -> p (c t)"), 256)

        fw = [xt[0].tb, xt[1].tb]
        if dbgst is not None:
            fw.append(dbgst.tb)
        P.emit(final_wait=fw)
        build.stats = P.stats
    return nc


def host_params(inp):
    f = np.float32
    w_in = np.asarray(inp["w_in"][0], f)
    kr = w_in[:, 384:448]
    krrot = np.concatenate([kr[:, 32:64], kr[:, 0:32]], axis=1)
    win = np.ascontiguousarray(np.concatenate([w_in, krrot], axis=1))
    wuq = np.asarray(inp["mla_w_uq"][0], f).reshape(256, 4, 192)
    nope = wuq[:, :, 0:128].reshape(256, 512)
    rp = wuq[:, :, 128:192]
    rot = np.concatenate([rp[:, :, 32:64], rp[:, :, 0:32]], axis=2)
    wuq_l = np.ascontiguousarray(np.concatenate([nope, rp.reshape(256, 256), rot.reshape(256, 256)], axis=1))
    wukv = np.asarray(inp["mla_w_ukv"][0], f).reshape(128, 4, 256)
    wukv_l = np.ascontiguousarray(np.concatenate([wukv[:, :, 0:128].reshape(128, 512),
                                                  wukv[:, :, 128:256].reshape(128, 512)], axis=1))
    w2a = np.ascontiguousarray(np.concatenate([np.asarray(inp["rw_w2"][0], f), np.asarray(inp["rw_a2"][0], f)], axis=0))
    wout = np.ascontiguousarray(np.asarray(inp["w_out"][0], f))
    pp = np.zeros((128, NPP), f)

    def colmajor(v, n):
        return np.asarray(v, f).reshape(n, 128).T

    pp[:, PP_GPRE:PP_GPRE + 8] = colmajor(inp["norm_pre_g"][0], 8)
    pp[:, PP_GQ:PP_GQ + 2] = colmajor(inp["mla_q_norm_g"][0], 2)
    pp[:, PP_GKV:PP_GKV + 1] = colmajor(inp["mla_kv_norm_g"][0], 1)
    pp[:, PP_W0:PP_W0 + 4] = colmajor(inp["rw_w0"][0], 4)
    pp[:, PP_A0:PP_A0 + 4] = colmajor(inp["rw_a0"][0], 4)
    pp[:, PP_KK:PP_KK + 4] = colmajor(inp["rw_k_k"][0], 4)
    pp[:, PP_KA:PP_KA + 4] = colmajor(inp["rw_k_a"][0], 4)
    pp[:, PP_RK:PP_RK + 4] = colmajor(np.asarray(inp["rw_r_k"][0]).reshape(512), 4)
    invf = (10000.0 ** (-np.arange(0, 64, 2, dtype=np.float32) / 64)).astype(f)
    invf_turn = (np.concatenate([invf, invf]).astype(np.float64) / (2 * np.pi)).astype(f)
    pp[0:64, PP_INVF] = invf_turn
    mu = np.asarray(inp["rw_mu"][0], f)
    pp[:, PP_MU:PP_MU + 4] = colmajor(mu[0:512], 4)
    pp[:, PP_MU + 4:PP_MU + 8] = colmajor(mu[512:1024], 4)
    pp[:, PP_MU + 8] = mu[1536:1664]
    rows = np.concatenate([mu[1024:1536], np.asarray(inp["rw_ln_g"][0], f), np.asarray(inp["rw_ln_b"][0], f),
                           np.asarray(inp["norm_post_g"][0], f)]).reshape(1, NROWS).astype(f)
    return {"win": win, "wuq": wuq_l, "wukv": wukv_l, "w2a": w2a, "wout": wout, "pp": pp, "rows": rows}


def kernel(**inp):
    x = np.asarray(inp["x"], np.float32)
    pos = np.asarray(inp["positions"], np.int32)
    B, T, _ = x.shape
    nbc = B // N_CORES
    shared = host_params(inp)
    nc = build(nbc, T // 128)
    in_maps = []
    for c in range(N_CORES):
        m = dict(shared)
        m["x"] = np.ascontiguousarray(x[c * nbc:(c + 1) * nbc].reshape(nbc * T, D))
        m["pos"] = np.ascontiguousarray(pos[c * nbc:(c + 1) * nbc])
        in_maps.append(m)
    res = run_bass_kernel_spmd(nc, in_maps, core_ids=list(range(N_CORES)))
    out = np.concatenate([r["out"].reshape(nbc, T, D) for r in res.results], axis=0)
    return out.astype(np.float32)
```

```python
import math
import os
import numpy as np
import concourse.bass as bass
import concourse.mybir as mybir
from concourse.bass_utils import run_bass_kernel_spmd
from contextlib import ExitStack

F32 = mybir.dt.float32
BF16 = mybir.dt.bfloat16
I32 = mybir.dt.int32
AF = mybir.ActivationFunctionType
ALU = mybir.AluOpType
AX = mybir.AxisListType

SAME_ENGINE_SYNC = True
N_CORES = 8
T_FULL = 2048
D = 1024


class TB:
    __slots__ = ("name", "last_w", "readers", "dma_sem", "dma_cnt", "inherit")

    def __init__(self, name, inherit=None):
        self.name = name
        self.last_w = []
        self.readers = {}
        self.dma_sem = None
        self.dma_cnt = 0
        self.inherit = inherit


class V:
    __slots__ = ("ap", "tb")

    def __init__(self, ap, tb):
        self.ap = ap
        self.tb = tb

    def __getitem__(self, idx):
        return V(self.ap[idx], self.tb)

    def bc(self, shape):
        return V(self.ap.to_broadcast(list(shape)), self.tb)

    def re(self, s, **kw):
        return V(self.ap.rearrange(s, **kw), self.tb)


class Prog:
    ENGS = ("pe", "act", "dve", "pool", "sp")

    def __init__(self, nc, es):
        self.nc = nc
        self.es = es
        self.main = []
        self.cur = self.main
        self.ops = []
        self.signal = set()

    def sb(self, name, shape, dt=F32):
        t = self.es.enter_context(self.nc.sbuf_tensor("s_" + name, list(shape), dt))
        return V(t[:], TB(name))

    def ps(self, name, shape, dt=F32):
        t = self.es.enter_context(self.nc.psum_tensor("p_" + name, list(shape), dt))
        return V(t[:], TB(name))

    def add(self, eng, emit, reads=(), writes=(), dma_tb=None):
        rt = [r.tb if isinstance(r, V) else r for r in reads]
        wt = [w.tb if isinstance(w, V) else w for w in writes]
        self.cur.append((eng, emit, rt, wt, dma_tb))

    def start_seg(self):
        self.cur = []

    def end_seg(self):
        seg, self.cur = self.cur, self.main
        return seg

    def cut(self):
        self.cur.append(None)

    def merge(self, sa, sb):
        def units(seg):
            out, u = [], []
            for r in seg:
                if r is None:
                    if u:
                        out.append(u)
                    u = []
                else:
                    u.append(r)
            if u:
                out.append(u)
            return out
        ua, ub = units(sa), units(sb)
        na, nb = len(ua), len(ub)
        ia = ib = 0
        while ia < na or ib < nb:
            if ib >= nb or (ia < na and ia * nb <= ib * na):
                self.main.extend(ua[ia])
                ia += 1
            else:
                self.main.extend(ub[ib])
                ib += 1

    @staticmethod
    def _touch(tb):
        if tb.inherit is not None:
            p = tb.inherit
            tb.inherit = None
            Prog._touch(p)
            tb.last_w = list(p.last_w)
            tb.readers = dict(p.readers)

    def finalize(self):
        for (eng, emit, rt, wt, dma_tb) in self.main:
            idx = len(self.ops)
            deps = []
            for tb in rt:
                self._touch(tb)
                deps.extend(tb.last_w)
            for tb in wt:
                self._touch(tb)
                deps.extend(tb.last_w)
                deps.extend(tb.readers.values())
            if dma_tb is not None:
                dma_tb.dma_cnt += 1
                me = ("d", dma_tb, 16 * dma_tb.dma_cnt)
                rkey = ("d", id(dma_tb))
            else:
                me = ("e", eng, idx)
                rkey = ("e", eng)
            for tb in rt:
                tb.readers[rkey] = me
            for tb in wt:
                tb.last_w = [me]
                tb.readers = {}
            seen = set()
            d2 = []
            for d in deps:
                k = (d[0], id(d[1]) if d[0] == "d" else d[1], d[2])
                if k in seen:
                    continue
                seen.add(k)
                d2.append(d)
                if d[0] == "e" and not (d[1] == eng and (eng == "pe" or not SAME_ENGINE_SYNC)):
                    self.signal.add(d[2])
            self.ops.append((eng, emit, d2, dma_tb))

    def dma(self, out, in_, eng="sp", reads=(), writes=()):
        sbv = out if isinstance(out, V) else in_
        o = out.ap if isinstance(out, V) else out
        i = in_.ap if isinstance(in_, V) else in_
        r = list(reads) + ([in_] if isinstance(in_, V) else [])
        w = list(writes) + ([out] if isinstance(out, V) else [])
        return self.add(eng, lambda e: e.dma_start(out=o, in_=i), r, w, dma_tb=sbv.tb)

    def mm(self, out, lhsT, rhs, start=True, stop=True):
        return self.add("pe", lambda e: e.matmul(out.ap, lhsT.ap, rhs.ap, start=start, stop=stop),
                        [lhsT, rhs], [out])

    def tr(self, out, in_, ident):
        return self.add("pe", lambda e: e.transpose(out.ap, in_.ap, ident.ap), [in_, ident], [out])

    def act(self, out, in_, func, bias=None, scale=1.0, accum_out=None):
        reads = [in_]
        kw = {}
        if isinstance(bias, V):
            reads.append(bias)
            kw["bias"] = bias.ap
        elif bias is not None:
            kw["bias"] = bias
        if isinstance(scale, V):
            reads.append(scale)
            kw["scale"] = scale.ap
        else:
            kw["scale"] = scale
        writes = [out]
        if accum_out is not None:
            writes.append(accum_out)
            kw["accum_out"] = accum_out.ap
        return self.add("act", lambda e: e.activation(out.ap, in_.ap, func, **kw), reads, writes)

    def tt(self, out, a, b, op, eng="dve"):
        return self.add(eng, lambda e: e.tensor_tensor(out.ap, a.ap, b.ap, op), [a, b], [out])

    def ts(self, out, a, s1, op0, s2=None, op1=None, eng="dve"):
        reads = [a]
        x1, x2 = s1, s2
        if isinstance(s1, V):
            reads.append(s1)
            x1 = s1.ap
        if isinstance(s2, V):
            reads.append(s2)
            x2 = s2.ap
        kw = {}
        if op1 is not None:
            kw["op1"] = op1
        return self.add(eng, lambda e: e.tensor_scalar(out.ap, a.ap, x1, x2, op0, **kw), reads, [out])

    def stt(self, out, a, s, b, op0, op1, eng="dve"):
        reads = [a, b]
        x = s
        if isinstance(s, V):
            reads.append(s)
            x = s.ap
        return self.add(eng, lambda e: e.scalar_tensor_tensor(out.ap, a.ap, x, b.ap, op0, op1), reads, [out])

    def copy(self, out, in_, eng="dve"):
        if eng == "act":
            return self.add("act", lambda e: e.copy(out.ap, in_.ap), [in_], [out])
        return self.add(eng, lambda e: e.tensor_copy(out.ap, in_.ap), [in_], [out])

    def memset(self, out, val, eng="dve"):
        return self.add(eng, lambda e: e.memset(out.ap, val), [], [out])

    def recip(self, out, in_):
        return self.add("dve", lambda e: e.reciprocal(out.ap, in_.ap), [in_], [out])

    def rsum(self, out, in_, eng="dve"):
        return self.add(eng, lambda e: e.tensor_reduce(out.ap, in_.ap, AX.X, ALU.add), [in_], [out])

    def scan(self, out, d0, d1, init, op0, op1):
        return self.add("dve", lambda e: e.tensor_tensor_scan(out.ap, d0.ap, d1.ap, init, op0, op1), [d0, d1], [out])

    def aselect(self, out, in_, pattern, cmp, fill, base, cm):
        return self.add("pool", lambda e: e.affine_select(out.ap, in_.ap, pattern, cmp, fill, base=base,
                                                          channel_multiplier=cm), [in_], [out])

    def emit(self, final_wait=()):
        nc, es = self.nc, self.es
        self.finalize()
        ordn = {}
        cnt = {e: 0 for e in self.ENGS}
        for i, (eng, _, _, dma_tb) in enumerate(self.ops):
            if dma_tb is None and i in self.signal:
                cnt[eng] += 1
                ordn[i] = cnt[eng]
        esem = {e: es.enter_context(nc.semaphore("sem_" + e)) for e in self.ENGS}
        for (eng, _, _, dma_tb) in self.ops:
            if dma_tb is not None and dma_tb.dma_sem is None:
                dma_tb.dma_sem = es.enter_context(nc.semaphore("dsem_%s" % dma_tb.name))
        per_eng = {e: [] for e in self.ENGS}
        for i, op in enumerate(self.ops):
            per_eng[op[0]].append(i)
        self.stats = {e: [len(per_eng[e]), cnt[e], 0] for e in self.ENGS}
        block = es.enter_context(nc.Block())
        ops, stats = self.ops, self.stats

        def run(engname, e):
            waited = {}
            for i in per_eng[engname]:
                _, emit, deps, dma_tb = ops[i]
                need = {}
                for d in deps:
                    if d[0] == "e":
                        if d[1] == engname and (engname == "pe" or not SAME_ENGINE_SYNC):
                            continue
                        sem, val, key = esem[d[1]], ordn[d[2]], "e" + d[1]
                    else:
                        sem, val, key = d[1].dma_sem, d[2], id(d[1])
                    if waited.get(key, 0) >= val:
                        continue
                    if key not in need or need[key][1] < val:
                        need[key] = (sem, val)
                for key, (sem, val) in need.items():
                    e.wait_ge(sem, val)
                    waited[key] = val
                    stats[engname][2] += 1
                ins = emit(e)
                if dma_tb is not None:
                    ins.then_inc(dma_tb.dma_sem, 16)
                elif i in ordn:
                    ins.then_inc(esem[engname], 1)
            if engname == "sp":
                for tb in final_wait:
                    if tb.dma_sem is not None:
                        e.wait_ge(tb.dma_sem, 16 * tb.dma_cnt)

        @block.tensor
        def _(e):
            run("pe", e)

        @block.scalar
        def _(e):
            run("act", e)

        @block.vector
        def _(e):
            run("dve", e)

        @block.gpsimd
        def _(e):
            run("pool", e)

        @block.sync
        def _(e):
            run("sp", e)


WC = 3136 + 64 + 512
C_CQ, C_CKV, C_KR, C_R, C_K, C_V, C_XW, C_Z = 0, 256, 384, 448, 960, 1472, 1984, 2112
C_KRROT, C_V2 = 3136, 3200
PP_GPRE, PP_GQ, PP_GKV, PP_W0, PP_A0, PP_KK, PP_KA, PP_RK, PP_INVF, PP_MU = 0, 8, 10, 11, 15, 19, 23, 27, 31, 32
NPP = 41
NROWS = 2560
GN_EPS = 64e-5
NORM_EPS = 1e-6
SM_SCALE = 192.0 ** -0.5


def build(NBC, NT, dbg_names=(), stop=None):
    nc = bass.Bass("TRN2", target_bir_lowering=False)

    def din(name, shape, dt=F32):
        return nc.dram_tensor(name, list(shape), dt, kind="ExternalInput").ap()

    T = NT * 128
    x_d = din("x", [NBC * T, D])
    pos_d = din("pos", [NBC, T], I32)
    win_d = din("win", [D, 3200])
    wuq_d = din("wuq", [256, 1024])
    wukv_d = din("wukv", [128, 1024])
    w2a_d = din("w2a", [128, 512])
    wout_d = din("wout", [D, D])
    pp_d = din("pp", [128, NPP])
    rows_d = din("rows", [1, NROWS])
    out_d = nc.dram_tensor("out", [NBC * T, D], F32, kind="ExternalOutput").ap()
    dbg_d = {}
    for (nm, shape) in dbg_names:
        dbg_d[nm] = nc.dram_tensor("dbg_" + nm, list(shape), F32, kind="ExternalOutput").ap()

    with ExitStack() as es:
        P = Prog(nc, es)
        ident = P.sb("ident", [128, 128])
        identb = P.sb("identb", [128, 128], BF16)
        onesf = P.sb("onesf", [128, 128])
        onesb = P.sb("onesb", [128, 128], BF16)
        blockones = P.sb("blockones", [128, 128])
        headind = P.sb("headind", [128, 2], BF16)
        m_su = P.sb("m_su", [128, 128], BF16)
        m_iu = P.sb("m_iu", [128, 128], BF16)
        m_sl = P.sb("m_sl", [128, 128], BF16)
        mask4 = P.sb("mask4", [128, 4, 128], BF16)
        P.memset(ident, 0.0, eng="pool")
        P.aselect(ident, ident, [[-1, 128]], ALU.not_equal, 1.0, 0, 1)
        P.copy(identb, ident)
        P.memset(onesf, 1.0)
        P.memset(onesb, 1.0)
        P.memset(blockones, 0.0)
        P.memset(blockones[0:64, 0:64], 1.0)
        P.memset(blockones[64:128, 64:128], 1.0)
        P.memset(headind, 0.0)
        P.memset(headind[0:64, 0:1], 1.0)
        P.memset(headind[64:128, 1:2], 1.0)
        for m in (m_su, m_iu, m_sl):
            P.memset(m, 1.0, eng="pool")
        P.aselect(m_su, m_su, [[1, 128]], ALU.is_gt, 0.0, 0, -1)
        P.aselect(m_iu, m_iu, [[1, 128]], ALU.is_ge, 0.0, 0, -1)
        P.aselect(m_sl, m_sl, [[-1, 128]], ALU.is_gt, 0.0, 0, 1)
        P.copy(mask4[:, 0, :], m_su)
        P.copy(mask4[:, 1, :], m_iu)
        P.copy(mask4[:, 2, :], m_su)
        P.copy(mask4[:, 3, :], m_iu)
        m4 = mask4

        pp = P.sb("pp", [128, NPP])
        P.dma(pp, pp_d)
        rows = P.sb("rows", [128, 2048])
        P.dma(rows, rows_d[:, 512:2560].broadcast_to([128, 2048]))
        der = P.sb("der", [128, 32])
        gneg = der[:, 0:8]
        gqneg = der[:, 8:10]
        omka = der[:, 10:14]
        P.ts(gneg, pp[:, PP_GPRE:PP_GPRE + 8], -1.0, ALU.mult)
        P.ts(gqneg, pp[:, PP_GQ:PP_GQ + 2], -1.0, ALU.mult)
        P.ts(omka, pp[:, PP_KA:PP_KA + 4], -1.0, ALU.mult, 1.0, ALU.add)
        Gt = es.enter_context(nc.sbuf_tensor("s_G", [128, 10, 512], F32))
        g = [V(Gt[:][:, i, :], TB("G%d" % i)) for i in range(10)]

        def g3(i, a, parts=128):
            return V(g[i].ap[0:parts, :].rearrange("p (a t) -> p a t", a=a), g[i].tb)

        muv, omuv = g[8], g[9]
        P.dma(muv, rows_d[:, 0:512].broadcast_to([128, 512]))
        P.ts(omuv, muv, -1.0, ALU.mult, 1.0, ALU.add)

        W = P.sb("W", [128, 8, WC], BF16)
        stg = [P.sb("stg0", [128, 3200]), P.sb("stg1", [128, 3200])]
        for c in range(8):
            s = stg[c % 2]
            P.dma(s, win_d[c * 128:(c + 1) * 128, :])
            gg = pp[:, PP_GPRE + c:PP_GPRE + c + 1]
            gn = gneg[:, c:c + 1]
            e1 = "dve" if c % 2 == 0 else "pool"
            e2_ = "pool" if c % 2 == 0 else "dve"
            P.ts(W[:, c, 0:C_V], s[:, 0:C_V], gg, ALU.mult, eng=e1)
            P.ts(W[:, c, C_XW:3136], s[:, C_XW:3136], gg, ALU.mult, eng=e2_)
            P.ts(W[:, c, C_KRROT:C_KRROT + 32], s[:, 3136:3168], gn, ALU.mult, eng=e1)
            P.ts(W[:, c, C_KRROT + 32:C_KRROT + 64], s[:, 3168:3200], gg, ALU.mult, eng=e1)
            P.stt(W[:, c, C_V:C_V + 512], s[:, C_V:C_V + 512], gg, omuv, ALU.mult, ALU.mult)
            P.stt(W[:, c, C_V2:C_V2 + 512], s[:, C_V:C_V + 512], gg, muv, ALU.mult, ALU.mult)
        Wq = P.sb("Wq", [128, 2, 1024], BF16)
        for c in range(2):
            s = stg[c % 2]
            P.dma(s[:, 0:1024], wuq_d[c * 128:(c + 1) * 128, :])
            gg = pp[:, PP_GQ + c:PP_GQ + c + 1]
            gn = gqneg[:, c:c + 1]
            P.ts(Wq[:, c, 0:768], s[:, 0:768], gg, ALU.mult)
            rot_o = Wq[:, c, 768:1024].re("p (h r) -> p h r", h=4)
            rot_in = s[:, 768:1024].re("p (h r) -> p h r", h=4)
            P.ts(rot_o[:, :, 0:32], rot_in[:, :, 0:32], gn, ALU.mult)
            P.ts(rot_o[:, :, 32:64], rot_in[:, :, 32:64], gg, ALU.mult)
        Wkv = P.sb("Wkv", [128, 1024], BF16)
        s = stg[0]
        P.dma(s[:, 0:1024], wukv_d)
        P.ts(Wkv, s[:, 0:1024], pp[:, PP_GKV:PP_GKV + 1], ALU.mult)
        W2A = P.sb("W2A", [128, 512], BF16)
        s = stg[1]
        P.dma(s[:, 0:512], w2a_d)
        P.copy(W2A, s[:, 0:512])
        Wout = P.sb("Wout", [128, 8, 1024], BF16)
        for c in range(8):
            s = stg[c % 2]
            P.dma(s[:, 0:1024], wout_d[c * 128:(c + 1) * 128, :])
            P.copy(Wout[:, c, :], s[:, 0:1024], eng=("dve" if c % 2 == 0 else "pool"))

        carve_off = [0, 0]

        def carve(si, name, shape, dt):
            esz = 4 if dt in (F32, I32) else 2
            n = 1
            for d_ in shape[1:]:
                n *= d_
            nb = n * esz
            c0 = carve_off[si] // 4
            carve_off[si] += nb
            assert carve_off[si] <= 12800, (name, carve_off)
            ap = stg[si].ap[0:shape[0], c0:c0 + nb // 4]
            if dt != F32:
                ap = ap.bitcast(dt)
            if len(shape) == 3:
                ap = ap.rearrange("p (a b) -> p a b", a=shape[1])
            elif len(shape) == 4:
                ap = ap.rearrange("p (a b c) -> p a b c", a=shape[1], b=shape[2])
            return V(ap, TB(name, inherit=stg[si].tb))

        AR = carve(0, "AR", [128, 4, 2, 128], BF16)
        Bt = carve(0, "Bt", [128, 4, 128], BF16)
        Kt = carve(0, "Kt", [128, 4, 128], BF16)
        bhat = carve(0, "bhat", [128, 4, 128], BF16)
        khat = carve(0, "khat", [128, 4, 128], BF16)
        BKtok = carve(0, "BKtok", [128, 8, 128], BF16)
        NK = carve(0, "NK", [128, 4, 4, 128], BF16)
        Am = P.sb("Am", [128, 4, 128], BF16) if os.environ.get("AMSB") else carve(1, "Am", [128, 4, 128], BF16)
        Qa = carve(1, "Qa", [128, 4, 128], BF16)
        QTa = carve(1, "QTa", [128, 4, 128], BF16)
        Mt = carve(1, "Mt", [128, 4, 128], BF16)
        Xb = carve(1, "Xb", [128, 4, 64], BF16)
        Ub = carve(1, "Ub", [128, 4, 64], BF16)
        yrb = carve(1, "yrb", [128, 512], BF16)
        prk = carve(1, "prk", [128, 4, 128], BF16)
        lor = carve(1, "lor", [128, 128], BF16)
        vtokb = carve(1, "vtokb", [128, 512], BF16)
        PT = [carve(1, "PT0", [128, 4, 128], BF16), carve(1, "PT1", [128, 4, 128], BF16)]
        cqn = carve(1, "cqn", [128, 3, 128], BF16)
        qnT = carve(1, "qnT", [128, 4, 128], BF16)

        rot_banks = [P.ps("bank%d" % i, [128, 512]) for i in range(4)]
        bank_y = P.ps("bank_y", [128, 512])
        bank_s = P.ps("bank_s", [128, 512])
        bank_x = P.ps("bank_x", [128, 512])
        bank_t = P.ps("bank_t", [128, 1024], BF16)
        rot_i = [0]

        def bank():
            bkk = rot_banks[rot_i[0] % 4]
            rot_i[0] += 1
            return bkk

        KnT = P.sb("KnT", [128, 4, T], BF16)
        krT = P.sb("krT", [64, T], BF16)
        Vm = P.sb("Vm", [128, NT, 512], BF16)
        uT = P.sb("uT", [128, 8, 128], BF16)
        uTs = P.sb("uTs", [128, 8, 128], BF16)
        xt = [P.sb("xt0", [128, D]), P.sb("xt1", [128, D])]
        sc = P.sb("sc", [128, 16])
        Praw = P.sb("Praw", [128, 9, 129])
        H32 = P.sb("H32", [128, 4, 64])
        Hbf = P.sb("Hbf", [128, 4, 64], BF16)
        qrT = P.sb("qrT", [64, 4, 128], BF16)
        zsT = P.sb("zsT", [128, 8, 128], BF16)
        ycatT = P.sb("ycatT", [128, 8, 128], BF16)
        mixed = P.sb("mixed", [128, 9, 128])
        vtokf = P.sb("vtokf", [128, 512])
        atmp = P.sb("atmp", [128, 2, 128])
        dbgst = P.sb("dbgst", [128, 1024]) if dbg_d else None

        def dump(nm, v, ncols):
            if nm not in dbg_d:
                return
            P.copy(dbgst[:, 0:ncols], v)
            P.dma(dbg_d[nm], dbgst[:, 0:ncols])

        def rsqrt(out, in_, scale, eps):
            P.act(out, in_, AF.Ln, bias=eps, scale=scale)
            P.act(out, out, AF.Exp, scale=-0.5)

        def bmid(v, shape):
            return V(v.ap.unsqueeze(1).to_broadcast(list(shape)), v.tb)

        def blast(v, shape):
            return V(v.ap.unsqueeze(2).to_broadcast(list(shape)), v.tb)

        ti_glob = 0
        for b in range(NBC):
            P.memset(uT[:, :, 127:128], 0.0)
            P.memset(Praw[:, :, 0:1], 0.0)
            P.memset(H32, 0.0)
            P.memset(Hbf, 0.0)
            for it in range(NT):
                r0 = b * T + it * 128
                tsl = slice(it * 128, (it + 1) * 128)
                xtile = xt[ti_glob % 2]
                ti_glob += 1
                P.dma(xtile, x_d[r0:r0 + 128, :])
                ss = sc[:, 0:1]
                rstd = sc[:, 1:2]
                P.memset(sc[:, 0:4], 0.0, eng="pool")
                xs = V(g[7].ap.bitcast(BF16), g[7].tb)
                sink = V(g[9].ap.bitcast(BF16), g[9].tb)
                P.act(sink, xtile, AF.Square, accum_out=ss)
                rsqrt(rstd, ss, 1.0 / D, NORM_EPS)
                P.ts(xs, xtile, rstd, ALU.mult)
                for c in range(8):
                    P.tr(bank_t[:, c * 128:(c + 1) * 128], xs[:, c * 128:(c + 1) * 128], identb)
                P.copy(uTs[:, :, 0:1], uT[:, :, 127:128], eng="pool")
                P.copy(uT, bank_t.re("p (c t) -> p c t", c=8), eng="act")
                P.copy(uTs[:, :, 1:128], uT[:, :, 0:127], eng="pool")

                rope = g3(5, 4, 64)
                ropei = V(g[6].ap[0:64, 0:256].bitcast(I32).rearrange("p (a t) -> p a t", a=2), g[6].tb)
                turns, rtmp, sinT, cosT = (rope[:, i, :] for i in range(4))
                P.dma(ropei[:, 0, :], pos_d[b:b + 1, tsl].broadcast_to([64, 128]))
                P.copy(rtmp, ropei[:, 0, :])
                P.ts(turns, rtmp, pp[0:64, PP_INVF:PP_INVF + 1], ALU.mult)
                P.copy(ropei[:, 1, :], turns)
                P.copy(rtmp, ropei[:, 1, :])
                P.tt(rtmp, turns, rtmp, ALU.subtract)
                P.act(sinT, rtmp, AF.Sin, scale=2.0 * math.pi)
                P.ts(turns, turns, 0.25, ALU.add)
                P.copy(ropei[:, 1, :], turns)
                P.copy(rtmp, ropei[:, 1, :])
                P.tt(rtmp, turns, rtmp, ALU.subtract)
                P.act(cosT, rtmp, AF.Sin, scale=2.0 * math.pi)

                def proj(outv, col0, ncols, shift=False):
                    for c in range(8):
                        rhs = uTs[:, c, :] if shift else uT[:, c, :]
                        P.mm(outv, W[:, c, col0:col0 + ncols], rhs, start=(c == 0), stop=(c == 7))

                cq = g3(0, 4)[:, 0:3, :]
                sq3 = g3(1, 4)[:, 0:3, :]
                rs3 = g3(2, 4)[:, 0:3, :]
                qtmp = g3(3, 4, 64)
                qtmp2 = g3(4, 4, 64)
                sg = g3(7, 4)
                bk = bank()
                for j in range(3):
                    proj(bk[:, j * 128:(j + 1) * 128], C_CQ + j * 128, 128)
                P.copy(cq, bk[:, 0:384].re("p (j t) -> p j t", j=3), eng="act")
                bk = bank()
                proj(bk[0:64, 0:128], C_KR, 64)
                proj(bk[0:64, 128:256], C_KRROT, 64)
                P.tt(qtmp[:, 0, :], bk[0:64, 0:128], cosT, ALU.mult)
                P.tt(qtmp[:, 1, :], bk[0:64, 128:256], sinT, ALU.mult)
                P.tt(krT[:, tsl], qtmp[:, 0, :], qtmp[:, 1, :], ALU.add, eng="pool")
                for half in range(2):
                    bk = bank()
                    for j in range(4):
                        proj(bk[:, j * 128:(j + 1) * 128], C_Z + (half * 4 + j) * 128, 128)
                    bk3 = bk.re("p (j t) -> p j t", j=4)
                    P.act(sg, bk3, AF.Sigmoid)
                    P.tt(zsT[:, half * 4:(half + 1) * 4, :], bk3, sg, ALU.mult)
                for q_, col0 in ((0, C_R), (1, C_K)):
                    bk = bank()
                    for j in range(4):
                        proj(bk[:, j * 128:(j + 1) * 128], col0 + j * 128, 128)
                    P.copy(Praw[:, q_ * 4:(q_ + 1) * 4, 1:129], bk.re("p (j t) -> p j t", j=4),
                           eng=("act" if q_ == 0 else "dve"))
                bk = bank()
                proj(bk[:, 0:128], C_XW, 128)
                P.copy(Praw[:, 8, 1:129], bk[:, 0:128], eng="act")
                bk = bank()
                for c in range(8):
                    P.mm(bk, uT[:, c, :], W[:, c, C_V:C_V + 512], start=(c == 0), stop=False)
                for c in range(8):
                    P.mm(bk, uTs[:, c, :], W[:, c, C_V2:C_V2 + 512], start=False, stop=(c == 7))
                P.copy(vtokf, bk, eng="act")
                P.copy(vtokb, vtokf, eng="pool")

                P.tt(sq3, cq, cq, ALU.mult, eng="pool")
                bk = bank()
                P.mm(bk[:, 0:128], onesf, sq3[:, 0, :], start=True, stop=False)
                P.mm(bk[:, 0:128], onesf, sq3[:, 1, :], start=False, stop=True)
                P.mm(bk[:, 128:256], onesf, sq3[:, 2, :], start=True, stop=True)
                rsqrt(rs3[:, 0, :], bk[:, 0:128], 1.0 / 256, NORM_EPS)
                rsqrt(rs3[:, 2, :], bk[:, 128:256], 1.0 / 128, NORM_EPS)
                P.tt(cqn[:, 0:2, :], cq[:, 0:2, :], bmid(rs3[:, 0, :], [128, 2, 128]), ALU.mult)
                P.tt(cqn[:, 2, :], cq[:, 2, :], rs3[:, 2, :], ALU.mult)
                bk = bank()
                for h in range(4):
                    for c in range(2):
                        P.mm(bk[:, h * 128:(h + 1) * 128], Wq[:, c, h * 128:(h + 1) * 128], cqn[:, c, :],
                             start=(c == 0), stop=(c == 1))
                P.copy(qnT, bk.re("p (h t) -> p h t", h=4), eng="act")
                bk = bank()
                bk2 = bank()
                for h in range(4):
                    for c in range(2):
                        P.mm(bk[0:64, h * 128:(h + 1) * 128], Wq[:, c, 512 + h * 64:512 + (h + 1) * 64], cqn[:, c, :],
                             start=(c == 0), stop=(c == 1))
                    for c in range(2):
                        P.mm(bk2[0:64, h * 128:(h + 1) * 128], Wq[:, c, 768 + h * 64:768 + (h + 1) * 64], cqn[:, c, :],
                             start=(c == 0), stop=(c == 1))
                P.tt(qtmp, bk[0:64, :].re("p (h t) -> p h t", h=4), bmid(cosT, [64, 4, 128]), ALU.mult)
                P.tt(qtmp2, bk2[0:64, :].re("p (h t) -> p h t", h=4), bmid(sinT, [64, 4, 128]), ALU.mult)
                P.tt(qrT, qtmp, qtmp2, ALU.add, eng="pool")
                bk = bank()
                for h in range(4):
                    P.mm(bk[:, h * 128:(h + 1) * 128], Wkv[:, h * 128:(h + 1) * 128], cqn[:, 2, :])
                P.copy(KnT[:, :, tsl], bk.re("p (h t) -> p h t", h=4), eng="act")
                bk = bank()
                P.mm(bk, cqn[:, 2, :], Wkv[:, 512:1024])
                P.copy(Vm[:, it, :], bk)

                P.start_seg()
                nj = it + 1
                pti = 0
                for h in range(4):
                    for jb in range(0, nj, 4):
                        njj = min(4, nj - jb)
                        bk = bank()
                        pt = PT[pti % 2]
                        pti += 1
                        for jj in range(njj):
                            j = jb + jj
                            P.mm(bk[:, jj * 128:(jj + 1) * 128], KnT[:, h, j * 128:(j + 1) * 128], qnT[:, h, :],
                                 start=True, stop=False)
                            P.mm(bk[:, jj * 128:(jj + 1) * 128], krT[:, j * 128:(j + 1) * 128], qrT[:, h, :],
                                 start=False, stop=True)
                        P.act(pt[:, 0:njj, :], bk[:, 0:njj * 128].re("p (j t) -> p j t", j=njj), AF.Exp, scale=SM_SCALE)
                        if jb + njj == nj:
                            P.tt(pt[:, njj - 1, :], pt[:, njj - 1, :], m_iu, ALU.mult, eng="pool")
                        for jj in range(njj):
                            j = jb + jj
                            P.mm(bank_y[:, h * 128:(h + 1) * 128], Vm[:, j, h * 128:(h + 1) * 128], pt[:, jj, :],
                                 start=(j == 0), stop=(j == nj - 1))
                            P.mm(bank_s[:, h * 128:(h + 1) * 128], onesb, pt[:, jj, :],
                                 start=(j == 0), stop=(j == nj - 1))
                        P.cut()
                    P.recip(atmp[:, 0, :], bank_s[:, h * 128:(h + 1) * 128])
                    P.tt(atmp[:, 1, :], bank_y[:, h * 128:(h + 1) * 128], atmp[:, 0, :], ALU.mult)
                    P.tt(ycatT[:, h, :], atmp[:, 1, :], zsT[:, h, :], ALU.mult, eng="pool")
                    P.cut()
                seg_d = P.end_seg()
                P.start_seg()

                e2, av, kk, kmod, bb, tm1, tm2, Lc, Lx, EL = (g3(i, 4) for i in range(10))
                mu_bc = blast(pp[:, PP_MU:PP_MU + 9], [128, 9, 128])
                P.tt(mixed, Praw[:, :, 0:128], Praw[:, :, 1:129], ALU.subtract, eng="pool")
                P.tt(mixed, mixed, mu_bc, ALU.mult, eng="pool")
                P.tt(mixed, mixed, Praw[:, :, 1:129], ALU.add, eng="pool")
                P.copy(Praw[:, :, 0:1], Praw[:, :, 128:129], eng="pool")
                rm = mixed[:, 0:4, :]
                km = mixed[:, 4:8, :]
                P.act(lor[0:64, :], mixed[0:64, 8, :], AF.Tanh)
                P.copy(lor[64:128, :], mixed[64:128, 8, :], eng="pool")
                P.cut()
                bkw = bank()
                bka = bank()
                for hp in range(4):
                    P.mm(bkw[:, hp * 128:(hp + 1) * 128], W2A[0:64, hp * 128:(hp + 1) * 128], lor[0:64, :])
                    P.mm(bka[:, hp * 128:(hp + 1) * 128], W2A[64:128, hp * 128:(hp + 1) * 128], lor[64:128, :])

                def pbc(col):
                    return blast(pp[:, col:col + 4], [128, 4, 128])

                P.tt(e2, bkw.re("p (h t) -> p h t", h=4), pbc(PP_W0), ALU.add)
                P.tt(av, bka.re("p (h t) -> p h t", h=4), pbc(PP_A0), ALU.add)
                P.act(e2, e2, AF.Sigmoid)
                P.act(av, av, AF.Sigmoid)
                P.ts(e2, e2, math.exp(-0.5), ALU.mult, eng="pool")
                P.cut()
                P.tt(kk, km, pbc(PP_KK), ALU.mult, eng="pool")
                P.tt(tm1, kk, kk, ALU.mult, eng="pool")
                bk = bank()
                for hp in range(4):
                    P.mm(bk[:, hp * 128:(hp + 1) * 128], blockones, tm1[:, hp, :])
                rsqrt(tm2, bk.re("p (h t) -> p h t", h=4), 1.0, 1e-24)
                P.tt(kk, kk, tm2, ALU.mult)
                P.cut()
                P.tt(tm1, av, pbc(PP_KA), ALU.mult, eng="pool")
                P.tt(tm1, tm1, blast(omka, [128, 4, 128]), ALU.add, eng="pool")
                P.tt(kmod, km, tm1, ALU.mult, eng="pool")
                P.tt(bb, kk, av, ALU.mult, eng="pool")
                P.tt(tm1, rm, kmod, ALU.mult, eng="pool")
                P.tt(prk, tm1, pbc(PP_RK), ALU.mult, eng="pool")
                P.cut()
                for hp in range(4):
                    P.scan(Lc[:, hp, :], onesf, e2[:, hp, :], 0.0, ALU.mult, ALU.subtract)
                P.tt(Lx, Lc, e2, ALU.add, eng="pool")
                P.act(EL, Lc, AF.Exp)
                P.act(Lx, Lx, AF.Exp)
                P.act(Lc, Lc, AF.Exp, scale=-1.0)
                P.cut()
                gC = EL[:, :, 127:128].bc([128, 4, 128])
                P.tt(AR[:, :, 1, :], rm, EL, ALU.mult)
                P.stt(AR[:, :, 0, :], kk, -1.0, Lx, ALU.mult, ALU.mult)
                P.tt(tm1, bb, Lc, ALU.mult, eng="pool")
                P.tt(tm2, kmod, Lc, ALU.mult, eng="pool")
                P.copy(Bt, tm1, eng="pool")
                P.copy(Kt, tm2, eng="pool")
                P.tt(bhat, tm1, gC, ALU.mult)
                P.tt(khat, tm2, gC, ALU.mult)
                P.cut()
                for hp in range(4):
                    P.tr(bank_t[:, hp * 128:(hp + 1) * 128], bhat[:, hp, :], identb)
                    P.tr(bank_t[:, (4 + hp) * 128:(5 + hp) * 128], khat[:, hp, :], identb)
                P.copy(BKtok, bank_t.re("p (c t) -> p c t", c=8), eng="act")
                P.cut()
                msl = bmid(m_sl, [128, 4, 128])
                idb = bmid(identb, [128, 4, 128])
                for hg in range(2):
                    for hl in range(4):
                        h = hg * 4 + hl
                        hp, pb = h // 2, (h % 2) * 64
                        bk = bank()
                        rhs = AR[pb:pb + 64, hp, :, :]
                        P.mm(bk[:, 0:256], Bt[pb:pb + 64, hp, :], rhs)
                        P.mm(bk[:, 256:512], Kt[pb:pb + 64, hp, :], rhs)
                        P.tt(NK[:, hl, :, :], bk.re("p (a t) -> p a t", a=4), m4, ALU.mult)
                        P.cut()
                    bke = bank()
                    bko = bank()
                    for hl in range(4):
                        h = hg * 4 + hl
                        hp, pb = h // 2, (h % 2) * 64
                        bsel = bke if hl % 2 == 0 else bko
                        P.mm(bsel[:, (hl // 2) * 128:(hl // 2 + 1) * 128], AR[pb:pb + 64, hp, 0, :], Bt[pb:pb + 64, hp, :])
                    msl2 = bmid(m_sl, [128, 2, 128])
                    Am4 = Am.re("p (a b) t -> p a b t", b=2)
                    P.tt(Am4[:, :, 0, :], bke[:, 0:256].re("p (h t) -> p h t", h=2), msl2, ALU.mult)
                    P.tt(Am4[:, :, 1, :], bko[:, 0:256].re("p (h t) -> p h t", h=2), msl2, ALU.mult)
                    P.tt(Mt, NK[:, :, 0, :], idb, ALU.add, eng="pool")
                    P.cut()
                    qc, qtc = NK[:, :, 0, :], Am
                    for k in range(1, 7):
                        bq = bank() if k < 6 else None
                        bqt = bank()
                        for hl in range(4):
                            if bq is not None:
                                P.mm(bq[:, hl * 128:(hl + 1) * 128], qtc[:, hl, :], qc[:, hl, :])
                            P.mm(bqt[:, hl * 128:(hl + 1) * 128], qc[:, hl, :], qtc[:, hl, :])
                        if k >= 2:
                            bp = bank()
                            for hl in range(4):
                                P.mm(bp[:, hl * 128:(hl + 1) * 128], qtc[:, hl, :], Mt[:, hl, :])
                        qn_, qtn_ = Qa, QTa
                        if bq is not None:
                            P.copy(qn_, bq.re("p (h t) -> p h t", h=4), eng="act")
                        P.copy(qtn_, bqt.re("p (h t) -> p h t", h=4))
                        if k >= 2:
                            P.tt(Mt, bp.re("p (h t) -> p h t", h=4), Mt, ALU.add)
                        qc, qtc = qn_, qtn_
                        P.cut()
                    bp = bank()
                    for hl in range(4):
                        P.mm(bp[:, hl * 128:(hl + 1) * 128], qtc[:, hl, :], Mt[:, hl, :])
                    P.tt(Mt, bp.re("p (h t) -> p h t", h=4), Mt, ALU.add)
                    P.cut()
                    if b == 0 and it == min(1, NT - 1) and hg == 0:
                        dump("mt", Mt.re("p c t -> p (c t)"), 512)
                    bk = bank()
                    for hl in range(4):
                        h = hg * 4 + hl
                        hp, pb = h // 2, (h % 2) * 64
                        P.mm(bk[:, hl * 64:(hl + 1) * 64], AR[pb:pb + 64, hp, 0, :], Hbf[pb:pb + 64, hp, :], start=True, stop=False)
                        P.mm(bk[:, hl * 64:(hl + 1) * 64], NK[:, hl, 2, :], vtokb[:, h * 64:(h + 1) * 64], start=False, stop=True)
                    P.copy(Xb, bk[:, 0:256].re("p (h v) -> p h v", h=4), eng="act")
                    P.cut()
                    bk = bank()
                    for hl in range(4):
                        P.mm(bk[:, hl * 64:(hl + 1) * 64], Mt[:, hl, :], Xb[:, hl, :])
                    P.copy(Ub, bk[:, 0:256].re("p (h v) -> p h v", h=4))
                    P.cut()
                    for hl in range(4):
                        h = hg * 4 + hl
                        hp, pb = h // 2, (h % 2) * 64
                        P.mm(bank_x[:, h * 64:(h + 1) * 64], AR[pb:pb + 64, hp, 1, :], Hbf[pb:pb + 64, hp, :], start=True, stop=False)
                        P.mm(bank_x[:, h * 64:(h + 1) * 64], NK[:, hl, 1, :], Ub[:, hl, :], start=False, stop=False)
                        P.mm(bank_x[:, h * 64:(h + 1) * 64], NK[:, hl, 3, :], vtokb[:, h * 64:(h + 1) * 64], start=False, stop=True)
                    bk = bank()
                    for hl in range(4):
                        h = hg * 4 + hl
                        hp = h // 2
                        P.mm(bk[:, hl * 64:(hl + 1) * 64], BKtok[:, hp, :], Ub[:, hl, :], start=True, stop=False)
                        P.mm(bk[:, hl * 64:(hl + 1) * 64], BKtok[:, 4 + hp, :], vtokb[:, h * 64:(h + 1) * 64], start=False, stop=True)
                    Hs = H32[:, 2 * hg:2 * hg + 2, :]
                    P.tt(Hs, Hs, EL[:, 2 * hg:2 * hg + 2, 127:128].bc([128, 2, 64]), ALU.mult)
                    bk4 = bk[:, 0:256].re("p (a b v) -> p a b v", a=2, b=2)
                    P.tt(Hs[0:64], bk4[0:64, :, 0, :], Hs[0:64], ALU.add)
                    P.tt(Hs[64:128], bk4[64:128, :, 1, :], Hs[64:128], ALU.add)
                    P.copy(Hbf[:, 2 * hg:2 * hg + 2, :], Hs, eng="pool")
                    P.cut()
                gst = sc[:, 8:16]
                yc = V(g[0].ap.rearrange("p (h v) -> p h v", h=8), g[0].tb)
                ysq = V(g[1].ap.rearrange("p (h v) -> p h v", h=8), g[1].tb)
                st1 = V(g[2].ap[:, 0:32].rearrange("p (a h) -> p a h", a=4), g[2].tb)
                y3 = bank_x.re("p (h v) -> p h v", h=8)
                P.rsum(st1[:, 0, :], y3)
                P.ts(st1[:, 0, :], st1[:, 0, :], 1.0 / 64, ALU.mult)
                P.tt(yc, y3, blast(st1[:, 0, :], [128, 8, 64]), ALU.subtract)
                P.tt(ysq, yc, yc, ALU.mult, eng="pool")
                P.rsum(st1[:, 1, :], ysq)
                rsqrt(st1[:, 2, :], st1[:, 1, :], 1.0 / 64, GN_EPS)
                P.cut()
                bkr = bank()
                for hp in range(4):
                    P.mm(bkr[:, hp * 2:hp * 2 + 2], prk[:, hp, :], headind)
                P.copy(st1[:, 3, :], bkr[:, 0:8])
                P.cut()
                P.tt(yc, yc, blast(st1[:, 2, :], [128, 8, 64]), ALU.mult)
                yc2 = yc.re("p h v -> p (h v)")
                P.tt(yc2, yc2, rows[:, 0:512], ALU.mult, eng="pool")
                P.tt(yc2, yc2, rows[:, 512:1024], ALU.add, eng="pool")
                P.tt(ysq, vtokf.re("p (h v) -> p h v", h=8), blast(st1[:, 3, :], [128, 8, 64]), ALU.mult, eng="pool")
                P.tt(yrb, yc2, ysq.re("p h v -> p (h v)"), ALU.add)
                for c in range(4):
                    P.tr(bank_t[:, c * 128:(c + 1) * 128], yrb[:, c * 128:(c + 1) * 128], identb)
                P.tt(ycatT[:, 4:8, :], bank_t[:, 0:512].re("p (c t) -> p c t", c=4), zsT[:, 4:8, :], ALU.mult)

                seg_e = P.end_seg()
                P.merge(seg_d, seg_e)
                bo = [bank(), bank()]
                for n in range(2):
                    for c in range(8):
                        P.mm(bo[n], ycatT[:, c, :], Wout[:, c, n * 512:(n + 1) * 512], start=(c == 0), stop=(c == 7))
                P.act(g[2], bo[0], AF.Square, accum_out=sc[:, 2:3])
                P.act(g[3], bo[1], AF.Square, accum_out=sc[:, 3:4])
                P.tt(sc[:, 4:5], sc[:, 2:3], sc[:, 3:4], ALU.add)
                rsqrt(sc[:, 5:6], sc[:, 4:5], 1.0 / D, NORM_EPS)
                for n in range(2):
                    P.stt(g[4 + n], bo[n], sc[:, 5:6], rows[:, 1024 + n * 512:1024 + (n + 1) * 512], ALU.mult, ALU.mult)
                    P.tt(xtile[:, n * 512:(n + 1) * 512], xtile[:, n * 512:(n + 1) * 512], g[4 + n], ALU.add, eng="pool")
                P.dma(out_d[r0:r0 + 128, :], xtile)

                if b == 0 and it == min(1, NT - 1):
                    dump("ycat", ycatT.re("p c t -> p (c t)"), 1024)
                    dump("yrw", yrb, 512)
                    dump("kk", kk.re("p c t -> p (c t)"), 512)
                    dump("e2", e2.re("p c t -> p (c t)"), 512)
                    dump("H", H32.re("p c t -> p (c t)"), 256)

        fw = [xt[0].tb, xt[1].tb]
        if dbgst is not None:
            fw.append(dbgst.tb)
        P.emit(final_wait=fw)
        build.stats = P.stats
    return nc


def host_params(inp):
    f = np.float32
    w_in = np.asarray(inp["w_in"][0], f)
    kr = w_in[:, 384:448]
    krrot = np.concatenate([kr[:, 32:64], kr[:, 0:32]], axis=1)
    win = np.ascontiguousarray(np.concatenate([w_in, krrot], axis=1))
    wuq = np.asarray(inp["mla_w_uq"][0], f).reshape(256, 4, 192)
    nope = wuq[:, :, 0:128].reshape(256, 512)
    rp = wuq[:, :, 128:192]
    rot = np.concatenate([rp[:, :, 32:64], rp[:, :, 0:32]], axis=2)
    wuq_l = np.ascontiguousarray(np.concatenate([nope, rp.reshape(256, 256), rot.reshape(256, 256)], axis=1))
    wukv = np.asarray(inp["mla_w_ukv"][0], f).reshape(128, 4, 256)
    wukv_l = np.ascontiguousarray(np.concatenate([wukv[:, :, 0:128].reshape(128, 512),
                                                  wukv[:, :, 128:256].reshape(128, 512)], axis=1))
    w2a = np.ascontiguousarray(np.concatenate([np.asarray(inp["rw_w2"][0], f), np.asarray(inp["rw_a2"][0], f)], axis=0))
    wout = np.ascontiguousarray(np.asarray(inp["w_out"][0], f))
    pp = np.zeros((128, NPP), f)

    def colmajor(v, n):
        return np.asarray(v, f).reshape(n, 128).T

    pp[:, PP_GPRE:PP_GPRE + 8] = colmajor(inp["norm_pre_g"][0], 8)
    pp[:, PP_GQ:PP_GQ + 2] = colmajor(inp["mla_q_norm_g"][0], 2)
    pp[:, PP_GKV:PP_GKV + 1] = colmajor(inp["mla_kv_norm_g"][0], 1)
    pp[:, PP_W0:PP_W0 + 4] = colmajor(inp["rw_w0"][0], 4)
    pp[:, PP_A0:PP_A0 + 4] = colmajor(inp["rw_a0"][0], 4)
    pp[:, PP_KK:PP_KK + 4] = colmajor(inp["rw_k_k"][0], 4)
    pp[:, PP_KA:PP_KA + 4] = colmajor(inp["rw_k_a"][0], 4)
    pp[:, PP_RK:PP_RK + 4] = colmajor(np.asarray(inp["rw_r_k"][0]).reshape(512), 4)
    invf = (10000.0 ** (-np.arange(0, 64, 2, dtype=np.float32) / 64)).astype(f)
    invf_turn = (np.concatenate([invf, invf]).astype(np.float64) / (2 * np.pi)).astype(f)
    pp[0:64, PP_INVF] = invf_turn
    mu = np.asarray(inp["rw_mu"][0], f)
    pp[:, PP_MU:PP_MU + 4] = colmajor(mu[0:512], 4)
    pp[:, PP_MU + 4:PP_MU + 8] = colmajor(mu[512:1024], 4)
    pp[:, PP_MU + 8] = mu[1536:1664]
    rows = np.concatenate([mu[1024:1536], np.asarray(inp["rw_ln_g"][0], f), np.asarray(inp["rw_ln_b"][0], f),
                           np.asarray(inp["norm_post_g"][0], f)]).reshape(1, NROWS).astype(f)
    return {"win": win, "wuq": wuq_l, "wukv": wukv_l, "w2a": w2a, "wout": wout, "pp": pp, "rows": rows}


def kernel(**inp):
    x = np.asarray(inp["x"], np.float32)
    pos = np.asarray(inp["positions"], np.int32)
    B, T, _ = x.shape
    nbc = B // N_CORES
    shared = host_params(inp)
    nc = build(nbc, T // 128)
    in_maps = []
    for c in range(N_CORES):
        m = dict(shared)
        m["x"] = np.ascontiguousarray(x[c * nbc:(c + 1) * nbc].reshape(nbc * T, D))
        m["pos"] = np.ascontiguousarray(pos[c * nbc:(c + 1) * nbc])
        in_maps.append(m)
    res = run_bass_kernel_spmd(nc, in_maps, core_ids=list(range(N_CORES)))
    out = np.concatenate([r["out"].reshape(nbc, T, D) for r in res.results], axis=0)
    return out.astype(np.float32)
```

```python
import math
import os
import numpy as np
import concourse.bass as bass
import concourse.mybir as mybir
from concourse.bass_utils import run_bass_kernel_spmd
from contextlib import ExitStack

F32 = mybir.dt.float32
BF16 = mybir.dt.bfloat16
I32 = mybir.dt.int32
AF = mybir.ActivationFunctionType
ALU = mybir.AluOpType
AX = mybir.AxisListType

SAME_ENGINE_SYNC = True
N_CORES = 8
T_FULL = 2048
D = 1024


class TB:
    __slots__ = ("name", "last_w", "readers", "dma_sem", "dma_cnt", "inherit")

    def __init__(self, name, inherit=None):
        self.name = name
        self.last_w = []
        self.readers = {}
        self.dma_sem = None
        self.dma_cnt = 0
        self.inherit = inherit


class V:
    __slots__ = ("ap", "tb")

    def __init__(self, ap, tb):
        self.ap = ap
        self.tb = tb

    def __getitem__(self, idx):
        return V(self.ap[idx], self.tb)

    def bc(self, shape):
        return V(self.ap.to_broadcast(list(shape)), self.tb)

    def re(self, s, **kw):
        return V(self.ap.rearrange(s, **kw), self.tb)


class Prog:
    ENGS = ("pe", "act", "dve", "pool", "sp")

    def __init__(self, nc, es):
        self.nc = nc
        self.es = es
        self.main = []
        self.cur = self.main
        self.stack = []
        self.ops = []
        self.signal = set()

    def sb(self, name, shape, dt=F32):
        t = self.es.enter_context(self.nc.sbuf_tensor("s_" + name, list(shape), dt))
        return V(t[:], TB(name))

    def ps(self, name, shape, dt=F32):
        t = self.es.enter_context(self.nc.psum_tensor("p_" + name, list(shape), dt))
        return V(t[:], TB(name))

    def add(self, eng, emit, reads=(), writes=(), dma_tb=None):
        rt = [r.tb if isinstance(r, V) else r for r in reads]
        wt = [w.tb if isinstance(w, V) else w for w in writes]
        self.cur.append((eng, emit, rt, wt, dma_tb))

    def start_seg(self):
        self.stack.append(self.cur)
        self.cur = []

    def end_seg(self):
        seg = self.cur
        self.cur = self.stack.pop()
        return seg

    def cut(self):
        self.cur.append(None)

    def merge(self, sa, sb):
        def units(seg):
            out, u = [], []
            for r in seg:
                if r is None:
                    if u:
                        out.append(u)
                    u = []
                else:
                    u.append(r)
            if u:
                out.append(u)
            return out
        ua, ub = units(sa), units(sb)
        na, nb = len(ua), len(ub)
        ia = ib = 0
        while ia < na or ib < nb:
            if ib >= nb or (ia < na and ia * nb <= ib * na):
                self.cur.extend(ua[ia])
                ia += 1
            else:
                self.cur.extend(ub[ib])
                ib += 1
            self.cur.append(None)

    @staticmethod
    def _touch(tb):
        if tb.inherit is not None:
            p = tb.inherit
            tb.inherit = None
            Prog._touch(p)
            tb.last_w = list(p.last_w)
            tb.readers = dict(p.readers)

    def finalize(self):
        for rec in self.main:
            if rec is None:
                continue
            (eng, emit, rt, wt, dma_tb) = rec
            idx = len(self.ops)
            deps = []
            for tb in rt:
                self._touch(tb)
                deps.extend(tb.last_w)
            for tb in wt:
                self._touch(tb)
                deps.extend(tb.last_w)
                deps.extend(tb.readers.values())
            if dma_tb is not None:
                dma_tb.dma_cnt += 1
                me = ("d", dma_tb, 16 * dma_tb.dma_cnt)
                rkey = ("d", id(dma_tb))
            else:
                me = ("e", eng, idx)
                rkey = ("e", eng)
            for tb in rt:
                tb.readers[rkey] = me
            for tb in wt:
                tb.last_w = [me]
                tb.readers = {}
            seen = set()
            d2 = []
            for d in deps:
                k = (d[0], id(d[1]) if d[0] == "d" else d[1], d[2])
                if k in seen:
                    continue
                seen.add(k)
                d2.append(d)
                if d[0] == "e" and not (d[1] == eng and (eng == "pe" or not SAME_ENGINE_SYNC)):
                    self.signal.add(d[2])
            self.ops.append((eng, emit, d2, dma_tb))

    def dma(self, out, in_, eng="sp", reads=(), writes=()):
        sbv = out if isinstance(out, V) else in_
        o = out.ap if isinstance(out, V) else out
        i = in_.ap if isinstance(in_, V) else in_
        r = list(reads) + ([in_] if isinstance(in_, V) else [])
        w = list(writes) + ([out] if isinstance(out, V) else [])
        return self.add(eng, lambda e: e.dma_start(out=o, in_=i), r, w, dma_tb=sbv.tb)

    def mm(self, out, lhsT, rhs, start=True, stop=True):
        return self.add("pe", lambda e: e.matmul(out.ap, lhsT.ap, rhs.ap, start=start, stop=stop),
                        [lhsT, rhs], [out])

    def tr(self, out, in_, ident):
        return self.add("pe", lambda e: e.transpose(out.ap, in_.ap, ident.ap), [in_, ident], [out])

    def act(self, out, in_, func, bias=None, scale=1.0, accum_out=None):
        reads = [in_]
        kw = {}
        if isinstance(bias, V):
            reads.append(bias)
            kw["bias"] = bias.ap
        elif bias is not None:
            kw["bias"] = bias
        if isinstance(scale, V):
            reads.append(scale)
            kw["scale"] = scale.ap
        else:
            kw["scale"] = scale
        writes = [out]
        if accum_out is not None:
            writes.append(accum_out)
            kw["accum_out"] = accum_out.ap
        return self.add("act", lambda e: e.activation(out.ap, in_.ap, func, **kw), reads, writes)

    def tt(self, out, a, b, op, eng="dve"):
        return self.add(eng, lambda e: e.tensor_tensor(out.ap, a.ap, b.ap, op), [a, b], [out])

    def ts(self, out, a, s1, op0, s2=None, op1=None, eng="dve"):
        reads = [a]
        x1, x2 = s1, s2
        if isinstance(s1, V):
            reads.append(s1)
            x1 = s1.ap
        if isinstance(s2, V):
            reads.append(s2)
            x2 = s2.ap
        kw = {}
        if op1 is not None:
            kw["op1"] = op1
        return self.add(eng, lambda e: e.tensor_scalar(out.ap, a.ap, x1, x2, op0, **kw), reads, [out])

    def stt(self, out, a, s, b, op0, op1, eng="dve"):
        reads = [a, b]
        x = s
        if isinstance(s, V):
            reads.append(s)
            x = s.ap
        return self.add(eng, lambda e: e.scalar_tensor_tensor(out.ap, a.ap, x, b.ap, op0, op1), reads, [out])

    def copy(self, out, in_, eng="dve"):
        if eng == "act":
            return self.add("act", lambda e: e.copy(out.ap, in_.ap), [in_], [out])
        return self.add(eng, lambda e: e.tensor_copy(out.ap, in_.ap), [in_], [out])

    def memset(self, out, val, eng="dve"):
        return self.add(eng, lambda e: e.memset(out.ap, val), [], [out])

    def recip(self, out, in_):
        return self.add("dve", lambda e: e.reciprocal(out.ap, in_.ap), [in_], [out])

    def rsum(self, out, in_, eng="dve"):
        return self.add(eng, lambda e: e.tensor_reduce(out.ap, in_.ap, AX.X, ALU.add), [in_], [out])

    def scan(self, out, d0, d1, init, op0, op1):
        return self.add("dve", lambda e: e.tensor_tensor_scan(out.ap, d0.ap, d1.ap, init, op0, op1), [d0, d1], [out])

    def aselect(self, out, in_, pattern, cmp, fill, base, cm):
        return self.add("pool", lambda e: e.affine_select(out.ap, in_.ap, pattern, cmp, fill, base=base,
                                                          channel_multiplier=cm), [in_], [out])

    def emit(self, final_wait=()):
        nc, es = self.nc, self.es
        self.finalize()
        ordn = {}
        cnt = {e: 0 for e in self.ENGS}
        for i, (eng, _, _, dma_tb) in enumerate(self.ops):
            if dma_tb is None and i in self.signal:
                cnt[eng] += 1
                ordn[i] = cnt[eng]
        esem = {e: es.enter_context(nc.semaphore("sem_" + e)) for e in self.ENGS}
        for (eng, _, _, dma_tb) in self.ops:
            if dma_tb is not None and dma_tb.dma_sem is None:
                dma_tb.dma_sem = es.enter_context(nc.semaphore("dsem_%s" % dma_tb.name))
        per_eng = {e: [] for e in self.ENGS}
        for i, op in enumerate(self.ops):
            per_eng[op[0]].append(i)
        self.stats = {e: [len(per_eng[e]), cnt[e], 0] for e in self.ENGS}
        block = es.enter_context(nc.Block())
        ops, stats = self.ops, self.stats

        def run(engname, e):
            waited = {}
            for i in per_eng[engname]:
                _, emit, deps, dma_tb = ops[i]
                need = {}
                for d in deps:
                    if d[0] == "e":
                        if d[1] == engname and (engname == "pe" or not SAME_ENGINE_SYNC):
                            continue
                        sem, val, key = esem[d[1]], ordn[d[2]], "e" + d[1]
                    else:
                        sem, val, key = d[1].dma_sem, d[2], id(d[1])
                    if waited.get(key, 0) >= val:
                        continue
                    if key not in need or need[key][1] < val:
                        need[key] = (sem, val)
                for key, (sem, val) in need.items():
                    e.wait_ge(sem, val)
                    waited[key] = val
                    stats[engname][2] += 1
                ins = emit(e)
                if dma_tb is not None:
                    ins.then_inc(dma_tb.dma_sem, 16)
                elif i in ordn:
                    ins.then_inc(esem[engname], 1)
            if engname == "sp":
                for tb in final_wait:
                    if tb.dma_sem is not None:
                        e.wait_ge(tb.dma_sem, 16 * tb.dma_cnt)

        @block.tensor
        def _(e):
            run("pe", e)

        @block.scalar
        def _(e):
            run("act", e)

        @block.vector
        def _(e):
            run("dve", e)

        @block.gpsimd
        def _(e):
            run("pool", e)

        @block.sync
        def _(e):
            run("sp", e)


WC = 3136 + 64 + 512
C_CQ, C_CKV, C_KR, C_R, C_K, C_V, C_XW, C_Z = 0, 256, 384, 448, 960, 1472, 1984, 2112
C_KRROT, C_V2 = 3136, 3200
PP_GPRE, PP_GQ, PP_GKV, PP_W0, PP_A0, PP_KK, PP_KA, PP_RK, PP_INVF, PP_MU = 0, 8, 10, 11, 15, 19, 23, 27, 31, 32
NPP = 41
NROWS = 2560
GN_EPS = 64e-5
NORM_EPS = 1e-6
SM_SCALE = 192.0 ** -0.5


def build(NBC, NT, dbg_names=(), stop=None):
    nc = bass.Bass("TRN2", target_bir_lowering=False)

    def din(name, shape, dt=F32):
        return nc.dram_tensor(name, list(shape), dt, kind="ExternalInput").ap()

    T = NT * 128
    x_d = din("x", [NBC * T, D])
    pos_d = din("pos", [NBC, T], I32)
    win_d = din("win", [D, 3200])
    wuq_d = din("wuq", [256, 1024])
    wukv_d = din("wukv", [128, 1024])
    w2a_d = din("w2a", [128, 512])
    wout_d = din("wout", [D, D])
    pp_d = din("pp", [128, NPP])
    rows_d = din("rows", [1, NROWS])
    out_d = nc.dram_tensor("out", [NBC * T, D], F32, kind="ExternalOutput").ap()
    dbg_d = {}
    for (nm, shape) in dbg_names:
        dbg_d[nm] = nc.dram_tensor("dbg_" + nm, list(shape), F32, kind="ExternalOutput").ap()

    with ExitStack() as es:
        P = Prog(nc, es)
        ident = P.sb("ident", [128, 128])
        identb = P.sb("identb", [128, 128], BF16)
        onesf = P.sb("onesf", [128, 128])
        onesb = P.sb("onesb", [128, 128], BF16)
        blockones = P.sb("blockones", [128, 128])
        headind = P.sb("headind", [128, 2], BF16)
        m_su = P.sb("m_su", [128, 128], BF16)
        m_iu = P.sb("m_iu", [128, 128], BF16)
        m_sl = P.sb("m_sl", [128, 128], BF16)
        mask4 = P.sb("mask4", [128, 4, 128], BF16)
        P.memset(ident, 0.0, eng="pool")
        P.aselect(ident, ident, [[-1, 128]], ALU.not_equal, 1.0, 0, 1)
        P.copy(identb, ident)
        P.memset(onesf, 1.0)
        P.memset(onesb, 1.0)
        P.memset(blockones, 0.0)
        P.memset(blockones[0:64, 0:64], 1.0)
        P.memset(blockones[64:128, 64:128], 1.0)
        P.memset(headind, 0.0)
        P.memset(headind[0:64, 0:1], 1.0)
        P.memset(headind[64:128, 1:2], 1.0)
        for m in (m_su, m_iu, m_sl):
            P.memset(m, 1.0, eng="pool")
        P.aselect(m_su, m_su, [[1, 128]], ALU.is_gt, 0.0, 0, -1)
        P.aselect(m_iu, m_iu, [[1, 128]], ALU.is_ge, 0.0, 0, -1)
        P.aselect(m_sl, m_sl, [[-1, 128]], ALU.is_gt, 0.0, 0, 1)
        P.copy(mask4[:, 0, :], m_su)
        P.copy(mask4[:, 1, :], m_iu)
        P.copy(mask4[:, 2, :], m_su)
        P.copy(mask4[:, 3, :], m_iu)
        m4 = mask4

        pp = P.sb("pp", [128, NPP])
        P.dma(pp, pp_d)
        rows = P.sb("rows", [128, 2048])
        P.dma(rows, rows_d[:, 512:2560].broadcast_to([128, 2048]))
        der = P.sb("der", [128, 32])
        gneg = der[:, 0:8]
        gqneg = der[:, 8:10]
        omka = der[:, 10:14]
        P.ts(gneg, pp[:, PP_GPRE:PP_GPRE + 8], -1.0, ALU.mult)
        P.ts(gqneg, pp[:, PP_GQ:PP_GQ + 2], -1.0, ALU.mult)
        P.ts(omka, pp[:, PP_KA:PP_KA + 4], -1.0, ALU.mult, 1.0, ALU.add)
        Gt = es.enter_context(nc.sbuf_tensor("s_G", [128, 10, 512], F32))
        g = [V(Gt[:][:, i, :], TB("G%d" % i)) for i in range(10)]

        def g3(i, a, parts=128):
            return V(g[i].ap[0:parts, :].rearrange("p (a t) -> p a t", a=a), g[i].tb)

        muv, omuv = g[8], g[9]
        P.dma(muv, rows_d[:, 0:512].broadcast_to([128, 512]))
        P.ts(omuv, muv, -1.0, ALU.mult, 1.0, ALU.add)

        W = P.sb("W", [128, 8, WC], BF16)
        stg = [P.sb("stg0", [128, 3200]), P.sb("stg1", [128, 3200])]
        for c in range(8):
            s = stg[c % 2]
            P.dma(s, win_d[c * 128:(c + 1) * 128, :])
            gg = pp[:, PP_GPRE + c:PP_GPRE + c + 1]
            gn = gneg[:, c:c + 1]
            e1 = "dve" if c % 2 == 0 else "pool"
            e2_ = "pool" if c % 2 == 0 else "dve"
            P.ts(W[:, c, 0:C_V], s[:, 0:C_V], gg, ALU.mult, eng=e1)
            P.ts(W[:, c, C_XW:3136], s[:, C_XW:3136], gg, ALU.mult, eng=e2_)
            P.ts(W[:, c, C_KRROT:C_KRROT + 32], s[:, 3136:3168], gn, ALU.mult, eng=e1)
            P.ts(W[:, c, C_KRROT + 32:C_KRROT + 64], s[:, 3168:3200], gg, ALU.mult, eng=e1)
            P.stt(W[:, c, C_V:C_V + 512], s[:, C_V:C_V + 512], gg, omuv, ALU.mult, ALU.mult)
            P.stt(W[:, c, C_V2:C_V2 + 512], s[:, C_V:C_V + 512], gg, muv, ALU.mult, ALU.mult)
        Wq = P.sb("Wq", [128, 2, 1024], BF16)
        for c in range(2):
            s = stg[c % 2]
            P.dma(s[:, 0:1024], wuq_d[c * 128:(c + 1) * 128, :])
            gg = pp[:, PP_GQ + c:PP_GQ + c + 1]
            gn = gqneg[:, c:c + 1]
            P.ts(Wq[:, c, 0:768], s[:, 0:768], gg, ALU.mult)
            rot_o = Wq[:, c, 768:1024].re("p (h r) -> p h r", h=4)
            rot_in = s[:, 768:1024].re("p (h r) -> p h r", h=4)
            P.ts(rot_o[:, :, 0:32], rot_in[:, :, 0:32], gn, ALU.mult)
            P.ts(rot_o[:, :, 32:64], rot_in[:, :, 32:64], gg, ALU.mult)
        Wkv = P.sb("Wkv", [128, 1024], BF16)
        s = stg[0]
        P.dma(s[:, 0:1024], wukv_d)
        P.ts(Wkv, s[:, 0:1024], pp[:, PP_GKV:PP_GKV + 1], ALU.mult)
        W2A = P.sb("W2A", [128, 512], BF16)
        s = stg[1]
        P.dma(s[:, 0:512], w2a_d)
        P.copy(W2A, s[:, 0:512])
        Wout = P.sb("Wout", [128, 8, 1024], BF16)
        for c in range(8):
            s = stg[c % 2]
            P.dma(s[:, 0:1024], wout_d[c * 128:(c + 1) * 128, :])
            P.copy(Wout[:, c, :], s[:, 0:1024], eng=("dve" if c % 2 == 0 else "pool"))

        carve_off = [0, 0]

        def carve(si, name, shape, dt):
            esz = 4 if dt in (F32, I32) else 2
            n = 1
            for d_ in shape[1:]:
                n *= d_
            nb = n * esz
            c0 = carve_off[si] // 4
            carve_off[si] += nb
            assert carve_off[si] <= 12800, (name, carve_off)
            ap = stg[si].ap[0:shape[0], c0:c0 + nb // 4]
            if dt != F32:
                ap = ap.bitcast(dt)
            if len(shape) == 3:
                ap = ap.rearrange("p (a b) -> p a b", a=shape[1])
            elif len(shape) == 4:
                ap = ap.rearrange("p (a b c) -> p a b c", a=shape[1], b=shape[2])
            return V(ap, TB(name, inherit=stg[si].tb))

        AR = carve(0, "AR", [128, 4, 2, 128], BF16)
        Bt = carve(0, "Bt", [128, 4, 128], BF16)
        Kt = carve(0, "Kt", [128, 4, 128], BF16)
        bhat = carve(0, "bhat", [128, 4, 128], BF16)
        khat = carve(0, "khat", [128, 4, 128], BF16)
        BKtok = carve(0, "BKtok", [128, 8, 128], BF16)
        BUFS = []
        for si_ in range(2):
            BUFS.append((carve(0, "NK%d" % si_, [128, 2, 4, 128], BF16),
                         carve(1, "Am%d" % si_, [128, 2, 128], BF16),
                         carve(1, "QQ%d" % si_, [128, 4, 128], BF16),
                         carve(1, "Mt%d" % si_, [128, 2, 128], BF16),
                         carve(1, "Xb%d" % si_, [128, 2, 64], BF16),
                         carve(1, "Ub%d" % si_, [128, 2, 64], BF16)))
        yrb = carve(1, "yrb", [128, 512], BF16)
        prk = carve(1, "prk", [128, 4, 128], BF16)
        lor = carve(1, "lor", [128, 128], BF16)
        vtokb = carve(1, "vtokb", [128, 512], BF16)
        PT = [carve(1, "PT0", [128, 4, 128], BF16), carve(1, "PT1", [128, 4, 128], BF16)]
        cqn = carve(1, "cqn", [128, 3, 128], BF16)
        qnT = carve(1, "qnT", [128, 4, 128], BF16)

        rot_banks = [P.ps("bank%d" % i, [128, 512]) for i in range(4)]
        bank_y = P.ps("bank_y", [128, 512])
        bank_s = P.ps("bank_s", [128, 512])
        bank_x = P.ps("bank_x", [128, 512])
        bank_t = P.ps("bank_t", [128, 1024], BF16)
        rot_i = [0]
        pool_sel = [rot_banks]
        bank_tf = V(bank_t.ap.bitcast(F32), bank_t.tb)

        def bank():
            pl = pool_sel[0]
            bkk = pl[rot_i[0] % len(pl)]
            rot_i[0] += 1
            return bkk

        KnT = P.sb("KnT", [128, 4, T], BF16)
        krT = P.sb("krT", [64, T], BF16)
        Vm = P.sb("Vm", [128, NT, 512], BF16)
        uT = P.sb("uT", [128, 8, 128], BF16)
        uTs = P.sb("uTs", [128, 8, 128], BF16)
        xt = [P.sb("xt0", [128, D]), P.sb("xt1", [128, D])]
        sc = P.sb("sc", [128, 16])
        Praw = P.sb("Praw", [128, 9, 129])
        H32 = P.sb("H32", [128, 4, 64])
        Hbf = P.sb("Hbf", [128, 4, 64], BF16)
        qrT = P.sb("qrT", [64, 4, 128], BF16)
        zsT = P.sb("zsT", [128, 8, 128], BF16)
        ycatT = P.sb("ycatT", [128, 8, 128], BF16)
        mixed = P.sb("mixed", [128, 9, 128])
        vtokf = P.sb("vtokf", [128, 512])
        atmp = P.sb("atmp", [128, 2, 128])
        dbgst = P.sb("dbgst", [128, 1024]) if dbg_d else None

        def dump(nm, v, ncols):
            if nm not in dbg_d:
                return
            P.copy(dbgst[:, 0:ncols], v)
            P.dma(dbg_d[nm], dbgst[:, 0:ncols])

        def rsqrt(out, in_, scale, eps):
            P.act(out, in_, AF.Ln, bias=eps, scale=scale)
            P.act(out, out, AF.Exp, scale=-0.5)

        def bmid(v, shape):
            return V(v.ap.unsqueeze(1).to_broadcast(list(shape)), v.tb)

        def blast(v, shape):
            return V(v.ap.unsqueeze(2).to_broadcast(list(shape)), v.tb)

        ti_glob = 0
        for b in range(NBC):
            P.memset(uT[:, :, 127:128], 0.0)
            P.memset(Praw[:, :, 0:1], 0.0)
            P.memset(H32, 0.0)
            P.memset(Hbf, 0.0)
            for it in range(NT):
                r0 = b * T + it * 128
                tsl = slice(it * 128, (it + 1) * 128)
                xtile = xt[ti_glob % 2]
                ti_glob += 1
                P.dma(xtile, x_d[r0:r0 + 128, :])
                ss = sc[:, 0:1]
                rstd = sc[:, 1:2]
                P.memset(sc[:, 0:4], 0.0, eng="pool")
                xs = V(g[7].ap.bitcast(BF16), g[7].tb)
                sink = V(g[9].ap.bitcast(BF16), g[9].tb)
                P.act(sink, xtile, AF.Square, accum_out=ss)
                rsqrt(rstd, ss, 1.0 / D, NORM_EPS)
                P.ts(xs, xtile, rstd, ALU.mult)
                for c in range(8):
                    P.tr(bank_t[:, c * 128:(c + 1) * 128], xs[:, c * 128:(c + 1) * 128], identb)
                P.copy(uTs[:, :, 0:1], uT[:, :, 127:128], eng="pool")
                P.copy(uT, bank_t.re("p (c t) -> p c t", c=8), eng="act")
                P.copy(uTs[:, :, 1:128], uT[:, :, 0:127], eng="pool")

                rope = g3(5, 4, 64)
                ropei = V(g[6].ap[0:64, 0:256].bitcast(I32).rearrange("p (a t) -> p a t", a=2), g[6].tb)
                turns, rtmp, sinT, cosT = (rope[:, i, :] for i in range(4))
                P.dma(ropei[:, 0, :], pos_d[b:b + 1, tsl].broadcast_to([64, 128]))
                P.copy(rtmp, ropei[:, 0, :])
                P.ts(turns, rtmp, pp[0:64, PP_INVF:PP_INVF + 1], ALU.mult)
                P.copy(ropei[:, 1, :], turns)
                P.copy(rtmp, ropei[:, 1, :])
                P.tt(rtmp, turns, rtmp, ALU.subtract)
                P.act(sinT, rtmp, AF.Sin, scale=2.0 * math.pi)
                P.ts(turns, turns, 0.25, ALU.add)
                P.copy(ropei[:, 1, :], turns)
                P.copy(rtmp, ropei[:, 1, :])
                P.tt(rtmp, turns, rtmp, ALU.subtract)
                P.act(cosT, rtmp, AF.Sin, scale=2.0 * math.pi)

                def proj(outv, col0, ncols, shift=False):
                    for c in range(8):
                        rhs = uTs[:, c, :] if shift else uT[:, c, :]
                        P.mm(outv, W[:, c, col0:col0 + ncols], rhs, start=(c == 0), stop=(c == 7))

                cq = g3(0, 4)[:, 0:3, :]
                sq3 = g3(1, 4)[:, 0:3, :]
                rs3 = g3(2, 4)[:, 0:3, :]
                qtmp = g3(3, 4, 64)
                qtmp2 = g3(4, 4, 64)
                sg = g3(7, 4)
                bk = bank()
                for j in range(3):
                    proj(bk[:, j * 128:(j + 1) * 128], C_CQ + j * 128, 128)
                P.copy(cq, bk[:, 0:384].re("p (j t) -> p j t", j=3), eng="act")
                bk = bank()
                proj(bk[0:64, 0:128], C_KR, 64)
                proj(bk[0:64, 128:256], C_KRROT, 64)
                P.tt(qtmp[:, 0, :], bk[0:64, 0:128], cosT, ALU.mult)
                P.tt(qtmp[:, 1, :], bk[0:64, 128:256], sinT, ALU.mult)
                P.tt(krT[:, tsl], qtmp[:, 0, :], qtmp[:, 1, :], ALU.add, eng="pool")
                for half in range(2):
                    bk = bank()
                    for j in range(4):
                        proj(bk[:, j * 128:(j + 1) * 128], C_Z + (half * 4 + j) * 128, 128)
                    bk3 = bk.re("p (j t) -> p j t", j=4)
                    P.act(sg, bk3, AF.Sigmoid)
                    P.tt(zsT[:, half * 4:(half + 1) * 4, :], bk3, sg, ALU.mult)
                for q_, col0 in ((0, C_R), (1, C_K)):
                    bk = bank()
                    for j in range(4):
                        proj(bk[:, j * 128:(j + 1) * 128], col0 + j * 128, 128)
                    P.copy(Praw[:, q_ * 4:(q_ + 1) * 4, 1:129], bk.re("p (j t) -> p j t", j=4),
                           eng=("act" if q_ == 0 else "dve"))
                bk = bank()
                proj(bk[:, 0:128], C_XW, 128)
                P.copy(Praw[:, 8, 1:129], bk[:, 0:128], eng="act")
                bk = bank()
                for c in range(8):
                    P.mm(bk, uT[:, c, :], W[:, c, C_V:C_V + 512], start=(c == 0), stop=False)
                for c in range(8):
                    P.mm(bk, uTs[:, c, :], W[:, c, C_V2:C_V2 + 512], start=False, stop=(c == 7))
                P.copy(vtokf, bk, eng="act")
                P.copy(vtokb, vtokf, eng="pool")

                P.tt(sq3, cq, cq, ALU.mult, eng="pool")
                bk = bank()
                P.mm(bk[:, 0:128], onesf, sq3[:, 0, :], start=True, stop=False)
                P.mm(bk[:, 0:128], onesf, sq3[:, 1, :], start=False, stop=True)
                P.mm(bk[:, 128:256], onesf, sq3[:, 2, :], start=True, stop=True)
                rsqrt(rs3[:, 0, :], bk[:, 0:128], 1.0 / 256, NORM_EPS)
                rsqrt(rs3[:, 2, :], bk[:, 128:256], 1.0 / 128, NORM_EPS)
                P.tt(cqn[:, 0:2, :], cq[:, 0:2, :], bmid(rs3[:, 0, :], [128, 2, 128]), ALU.mult)
                P.tt(cqn[:, 2, :], cq[:, 2, :], rs3[:, 2, :], ALU.mult)
                bk = bank()
                for h in range(4):
                    for c in range(2):
                        P.mm(bk[:, h * 128:(h + 1) * 128], Wq[:, c, h * 128:(h + 1) * 128], cqn[:, c, :],
                             start=(c == 0), stop=(c == 1))
                P.copy(qnT, bk.re("p (h t) -> p h t", h=4), eng="act")
                bk = bank()
                bk2 = bank()
                for h in range(4):
                    for c in range(2):
                        P.mm(bk[0:64, h * 128:(h + 1) * 128], Wq[:, c, 512 + h * 64:512 + (h + 1) * 64], cqn[:, c, :],
                             start=(c == 0), stop=(c == 1))
                    for c in range(2):
                        P.mm(bk2[0:64, h * 128:(h + 1) * 128], Wq[:, c, 768 + h * 64:768 + (h + 1) * 64], cqn[:, c, :],
                             start=(c == 0), stop=(c == 1))
                P.tt(qtmp, bk[0:64, :].re("p (h t) -> p h t", h=4), bmid(cosT, [64, 4, 128]), ALU.mult)
                P.tt(qtmp2, bk2[0:64, :].re("p (h t) -> p h t", h=4), bmid(sinT, [64, 4, 128]), ALU.mult)
                P.tt(qrT, qtmp, qtmp2, ALU.add, eng="pool")
                bk = bank()
                for h in range(4):
                    P.mm(bk[:, h * 128:(h + 1) * 128], Wkv[:, h * 128:(h + 1) * 128], cqn[:, 2, :])
                P.copy(KnT[:, :, tsl], bk.re("p (h t) -> p h t", h=4), eng="act")
                bk = bank()
                P.mm(bk, cqn[:, 2, :], Wkv[:, 512:1024])
                P.copy(Vm[:, it, :], bk)

                P.start_seg()
                nj = it + 1
                pti = 0
                for h in range(4):
                    for jb in range(0, nj, 4):
                        njj = min(4, nj - jb)
                        bk = bank() if os.environ.get('NOTF') else bank_tf
                        pt = PT[pti % 2]
                        pti += 1
                        for jj in range(njj):
                            j = jb + jj
                            P.mm(bk[:, jj * 128:(jj + 1) * 128], KnT[:, h, j * 128:(j + 1) * 128], qnT[:, h, :],
                                 start=True, stop=False)
                            P.mm(bk[:, jj * 128:(jj + 1) * 128], krT[:, j * 128:(j + 1) * 128], qrT[:, h, :],
                                 start=False, stop=True)
                        P.act(pt[:, 0:njj, :], bk[:, 0:njj * 128].re("p (j t) -> p j t", j=njj), AF.Exp, scale=SM_SCALE)
                        if jb + njj == nj:
                            P.tt(pt[:, njj - 1, :], pt[:, njj - 1, :], m_iu, ALU.mult, eng="pool")
                        for jj in range(njj):
                            j = jb + jj
                            P.mm(bank_y[:, h * 128:(h + 1) * 128], Vm[:, j, h * 128:(h + 1) * 128], pt[:, jj, :],
                                 start=(j == 0), stop=(j == nj - 1))
                            P.mm(bank_s[:, h * 128:(h + 1) * 128], onesb, pt[:, jj, :],
                                 start=(j == 0), stop=(j == nj - 1))
                        P.cut()
                    P.recip(atmp[:, 0, :], bank_s[:, h * 128:(h + 1) * 128])
                    P.tt(atmp[:, 1, :], bank_y[:, h * 128:(h + 1) * 128], atmp[:, 0, :], ALU.mult)
                    P.tt(ycatT[:, h, :], atmp[:, 1, :], zsT[:, h, :], ALU.mult, eng="pool")
                    P.cut()
                seg_d = P.end_seg()
                P.start_seg()

                e2, av, kk, kmod, bb, tm1, tm2, Lc, Lx, EL = (g3(i, 4) for i in range(10))
                mu_bc = blast(pp[:, PP_MU:PP_MU + 9], [128, 9, 128])
                P.tt(mixed, Praw[:, :, 0:128], Praw[:, :, 1:129], ALU.subtract, eng="pool")
                P.tt(mixed, mixed, mu_bc, ALU.mult, eng="pool")
                P.tt(mixed, mixed, Praw[:, :, 1:129], ALU.add, eng="pool")
                P.copy(Praw[:, :, 0:1], Praw[:, :, 128:129], eng="pool")
                rm = mixed[:, 0:4, :]
                km = mixed[:, 4:8, :]
                P.act(lor[0:64, :], mixed[0:64, 8, :], AF.Tanh)
                P.copy(lor[64:128, :], mixed[64:128, 8, :], eng="pool")
                P.cut()
                bkw = bank()
                bka = bank()
                for hp in range(4):
                    P.mm(bkw[:, hp * 128:(hp + 1) * 128], W2A[0:64, hp * 128:(hp + 1) * 128], lor[0:64, :])
                    P.mm(bka[:, hp * 128:(hp + 1) * 128], W2A[64:128, hp * 128:(hp + 1) * 128], lor[64:128, :])

                def pbc(col):
                    return blast(pp[:, col:col + 4], [128, 4, 128])

                P.tt(e2, bkw.re("p (h t) -> p h t", h=4), pbc(PP_W0), ALU.add)
                P.tt(av, bka.re("p (h t) -> p h t", h=4), pbc(PP_A0), ALU.add)
                P.act(e2, e2, AF.Sigmoid)
                P.act(av, av, AF.Sigmoid)
                P.ts(e2, e2, math.exp(-0.5), ALU.mult)
                P.cut()
                P.tt(kk, km, pbc(PP_KK), ALU.mult, eng="pool")
                P.tt(tm1, kk, kk, ALU.mult, eng="pool")
                bk = bank()
                for hp in range(4):
                    P.mm(bk[:, hp * 128:(hp + 1) * 128], blockones, tm1[:, hp, :])
                rsqrt(tm2, bk.re("p (h t) -> p h t", h=4), 1.0, 1e-24)
                P.tt(kk, kk, tm2, ALU.mult)
                P.cut()
                P.tt(tm1, av, pbc(PP_KA), ALU.mult, eng="pool")
                P.tt(tm1, tm1, blast(omka, [128, 4, 128]), ALU.add, eng="pool")
                P.tt(kmod, km, tm1, ALU.mult, eng="pool")
                P.tt(bb, kk, av, ALU.mult, eng="pool")
                P.tt(tm1, rm, kmod, ALU.mult, eng="pool")
                P.tt(prk, tm1, pbc(PP_RK), ALU.mult, eng="pool")
                P.cut()
                for hp in range(4):
                    P.scan(Lc[:, hp, :], onesf, e2[:, hp, :], 0.0, ALU.mult, ALU.subtract)
                P.tt(Lx, Lc, e2, ALU.add, eng="pool")
                P.act(EL, Lc, AF.Exp)
                P.act(Lx, Lx, AF.Exp)
                P.act(Lc, Lc, AF.Exp, scale=-1.0)
                P.cut()
                gC = EL[:, :, 127:128].bc([128, 4, 128])
                P.tt(AR[:, :, 1, :], rm, EL, ALU.mult)
                P.stt(AR[:, :, 0, :], kk, -1.0, Lx, ALU.mult, ALU.mult)
                P.tt(tm1, bb, Lc, ALU.mult, eng="pool")
                P.tt(tm2, kmod, Lc, ALU.mult, eng="pool")
                P.copy(Bt, tm1, eng="pool")
                P.copy(Kt, tm2, eng="pool")
                P.tt(bhat, tm1, gC, ALU.mult)
                P.tt(khat, tm2, gC, ALU.mult)
                P.cut()
                for hp in range(4):
                    P.tr(bank_t[:, hp * 128:(hp + 1) * 128], bhat[:, hp, :], identb)
                    P.tr(bank_t[:, (4 + hp) * 128:(5 + hp) * 128], khat[:, hp, :], identb)
                P.copy(BKtok, bank_t.re("p (c t) -> p c t", c=8), eng="act")
                P.cut()
                msl1 = m_sl
                idb2 = bmid(identb, [128, 2, 128])

                def emit_group(hbase, S):
                    NKg, Amg, QQg, Mtg, Xg, Ug = S
                    Qg, QTg = QQg[:, 0:2, :], QQg[:, 2:4, :]
                    hp = hbase // 2
                    for hl in range(2):
                        pb = hl * 64
                        bk = bank()
                        rhs = AR[pb:pb + 64, hp, :, :]
                        P.mm(bk[:, 0:256], Bt[pb:pb + 64, hp, :], rhs)
                        P.mm(bk[:, 256:512], Kt[pb:pb + 64, hp, :], rhs)
                        P.tt(NKg[:, hl, :, :], bk.re("p (a t) -> p a t", a=4), m4, ALU.mult)
                        P.cut()
                    bke = bank()
                    bko = bank()
                    P.mm(bke[:, 0:128], AR[0:64, hp, 0, :], Bt[0:64, hp, :])
                    P.mm(bko[:, 0:128], AR[64:128, hp, 0, :], Bt[64:128, hp, :])
                    P.tt(Amg[:, 0, :], bke[:, 0:128], msl1, ALU.mult)
                    P.tt(Amg[:, 1, :], bko[:, 0:128], msl1, ALU.mult)
                    P.tt(Mtg, NKg[:, :, 0, :], idb2, ALU.add, eng="pool")
                    P.cut()
                    qc, qtc = NKg[:, :, 0, :], Amg
                    for k in range(1, 7):
                        bsq = bank()
                        for hl in range(2):
                            if k < 6:
                                P.mm(bsq[:, hl * 128:(hl + 1) * 128], qtc[:, hl, :], qc[:, hl, :])
                            P.mm(bsq[:, 256 + hl * 128:256 + (hl + 1) * 128], qc[:, hl, :], qtc[:, hl, :])
                        if k >= 2:
                            bp = bank()
                            for hl in range(2):
                                P.mm(bp[:, hl * 128:(hl + 1) * 128], qtc[:, hl, :], Mtg[:, hl, :])
                        if k < 6:
                            P.copy(QQg, bsq.re("p (h t) -> p h t", h=4), eng="act")
                        else:
                            P.copy(QQg[:, 2:4, :], bsq[:, 256:512].re("p (h t) -> p h t", h=2), eng="act")
                        if k >= 2:
                            P.tt(Mtg, bp[:, 0:256].re("p (h t) -> p h t", h=2), Mtg, ALU.add)
                        qc, qtc = Qg, QTg
                        P.cut()
                    bp = bank()
                    for hl in range(2):
                        P.mm(bp[:, hl * 128:(hl + 1) * 128], qtc[:, hl, :], Mtg[:, hl, :])
                    P.tt(Mtg, bp[:, 0:256].re("p (h t) -> p h t", h=2), Mtg, ALU.add)
                    P.cut()
                    bk = bank()
                    for hl in range(2):
                        h, pb = hbase + hl, hl * 64
                        P.mm(bk[:, hl * 64:(hl + 1) * 64], AR[pb:pb + 64, hp, 0, :], Hbf[pb:pb + 64, hp, :], start=True, stop=False)
                        P.mm(bk[:, hl * 64:(hl + 1) * 64], NKg[:, hl, 2, :], vtokb[:, h * 64:(h + 1) * 64], start=False, stop=True)
                    P.copy(Xg, bk[:, 0:128].re("p (h v) -> p h v", h=2), eng="act")
                    P.cut()
                    bk = bank()
                    for hl in range(2):
                        P.mm(bk[:, hl * 64:(hl + 1) * 64], Mtg[:, hl, :], Xg[:, hl, :])
                    P.copy(Ug, bk[:, 0:128].re("p (h v) -> p h v", h=2))
                    P.cut()
                    for hl in range(2):
                        h, pb = hbase + hl, hl * 64
                        P.mm(bank_x[:, h * 64:(h + 1) * 64], AR[pb:pb + 64, hp, 1, :], Hbf[pb:pb + 64, hp, :], start=True, stop=False)
                        P.mm(bank_x[:, h * 64:(h + 1) * 64], NKg[:, hl, 1, :], Ug[:, hl, :], start=False, stop=False)
                        P.mm(bank_x[:, h * 64:(h + 1) * 64], NKg[:, hl, 3, :], vtokb[:, h * 64:(h + 1) * 64], start=False, stop=True)
                    bk = bank()
                    for hl in range(2):
                        h = hbase + hl
                        P.mm(bk[:, hl * 64:(hl + 1) * 64], BKtok[:, hp, :], Ug[:, hl, :], start=True, stop=False)
                        P.mm(bk[:, hl * 64:(hl + 1) * 64], BKtok[:, 4 + hp, :], vtokb[:, h * 64:(h + 1) * 64], start=False, stop=True)
                    Hs = H32[:, hp, :]
                    P.tt(Hs, Hs, EL[:, hp, 127:128].bc([128, 64]), ALU.mult)
                    P.tt(Hs[0:64], bk[0:64, 0:64], Hs[0:64], ALU.add)
                    P.tt(Hs[64:128], bk[64:128, 64:128], Hs[64:128], ALU.add)
                    P.copy(Hbf[:, hp, :], Hs, eng="pool")
                    P.cut()

                for rnd in range(2):
                    P.start_seg()
                    pool_sel[0] = rot_banks[0:2]
                    emit_group(4 * rnd, BUFS[0])
                    sg0 = P.end_seg()
                    P.start_seg()
                    pool_sel[0] = rot_banks[2:4]
                    emit_group(4 * rnd + 2, BUFS[1])
                    sg1 = P.end_seg()
                    pool_sel[0] = rot_banks
                    if os.environ.get('NOGRP'):
                        P.cur.extend(sg0)
                        P.cur.extend(sg1)
                    else:
                        P.merge(sg0, sg1)
                gst = sc[:, 8:16]
                yc = V(g[0].ap.rearrange("p (h v) -> p h v", h=8), g[0].tb)
                ysq = V(g[1].ap.rearrange("p (h v) -> p h v", h=8), g[1].tb)
                st1 = V(g[2].ap[:, 0:32].rearrange("p (a h) -> p a h", a=4), g[2].tb)
                y3 = bank_x.re("p (h v) -> p h v", h=8)
                P.rsum(st1[:, 0, :], y3)
                P.ts(st1[:, 0, :], st1[:, 0, :], 1.0 / 64, ALU.mult)
                P.tt(yc, y3, blast(st1[:, 0, :], [128, 8, 64]), ALU.subtract)
                P.tt(ysq, yc, yc, ALU.mult, eng="pool")
                P.rsum(st1[:, 1, :], ysq)
                rsqrt(st1[:, 2, :], st1[:, 1, :], 1.0 / 64, GN_EPS)
                P.cut()
                bkr = bank()
                for hp in range(4):
                    P.mm(bkr[:, hp * 2:hp * 2 + 2], prk[:, hp, :], headind)
                P.copy(st1[:, 3, :], bkr[:, 0:8])
                P.cut()
                P.tt(yc, yc, blast(st1[:, 2, :], [128, 8, 64]), ALU.mult)
                yc2 = yc.re("p h v -> p (h v)")
                P.tt(yc2, yc2, rows[:, 0:512], ALU.mult, eng="pool")
                P.tt(yc2, yc2, rows[:, 512:1024], ALU.add, eng="pool")
                P.tt(ysq, vtokf.re("p (h v) -> p h v", h=8), blast(st1[:, 3, :], [128, 8, 64]), ALU.mult, eng="pool")
                P.tt(yrb, yc2, ysq.re("p h v -> p (h v)"), ALU.add)
                for c in range(4):
                    P.tr(bank_t[:, c * 128:(c + 1) * 128], yrb[:, c * 128:(c + 1) * 128], identb)
                P.tt(ycatT[:, 4:8, :], bank_t[:, 0:512].re("p (c t) -> p c t", c=4), zsT[:, 4:8, :], ALU.mult)

                seg_e = P.end_seg()
                P.merge(seg_d, seg_e)
                bo = [bank(), bank()]
                for n in range(2):
                    for c in range(8):
                        P.mm(bo[n], ycatT[:, c, :], Wout[:, c, n * 512:(n + 1) * 512], start=(c == 0), stop=(c == 7))
                P.act(g[2], bo[0], AF.Square, accum_out=sc[:, 2:3])
                P.act(g[3], bo[1], AF.Square, accum_out=sc[:, 3:4])
                P.tt(sc[:, 4:5], sc[:, 2:3], sc[:, 3:4], ALU.add)
                rsqrt(sc[:, 5:6], sc[:, 4:5], 1.0 / D, NORM_EPS)
                for n in range(2):
                    P.stt(g[4 + n], bo[n], sc[:, 5:6], rows[:, 1024 + n * 512:1024 + (n + 1) * 512], ALU.mult, ALU.mult)
                    P.tt(xtile[:, n * 512:(n + 1) * 512], xtile[:, n * 512:(n + 1) * 512], g[4 + n], ALU.add, eng="pool")
                P.dma(out_d[r0:r0 + 128, :], xtile)

                if b == 0 and it == min(1, NT - 1):
                    dump("ycat", ycatT.re("p c t -> p (c t)"), 1024)
                    dump("yrw", yrb, 512)
                    dump("kk", kk.re("p c t -> p (c t)"), 512)
                    dump("e2", e2.re("p c t -> p (c t)"), 512)
                    dump("H", H32.re("p c t -> p (c t)"), 256)

        fw = [xt[0].tb, xt[1].tb]
        if dbgst is not None:
            fw.append(dbgst.tb)
        P.emit(final_wait=fw)
        build.stats = P.stats
    return nc


def host_params(inp):
    f = np.float32
    w_in = np.asarray(inp["w_in"][0], f)
    kr = w_in[:, 384:448]
    krrot = np.concatenate([kr[:, 32:64], kr[:, 0:32]], axis=1)
    win = np.ascontiguousarray(np.concatenate([w_in, krrot], axis=1))
    wuq = np.asarray(inp["mla_w_uq"][0], f).reshape(256, 4, 192)
    nope = wuq[:, :, 0:128].reshape(256, 512)
    rp = wuq[:, :, 128:192]
    rot = np.concatenate([rp[:, :, 32:64], rp[:, :, 0:32]], axis=2)
    wuq_l = np.ascontiguousarray(np.concatenate([nope, rp.reshape(256, 256), rot.reshape(256, 256)], axis=1))
    wukv = np.asarray(inp["mla_w_ukv"][0], f).reshape(128, 4, 256)
    wukv_l = np.ascontiguousarray(np.concatenate([wukv[:, :, 0:128].reshape(128, 512),
                                                  wukv[:, :, 128:256].reshape(128, 512)], axis=1))
    w2a = np.ascontiguousarray(np.concatenate([np.asarray(inp["rw_w2"][0], f), np.asarray(inp["rw_a2"][0], f)], axis=0))
    wout = np.ascontiguousarray(np.asarray(inp["w_out"][0], f))
    pp = np.zeros((128, NPP), f)

    def colmajor(v, n):
        return np.asarray(v, f).reshape(n, 128).T

    pp[:, PP_GPRE:PP_GPRE + 8] = colmajor(inp["norm_pre_g"][0], 8)
    pp[:, PP_GQ:PP_GQ + 2] = colmajor(inp["mla_q_norm_g"][0], 2)
    pp[:, PP_GKV:PP_GKV + 1] = colmajor(inp["mla_kv_norm_g"][0], 1)
    pp[:, PP_W0:PP_W0 + 4] = colmajor(inp["rw_w0"][0], 4)
    pp[:, PP_A0:PP_A0 + 4] = colmajor(inp["rw_a0"][0], 4)
    pp[:, PP_KK:PP_KK + 4] = colmajor(inp["rw_k_k"][0], 4)
    pp[:, PP_KA:PP_KA + 4] = colmajor(inp["rw_k_a"][0], 4)
    pp[:, PP_RK:PP_RK + 4] = colmajor(np.asarray(inp["rw_r_k"][0]).reshape(512), 4)
    invf = (10000.0 ** (-np.arange(0, 64, 2, dtype=np.float32) / 64)).astype(f)
    invf_turn = (np.concatenate([invf, invf]).astype(np.float64) / (2 * np.pi)).astype(f)
    pp[0:64, PP_INVF] = invf_turn
    mu = np.asarray(inp["rw_mu"][0], f)
    pp[:, PP_MU:PP_MU + 4] = colmajor(mu[0:512], 4)
    pp[:, PP_MU + 4:PP_MU + 8] = colmajor(mu[512:1024], 4)
    pp[:, PP_MU + 8] = mu[1536:1664]
    rows = np.concatenate([mu[1024:1536], np.asarray(inp["rw_ln_g"][0], f), np.asarray(inp["rw_ln_b"][0], f),
                           np.asarray(inp["norm_post_g"][0], f)]).reshape(1, NROWS).astype(f)
    return {"win": win, "wuq": wuq_l, "wukv": wukv_l, "w2a": w2a, "wout": wout, "pp": pp, "rows": rows}


def kernel(**inp):
    x = np.asarray(inp["x"], np.float32)
    pos = np.asarray(inp["positions"], np.int32)
    B, T, _ = x.shape
    nbc = B // N_CORES
    shared = host_params(inp)
    nc = build(nbc, T // 128)
    in_maps = []
    for c in range(N_CORES):
        m = dict(shared)
        m["x"] = np.ascontiguousarray(x[c * nbc:(c + 1) * nbc].reshape(nbc * T, D))
        m["pos"] = np.ascontiguousarray(pos[c * nbc:(c + 1) * nbc])
        in_maps.append(m)
    res = run_bass_kernel_spmd(nc, in_maps, core_ids=list(range(N_CORES)))
    out = np.concatenate([r["out"].reshape(nbc, T, D) for r in res.results], axis=0)
    return out.astype(np.float32)
```

```python
import math
import os
import numpy as np
import concourse.bass as bass
import concourse.mybir as mybir
from concourse.bass_utils import run_bass_kernel_spmd
from contextlib import ExitStack

F32 = mybir.dt.float32
BF16 = mybir.dt.bfloat16
I32 = mybir.dt.int32
AF = mybir.ActivationFunctionType
ALU = mybir.AluOpType
AX = mybir.AxisListType

SAME_ENGINE_SYNC = True
N_CORES = 8
T_FULL = 2048
D = 1024


class TB:
    __slots__ = ("name", "last_w", "readers", "dma_sem", "dma_cnt", "inherit")

    def __init__(self, name, inherit=None):
        self.name = name
        self.last_w = []
        self.readers = {}
        self.dma_sem = None
        self.dma_cnt = 0
        self.inherit = inherit


class V:
    __slots__ = ("ap", "tb")

    def __init__(self, ap, tb):
        self.ap = ap
        self.tb = tb

    def __getitem__(self, idx):
        return V(self.ap[idx], self.tb)

    def bc(self, shape):
        return V(self.ap.to_broadcast(list(shape)), self.tb)

    def re(self, s, **kw):
        return V(self.ap.rearrange(s, **kw), self.tb)


class Prog:
    ENGS = ("pe", "act", "dve", "pool", "sp")

    def __init__(self, nc, es):
        self.nc = nc
        self.es = es
        self.main = []
        self.cur = self.main
        self.stack = []
        self.ops = []
        self.signal = set()

    def sb(self, name, shape, dt=F32):
        t = self.es.enter_context(self.nc.sbuf_tensor("s_" + name, list(shape), dt))
        return V(t[:], TB(name))

    def ps(self, name, shape, dt=F32):
        t = self.es.enter_context(self.nc.psum_tensor("p_" + name, list(shape), dt))
        return V(t[:], TB(name))

    def add(self, eng, emit, reads=(), writes=(), dma_tb=None):
        rt = [r.tb if isinstance(r, V) else r for r in reads]
        wt = [w.tb if isinstance(w, V) else w for w in writes]
        self.cur.append((eng, emit, rt, wt, dma_tb))

    def start_seg(self):
        self.stack.append(self.cur)
        self.cur = []

    def end_seg(self):
        seg = self.cur
        self.cur = self.stack.pop()
        return seg

    def cut(self):
        self.cur.append(None)

    def merge(self, sa, sb):
        def units(seg):
            out, u = [], []
            for r in seg:
                if r is None:
                    if u:
                        out.append(u)
                    u = []
                else:
                    u.append(r)
            if u:
                out.append(u)
            return out
        ua, ub = units(sa), units(sb)
        na, nb = len(ua), len(ub)
        ia = ib = 0
        while ia < na or ib < nb:
            if ib >= nb or (ia < na and ia * nb <= ib * na):
                self.cur.extend(ua[ia])
                ia += 1
            else:
                self.cur.extend(ub[ib])
                ib += 1
            self.cur.append(None)

    @staticmethod
    def _touch(tb):
        if tb.inherit is not None:
            p = tb.inherit
            tb.inherit = None
            Prog._touch(p)
            tb.last_w = list(p.last_w)
            tb.readers = dict(p.readers)

    def finalize(self):
        for rec in self.main:
            if rec is None:
                continue
            (eng, emit, rt, wt, dma_tb) = rec
            idx = len(self.ops)
            deps = []
            for tb in rt:
                self._touch(tb)
                deps.extend(tb.last_w)
            for tb in wt:
                self._touch(tb)
                deps.extend(tb.last_w)
                deps.extend(tb.readers.values())
            if dma_tb is not None:
                dma_tb.dma_cnt += 1
                me = ("d", dma_tb, 16 * dma_tb.dma_cnt)
                rkey = ("d", id(dma_tb))
            else:
                me = ("e", eng, idx)
                rkey = ("e", eng)
            for tb in rt:
                tb.readers[rkey] = me
            for tb in wt:
                tb.last_w = [me]
                tb.readers = {}
            seen = set()
            d2 = []
            for d in deps:
                k = (d[0], id(d[1]) if d[0] == "d" else d[1], d[2])
                if k in seen:
                    continue
                seen.add(k)
                d2.append(d)
                if d[0] == "e" and not (d[1] == eng and (eng == "pe" or not SAME_ENGINE_SYNC)):
                    self.signal.add(d[2])
            self.ops.append((eng, emit, d2, dma_tb))

    def dma(self, out, in_, eng="sp", reads=(), writes=()):
        sbv = out if isinstance(out, V) else in_
        o = out.ap if isinstance(out, V) else out
        i = in_.ap if isinstance(in_, V) else in_
        r = list(reads) + ([in_] if isinstance(in_, V) else [])
        w = list(writes) + ([out] if isinstance(out, V) else [])
        return self.add(eng, lambda e: e.dma_start(out=o, in_=i), r, w, dma_tb=sbv.tb)

    def mm(self, out, lhsT, rhs, start=True, stop=True):
        return self.add("pe", lambda e: e.matmul(out.ap, lhsT.ap, rhs.ap, start=start, stop=stop),
                        [lhsT, rhs], [out])

    def tr(self, out, in_, ident):
        return self.add("pe", lambda e: e.transpose(out.ap, in_.ap, ident.ap), [in_, ident], [out])

    def act(self, out, in_, func, bias=None, scale=1.0, accum_out=None):
        reads = [in_]
        kw = {}
        if isinstance(bias, V):
            reads.append(bias)
            kw["bias"] = bias.ap
        elif bias is not None:
            kw["bias"] = bias
        if isinstance(scale, V):
            reads.append(scale)
            kw["scale"] = scale.ap
        else:
            kw["scale"] = scale
        writes = [out]
        if accum_out is not None:
            writes.append(accum_out)
            kw["accum_out"] = accum_out.ap
        return self.add("act", lambda e: e.activation(out.ap, in_.ap, func, **kw), reads, writes)

    def tt(self, out, a, b, op, eng="dve"):
        return self.add(eng, lambda e: e.tensor_tensor(out.ap, a.ap, b.ap, op), [a, b], [out])

    def ts(self, out, a, s1, op0, s2=None, op1=None, eng="dve"):
        reads = [a]
        x1, x2 = s1, s2
        if isinstance(s1, V):
            reads.append(s1)
            x1 = s1.ap
        if isinstance(s2, V):
            reads.append(s2)
            x2 = s2.ap
        kw = {}
        if op1 is not None:
            kw["op1"] = op1
        return self.add(eng, lambda e: e.tensor_scalar(out.ap, a.ap, x1, x2, op0, **kw), reads, [out])

    def stt(self, out, a, s, b, op0, op1, eng="dve"):
        reads = [a, b]
        x = s
        if isinstance(s, V):
            reads.append(s)
            x = s.ap
        return self.add(eng, lambda e: e.scalar_tensor_tensor(out.ap, a.ap, x, b.ap, op0, op1), reads, [out])

    def copy(self, out, in_, eng="dve"):
        if eng == "act":
            return self.add("act", lambda e: e.copy(out.ap, in_.ap), [in_], [out])
        return self.add(eng, lambda e: e.tensor_copy(out.ap, in_.ap), [in_], [out])

    def memset(self, out, val, eng="dve"):
        return self.add(eng, lambda e: e.memset(out.ap, val), [], [out])

    def recip(self, out, in_):
        return self.add("dve", lambda e: e.reciprocal(out.ap, in_.ap), [in_], [out])

    def rsum(self, out, in_, eng="dve"):
        return self.add(eng, lambda e: e.tensor_reduce(out.ap, in_.ap, AX.X, ALU.add), [in_], [out])

    def scan(self, out, d0, d1, init, op0, op1):
        return self.add("dve", lambda e: e.tensor_tensor_scan(out.ap, d0.ap, d1.ap, init, op0, op1), [d0, d1], [out])

    def aselect(self, out, in_, pattern, cmp, fill, base, cm):
        return self.add("pool", lambda e: e.affine_select(out.ap, in_.ap, pattern, cmp, fill, base=base,
                                                          channel_multiplier=cm), [in_], [out])

    def emit(self, final_wait=()):
        nc, es = self.nc, self.es
        self.finalize()
        ordn = {}
        cnt = {e: 0 for e in self.ENGS}
        for i, (eng, _, _, dma_tb) in enumerate(self.ops):
            if dma_tb is None and i in self.signal:
                cnt[eng] += 1
                ordn[i] = cnt[eng]
        esem = {e: es.enter_context(nc.semaphore("sem_" + e)) for e in self.ENGS}
        for (eng, _, _, dma_tb) in self.ops:
            if dma_tb is not None and dma_tb.dma_sem is None:
                dma_tb.dma_sem = es.enter_context(nc.semaphore("dsem_%s" % dma_tb.name))
        per_eng = {e: [] for e in self.ENGS}
        for i, op in enumerate(self.ops):
            per_eng[op[0]].append(i)
        self.stats = {e: [len(per_eng[e]), cnt[e], 0] for e in self.ENGS}
        block = es.enter_context(nc.Block())
        ops, stats = self.ops, self.stats

        def run(engname, e):
            waited = {}
            for i in per_eng[engname]:
                _, emit, deps, dma_tb = ops[i]
                need = {}
                for d in deps:
                    if d[0] == "e":
                        if d[1] == engname and (engname == "pe" or not SAME_ENGINE_SYNC):
                            continue
                        sem, val, key = esem[d[1]], ordn[d[2]], "e" + d[1]
                    else:
                        sem, val, key = d[1].dma_sem, d[2], id(d[1])
                    if waited.get(key, 0) >= val:
                        continue
                    if key not in need or need[key][1] < val:
                        need[key] = (sem, val)
                for key, (sem, val) in need.items():
                    e.wait_ge(sem, val)
                    waited[key] = val
                    stats[engname][2] += 1
                ins = emit(e)
                if dma_tb is not None:
                    ins.then_inc(dma_tb.dma_sem, 16)
                elif i in ordn:
                    ins.then_inc(esem[engname], 1)
            if engname == "sp":
                for tb in final_wait:
                    if tb.dma_sem is not None:
                        e.wait_ge(tb.dma_sem, 16 * tb.dma_cnt)

        @block.tensor
        def _(e):
            run("pe", e)

        @block.scalar
        def _(e):
            run("act", e)

        @block.vector
        def _(e):
            run("dve", e)

        @block.gpsimd
        def _(e):
            run("pool", e)

        @block.sync
        def _(e):
            run("sp", e)


WC = 3136 + 64 + 512
C_CQ, C_CKV, C_KR, C_R, C_K, C_V, C_XW, C_Z = 0, 256, 384, 448, 960, 1472, 1984, 2112
C_KRROT, C_V2 = 3136, 3200
PP_GPRE, PP_GQ, PP_GKV, PP_W0, PP_A0, PP_KK, PP_KA, PP_RK, PP_INVF, PP_MU = 0, 8, 10, 11, 15, 19, 23, 27, 31, 32
NPP = 41
NROWS = 2560
GN_EPS = 64e-5
NORM_EPS = 1e-6
SM_SCALE = 192.0 ** -0.5


def build(NBC, NT, dbg_names=(), stop=None):
    nc = bass.Bass("TRN2", target_bir_lowering=False)

    def din(name, shape, dt=F32):
        return nc.dram_tensor(name, list(shape), dt, kind="ExternalInput").ap()

    T = NT * 128
    x_d = din("x", [NBC * T, D])
    pos_d = din("pos", [NBC, T], I32)
    win_d = din("win", [D, 3200])
    wuq_d = din("wuq", [256, 1024])
    wukv_d = din("wukv", [128, 1024])
    w2a_d = din("w2a", [128, 512])
    wout_d = din("wout", [D, D])
    pp_d = din("pp", [128, NPP])
    rows_d = din("rows", [1, NROWS])
    out_d = nc.dram_tensor("out", [NBC * T, D], F32, kind="ExternalOutput").ap()
    dbg_d = {}
    for (nm, shape) in dbg_names:
        dbg_d[nm] = nc.dram_tensor("dbg_" + nm, list(shape), F32, kind="ExternalOutput").ap()

    with ExitStack() as es:
        P = Prog(nc, es)
        ident = P.sb("ident", [128, 128])
        identb = P.sb("identb", [128, 128], BF16)
        onesf = P.sb("onesf", [128, 128])
        onesb = P.sb("onesb", [128, 128], BF16)
        blockones = P.sb("blockones", [128, 128])
        headind = P.sb("headind", [128, 2], BF16)
        m_su = P.sb("m_su", [128, 128], BF16)
        m_iu = P.sb("m_iu", [128, 128], BF16)
        m_sl = P.sb("m_sl", [128, 128], BF16)
        mask4 = P.sb("mask4", [128, 4, 128], BF16)
        P.memset(ident, 0.0, eng="pool")
        P.aselect(ident, ident, [[-1, 128]], ALU.not_equal, 1.0, 0, 1)
        P.copy(identb, ident)
        P.memset(onesf, 1.0)
        P.memset(onesb, 1.0)
        P.memset(blockones, 0.0)
        P.memset(blockones[0:64, 0:64], 1.0)
        P.memset(blockones[64:128, 64:128], 1.0)
        P.memset(headind, 0.0)
        P.memset(headind[0:64, 0:1], 1.0)
        P.memset(headind[64:128, 1:2], 1.0)
        for m in (m_su, m_iu, m_sl):
            P.memset(m, 1.0, eng="pool")
        P.aselect(m_su, m_su, [[1, 128]], ALU.is_gt, 0.0, 0, -1)
        P.aselect(m_iu, m_iu, [[1, 128]], ALU.is_ge, 0.0, 0, -1)
        P.aselect(m_sl, m_sl, [[-1, 128]], ALU.is_gt, 0.0, 0, 1)
        P.copy(mask4[:, 0, :], m_su)
        P.copy(mask4[:, 1, :], m_iu)
        P.copy(mask4[:, 2, :], m_su)
        P.copy(mask4[:, 3, :], m_iu)
        m4 = mask4

        pp = P.sb("pp", [128, NPP])
        P.dma(pp, pp_d)
        rows = P.sb("rows", [128, 2048])
        P.dma(rows, rows_d[:, 512:2560].broadcast_to([128, 2048]))
        der = P.sb("der", [128, 32])
        gneg = der[:, 0:8]
        gqneg = der[:, 8:10]
        omka = der[:, 10:14]
        P.ts(gneg, pp[:, PP_GPRE:PP_GPRE + 8], -1.0, ALU.mult)
        P.ts(gqneg, pp[:, PP_GQ:PP_GQ + 2], -1.0, ALU.mult)
        P.ts(omka, pp[:, PP_KA:PP_KA + 4], -1.0, ALU.mult, 1.0, ALU.add)
        Gt = es.enter_context(nc.sbuf_tensor("s_G", [128, 10, 512], F32))
        g = [V(Gt[:][:, i, :], TB("G%d" % i)) for i in range(10)]

        def g3(i, a, parts=128):
            return V(g[i].ap[0:parts, :].rearrange("p (a t) -> p a t", a=a), g[i].tb)

        muv, omuv = g[8], g[9]
        P.dma(muv, rows_d[:, 0:512].broadcast_to([128, 512]))
        P.ts(omuv, muv, -1.0, ALU.mult, 1.0, ALU.add)

        W = P.sb("W", [128, 8, WC], BF16)
        stg = [P.sb("stg0", [128, 3200]), P.sb("stg1", [128, 3200])]
        for c in range(8):
            s = stg[c % 2]
            P.dma(s, win_d[c * 128:(c + 1) * 128, :])
            gg = pp[:, PP_GPRE + c:PP_GPRE + c + 1]
            gn = gneg[:, c:c + 1]
            e1 = "dve" if c % 2 == 0 else "pool"
            e2_ = "pool" if c % 2 == 0 else "dve"
            P.ts(W[:, c, 0:C_V], s[:, 0:C_V], gg, ALU.mult, eng=e1)
            P.ts(W[:, c, C_XW:3136], s[:, C_XW:3136], gg, ALU.mult, eng=e2_)
            P.ts(W[:, c, C_KRROT:C_KRROT + 32], s[:, 3136:3168], gn, ALU.mult, eng=e1)
            P.ts(W[:, c, C_KRROT + 32:C_KRROT + 64], s[:, 3168:3200], gg, ALU.mult, eng=e1)
            P.stt(W[:, c, C_V:C_V + 512], s[:, C_V:C_V + 512], gg, omuv, ALU.mult, ALU.mult)
            P.stt(W[:, c, C_V2:C_V2 + 512], s[:, C_V:C_V + 512], gg, muv, ALU.mult, ALU.mult)
        Wq = P.sb("Wq", [128, 2, 1024], BF16)
        for c in range(2):
            s = stg[c % 2]
            P.dma(s[:, 0:1024], wuq_d[c * 128:(c + 1) * 128, :])
            gg = pp[:, PP_GQ + c:PP_GQ + c + 1]
            gn = gqneg[:, c:c + 1]
            P.ts(Wq[:, c, 0:768], s[:, 0:768], gg, ALU.mult)
            rot_o = Wq[:, c, 768:1024].re("p (h r) -> p h r", h=4)
            rot_in = s[:, 768:1024].re("p (h r) -> p h r", h=4)
            P.ts(rot_o[:, :, 0:32], rot_in[:, :, 0:32], gn, ALU.mult)
            P.ts(rot_o[:, :, 32:64], rot_in[:, :, 32:64], gg, ALU.mult)
        Wkv = P.sb("Wkv", [128, 1024], BF16)
        s = stg[0]
        P.dma(s[:, 0:1024], wukv_d)
        P.ts(Wkv, s[:, 0:1024], pp[:, PP_GKV:PP_GKV + 1], ALU.mult)
        W2A = P.sb("W2A", [128, 512], BF16)
        s = stg[1]
        P.dma(s[:, 0:512], w2a_d)
        P.copy(W2A, s[:, 0:512])
        Wout = P.sb("Wout", [128, 8, 1024], BF16)
        for c in range(8):
            s = stg[c % 2]
            P.dma(s[:, 0:1024], wout_d[c * 128:(c + 1) * 128, :])
            P.copy(Wout[:, c, :], s[:, 0:1024], eng=("dve" if c % 2 == 0 else "pool"))

        carve_off = [0, 0]

        def carve(si, name, shape, dt):
            esz = 4 if dt in (F32, I32) else 2
            n = 1
            for d_ in shape[1:]:
                n *= d_
            nb = n * esz
            c0 = carve_off[si] // 4
            carve_off[si] += nb
            assert carve_off[si] <= 12800, (name, carve_off)
            ap = stg[si].ap[0:shape[0], c0:c0 + nb // 4]
            if dt != F32:
                ap = ap.bitcast(dt)
            if len(shape) == 3:
                ap = ap.rearrange("p (a b) -> p a b", a=shape[1])
            elif len(shape) == 4:
                ap = ap.rearrange("p (a b c) -> p a b c", a=shape[1], b=shape[2])
            return V(ap, TB(name, inherit=stg[si].tb))

        AR = carve(0, "AR", [128, 4, 2, 128], BF16)
        Bt = carve(0, "Bt", [128, 4, 128], BF16)
        Kt = carve(0, "Kt", [128, 4, 128], BF16)
        bhat = carve(0, "bhat", [128, 4, 128], BF16)
        khat = carve(0, "khat", [128, 4, 128], BF16)
        BKtok = carve(0, "BKtok", [128, 8, 128], BF16)
        BUFS = []
        for si_ in range(2):
            BUFS.append((carve(0, "NK%d" % si_, [128, 2, 4, 128], BF16),
                         carve(1, "Am%d" % si_, [128, 2, 128], BF16),
                         carve(1, "QQ%d" % si_, [128, 4, 128], BF16),
                         carve(1, "Mt%d" % si_, [128, 2, 128], BF16),
                         carve(1, "Xb%d" % si_, [128, 2, 64], BF16),
                         carve(1, "Ub%d" % si_, [128, 2, 64], BF16)))
        yrb = carve(1, "yrb", [128, 512], BF16)
        prk = carve(1, "prk", [128, 4, 128], BF16)
        lor = carve(1, "lor", [128, 128], BF16)
        vtokb = carve(1, "vtokb", [128, 512], BF16)
        PT = [carve(1, "PT0", [128, 4, 128], BF16), carve(1, "PT1", [128, 4, 128], BF16)]
        cqn = carve(1, "cqn", [128, 3, 128], BF16)
        qnT = carve(1, "qnT", [128, 4, 128], BF16)

        rot_banks = [P.ps("bank%d" % i, [128, 512]) for i in range(4)]
        bank_y = P.ps("bank_y", [128, 512])
        bank_s = P.ps("bank_s", [128, 512])
        bank_x = P.ps("bank_x", [128, 512])
        bank_t = P.ps("bank_t", [128, 1024], BF16)
        rot_i = [0]
        pool_sel = [rot_banks]
        bank_tf = V(bank_t.ap.bitcast(F32), bank_t.tb)

        def bank():
            pl = pool_sel[0]
            bkk = pl[rot_i[0] % len(pl)]
            rot_i[0] += 1
            return bkk

        KnT = P.sb("KnT", [128, 4, T], BF16)
        krT = P.sb("krT", [64, T], BF16)
        Vm = P.sb("Vm", [128, NT, 512], BF16)
        uT = P.sb("uT", [128, 8, 128], BF16)
        uTs = P.sb("uTs", [128, 8, 128], BF16)
        xt = [P.sb("xt0", [128, D]), P.sb("xt1", [128, D])]
        sc = P.sb("sc", [128, 16])
        Praw = P.sb("Praw", [128, 9, 129])
        H32 = P.sb("H32", [128, 4, 64])
        Hbf = P.sb("Hbf", [128, 4, 64], BF16)
        qrT = P.sb("qrT", [64, 4, 128], BF16)
        zsT = P.sb("zsT", [128, 8, 128], BF16)
        ycatT = P.sb("ycatT", [128, 8, 128], BF16)
        mixed = P.sb("mixed", [128, 9, 128])
        vtokf = P.sb("vtokf", [128, 512])
        atmp = P.sb("atmp", [128, 2, 128])
        dbgst = P.sb("dbgst", [128, 1024]) if dbg_d else None

        def dump(nm, v, ncols):
            if nm not in dbg_d:
                return
            P.copy(dbgst[:, 0:ncols], v)
            P.dma(dbg_d[nm], dbgst[:, 0:ncols])

        def rsqrt(out, in_, scale, eps):
            P.act(out, in_, AF.Ln, bias=eps, scale=scale)
            P.act(out, out, AF.Exp, scale=-0.5)

        def bmid(v, shape):
            return V(v.ap.unsqueeze(1).to_broadcast(list(shape)), v.tb)

        def blast(v, shape):
            return V(v.ap.unsqueeze(2).to_broadcast(list(shape)), v.tb)

        ti_glob = 0
        for b in range(NBC):
            P.memset(uT[:, :, 127:128], 0.0)
            P.memset(Praw[:, :, 0:1], 0.0)
            P.memset(H32, 0.0)
            P.memset(Hbf, 0.0)
            for it in range(NT):
                r0 = b * T + it * 128
                tsl = slice(it * 128, (it + 1) * 128)
                xtile = xt[ti_glob % 2]
                ti_glob += 1
                P.dma(xtile, x_d[r0:r0 + 128, :])
                ss = sc[:, 0:1]
                rstd = sc[:, 1:2]
                P.memset(sc[:, 0:4], 0.0, eng="pool")
                xs = V(g[7].ap.bitcast(BF16), g[7].tb)
                sink = V(g[9].ap.bitcast(BF16), g[9].tb)
                P.act(sink, xtile, AF.Square, accum_out=ss)
                rsqrt(rstd, ss, 1.0 / D, NORM_EPS)
                P.ts(xs, xtile, rstd, ALU.mult)
                for c in range(8):
                    P.tr(bank_t[:, c * 128:(c + 1) * 128], xs[:, c * 128:(c + 1) * 128], identb)
                P.copy(uTs[:, :, 0:1], uT[:, :, 127:128], eng="pool")
                P.copy(uT, bank_t.re("p (c t) -> p c t", c=8), eng="act")
                P.copy(uTs[:, :, 1:128], uT[:, :, 0:127], eng="pool")

                rope = g3(5, 4, 64)
                ropei = V(g[6].ap[0:64, 0:256].bitcast(I32).rearrange("p (a t) -> p a t", a=2), g[6].tb)
                turns, rtmp, sinT, cosT = (rope[:, i, :] for i in range(4))
                P.dma(ropei[:, 0, :], pos_d[b:b + 1, tsl].broadcast_to([64, 128]))
                P.copy(rtmp, ropei[:, 0, :])
                P.ts(turns, rtmp, pp[0:64, PP_INVF:PP_INVF + 1], ALU.mult)
                P.copy(ropei[:, 1, :], turns)
                P.copy(rtmp, ropei[:, 1, :])
                P.tt(rtmp, turns, rtmp, ALU.subtract)
                P.act(sinT, rtmp, AF.Sin, scale=2.0 * math.pi)
                P.ts(turns, turns, 0.25, ALU.add)
                P.copy(ropei[:, 1, :], turns)
                P.copy(rtmp, ropei[:, 1, :])
                P.tt(rtmp, turns, rtmp, ALU.subtract)
                P.act(cosT, rtmp, AF.Sin, scale=2.0 * math.pi)

                def proj(outv, col0, ncols, shift=False):
                    for c in range(8):
                        rhs = uTs[:, c, :] if shift else uT[:, c, :]
                        P.mm(outv, W[:, c, col0:col0 + ncols], rhs, start=(c == 0), stop=(c == 7))

                cq = g3(0, 4)[:, 0:3, :]
                sq3 = g3(1, 4)[:, 0:3, :]
                rs3 = g3(2, 4)[:, 0:3, :]
                qtmp = g3(3, 4, 64)
                qtmp2 = g3(4, 4, 64)
                sg = g3(7, 4)
                bk = bank()
                for j in range(3):
                    proj(bk[:, j * 128:(j + 1) * 128], C_CQ + j * 128, 128)
                P.copy(cq, bk[:, 0:384].re("p (j t) -> p j t", j=3), eng="act")
                bk = bank()
                proj(bk[0:64, 0:128], C_KR, 64)
                proj(bk[0:64, 128:256], C_KRROT, 64)
                P.tt(qtmp[:, 0, :], bk[0:64, 0:128], cosT, ALU.mult)
                P.tt(qtmp[:, 1, :], bk[0:64, 128:256], sinT, ALU.mult)
                P.tt(krT[:, tsl], qtmp[:, 0, :], qtmp[:, 1, :], ALU.add, eng="pool")
                for half in range(2):
                    bk = bank()
                    for j in range(4):
                        proj(bk[:, j * 128:(j + 1) * 128], C_Z + (half * 4 + j) * 128, 128)
                    bk3 = bk.re("p (j t) -> p j t", j=4)
                    P.act(sg, bk3, AF.Sigmoid)
                    P.tt(zsT[:, half * 4:(half + 1) * 4, :], bk3, sg, ALU.mult)
                for q_, col0 in ((0, C_R), (1, C_K)):
                    bk = bank()
                    for j in range(4):
                        proj(bk[:, j * 128:(j + 1) * 128], col0 + j * 128, 128)
                    P.copy(Praw[:, q_ * 4:(q_ + 1) * 4, 1:129], bk.re("p (j t) -> p j t", j=4),
                           eng=("act" if q_ == 0 else "dve"))
                bk = bank()
                proj(bk[:, 0:128], C_XW, 128)
                P.copy(Praw[:, 8, 1:129], bk[:, 0:128], eng="act")
                bk = bank()
                for c in range(8):
                    P.mm(bk, uT[:, c, :], W[:, c, C_V:C_V + 512], start=(c == 0), stop=False)
                for c in range(8):
                    P.mm(bk, uTs[:, c, :], W[:, c, C_V2:C_V2 + 512], start=False, stop=(c == 7))
                P.copy(vtokf, bk, eng="act")
                P.copy(vtokb, vtokf, eng="pool")

                P.tt(sq3, cq, cq, ALU.mult, eng="pool")
                bk = bank()
                P.mm(bk[:, 0:128], onesf, sq3[:, 0, :], start=True, stop=False)
                P.mm(bk[:, 0:128], onesf, sq3[:, 1, :], start=False, stop=True)
                P.mm(bk[:, 128:256], onesf, sq3[:, 2, :], start=True, stop=True)
                rsqrt(rs3[:, 0, :], bk[:, 0:128], 1.0 / 256, NORM_EPS)
                rsqrt(rs3[:, 2, :], bk[:, 128:256], 1.0 / 128, NORM_EPS)
                P.tt(cqn[:, 0:2, :], cq[:, 0:2, :], bmid(rs3[:, 0, :], [128, 2, 128]), ALU.mult)
                P.tt(cqn[:, 2, :], cq[:, 2, :], rs3[:, 2, :], ALU.mult)
                bk = bank()
                for h in range(4):
                    for c in range(2):
                        P.mm(bk[:, h * 128:(h + 1) * 128], Wq[:, c, h * 128:(h + 1) * 128], cqn[:, c, :],
                             start=(c == 0), stop=(c == 1))
                P.copy(qnT, bk.re("p (h t) -> p h t", h=4), eng="act")
                bk = bank()
                bk2 = bank()
                for h in range(4):
                    for c in range(2):
                        P.mm(bk[0:64, h * 128:(h + 1) * 128], Wq[:, c, 512 + h * 64:512 + (h + 1) * 64], cqn[:, c, :],
                             start=(c == 0), stop=(c == 1))
                    for c in range(2):
                        P.mm(bk2[0:64, h * 128:(h + 1) * 128], Wq[:, c, 768 + h * 64:768 + (h + 1) * 64], cqn[:, c, :],
                             start=(c == 0), stop=(c == 1))
                P.tt(qtmp, bk[0:64, :].re("p (h t) -> p h t", h=4), bmid(cosT, [64, 4, 128]), ALU.mult)
                P.tt(qtmp2, bk2[0:64, :].re("p (h t) -> p h t", h=4), bmid(sinT, [64, 4, 128]), ALU.mult)
                P.tt(qrT, qtmp, qtmp2, ALU.add, eng="pool")
                bk = bank()
                for h in range(4):
                    P.mm(bk[:, h * 128:(h + 1) * 128], Wkv[:, h * 128:(h + 1) * 128], cqn[:, 2, :])
                P.copy(KnT[:, :, tsl], bk.re("p (h t) -> p h t", h=4), eng="act")
                bk = bank()
                P.mm(bk, cqn[:, 2, :], Wkv[:, 512:1024])
                P.copy(Vm[:, it, :], bk)

                P.start_seg()
                nj = it + 1
                pti = 0
                for h in range(4):
                    for jb in range(0, nj, 4):
                        njj = min(4, nj - jb)
                        bk = bank() if os.environ.get('NOTF') else bank_tf
                        pt = PT[pti % 2]
                        pti += 1
                        for jj in range(njj):
                            j = jb + jj
                            P.mm(bk[:, jj * 128:(jj + 1) * 128], KnT[:, h, j * 128:(j + 1) * 128], qnT[:, h, :],
                                 start=True, stop=False)
                            P.mm(bk[:, jj * 128:(jj + 1) * 128], krT[:, j * 128:(j + 1) * 128], qrT[:, h, :],
                                 start=False, stop=True)
                        P.act(pt[:, 0:njj, :], bk[:, 0:njj * 128].re("p (j t) -> p j t", j=njj), AF.Exp, scale=SM_SCALE)
                        if jb + njj == nj:
                            P.tt(pt[:, njj - 1, :], pt[:, njj - 1, :], m_iu, ALU.mult, eng="pool")
                        for jj in range(njj):
                            j = jb + jj
                            P.mm(bank_y[:, h * 128:(h + 1) * 128], Vm[:, j, h * 128:(h + 1) * 128], pt[:, jj, :],
                                 start=(j == 0), stop=(j == nj - 1))
                            P.mm(bank_s[:, h * 128:(h + 1) * 128], onesb, pt[:, jj, :],
                                 start=(j == 0), stop=(j == nj - 1))
                        P.cut()
                    P.recip(atmp[:, 0, :], bank_s[:, h * 128:(h + 1) * 128])
                    P.tt(atmp[:, 1, :], bank_y[:, h * 128:(h + 1) * 128], atmp[:, 0, :], ALU.mult)
                    P.tt(ycatT[:, h, :], atmp[:, 1, :], zsT[:, h, :], ALU.mult, eng="pool")
                    P.cut()
                seg_d = P.end_seg()
                P.start_seg()

                e2, av, kk, kmod, bb, tm1, tm2, Lc, Lx, EL = (g3(i, 4) for i in range(10))
                mu_bc = blast(pp[:, PP_MU:PP_MU + 9], [128, 9, 128])
                P.tt(mixed, Praw[:, :, 0:128], Praw[:, :, 1:129], ALU.subtract)
                P.tt(mixed, mixed, mu_bc, ALU.mult)
                P.tt(mixed, mixed, Praw[:, :, 1:129], ALU.add)
                P.copy(Praw[:, :, 0:1], Praw[:, :, 128:129], eng="pool")
                rm = mixed[:, 0:4, :]
                km = mixed[:, 4:8, :]
                P.act(lor[0:64, :], mixed[0:64, 8, :], AF.Tanh)
                P.copy(lor[64:128, :], mixed[64:128, 8, :], eng="pool")
                P.cut()
                bkw = bank()
                bka = bank()
                for hp in range(4):
                    P.mm(bkw[:, hp * 128:(hp + 1) * 128], W2A[0:64, hp * 128:(hp + 1) * 128], lor[0:64, :])
                    P.mm(bka[:, hp * 128:(hp + 1) * 128], W2A[64:128, hp * 128:(hp + 1) * 128], lor[64:128, :])

                def pbc(col):
                    return blast(pp[:, col:col + 4], [128, 4, 128])

                P.tt(e2, bkw.re("p (h t) -> p h t", h=4), pbc(PP_W0), ALU.add)
                P.tt(av, bka.re("p (h t) -> p h t", h=4), pbc(PP_A0), ALU.add)
                P.act(e2, e2, AF.Sigmoid)
                P.act(av, av, AF.Sigmoid)
                P.ts(e2, e2, math.exp(-0.5), ALU.mult)
                P.cut()
                P.tt(kk, km, pbc(PP_KK), ALU.mult)
                P.tt(tm1, kk, kk, ALU.mult)
                bk = bank()
                for hp in range(4):
                    P.mm(bk[:, hp * 128:(hp + 1) * 128], blockones, tm1[:, hp, :])
                rsqrt(tm2, bk.re("p (h t) -> p h t", h=4), 1.0, 1e-24)
                P.tt(kk, kk, tm2, ALU.mult)
                P.cut()
                P.tt(tm1, av, pbc(PP_KA), ALU.mult, eng="pool")
                P.tt(tm1, tm1, blast(omka, [128, 4, 128]), ALU.add, eng="pool")
                P.tt(kmod, km, tm1, ALU.mult, eng="pool")
                P.tt(bb, kk, av, ALU.mult, eng="pool")
                P.tt(tm1, rm, kmod, ALU.mult, eng="pool")
                P.tt(prk, tm1, pbc(PP_RK), ALU.mult, eng="pool")
                P.cut()
                for hp in range(4):
                    P.scan(Lc[:, hp, :], onesf, e2[:, hp, :], 0.0, ALU.mult, ALU.subtract)
                P.tt(Lx, Lc, e2, ALU.add)
                P.act(EL, Lc, AF.Exp)
                P.act(Lx, Lx, AF.Exp)
                P.act(Lc, Lc, AF.Exp, scale=-1.0)
                P.cut()
                gC = EL[:, :, 127:128].bc([128, 4, 128])
                P.tt(AR[:, :, 1, :], rm, EL, ALU.mult)
                P.stt(AR[:, :, 0, :], kk, -1.0, Lx, ALU.mult, ALU.mult)
                P.tt(tm1, bb, Lc, ALU.mult)
                P.tt(tm2, kmod, Lc, ALU.mult, eng="pool")
                P.copy(Bt, tm1, eng="pool")
                P.copy(Kt, tm2, eng="pool")
                P.tt(bhat, tm1, gC, ALU.mult)
                P.tt(khat, tm2, gC, ALU.mult)
                P.cut()
                for hp in range(4):
                    P.tr(bank_t[:, hp * 128:(hp + 1) * 128], bhat[:, hp, :], identb)
                    P.tr(bank_t[:, (4 + hp) * 128:(5 + hp) * 128], khat[:, hp, :], identb)
                P.copy(BKtok, bank_t.re("p (c t) -> p c t", c=8), eng="act")
                P.cut()
                msl1 = m_sl
                idb2 = bmid(identb, [128, 2, 128])

                def emit_group(hbase, S):
                    NKg, Amg, QQg, Mtg, Xg, Ug = S
                    Qg, QTg = QQg[:, 0:2, :], QQg[:, 2:4, :]
                    hp = hbase // 2
                    for hl in range(2):
                        pb = hl * 64
                        bk = bank()
                        rhs = AR[pb:pb + 64, hp, :, :]
                        P.mm(bk[:, 0:256], Bt[pb:pb + 64, hp, :], rhs)
                        P.mm(bk[:, 256:512], Kt[pb:pb + 64, hp, :], rhs)
                        P.tt(NKg[:, hl, :, :], bk.re("p (a t) -> p a t", a=4), m4, ALU.mult)
                        P.cut()
                    bke = bank()
                    bko = bank()
                    P.mm(bke[:, 0:128], AR[0:64, hp, 0, :], Bt[0:64, hp, :])
                    P.mm(bko[:, 0:128], AR[64:128, hp, 0, :], Bt[64:128, hp, :])
                    P.tt(Amg[:, 0, :], bke[:, 0:128], msl1, ALU.mult)
                    P.tt(Amg[:, 1, :], bko[:, 0:128], msl1, ALU.mult)
                    P.tt(Mtg, NKg[:, :, 0, :], idb2, ALU.add, eng="pool")
                    P.cut()
                    qc, qtc = NKg[:, :, 0, :], Amg
                    for k in range(1, 7):
                        bsq = bank()
                        for hl in range(2):
                            if k < 6:
                                P.mm(bsq[:, hl * 128:(hl + 1) * 128], qtc[:, hl, :], qc[:, hl, :])
                            P.mm(bsq[:, 256 + hl * 128:256 + (hl + 1) * 128], qc[:, hl, :], qtc[:, hl, :])
                        if k >= 2:
                            bp = bank()
                            for hl in range(2):
                                P.mm(bp[:, hl * 128:(hl + 1) * 128], qtc[:, hl, :], Mtg[:, hl, :])
                        if k < 6:
                            P.copy(QQg, bsq.re("p (h t) -> p h t", h=4), eng="act")
                        else:
                            P.copy(QQg[:, 2:4, :], bsq[:, 256:512].re("p (h t) -> p h t", h=2), eng="act")
                        if k >= 2:
                            P.tt(Mtg, bp[:, 0:256].re("p (h t) -> p h t", h=2), Mtg, ALU.add)
                        qc, qtc = Qg, QTg
                        P.cut()
                    bp = bank()
                    for hl in range(2):
                        P.mm(bp[:, hl * 128:(hl + 1) * 128], qtc[:, hl, :], Mtg[:, hl, :])
                    P.tt(Mtg, bp[:, 0:256].re("p (h t) -> p h t", h=2), Mtg, ALU.add)
                    P.cut()
                    bk = bank()
                    for hl in range(2):
                        h, pb = hbase + hl, hl * 64
                        P.mm(bk[:, hl * 64:(hl + 1) * 64], AR[pb:pb + 64, hp, 0, :], Hbf[pb:pb + 64, hp, :], start=True, stop=False)
                        P.mm(bk[:, hl * 64:(hl + 1) * 64], NKg[:, hl, 2, :], vtokb[:, h * 64:(h + 1) * 64], start=False, stop=True)
                    P.copy(Xg, bk[:, 0:128].re("p (h v) -> p h v", h=2), eng="act")
                    P.cut()
                    bk = bank()
                    for hl in range(2):
                        P.mm(bk[:, hl * 64:(hl + 1) * 64], Mtg[:, hl, :], Xg[:, hl, :])
                    P.copy(Ug, bk[:, 0:128].re("p (h v) -> p h v", h=2))
                    P.cut()
                    for hl in range(2):
                        h, pb = hbase + hl, hl * 64
                        P.mm(bank_x[:, h * 64:(h + 1) * 64], AR[pb:pb + 64, hp, 1, :], Hbf[pb:pb + 64, hp, :], start=True, stop=False)
                        P.mm(bank_x[:, h * 64:(h + 1) * 64], NKg[:, hl, 1, :], Ug[:, hl, :], start=False, stop=False)
                        P.mm(bank_x[:, h * 64:(h + 1) * 64], NKg[:, hl, 3, :], vtokb[:, h * 64:(h + 1) * 64], start=False, stop=True)
                    bk = bank()
                    for hl in range(2):
                        h = hbase + hl
                        P.mm(bk[:, hl * 64:(hl + 1) * 64], BKtok[:, hp, :], Ug[:, hl, :], start=True, stop=False)
                        P.mm(bk[:, hl * 64:(hl + 1) * 64], BKtok[:, 4 + hp, :], vtokb[:, h * 64:(h + 1) * 64], start=False, stop=True)
                    Hs = H32[:, hp, :]
                    P.tt(Hs, Hs, EL[:, hp, 127:128].bc([128, 64]), ALU.mult)
                    P.tt(Hs[0:64], bk[0:64, 0:64], Hs[0:64], ALU.add)
                    P.tt(Hs[64:128], bk[64:128, 64:128], Hs[64:128], ALU.add)
                    P.copy(Hbf[:, hp, :], Hs, eng="pool")
                    P.cut()

                for rnd in range(2):
                    P.start_seg()
                    pool_sel[0] = rot_banks[0:2]
                    emit_group(4 * rnd, BUFS[0])
                    sg0 = P.end_seg()
                    P.start_seg()
                    pool_sel[0] = rot_banks[2:4]
                    emit_group(4 * rnd + 2, BUFS[1])
                    sg1 = P.end_seg()
                    pool_sel[0] = rot_banks
                    if os.environ.get('NOGRP'):
                        P.cur.extend(sg0)
                        P.cur.extend(sg1)
                    else:
                        P.merge(sg0, sg1)
                gst = sc[:, 8:16]
                yc = V(g[0].ap.rearrange("p (h v) -> p h v", h=8), g[0].tb)
                ysq = V(g[1].ap.rearrange("p (h v) -> p h v", h=8), g[1].tb)
                st1 = V(g[2].ap[:, 0:32].rearrange("p (a h) -> p a h", a=4), g[2].tb)
                y3 = bank_x.re("p (h v) -> p h v", h=8)
                P.rsum(st1[:, 0, :], y3)
                P.ts(st1[:, 0, :], st1[:, 0, :], 1.0 / 64, ALU.mult)
                P.tt(yc, y3, blast(st1[:, 0, :], [128, 8, 64]), ALU.subtract)
                P.tt(ysq, yc, yc, ALU.mult, eng="pool")
                P.rsum(st1[:, 1, :], ysq)
                rsqrt(st1[:, 2, :], st1[:, 1, :], 1.0 / 64, GN_EPS)
                P.cut()
                bkr = bank()
                for hp in range(4):
                    P.mm(bkr[:, hp * 2:hp * 2 + 2], prk[:, hp, :], headind)
                P.copy(st1[:, 3, :], bkr[:, 0:8])
                P.cut()
                P.tt(yc, yc, blast(st1[:, 2, :], [128, 8, 64]), ALU.mult)
                yc2 = yc.re("p h v -> p (h v)")
                P.tt(yc2, yc2, rows[:, 0:512], ALU.mult, eng="pool")
                P.tt(yc2, yc2, rows[:, 512:1024], ALU.add, eng="pool")
                P.tt(ysq, vtokf.re("p (h v) -> p h v", h=8), blast(st1[:, 3, :], [128, 8, 64]), ALU.mult, eng="pool")
                P.tt(yrb, yc2, ysq.re("p h v -> p (h v)"), ALU.add)
                for c in range(4):
                    P.tr(bank_t[:, c * 128:(c + 1) * 128], yrb[:, c * 128:(c + 1) * 128], identb)
                P.tt(ycatT[:, 4:8, :], bank_t[:, 0:512].re("p (c t) -> p c t", c=4), zsT[:, 4:8, :], ALU.mult)

                seg_e = P.end_seg()
                P.merge(seg_d, seg_e)
                bo = [bank(), bank()]
                for n in range(2):
                    for c in range(8):
                        P.mm(bo[n], ycatT[:, c, :], Wout[:, c, n * 512:(n + 1) * 512], start=(c == 0), stop=(c == 7))
                P.act(g[2], bo[0], AF.Square, accum_out=sc[:, 2:3])
                P.act(g[3], bo[1], AF.Square, accum_out=sc[:, 3:4])
                P.tt(sc[:, 4:5], sc[:, 2:3], sc[:, 3:4], ALU.add)
                rsqrt(sc[:, 5:6], sc[:, 4:5], 1.0 / D, NORM_EPS)
                for n in range(2):
                    P.stt(g[4 + n], bo[n], sc[:, 5:6], rows[:, 1024 + n * 512:1024 + (n + 1) * 512], ALU.mult, ALU.mult)
                    P.tt(xtile[:, n * 512:(n + 1) * 512], xtile[:, n * 512:(n + 1) * 512], g[4 + n], ALU.add, eng="pool")
                P.dma(out_d[r0:r0 + 128, :], xtile)

                if b == 0 and it == min(1, NT - 1):
                    dump("ycat", ycatT.re("p c t -> p (c t)"), 1024)
                    dump("yrw", yrb, 512)
                    dump("kk", kk.re("p c t -> p (c t)"), 512)
                    dump("e2", e2.re("p c t -> p (c t)"), 512)
                    dump("H", H32.re("p c t -> p (c t)"), 256)

        fw = [xt[0].tb, xt[1].tb]
        if dbgst is not None:
            fw.append(dbgst.tb)
        P.emit(final_wait=fw)
        build.stats = P.stats
    return nc


def host_params(inp):
    f = np.float32
    w_in = np.asarray(inp["w_in"][0], f)
    kr = w_in[:, 384:448]
    krrot = np.concatenate([kr[:, 32:64], kr[:, 0:32]], axis=1)
    win = np.ascontiguousarray(np.concatenate([w_in, krrot], axis=1))
    wuq = np.asarray(inp["mla_w_uq"][0], f).reshape(256, 4, 192)
    nope = wuq[:, :, 0:128].reshape(256, 512)
    rp = wuq[:, :, 128:192]
    rot = np.concatenate([rp[:, :, 32:64], rp[:, :, 0:32]], axis=2)
    wuq_l = np.ascontiguousarray(np.concatenate([nope, rp.reshape(256, 256), rot.reshape(256, 256)], axis=1))
    wukv = np.asarray(inp["mla_w_ukv"][0], f).reshape(128, 4, 256)
    wukv_l = np.ascontiguousarray(np.concatenate([wukv[:, :, 0:128].reshape(128, 512),
                                                  wukv[:, :, 128:256].reshape(128, 512)], axis=1))
    w2a = np.ascontiguousarray(np.concatenate([np.asarray(inp["rw_w2"][0], f), np.asarray(inp["rw_a2"][0], f)], axis=0))
    wout = np.ascontiguousarray(np.asarray(inp["w_out"][0], f))
    pp = np.zeros((128, NPP), f)

    def colmajor(v, n):
        return np.asarray(v, f).reshape(n, 128).T

    pp[:, PP_GPRE:PP_GPRE + 8] = colmajor(inp["norm_pre_g"][0], 8)
    pp[:, PP_GQ:PP_GQ + 2] = colmajor(inp["mla_q_norm_g"][0], 2)
    pp[:, PP_GKV:PP_GKV + 1] = colmajor(inp["mla_kv_norm_g"][0], 1)
    pp[:, PP_W0:PP_W0 + 4] = colmajor(inp["rw_w0"][0], 4)
    pp[:, PP_A0:PP_A0 + 4] = colmajor(inp["rw_a0"][0], 4)
    pp[:, PP_KK:PP_KK + 4] = colmajor(inp["rw_k_k"][0], 4)
    pp[:, PP_KA:PP_KA + 4] = colmajor(inp["rw_k_a"][0], 4)
    pp[:, PP_RK:PP_RK + 4] = colmajor(np.asarray(inp["rw_r_k"][0]).reshape(512), 4)
    invf = (10000.0 ** (-np.arange(0, 64, 2, dtype=np.float32) / 64)).astype(f)
    invf_turn = (np.concatenate([invf, invf]).astype(np.float64) / (2 * np.pi)).astype(f)
    pp[0:64, PP_INVF] = invf_turn
    mu = np.asarray(inp["rw_mu"][0], f)
    pp[:, PP_MU:PP_MU + 4] = colmajor(mu[0:512], 4)
    pp[:, PP_MU + 4:PP_MU + 8] = colmajor(mu[512:1024], 4)
    pp[:, PP_MU + 8] = mu[1536:1664]
    rows = np.concatenate([mu[1024:1536], np.asarray(inp["rw_ln_g"][0], f), np.asarray(inp["rw_ln_b"][0], f),
                           np.asarray(inp["norm_post_g"][0], f)]).reshape(1, NROWS).astype(f)
    return {"win": win, "wuq": wuq_l, "wukv": wukv_l, "w2a": w2a, "wout": wout, "pp": pp, "rows": rows}


def kernel(**inp):
    x = np.asarray(inp["x"], np.float32)
    pos = np.asarray(inp["positions"], np.int32)
    B, T, _ = x.shape
    nbc = B // N_CORES
    shared = host_params(inp)
    nc = build(nbc, T // 128)
    in_maps = []
    for c in range(N_CORES):
        m = dict(shared)
        m["x"] = np.ascontiguousarray(x[c * nbc:(c + 1) * nbc].reshape(nbc * T, D))
        m["pos"] = np.ascontiguousarray(pos[c * nbc:(c + 1) * nbc])
        in_maps.append(m)
    res = run_bass_kernel_spmd(nc, in_maps, core_ids=list(range(N_CORES)))
    out = np.concatenate([r["out"].reshape(nbc, T, D) for r in res.results], axis=0)
    return out.astype(np.float32)
```

```python
import math
import os
import numpy as np
import concourse.bass as bass
import concourse.mybir as mybir
from concourse.bass_utils import run_bass_kernel_spmd
from contextlib import ExitStack

F32 = mybir.dt.float32
BF16 = mybir.dt.bfloat16
I32 = mybir.dt.int32
AF = mybir.ActivationFunctionType
ALU = mybir.AluOpType
AX = mybir.AxisListType

SAME_ENGINE_SYNC = True
N_CORES = 8
T_FULL = 2048
D = 1024


class TB:
    __slots__ = ("name", "last_w", "readers", "dma_sem", "dma_cnt", "inherit")

    def __init__(self, name, inherit=None):
        self.name = name
        self.last_w = []
        self.readers = {}
        self.dma_sem = None
        self.dma_cnt = 0
        self.inherit = inherit


class V:
    __slots__ = ("ap", "tb")

    def __init__(self, ap, tb):
        self.ap = ap
        self.tb = tb

    def __getitem__(self, idx):
        return V(self.ap[idx], self.tb)

    def bc(self, shape):
        return V(self.ap.to_broadcast(list(shape)), self.tb)

    def re(self, s, **kw):
        return V(self.ap.rearrange(s, **kw), self.tb)


class Prog:
    ENGS = ("pe", "act", "dve", "pool", "sp")

    def __init__(self, nc, es):
        self.nc = nc
        self.es = es
        self.main = []
        self.cur = self.main
        self.stack = []
        self.ops = []
        self.signal = set()

    def sb(self, name, shape, dt=F32):
        t = self.es.enter_context(self.nc.sbuf_tensor("s_" + name, list(shape), dt))
        return V(t[:], TB(name))

    def ps(self, name, shape, dt=F32):
        t = self.es.enter_context(self.nc.psum_tensor("p_" + name, list(shape), dt))
        return V(t[:], TB(name))

    def add(self, eng, emit, reads=(), writes=(), dma_tb=None):
        rt = [r.tb if isinstance(r, V) else r for r in reads]
        wt = [w.tb if isinstance(w, V) else w for w in writes]
        self.cur.append((eng, emit, rt, wt, dma_tb))

    def start_seg(self):
        self.stack.append(self.cur)
        self.cur = []

    def end_seg(self):
        seg = self.cur
        self.cur = self.stack.pop()
        return seg

    def cut(self):
        self.cur.append(None)

    def merge(self, sa, sb):
        def units(seg):
            out, u = [], []
            for r in seg:
                if r is None:
                    if u:
                        out.append(u)
                    u = []
                else:
                    u.append(r)
            if u:
                out.append(u)
            return out
        ua, ub = units(sa), units(sb)
        na, nb = len(ua), len(ub)
        ia = ib = 0
        while ia < na or ib < nb:
            if ib >= nb or (ia < na and ia * nb <= ib * na):
                self.cur.extend(ua[ia])
                ia += 1
            else:
                self.cur.extend(ub[ib])
                ib += 1
            self.cur.append(None)

    @staticmethod
    def _touch(tb):
        if tb.inherit is not None:
            p = tb.inherit
            tb.inherit = None
            Prog._touch(p)
            tb.last_w = list(p.last_w)
            tb.readers = dict(p.readers)

    def finalize(self):
        for rec in self.main:
            if rec is None:
                continue
            (eng, emit, rt, wt, dma_tb) = rec
            idx = len(self.ops)
            deps = []
            for tb in rt:
                self._touch(tb)
                deps.extend(tb.last_w)
            for tb in wt:
                self._touch(tb)
                deps.extend(tb.last_w)
                deps.extend(tb.readers.values())
            if dma_tb is not None:
                dma_tb.dma_cnt += 1
                me = ("d", dma_tb, 16 * dma_tb.dma_cnt)
                rkey = ("d", id(dma_tb))
            else:
                me = ("e", eng, idx)
                rkey = ("e", eng)
            for tb in rt:
                tb.readers[rkey] = me
            for tb in wt:
                tb.last_w = [me]
                tb.readers = {}
            seen = set()
            d2 = []
            for d in deps:
                k = (d[0], id(d[1]) if d[0] == "d" else d[1], d[2])
                if k in seen:
                    continue
                seen.add(k)
                d2.append(d)
                if d[0] == "e" and not (d[1] == eng and (eng == "pe" or not SAME_ENGINE_SYNC)):
                    self.signal.add(d[2])
            self.ops.append((eng, emit, d2, dma_tb))

    def dma(self, out, in_, eng="sp", reads=(), writes=()):
        sbv = out if isinstance(out, V) else in_
        o = out.ap if isinstance(out, V) else out
        i = in_.ap if isinstance(in_, V) else in_
        r = list(reads) + ([in_] if isinstance(in_, V) else [])
        w = list(writes) + ([out] if isinstance(out, V) else [])
        return self.add(eng, lambda e: e.dma_start(out=o, in_=i), r, w, dma_tb=sbv.tb)

    def mm(self, out, lhsT, rhs, start=True, stop=True):
        return self.add("pe", lambda e: e.matmul(out.ap, lhsT.ap, rhs.ap, start=start, stop=stop),
                        [lhsT, rhs], [out])

    def tr(self, out, in_, ident):
        return self.add("pe", lambda e: e.transpose(out.ap, in_.ap, ident.ap), [in_, ident], [out])

    def act(self, out, in_, func, bias=None, scale=1.0, accum_out=None):
        reads = [in_]
        kw = {}
        if isinstance(bias, V):
            reads.append(bias)
            kw["bias"] = bias.ap
        elif bias is not None:
            kw["bias"] = bias
        if isinstance(scale, V):
            reads.append(scale)
            kw["scale"] = scale.ap
        else:
            kw["scale"] = scale
        writes = [out]
        if accum_out is not None:
            writes.append(accum_out)
            kw["accum_out"] = accum_out.ap
        return self.add("act", lambda e: e.activation(out.ap, in_.ap, func, **kw), reads, writes)

    def tt(self, out, a, b, op, eng="dve"):
        return self.add(eng, lambda e: e.tensor_tensor(out.ap, a.ap, b.ap, op), [a, b], [out])

    def ts(self, out, a, s1, op0, s2=None, op1=None, eng="dve"):
        reads = [a]
        x1, x2 = s1, s2
        if isinstance(s1, V):
            reads.append(s1)
            x1 = s1.ap
        if isinstance(s2, V):
            reads.append(s2)
            x2 = s2.ap
        kw = {}
        if op1 is not None:
            kw["op1"] = op1
        return self.add(eng, lambda e: e.tensor_scalar(out.ap, a.ap, x1, x2, op0, **kw), reads, [out])

    def stt(self, out, a, s, b, op0, op1, eng="dve"):
        reads = [a, b]
        x = s
        if isinstance(s, V):
            reads.append(s)
            x = s.ap
        return self.add(eng, lambda e: e.scalar_tensor_tensor(out.ap, a.ap, x, b.ap, op0, op1), reads, [out])

    def copy(self, out, in_, eng="dve"):
        if eng == "act":
            return self.add("act", lambda e: e.copy(out.ap, in_.ap), [in_], [out])
        return self.add(eng, lambda e: e.tensor_copy(out.ap, in_.ap), [in_], [out])

    def memset(self, out, val, eng="dve"):
        return self.add(eng, lambda e: e.memset(out.ap, val), [], [out])

    def recip(self, out, in_):
        return self.add("dve", lambda e: e.reciprocal(out.ap, in_.ap), [in_], [out])

    def rsum(self, out, in_, eng="dve"):
        return self.add(eng, lambda e: e.tensor_reduce(out.ap, in_.ap, AX.X, ALU.add), [in_], [out])

    def scan(self, out, d0, d1, init, op0, op1):
        return self.add("dve", lambda e: e.tensor_tensor_scan(out.ap, d0.ap, d1.ap, init, op0, op1), [d0, d1], [out])

    def aselect(self, out, in_, pattern, cmp, fill, base, cm):
        return self.add("pool", lambda e: e.affine_select(out.ap, in_.ap, pattern, cmp, fill, base=base,
                                                          channel_multiplier=cm), [in_], [out])

    def emit(self, final_wait=()):
        nc, es = self.nc, self.es
        self.finalize()
        ordn = {}
        cnt = {e: 0 for e in self.ENGS}
        for i, (eng, _, _, dma_tb) in enumerate(self.ops):
            if dma_tb is None and i in self.signal:
                cnt[eng] += 1
                ordn[i] = cnt[eng]
        esem = {e: es.enter_context(nc.semaphore("sem_" + e)) for e in self.ENGS}
        for (eng, _, _, dma_tb) in self.ops:
            if dma_tb is not None and dma_tb.dma_sem is None:
                dma_tb.dma_sem = es.enter_context(nc.semaphore("dsem_%s" % dma_tb.name))
        per_eng = {e: [] for e in self.ENGS}
        for i, op in enumerate(self.ops):
            per_eng[op[0]].append(i)
        self.stats = {e: [len(per_eng[e]), cnt[e], 0] for e in self.ENGS}
        block = es.enter_context(nc.Block())
        ops, stats = self.ops, self.stats

        def run(engname, e):
            waited = {}
            for i in per_eng[engname]:
                _, emit, deps, dma_tb = ops[i]
                need = {}
                for d in deps:
                    if d[0] == "e":
                        if d[1] == engname and (engname == "pe" or not SAME_ENGINE_SYNC):
                            continue
                        sem, val, key = esem[d[1]], ordn[d[2]], "e" + d[1]
                    else:
                        sem, val, key = d[1].dma_sem, d[2], id(d[1])
                    if waited.get(key, 0) >= val:
                        continue
                    if key not in need or need[key][1] < val:
                        need[key] = (sem, val)
                for key, (sem, val) in need.items():
                    e.wait_ge(sem, val)
                    waited[key] = val
                    stats[engname][2] += 1
                ins = emit(e)
                if dma_tb is not None:
                    ins.then_inc(dma_tb.dma_sem, 16)
                elif i in ordn:
                    ins.then_inc(esem[engname], 1)
            if engname == "sp":
                for tb in final_wait:
                    if tb.dma_sem is not None:
                        e.wait_ge(tb.dma_sem, 16 * tb.dma_cnt)

        @block.tensor
        def _(e):
            run("pe", e)

        @block.scalar
        def _(e):
            run("act", e)

        @block.vector
        def _(e):
            run("dve", e)

        @block.gpsimd
        def _(e):
            run("pool", e)

        @block.sync
        def _(e):
            run("sp", e)


WC = 3136 + 64 + 512
C_CQ, C_CKV, C_KR, C_R, C_K, C_V, C_XW, C_Z = 0, 256, 384, 448, 960, 1472, 1984, 2112
C_KRROT, C_V2 = 3136, 3200
PP_GPRE, PP_GQ, PP_GKV, PP_W0, PP_A0, PP_KK, PP_KA, PP_RK, PP_INVF, PP_MU = 0, 8, 10, 11, 15, 19, 23, 27, 31, 32
NPP = 41
NROWS = 2560
GN_EPS = 64e-5
NORM_EPS = 1e-6
SM_SCALE = 192.0 ** -0.5


def build(NBC, NT, dbg_names=(), stop=None):
    nc = bass.Bass("TRN2", target_bir_lowering=False)

    def din(name, shape, dt=F32):
        return nc.dram_tensor(name, list(shape), dt, kind="ExternalInput").ap()

    T = NT * 128
    x_d = din("x", [NBC * T, D])
    pos_d = din("pos", [NBC, T], I32)
    win_d = din("win", [D, 3200])
    wuq_d = din("wuq", [256, 1024])
    wukv_d = din("wukv", [128, 1024])
    w2a_d = din("w2a", [128, 512])
    wout_d = din("wout", [D, D])
    pp_d = din("pp", [128, NPP])
    rows_d = din("rows", [1, NROWS])
    out_d = nc.dram_tensor("out", [NBC * T, D], F32, kind="ExternalOutput").ap()
    dbg_d = {}
    for (nm, shape) in dbg_names:
        dbg_d[nm] = nc.dram_tensor("dbg_" + nm, list(shape), F32, kind="ExternalOutput").ap()

    with ExitStack() as es:
        P = Prog(nc, es)
        ident = P.sb("ident", [128, 128])
        identb = P.sb("identb", [128, 128], BF16)
        onesf = P.sb("onesf", [128, 128])
        onesb = P.sb("onesb", [128, 128], BF16)
        blockones = P.sb("blockones", [128, 128])
        headind = P.sb("headind", [128, 2], BF16)
        m_su = P.sb("m_su", [128, 128], BF16)
        m_iu = P.sb("m_iu", [128, 128], BF16)
        m_sl = P.sb("m_sl", [128, 128], BF16)
        mask4 = P.sb("mask4", [128, 4, 128], BF16)
        P.memset(ident, 0.0, eng="pool")
        P.aselect(ident, ident, [[-1, 128]], ALU.not_equal, 1.0, 0, 1)
        P.copy(identb, ident)
        P.memset(onesf, 1.0)
        P.memset(onesb, 1.0)
        P.memset(blockones, 0.0)
        P.memset(blockones[0:64, 0:64], 1.0)
        P.memset(blockones[64:128, 64:128], 1.0)
        P.memset(headind, 0.0)
        P.memset(headind[0:64, 0:1], 1.0)
        P.memset(headind[64:128, 1:2], 1.0)
        for m in (m_su, m_iu, m_sl):
            P.memset(m, 1.0, eng="pool")
        P.aselect(m_su, m_su, [[1, 128]], ALU.is_gt, 0.0, 0, -1)
        P.aselect(m_iu, m_iu, [[1, 128]], ALU.is_ge, 0.0, 0, -1)
        P.aselect(m_sl, m_sl, [[-1, 128]], ALU.is_gt, 0.0, 0, 1)
        P.copy(mask4[:, 0, :], m_su)
        P.copy(mask4[:, 1, :], m_iu)
        P.copy(mask4[:, 2, :], m_su)
        P.copy(mask4[:, 3, :], m_iu)
        m4 = mask4

        pp = P.sb("pp", [128, NPP])
        P.dma(pp, pp_d)
        rows = P.sb("rows", [128, 2048])
        P.dma(rows, rows_d[:, 512:2560].broadcast_to([128, 2048]))
        der = P.sb("der", [128, 32])
        gneg = der[:, 0:8]
        gqneg = der[:, 8:10]
        omka = der[:, 10:14]
        P.ts(gneg, pp[:, PP_GPRE:PP_GPRE + 8], -1.0, ALU.mult)
        P.ts(gqneg, pp[:, PP_GQ:PP_GQ + 2], -1.0, ALU.mult)
        P.ts(omka, pp[:, PP_KA:PP_KA + 4], -1.0, ALU.mult, 1.0, ALU.add)
        Gt = es.enter_context(nc.sbuf_tensor("s_G", [128, 10, 512], F32))
        g = [V(Gt[:][:, i, :], TB("G%d" % i)) for i in range(10)]

        def g3(i, a, parts=128):
            return V(g[i].ap[0:parts, :].rearrange("p (a t) -> p a t", a=a), g[i].tb)

        muv, omuv = g[8], g[9]
        P.dma(muv, rows_d[:, 0:512].broadcast_to([128, 512]))
        P.ts(omuv, muv, -1.0, ALU.mult, 1.0, ALU.add)

        W = P.sb("W", [128, 8, WC], BF16)
        stg = [P.sb("stg0", [128, 3200]), P.sb("stg1", [128, 3200])]
        for c in range(8):
            s = stg[c % 2]
            P.dma(s, win_d[c * 128:(c + 1) * 128, :])
            gg = pp[:, PP_GPRE + c:PP_GPRE + c + 1]
            gn = gneg[:, c:c + 1]
            e1 = "dve" if c % 2 == 0 else "pool"
            e2_ = "pool" if c % 2 == 0 else "dve"
            P.ts(W[:, c, 0:C_V], s[:, 0:C_V], gg, ALU.mult, eng=e1)
            P.ts(W[:, c, C_XW:3136], s[:, C_XW:3136], gg, ALU.mult, eng=e2_)
            P.ts(W[:, c, C_KRROT:C_KRROT + 32], s[:, 3136:3168], gn, ALU.mult, eng=e1)
            P.ts(W[:, c, C_KRROT + 32:C_KRROT + 64], s[:, 3168:3200], gg, ALU.mult, eng=e1)
            P.stt(W[:, c, C_V:C_V + 512], s[:, C_V:C_V + 512], gg, omuv, ALU.mult, ALU.mult)
            P.stt(W[:, c, C_V2:C_V2 + 512], s[:, C_V:C_V + 512], gg, muv, ALU.mult, ALU.mult)
        Wq = P.sb("Wq", [128, 2, 1024], BF16)
        for c in range(2):
            s = stg[c % 2]
            P.dma(s[:, 0:1024], wuq_d[c * 128:(c + 1) * 128, :])
            gg = pp[:, PP_GQ + c:PP_GQ + c + 1]
            gn = gqneg[:, c:c + 1]
            P.ts(Wq[:, c, 0:768], s[:, 0:768], gg, ALU.mult)
            rot_o = Wq[:, c, 768:1024].re("p (h r) -> p h r", h=4)
            rot_in = s[:, 768:1024].re("p (h r) -> p h r", h=4)
            P.ts(rot_o[:, :, 0:32], rot_in[:, :, 0:32], gn, ALU.mult)
            P.ts(rot_o[:, :, 32:64], rot_in[:, :, 32:64], gg, ALU.mult)
        Wkv = P.sb("Wkv", [128, 1024], BF16)
        s = stg[0]
        P.dma(s[:, 0:1024], wukv_d)
        P.ts(Wkv, s[:, 0:1024], pp[:, PP_GKV:PP_GKV + 1], ALU.mult)
        W2A = P.sb("W2A", [128, 512], BF16)
        s = stg[1]
        P.dma(s[:, 0:512], w2a_d)
        P.copy(W2A, s[:, 0:512])
        Wout = P.sb("Wout", [128, 8, 1024], BF16)
        for c in range(8):
            s = stg[c % 2]
            P.dma(s[:, 0:1024], wout_d[c * 128:(c + 1) * 128, :])
            P.copy(Wout[:, c, :], s[:, 0:1024], eng=("dve" if c % 2 == 0 else "pool"))

        carve_off = [0, 0]

        def carve(si, name, shape, dt):
            esz = 4 if dt in (F32, I32) else 2
            n = 1
            for d_ in shape[1:]:
                n *= d_
            nb = n * esz
            c0 = carve_off[si] // 4
            carve_off[si] += nb
            assert carve_off[si] <= 12800, (name, carve_off)
            ap = stg[si].ap[0:shape[0], c0:c0 + nb // 4]
            if dt != F32:
                ap = ap.bitcast(dt)
            if len(shape) == 3:
                ap = ap.rearrange("p (a b) -> p a b", a=shape[1])
            elif len(shape) == 4:
                ap = ap.rearrange("p (a b c) -> p a b c", a=shape[1], b=shape[2])
            return V(ap, TB(name, inherit=stg[si].tb))

        AR = carve(0, "AR", [128, 4, 2, 128], BF16)
        Bt = carve(0, "Bt", [128, 4, 128], BF16)
        Kt = carve(0, "Kt", [128, 4, 128], BF16)
        bhat = carve(0, "bhat", [128, 4, 128], BF16)
        khat = carve(0, "khat", [128, 4, 128], BF16)
        BKtok = carve(0, "BKtok", [128, 8, 128], BF16)
        BUFS = []
        for si_ in range(2):
            BUFS.append((carve(0, "NK%d" % si_, [128, 2, 4, 128], BF16),
                         carve(1, "Am%d" % si_, [128, 2, 128], BF16),
                         carve(1, "QQ%d" % si_, [128, 4, 128], BF16),
                         carve(1, "Mt%d" % si_, [128, 2, 128], BF16),
                         carve(1, "Xb%d" % si_, [128, 2, 64], BF16),
                         carve(1, "Ub%d" % si_, [128, 2, 64], BF16)))
        yrb = carve(1, "yrb", [128, 512], BF16)
        prk = carve(1, "prk", [128, 4, 128], BF16)
        lor = carve(1, "lor", [128, 128], BF16)
        vtokb = carve(1, "vtokb", [128, 512], BF16)
        PT = [carve(1, "PT0", [128, 4, 128], BF16), carve(1, "PT1", [128, 4, 128], BF16)]
        cqn = carve(1, "cqn", [128, 3, 128], BF16)
        qnT = carve(1, "qnT", [128, 4, 128], BF16)

        rot_banks = [P.ps("bank%d" % i, [128, 512]) for i in range(4)]
        bank_y = P.ps("bank_y", [128, 512])
        bank_s = P.ps("bank_s", [128, 512])
        bank_x = P.ps("bank_x", [128, 512])
        bank_t = P.ps("bank_t", [128, 1024], BF16)
        rot_i = [0]
        pool_sel = [rot_banks]
        bank_tf = V(bank_t.ap.bitcast(F32), bank_t.tb)

        def bank():
            pl = pool_sel[0]
            bkk = pl[rot_i[0] % len(pl)]
            rot_i[0] += 1
            return bkk

        KnT = P.sb("KnT", [128, 4, T], BF16)
        krT = P.sb("krT", [64, T], BF16)
        Vm = P.sb("Vm", [128, NT, 512], BF16)
        uT = P.sb("uT", [128, 8, 128], BF16)
        uTs = P.sb("uTs", [128, 8, 128], BF16)
        xt = [P.sb("xt0", [128, D]), P.sb("xt1", [128, D])]
        sc = P.sb("sc", [128, 16])
        Praw = P.sb("Praw", [128, 9, 129])
        H32 = P.sb("H32", [128, 4, 64])
        Hbf = P.sb("Hbf", [128, 4, 64], BF16)
        qrT = P.sb("qrT", [64, 4, 128], BF16)
        zsT = P.sb("zsT", [128, 8, 128], BF16)
        ycatT = P.sb("ycatT", [128, 8, 128], BF16)
        mixed = P.sb("mixed", [128, 9, 128])
        vtokf = P.sb("vtokf", [128, 512])
        atmp = P.sb("atmp", [128, 2, 128])
        dbgst = P.sb("dbgst", [128, 1024]) if dbg_d else None

        def dump(nm, v, ncols):
            if nm not in dbg_d:
                return
            P.copy(dbgst[:, 0:ncols], v)
            P.dma(dbg_d[nm], dbgst[:, 0:ncols])

        def rsqrt(out, in_, scale, eps):
            P.act(out, in_, AF.Ln, bias=eps, scale=scale)
            P.act(out, out, AF.Exp, scale=-0.5)

        def bmid(v, shape):
            return V(v.ap.unsqueeze(1).to_broadcast(list(shape)), v.tb)

        def blast(v, shape):
            return V(v.ap.unsqueeze(2).to_broadcast(list(shape)), v.tb)

        pending_f = []

        def emit_f(xtile, r0):
            bo = [bank(), bank()]
            for n in range(2):
                for c in range(8):
                    P.mm(bo[n], ycatT[:, c, :], Wout[:, c, n * 512:(n + 1) * 512], start=(c == 0), stop=(c == 7))
            P.memset(sc[:, 2:4], 0.0, eng="pool")
            P.act(g[1], bo[0], AF.Square, accum_out=sc[:, 2:3])
            P.act(g[2], bo[1], AF.Square, accum_out=sc[:, 3:4])
            P.tt(sc[:, 4:5], sc[:, 2:3], sc[:, 3:4], ALU.add)
            rsqrt(sc[:, 5:6], sc[:, 4:5], 1.0 / D, NORM_EPS)
            for n, gi in ((0, 4), (1, 8)):
                P.stt(g[gi], bo[n], sc[:, 5:6], rows[:, 1024 + n * 512:1024 + (n + 1) * 512], ALU.mult, ALU.mult)
                P.tt(xtile[:, n * 512:(n + 1) * 512], xtile[:, n * 512:(n + 1) * 512], g[gi], ALU.add, eng="pool")
            P.dma(out_d[r0:r0 + 128, :], xtile)

        tiles = [(b_, it_) for b_ in range(NBC) for it_ in range(NT)]

        def stage_a(n):
            b_, it_ = tiles[n]
            r0_ = b_ * T + it_ * 128
            xtile_ = xt[n % 2]
            P.dma(xtile_, x_d[r0_:r0_ + 128, :])
            ss = sc[:, 0:1]
            rstd = sc[:, 1:2]
            P.memset(ss, 0.0, eng="pool")
            xs = V(g[7].ap.bitcast(BF16), g[7].tb)
            P.act(xs, xtile_, AF.Square, accum_out=ss)
            rsqrt(rstd, ss, 1.0 / D, NORM_EPS)
            P.ts(xs, xtile_, rstd, ALU.mult)
            for c in range(8):
                P.tr(bank_t[:, c * 128:(c + 1) * 128], xs[:, c * 128:(c + 1) * 128], identb)
            if it_ == 0:
                P.memset(uTs[:, :, 0:1], 0.0, eng="pool")
            else:
                P.copy(uTs[:, :, 0:1], uT[:, :, 127:128], eng="pool")
            P.copy(uT, bank_t.re("p (c t) -> p c t", c=8), eng="act")
            P.copy(uTs[:, :, 1:128], uT[:, :, 0:127], eng="pool")

        ti_glob = 0
        for b in range(NBC):
            P.memset(Praw[:, :, 0:1], 0.0)
            P.memset(H32, 0.0)
            P.memset(Hbf, 0.0)
            for it in range(NT):
                r0 = b * T + it * 128
                tsl = slice(it * 128, (it + 1) * 128)
                n_tile = ti_glob
                xtile = xt[ti_glob % 2]
                ti_glob += 1
                stage_a(n_tile)
                if pending_f:
                    emit_f(*pending_f.pop())

                rope = g3(5, 4, 64)
                ropei = V(g[6].ap[0:64, 0:256].bitcast(I32).rearrange("p (a t) -> p a t", a=2), g[6].tb)
                turns, rtmp, sinT, cosT = (rope[:, i, :] for i in range(4))
                P.dma(ropei[:, 0, :], pos_d[b:b + 1, tsl].broadcast_to([64, 128]))
                P.copy(rtmp, ropei[:, 0, :])
                P.ts(turns, rtmp, pp[0:64, PP_INVF:PP_INVF + 1], ALU.mult)
                P.copy(ropei[:, 1, :], turns)
                P.copy(rtmp, ropei[:, 1, :])
                P.tt(rtmp, turns, rtmp, ALU.subtract)
                P.act(sinT, rtmp, AF.Sin, scale=2.0 * math.pi)
                P.ts(turns, turns, 0.25, ALU.add)
                P.copy(ropei[:, 1, :], turns)
                P.copy(rtmp, ropei[:, 1, :])
                P.tt(rtmp, turns, rtmp, ALU.subtract)
                P.act(cosT, rtmp, AF.Sin, scale=2.0 * math.pi)

                def proj(outv, col0, ncols, shift=False):
                    for c in range(8):
                        rhs = uTs[:, c, :] if shift else uT[:, c, :]
                        P.mm(outv, W[:, c, col0:col0 + ncols], rhs, start=(c == 0), stop=(c == 7))

                cq = g3(0, 4)[:, 0:3, :]
                sq3 = g3(1, 4)[:, 0:3, :]
                rs3 = g3(2, 4)[:, 0:3, :]
                qtmp = g3(3, 4, 64)
                qtmp2 = g3(4, 4, 64)
                sg = g3(7, 4)
                bk = bank()
                for j in range(3):
                    proj(bk[:, j * 128:(j + 1) * 128], C_CQ + j * 128, 128)
                P.copy(cq, bk[:, 0:384].re("p (j t) -> p j t", j=3), eng="act")
                bk = bank()
                proj(bk[0:64, 0:128], C_KR, 64)
                proj(bk[0:64, 128:256], C_KRROT, 64)
                P.tt(qtmp[:, 0, :], bk[0:64, 0:128], cosT, ALU.mult)
                P.tt(qtmp[:, 1, :], bk[0:64, 128:256], sinT, ALU.mult)
                P.tt(krT[:, tsl], qtmp[:, 0, :], qtmp[:, 1, :], ALU.add, eng="pool")
                for half in range(2):
                    bk = bank()
                    for j in range(4):
                        proj(bk[:, j * 128:(j + 1) * 128], C_Z + (half * 4 + j) * 128, 128)
                    bk3 = bk.re("p (j t) -> p j t", j=4)
                    P.act(sg, bk3, AF.Sigmoid)
                    P.tt(zsT[:, half * 4:(half + 1) * 4, :], bk3, sg, ALU.mult)
                for q_, col0 in ((0, C_R), (1, C_K)):
                    bk = bank()
                    for j in range(4):
                        proj(bk[:, j * 128:(j + 1) * 128], col0 + j * 128, 128)
                    P.copy(Praw[:, q_ * 4:(q_ + 1) * 4, 1:129], bk.re("p (j t) -> p j t", j=4),
                           eng=("act" if q_ == 0 else "dve"))
                bk = bank()
                proj(bk[:, 0:128], C_XW, 128)
                P.copy(Praw[:, 8, 1:129], bk[:, 0:128], eng="act")
                bk = bank()
                for c in range(8):
                    P.mm(bk, uT[:, c, :], W[:, c, C_V:C_V + 512], start=(c == 0), stop=False)
                for c in range(8):
                    P.mm(bk, uTs[:, c, :], W[:, c, C_V2:C_V2 + 512], start=False, stop=(c == 7))
                P.copy(vtokf, bk, eng="act")
                P.copy(vtokb, vtokf, eng="pool")

                P.tt(sq3, cq, cq, ALU.mult, eng="pool")
                bk = bank()
                P.mm(bk[:, 0:128], onesf, sq3[:, 0, :], start=True, stop=False)
                P.mm(bk[:, 0:128], onesf, sq3[:, 1, :], start=False, stop=True)
                P.mm(bk[:, 128:256], onesf, sq3[:, 2, :], start=True, stop=True)
                rsqrt(rs3[:, 0, :], bk[:, 0:128], 1.0 / 256, NORM_EPS)
                rsqrt(rs3[:, 2, :], bk[:, 128:256], 1.0 / 128, NORM_EPS)
                P.tt(cqn[:, 0:2, :], cq[:, 0:2, :], bmid(rs3[:, 0, :], [128, 2, 128]), ALU.mult)
                P.tt(cqn[:, 2, :], cq[:, 2, :], rs3[:, 2, :], ALU.mult)
                bk = bank()
                for h in range(4):
                    for c in range(2):
                        P.mm(bk[:, h * 128:(h + 1) * 128], Wq[:, c, h * 128:(h + 1) * 128], cqn[:, c, :],
                             start=(c == 0), stop=(c == 1))
                P.copy(qnT, bk.re("p (h t) -> p h t", h=4), eng="act")
                bk = bank()
                bk2 = bank()
                for h in range(4):
                    for c in range(2):
                        P.mm(bk[0:64, h * 128:(h + 1) * 128], Wq[:, c, 512 + h * 64:512 + (h + 1) * 64], cqn[:, c, :],
                             start=(c == 0), stop=(c == 1))
                    for c in range(2):
                        P.mm(bk2[0:64, h * 128:(h + 1) * 128], Wq[:, c, 768 + h * 64:768 + (h + 1) * 64], cqn[:, c, :],
                             start=(c == 0), stop=(c == 1))
                P.tt(qtmp, bk[0:64, :].re("p (h t) -> p h t", h=4), bmid(cosT, [64, 4, 128]), ALU.mult)
                P.tt(qtmp2, bk2[0:64, :].re("p (h t) -> p h t", h=4), bmid(sinT, [64, 4, 128]), ALU.mult)
                P.tt(qrT, qtmp, qtmp2, ALU.add, eng="pool")
                bk = bank()
                for h in range(4):
                    P.mm(bk[:, h * 128:(h + 1) * 128], Wkv[:, h * 128:(h + 1) * 128], cqn[:, 2, :])
                P.copy(KnT[:, :, tsl], bk.re("p (h t) -> p h t", h=4), eng="act")
                bk = bank()
                P.mm(bk, cqn[:, 2, :], Wkv[:, 512:1024])
                P.copy(Vm[:, it, :], bk)

                P.start_seg()
                nj = it + 1
                pti = 0
                for h in range(4):
                    for jb in range(0, nj, 4):
                        njj = min(4, nj - jb)
                        bk = bank() if os.environ.get('NOTF') else bank_tf
                        pt = PT[pti % 2]
                        pti += 1
                        for jj in range(njj):
                            j = jb + jj
                            P.mm(bk[:, jj * 128:(jj + 1) * 128], KnT[:, h, j * 128:(j + 1) * 128], qnT[:, h, :],
                                 start=True, stop=False)
                            P.mm(bk[:, jj * 128:(jj + 1) * 128], krT[:, j * 128:(j + 1) * 128], qrT[:, h, :],
                                 start=False, stop=True)
                        P.act(pt[:, 0:njj, :], bk[:, 0:njj * 128].re("p (j t) -> p j t", j=njj), AF.Exp, scale=SM_SCALE)
                        if jb + njj == nj:
                            P.tt(pt[:, njj - 1, :], pt[:, njj - 1, :], m_iu, ALU.mult, eng="pool")
                        for jj in range(njj):
                            j = jb + jj
                            P.mm(bank_y[:, h * 128:(h + 1) * 128], Vm[:, j, h * 128:(h + 1) * 128], pt[:, jj, :],
                                 start=(j == 0), stop=(j == nj - 1))
                            P.mm(bank_s[:, h * 128:(h + 1) * 128], onesb, pt[:, jj, :],
                                 start=(j == 0), stop=(j == nj - 1))
                        P.cut()
                    P.recip(atmp[:, 0, :], bank_s[:, h * 128:(h + 1) * 128])
                    P.tt(atmp[:, 1, :], bank_y[:, h * 128:(h + 1) * 128], atmp[:, 0, :], ALU.mult)
                    P.tt(ycatT[:, h, :], atmp[:, 1, :], zsT[:, h, :], ALU.mult, eng="pool")
                    P.cut()
                seg_d = P.end_seg()
                P.start_seg()

                e2, av, kk, kmod, bb, tm1, tm2, Lc, Lx, EL = (g3(i, 4) for i in range(10))
                mu_bc = blast(pp[:, PP_MU:PP_MU + 9], [128, 9, 128])
                P.tt(mixed, Praw[:, :, 0:128], Praw[:, :, 1:129], ALU.subtract)
                P.tt(mixed, mixed, mu_bc, ALU.mult)
                P.tt(mixed, mixed, Praw[:, :, 1:129], ALU.add)
                P.copy(Praw[:, :, 0:1], Praw[:, :, 128:129], eng="pool")
                rm = mixed[:, 0:4, :]
                km = mixed[:, 4:8, :]
                P.act(lor[0:64, :], mixed[0:64, 8, :], AF.Tanh)
                P.copy(lor[64:128, :], mixed[64:128, 8, :], eng="pool")
                P.cut()
                bkw = bank()
                bka = bank()
                for hp in range(4):
                    P.mm(bkw[:, hp * 128:(hp + 1) * 128], W2A[0:64, hp * 128:(hp + 1) * 128], lor[0:64, :])
                    P.mm(bka[:, hp * 128:(hp + 1) * 128], W2A[64:128, hp * 128:(hp + 1) * 128], lor[64:128, :])

                def pbc(col):
                    return blast(pp[:, col:col + 4], [128, 4, 128])

                P.tt(e2, bkw.re("p (h t) -> p h t", h=4), pbc(PP_W0), ALU.add)
                P.tt(av, bka.re("p (h t) -> p h t", h=4), pbc(PP_A0), ALU.add)
                P.act(e2, e2, AF.Sigmoid)
                P.act(av, av, AF.Sigmoid)
                P.ts(e2, e2, math.exp(-0.5), ALU.mult)
                P.cut()
                P.tt(kk, km, pbc(PP_KK), ALU.mult)
                P.tt(tm1, kk, kk, ALU.mult)
                bk = bank()
                for hp in range(4):
                    P.mm(bk[:, hp * 128:(hp + 1) * 128], blockones, tm1[:, hp, :])
                rsqrt(tm2, bk.re("p (h t) -> p h t", h=4), 1.0, 1e-24)
                P.tt(kk, kk, tm2, ALU.mult)
                P.cut()
                P.tt(tm1, av, pbc(PP_KA), ALU.mult, eng="pool")
                P.tt(tm1, tm1, blast(omka, [128, 4, 128]), ALU.add, eng="pool")
                P.tt(kmod, km, tm1, ALU.mult, eng="pool")
                P.tt(bb, kk, av, ALU.mult, eng="pool")
                P.tt(tm1, rm, kmod, ALU.mult, eng="pool")
                P.tt(prk, tm1, pbc(PP_RK), ALU.mult, eng="pool")
                P.cut()
                for hp in range(4):
                    P.scan(Lc[:, hp, :], onesf, e2[:, hp, :], 0.0, ALU.mult, ALU.subtract)
                P.tt(Lx, Lc, e2, ALU.add)
                P.act(EL, Lc, AF.Exp)
                P.act(Lx, Lx, AF.Exp)
                P.act(Lc, Lc, AF.Exp, scale=-1.0)
                P.cut()
                gC = EL[:, :, 127:128].bc([128, 4, 128])
                P.tt(AR[:, :, 1, :], rm, EL, ALU.mult)
                P.stt(AR[:, :, 0, :], kk, -1.0, Lx, ALU.mult, ALU.mult)
                P.tt(tm1, bb, Lc, ALU.mult)
                P.tt(tm2, kmod, Lc, ALU.mult, eng="pool")
                P.copy(Bt, tm1, eng="pool")
                P.copy(Kt, tm2, eng="pool")
                P.tt(bhat, tm1, gC, ALU.mult)
                P.tt(khat, tm2, gC, ALU.mult)
                P.cut()
                for hp in range(4):
                    P.tr(bank_t[:, hp * 128:(hp + 1) * 128], bhat[:, hp, :], identb)
                    P.tr(bank_t[:, (4 + hp) * 128:(5 + hp) * 128], khat[:, hp, :], identb)
                P.copy(BKtok, bank_t.re("p (c t) -> p c t", c=8), eng="act")
                P.cut()
                msl1 = m_sl
                idb2 = bmid(identb, [128, 2, 128])

                def emit_group(hbase, S):
                    NKg, Amg, QQg, Mtg, Xg, Ug = S
                    Qg, QTg = QQg[:, 0:2, :], QQg[:, 2:4, :]
                    hp = hbase // 2
                    for hl in range(2):
                        pb = hl * 64
                        bk = bank()
                        rhs = AR[pb:pb + 64, hp, :, :]
                        P.mm(bk[:, 0:256], Bt[pb:pb + 64, hp, :], rhs)
                        P.mm(bk[:, 256:512], Kt[pb:pb + 64, hp, :], rhs)
                        P.tt(NKg[:, hl, :, :], bk.re("p (a t) -> p a t", a=4), m4, ALU.mult)
                        P.cut()
                    bke = bank()
                    bko = bank()
                    P.mm(bke[:, 0:128], AR[0:64, hp, 0, :], Bt[0:64, hp, :])
                    P.mm(bko[:, 0:128], AR[64:128, hp, 0, :], Bt[64:128, hp, :])
                    P.tt(Amg[:, 0, :], bke[:, 0:128], msl1, ALU.mult)
                    P.tt(Amg[:, 1, :], bko[:, 0:128], msl1, ALU.mult)
                    P.tt(Mtg, NKg[:, :, 0, :], idb2, ALU.add, eng="pool")
                    P.cut()
                    qc, qtc = NKg[:, :, 0, :], Amg
                    for k in range(1, 7):
                        bsq = bank()
                        for hl in range(2):
                            if k < 6:
                                P.mm(bsq[:, hl * 128:(hl + 1) * 128], qtc[:, hl, :], qc[:, hl, :])
                            P.mm(bsq[:, 256 + hl * 128:256 + (hl + 1) * 128], qc[:, hl, :], qtc[:, hl, :])
                        if k >= 2:
                            bp = bank()
                            for hl in range(2):
                                P.mm(bp[:, hl * 128:(hl + 1) * 128], qtc[:, hl, :], Mtg[:, hl, :])
                        if k < 6:
                            P.copy(QQg, bsq.re("p (h t) -> p h t", h=4), eng="act")
                        else:
                            P.copy(QQg[:, 2:4, :], bsq[:, 256:512].re("p (h t) -> p h t", h=2), eng="act")
                        if k >= 2:
                            P.tt(Mtg, bp[:, 0:256].re("p (h t) -> p h t", h=2), Mtg, ALU.add)
                        qc, qtc = Qg, QTg
                        P.cut()
                    bp = bank()
                    for hl in range(2):
                        P.mm(bp[:, hl * 128:(hl + 1) * 128], qtc[:, hl, :], Mtg[:, hl, :])
                    P.tt(Mtg, bp[:, 0:256].re("p (h t) -> p h t", h=2), Mtg, ALU.add)
                    P.cut()
                    bk = bank()
                    for hl in range(2):
                        h, pb = hbase + hl, hl * 64
                        P.mm(bk[:, hl * 64:(hl + 1) * 64], AR[pb:pb + 64, hp, 0, :], Hbf[pb:pb + 64, hp, :], start=True, stop=False)
                        P.mm(bk[:, hl * 64:(hl + 1) * 64], NKg[:, hl, 2, :], vtokb[:, h * 64:(h + 1) * 64], start=False, stop=True)
                    P.copy(Xg, bk[:, 0:128].re("p (h v) -> p h v", h=2), eng="act")
                    P.cut()
                    bk = bank()
                    for hl in range(2):
                        P.mm(bk[:, hl * 64:(hl + 1) * 64], Mtg[:, hl, :], Xg[:, hl, :])
                    P.copy(Ug, bk[:, 0:128].re("p (h v) -> p h v", h=2))
                    P.cut()
                    for hl in range(2):
                        h, pb = hbase + hl, hl * 64
                        P.mm(bank_x[:, h * 64:(h + 1) * 64], AR[pb:pb + 64, hp, 1, :], Hbf[pb:pb + 64, hp, :], start=True, stop=False)
                        P.mm(bank_x[:, h * 64:(h + 1) * 64], NKg[:, hl, 1, :], Ug[:, hl, :], start=False, stop=False)
                        P.mm(bank_x[:, h * 64:(h + 1) * 64], NKg[:, hl, 3, :], vtokb[:, h * 64:(h + 1) * 64], start=False, stop=True)
                    bk = bank()
                    for hl in range(2):
                        h = hbase + hl
                        P.mm(bk[:, hl * 64:(hl + 1) * 64], BKtok[:, hp, :], Ug[:, hl, :], start=True, stop=False)
                        P.mm(bk[:, hl * 64:(hl + 1) * 64], BKtok[:, 4 + hp, :], vtokb[:, h * 64:(h + 1) * 64], start=False, stop=True)
                    Hs = H32[:, hp, :]
                    P.tt(Hs, Hs, EL[:, hp, 127:128].bc([128, 64]), ALU.mult)
                    P.tt(Hs[0:64], bk[0:64, 0:64], Hs[0:64], ALU.add)
                    P.tt(Hs[64:128], bk[64:128, 64:128], Hs[64:128], ALU.add)
                    P.copy(Hbf[:, hp, :], Hs, eng="pool")
                    P.cut()

                for rnd in range(2):
                    P.start_seg()
                    pool_sel[0] = rot_banks[0:2]
                    emit_group(4 * rnd, BUFS[0])
                    sg0 = P.end_seg()
                    P.start_seg()
                    pool_sel[0] = rot_banks[2:4]
                    emit_group(4 * rnd + 2, BUFS[1])
                    sg1 = P.end_seg()
                    pool_sel[0] = rot_banks
                    P.merge(sg0, sg1)
                gst = sc[:, 8:16]
                yc = V(g[0].ap.rearrange("p (h v) -> p h v", h=8), g[0].tb)
                ysq = V(g[1].ap.rearrange("p (h v) -> p h v", h=8), g[1].tb)
                st1 = V(g[2].ap[:, 0:32].rearrange("p (a h) -> p a h", a=4), g[2].tb)
                y3 = bank_x.re("p (h v) -> p h v", h=8)
                P.rsum(st1[:, 0, :], y3)
                P.ts(st1[:, 0, :], st1[:, 0, :], 1.0 / 64, ALU.mult)
                P.tt(yc, y3, blast(st1[:, 0, :], [128, 8, 64]), ALU.subtract)
                P.tt(ysq, yc, yc, ALU.mult, eng="pool")
                P.rsum(st1[:, 1, :], ysq)
                rsqrt(st1[:, 2, :], st1[:, 1, :], 1.0 / 64, GN_EPS)
                P.cut()
                bkr = bank()
                for hp in range(4):
                    P.mm(bkr[:, hp * 2:hp * 2 + 2], prk[:, hp, :], headind)
                P.copy(st1[:, 3, :], bkr[:, 0:8])
                P.cut()
                P.tt(yc, yc, blast(st1[:, 2, :], [128, 8, 64]), ALU.mult)
                yc2 = yc.re("p h v -> p (h v)")
                P.tt(yc2, yc2, rows[:, 0:512], ALU.mult, eng="pool")
                P.tt(yc2, yc2, rows[:, 512:1024], ALU.add, eng="pool")
                P.tt(ysq, vtokf.re("p (h v) -> p h v", h=8), blast(st1[:, 3, :], [128, 8, 64]), ALU.mult, eng="pool")
                P.tt(yrb, yc2, ysq.re("p h v -> p (h v)"), ALU.add)
                for c in range(4):
                    P.tr(bank_t[:, c * 128:(c + 1) * 128], yrb[:, c * 128:(c + 1) * 128], identb)
                P.tt(ycatT[:, 4:8, :], bank_t[:, 0:512].re("p (c t) -> p c t", c=4), zsT[:, 4:8, :], ALU.mult)

                seg_e = P.end_seg()
                P.merge(seg_d, seg_e)
                pending_f.append((xtile, r0))

                if b == 0 and it == min(1, NT - 1):
                    dump("ycat", ycatT.re("p c t -> p (c t)"), 1024)
                    dump("yrw", yrb, 512)
                    dump("kk", kk.re("p c t -> p (c t)"), 512)
                    dump("e2", e2.re("p c t -> p (c t)"), 512)
                    dump("H", H32.re("p c t -> p (c t)"), 256)

        if pending_f:
            emit_f(*pending_f.pop())
        fw = [xt[0].tb, xt[1].tb]
        if dbgst is not None:
            fw.append(dbgst.tb)
        P.emit(final_wait=fw)
        build.stats = P.stats
    return nc


def host_params(inp):
    f = np.float32
    w_in = np.asarray(inp["w_in"][0], f)
    kr = w_in[:, 384:448]
    krrot = np.concatenate([kr[:, 32:64], kr[:, 0:32]], axis=1)
    win = np.ascontiguousarray(np.concatenate([w_in, krrot], axis=1))
    wuq = np.asarray(inp["mla_w_uq"][0], f).reshape(256, 4, 192)
    nope = wuq[:, :, 0:128].reshape(256, 512)
    rp = wuq[:, :, 128:192]
    rot = np.concatenate([rp[:, :, 32:64], rp[:, :, 0:32]], axis=2)
    wuq_l = np.ascontiguousarray(np.concatenate([nope, rp.reshape(256, 256), rot.reshape(256, 256)], axis=1))
    wukv = np.asarray(inp["mla_w_ukv"][0], f).reshape(128, 4, 256)
    wukv_l = np.ascontiguousarray(np.concatenate([wukv[:, :, 0:128].reshape(128, 512),
                                                  wukv[:, :, 128:256].reshape(128, 512)], axis=1))
    w2a = np.ascontiguousarray(np.concatenate([np.asarray(inp["rw_w2"][0], f), np.asarray(inp["rw_a2"][0], f)], axis=0))
    wout = np.ascontiguousarray(np.asarray(inp["w_out"][0], f))
    pp = np.zeros((128, NPP), f)

    def colmajor(v, n):
        return np.asarray(v, f).reshape(n, 128).T

    pp[:, PP_GPRE:PP_GPRE + 8] = colmajor(inp["norm_pre_g"][0], 8)
    pp[:, PP_GQ:PP_GQ + 2] = colmajor(inp["mla_q_norm_g"][0], 2)
    pp[:, PP_GKV:PP_GKV + 1] = colmajor(inp["mla_kv_norm_g"][0], 1)
    pp[:, PP_W0:PP_W0 + 4] = colmajor(inp["rw_w0"][0], 4)
    pp[:, PP_A0:PP_A0 + 4] = colmajor(inp["rw_a0"][0], 4)
    pp[:, PP_KK:PP_KK + 4] = colmajor(inp["rw_k_k"][0], 4)
    pp[:, PP_KA:PP_KA + 4] = colmajor(inp["rw_k_a"][0], 4)
    pp[:, PP_RK:PP_RK + 4] = colmajor(np.asarray(inp["rw_r_k"][0]).reshape(512), 4)
    invf = (10000.0 ** (-np.arange(0, 64, 2, dtype=np.float32) / 64)).astype(f)
    invf_turn = (np.concatenate([invf, invf]).astype(np.float64) / (2 * np.pi)).astype(f)
    pp[0:64, PP_INVF] = invf_turn
    mu = np.asarray(inp["rw_mu"][0], f)
    pp[:, PP_MU:PP_MU + 4] = colmajor(mu[0:512], 4)
    pp[:, PP_MU + 4:PP_MU + 8] = colmajor(mu[512:1024], 4)
    pp[:, PP_MU + 8] = mu[1536:1664]
    rows = np.concatenate([mu[1024:1536], np.asarray(inp["rw_ln_g"][0], f), np.asarray(inp["rw_ln_b"][0], f),
                           np.asarray(inp["norm_post_g"][0], f)]).reshape(1, NROWS).astype(f)
    return {"win": win, "wuq": wuq_l, "wukv": wukv_l, "w2a": w2a, "wout": wout, "pp": pp, "rows": rows}


def kernel(**inp):
    x = np.asarray(inp["x"], np.float32)
    pos = np.asarray(inp["positions"], np.int32)
    B, T, _ = x.shape
    nbc = B // N_CORES
    shared = host_params(inp)
    nc = build(nbc, T // 128)
    in_maps = []
    for c in range(N_CORES):
        m = dict(shared)
        m["x"] = np.ascontiguousarray(x[c * nbc:(c + 1) * nbc].reshape(nbc * T, D))
        m["pos"] = np.ascontiguousarray(pos[c * nbc:(c + 1) * nbc])
        in_maps.append(m)
    res = run_bass_kernel_spmd(nc, in_maps, core_ids=list(range(N_CORES)))
    out = np.concatenate([r["out"].reshape(nbc, T, D) for r in res.results], axis=0)
    return out.astype(np.float32)
```

```python
import math
import os
import numpy as np
import concourse.bass as bass
import concourse.mybir as mybir
from concourse.bass_utils import run_bass_kernel_spmd
from contextlib import ExitStack

F32 = mybir.dt.float32
BF16 = mybir.dt.bfloat16
I32 = mybir.dt.int32
AF = mybir.ActivationFunctionType
ALU = mybir.AluOpType
AX = mybir.AxisListType

SAME_ENGINE_SYNC = True
N_CORES = 8
T_FULL = 2048
D = 1024


class TB:
    __slots__ = ("name", "last_w", "readers", "dma_sem", "dma_cnt", "inherit")

    def __init__(self, name, inherit=None):
        self.name = name
        self.last_w = []
        self.readers = {}
        self.dma_sem = None
        self.dma_cnt = 0
        self.inherit = inherit


class V:
    __slots__ = ("ap", "tb")

    def __init__(self, ap, tb):
        self.ap = ap
        self.tb = tb

    def __getitem__(self, idx):
        return V(self.ap[idx], self.tb)

    def bc(self, shape):
        return V(self.ap.to_broadcast(list(shape)), self.tb)

    def re(self, s, **kw):
        return V(self.ap.rearrange(s, **kw), self.tb)


class Prog:
    ENGS = ("pe", "act", "dve", "pool", "sp")

    def __init__(self, nc, es):
        self.nc = nc
        self.es = es
        self.main = []
        self.cur = self.main
        self.stack = []
        self.ops = []
        self.signal = set()

    def sb(self, name, shape, dt=F32):
        t = self.es.enter_context(self.nc.sbuf_tensor("s_" + name, list(shape), dt))
        return V(t[:], TB(name))

    def ps(self, name, shape, dt=F32):
        t = self.es.enter_context(self.nc.psum_tensor("p_" + name, list(shape), dt))
        return V(t[:], TB(name))

    def add(self, eng, emit, reads=(), writes=(), dma_tb=None):
        rt = [r.tb if isinstance(r, V) else r for r in reads]
        wt = [w.tb if isinstance(w, V) else w for w in writes]
        self.cur.append((eng, emit, rt, wt, dma_tb))

    def start_seg(self):
        self.stack.append(self.cur)
        self.cur = []

    def end_seg(self):
        seg = self.cur
        self.cur = self.stack.pop()
        return seg

    def cut(self):
        self.cur.append(None)

    def merge(self, sa, sb):
        def units(seg):
            out, u = [], []
            for r in seg:
                if r is None:
                    if u:
                        out.append(u)
                    u = []
                else:
                    u.append(r)
            if u:
                out.append(u)
            return out
        ua, ub = units(sa), units(sb)
        na, nb = len(ua), len(ub)
        ia = ib = 0
        while ia < na or ib < nb:
            if ib >= nb or (ia < na and ia * nb <= ib * na):
                self.cur.extend(ua[ia])
                ia += 1
            else:
                self.cur.extend(ub[ib])
                ib += 1
            self.cur.append(None)

    @staticmethod
    def _touch(tb):
        if tb.inherit is not None:
            p = tb.inherit
            tb.inherit = None
            Prog._touch(p)
            tb.last_w = list(p.last_w)
            tb.readers = dict(p.readers)

    def finalize(self):
        for rec in self.main:
            if rec is None:
                continue
            (eng, emit, rt, wt, dma_tb) = rec
            idx = len(self.ops)
            deps = []
            for tb in rt:
                self._touch(tb)
                deps.extend(tb.last_w)
            for tb in wt:
                self._touch(tb)
                deps.extend(tb.last_w)
                deps.extend(tb.readers.values())
            if dma_tb is not None:
                dma_tb.dma_cnt += 1
                me = ("d", dma_tb, 16 * dma_tb.dma_cnt)
                rkey = ("d", id(dma_tb))
            else:
                me = ("e", eng, idx)
                rkey = ("e", eng)
            for tb in rt:
                tb.readers[rkey] = me
            for tb in wt:
                tb.last_w = [me]
                tb.readers = {}
            seen = set()
            d2 = []
            for d in deps:
                k = (d[0], id(d[1]) if d[0] == "d" else d[1], d[2])
                if k in seen:
                    continue
                seen.add(k)
                d2.append(d)
                if d[0] == "e" and not (d[1] == eng and (eng == "pe" or not SAME_ENGINE_SYNC)):
                    self.signal.add(d[2])
            self.ops.append((eng, emit, d2, dma_tb))

    def dma(self, out, in_, eng="sp", reads=(), writes=()):
        sbv = out if isinstance(out, V) else in_
        o = out.ap if isinstance(out, V) else out
        i = in_.ap if isinstance(in_, V) else in_
        r = list(reads) + ([in_] if isinstance(in_, V) else [])
        w = list(writes) + ([out] if isinstance(out, V) else [])
        return self.add(eng, lambda e: e.dma_start(out=o, in_=i), r, w, dma_tb=sbv.tb)

    def mm(self, out, lhsT, rhs, start=True, stop=True):
        return self.add("pe", lambda e: e.matmul(out.ap, lhsT.ap, rhs.ap, start=start, stop=stop),
                        [lhsT, rhs], [out])

    def tr(self, out, in_, ident):
        return self.add("pe", lambda e: e.transpose(out.ap, in_.ap, ident.ap), [in_, ident], [out])

    def act(self, out, in_, func, bias=None, scale=1.0, accum_out=None):
        reads = [in_]
        kw = {}
        if isinstance(bias, V):
            reads.append(bias)
            kw["bias"] = bias.ap
        elif bias is not None:
            kw["bias"] = bias
        if isinstance(scale, V):
            reads.append(scale)
            kw["scale"] = scale.ap
        else:
            kw["scale"] = scale
        writes = [out]
        if accum_out is not None:
            writes.append(accum_out)
            kw["accum_out"] = accum_out.ap
        return self.add("act", lambda e: e.activation(out.ap, in_.ap, func, **kw), reads, writes)

    def tt(self, out, a, b, op, eng="dve"):
        return self.add(eng, lambda e: e.tensor_tensor(out.ap, a.ap, b.ap, op), [a, b], [out])

    def ts(self, out, a, s1, op0, s2=None, op1=None, eng="dve"):
        reads = [a]
        x1, x2 = s1, s2
        if isinstance(s1, V):
            reads.append(s1)
            x1 = s1.ap
        if isinstance(s2, V):
            reads.append(s2)
            x2 = s2.ap
        kw = {}
        if op1 is not None:
            kw["op1"] = op1
        return self.add(eng, lambda e: e.tensor_scalar(out.ap, a.ap, x1, x2, op0, **kw), reads, [out])

    def stt(self, out, a, s, b, op0, op1, eng="dve"):
        reads = [a, b]
        x = s
        if isinstance(s, V):
            reads.append(s)
            x = s.ap
        return self.add(eng, lambda e: e.scalar_tensor_tensor(out.ap, a.ap, x, b.ap, op0, op1), reads, [out])

    def copy(self, out, in_, eng="dve"):
        if eng == "act":
            return self.add("act", lambda e: e.copy(out.ap, in_.ap), [in_], [out])
        return self.add(eng, lambda e: e.tensor_copy(out.ap, in_.ap), [in_], [out])

    def memset(self, out, val, eng="dve"):
        return self.add(eng, lambda e: e.memset(out.ap, val), [], [out])

    def recip(self, out, in_):
        return self.add("dve", lambda e: e.reciprocal(out.ap, in_.ap), [in_], [out])

    def rsum(self, out, in_, eng="dve"):
        return self.add(eng, lambda e: e.tensor_reduce(out.ap, in_.ap, AX.X, ALU.add), [in_], [out])

    def scan(self, out, d0, d1, init, op0, op1):
        return self.add("dve", lambda e: e.tensor_tensor_scan(out.ap, d0.ap, d1.ap, init, op0, op1), [d0, d1], [out])

    def aselect(self, out, in_, pattern, cmp, fill, base, cm):
        return self.add("pool", lambda e: e.affine_select(out.ap, in_.ap, pattern, cmp, fill, base=base,
                                                          channel_multiplier=cm), [in_], [out])

    def emit(self, final_wait=()):
        nc, es = self.nc, self.es
        self.finalize()
        ordn = {}
        cnt = {e: 0 for e in self.ENGS}
        for i, (eng, _, _, dma_tb) in enumerate(self.ops):
            if dma_tb is None and i in self.signal:
                cnt[eng] += 1
                ordn[i] = cnt[eng]
        esem = {e: es.enter_context(nc.semaphore("sem_" + e)) for e in self.ENGS}
        for (eng, _, _, dma_tb) in self.ops:
            if dma_tb is not None and dma_tb.dma_sem is None:
                dma_tb.dma_sem = es.enter_context(nc.semaphore("dsem_%s" % dma_tb.name))
        per_eng = {e: [] for e in self.ENGS}
        for i, op in enumerate(self.ops):
            per_eng[op[0]].append(i)
        self.stats = {e: [len(per_eng[e]), cnt[e], 0] for e in self.ENGS}
        block = es.enter_context(nc.Block())
        ops, stats = self.ops, self.stats

        def run(engname, e):
            waited = {}
            for i in per_eng[engname]:
                _, emit, deps, dma_tb = ops[i]
                need = {}
                for d in deps:
                    if d[0] == "e":
                        if d[1] == engname and (engname == "pe" or not SAME_ENGINE_SYNC):
                            continue
                        sem, val, key = esem[d[1]], ordn[d[2]], "e" + d[1]
                    else:
                        sem, val, key = d[1].dma_sem, d[2], id(d[1])
                    if waited.get(key, 0) >= val:
                        continue
                    if key not in need or need[key][1] < val:
                        need[key] = (sem, val)
                for key, (sem, val) in need.items():
                    e.wait_ge(sem, val)
                    waited[key] = val
                    stats[engname][2] += 1
                ins = emit(e)
                if dma_tb is not None:
                    ins.then_inc(dma_tb.dma_sem, 16)
                elif i in ordn:
                    ins.then_inc(esem[engname], 1)
            if engname == "sp":
                for tb in final_wait:
                    if tb.dma_sem is not None:
                        e.wait_ge(tb.dma_sem, 16 * tb.dma_cnt)

        @block.tensor
        def _(e):
            run("pe", e)

        @block.scalar
        def _(e):
            run("act", e)

        @block.vector
        def _(e):
            run("dve", e)

        @block.gpsimd
        def _(e):
            run("pool", e)

        @block.sync
        def _(e):
            run("sp", e)


WC = 3136 + 64 + 512
C_CQ, C_CKV, C_KR, C_R, C_K, C_V, C_XW, C_Z = 0, 256, 384, 448, 960, 1472, 1984, 2112
C_KRROT, C_V2 = 3136, 3200
PP_GPRE, PP_GQ, PP_GKV, PP_W0, PP_A0, PP_KK, PP_KA, PP_RK, PP_INVF, PP_MU = 0, 8, 10, 11, 15, 19, 23, 27, 31, 32
NPP = 41
NROWS = 2560
GN_EPS = 64e-5
NORM_EPS = 1e-6
SM_SCALE = 192.0 ** -0.5


def build(NBC, NT, dbg_names=(), stop=None):
    nc = bass.Bass("TRN2", target_bir_lowering=False)

    def din(name, shape, dt=F32):
        return nc.dram_tensor(name, list(shape), dt, kind="ExternalInput").ap()

    T = NT * 128
    x_d = din("x", [NBC * T, D])
    pos_d = din("pos", [NBC, T], I32)
    win_d = din("win", [D, 3200])
    wuq_d = din("wuq", [256, 1024])
    wukv_d = din("wukv", [128, 1024])
    w2a_d = din("w2a", [128, 512])
    wout_d = din("wout", [D, D])
    pp_d = din("pp", [128, NPP])
    rows_d = din("rows", [1, NROWS])
    out_d = nc.dram_tensor("out", [NBC * T, D], F32, kind="ExternalOutput").ap()
    dbg_d = {}
    for (nm, shape) in dbg_names:
        dbg_d[nm] = nc.dram_tensor("dbg_" + nm, list(shape), F32, kind="ExternalOutput").ap()

    with ExitStack() as es:
        P = Prog(nc, es)
        ident = P.sb("ident", [128, 128])
        identb = P.sb("identb", [128, 128], BF16)
        onesf = P.sb("onesf", [128, 128])
        onesb = P.sb("onesb", [128, 128], BF16)
        blockones = P.sb("blockones", [128, 128])
        headind = P.sb("headind", [128, 2], BF16)
        m_su = P.sb("m_su", [128, 128], BF16)
        m_iu = P.sb("m_iu", [128, 128], BF16)
        m_sl = P.sb("m_sl", [128, 128], BF16)
        mask4 = P.sb("mask4", [128, 4, 128], BF16)
        P.memset(ident, 0.0, eng="pool")
        P.aselect(ident, ident, [[-1, 128]], ALU.not_equal, 1.0, 0, 1)
        P.copy(identb, ident)
        P.memset(onesf, 1.0)
        P.memset(onesb, 1.0)
        P.memset(blockones, 0.0)
        P.memset(blockones[0:64, 0:64], 1.0)
        P.memset(blockones[64:128, 64:128], 1.0)
        P.memset(headind, 0.0)
        P.memset(headind[0:64, 0:1], 1.0)
        P.memset(headind[64:128, 1:2], 1.0)
        for m in (m_su, m_iu, m_sl):
            P.memset(m, 1.0, eng="pool")
        P.aselect(m_su, m_su, [[1, 128]], ALU.is_gt, 0.0, 0, -1)
        P.aselect(m_iu, m_iu, [[1, 128]], ALU.is_ge, 0.0, 0, -1)
        P.aselect(m_sl, m_sl, [[-1, 128]], ALU.is_gt, 0.0, 0, 1)
        P.copy(mask4[:, 0, :], m_su)
        P.copy(mask4[:, 1, :], m_iu)
        P.copy(mask4[:, 2, :], m_su)
        P.copy(mask4[:, 3, :], m_iu)
        m4 = mask4

        pp = P.sb("pp", [128, NPP])
        P.dma(pp, pp_d)
        rows = P.sb("rows", [128, 2048])
        P.dma(rows, rows_d[:, 512:2560].broadcast_to([128, 2048]))
        der = P.sb("der", [128, 32])
        gneg = der[:, 0:8]
        gqneg = der[:, 8:10]
        omka = der[:, 10:14]
        P.ts(gneg, pp[:, PP_GPRE:PP_GPRE + 8], -1.0, ALU.mult)
        P.ts(gqneg, pp[:, PP_GQ:PP_GQ + 2], -1.0, ALU.mult)
        P.ts(omka, pp[:, PP_KA:PP_KA + 4], -1.0, ALU.mult, 1.0, ALU.add)
        Gt = es.enter_context(nc.sbuf_tensor("s_G", [128, 10, 512], F32))
        g = [V(Gt[:][:, i, :], TB("G%d" % i)) for i in range(10)]

        def g3(i, a, parts=128):
            return V(g[i].ap[0:parts, :].rearrange("p (a t) -> p a t", a=a), g[i].tb)

        muv, omuv = g[8], g[9]
        P.dma(muv, rows_d[:, 0:512].broadcast_to([128, 512]))
        P.ts(omuv, muv, -1.0, ALU.mult, 1.0, ALU.add)

        W = P.sb("W", [128, 8, WC], BF16)
        stg = [P.sb("stg0", [128, 3200]), P.sb("stg1", [128, 3200])]
        for c in range(8):
            s = stg[c % 2]
            P.dma(s, win_d[c * 128:(c + 1) * 128, :])
            gg = pp[:, PP_GPRE + c:PP_GPRE + c + 1]
            gn = gneg[:, c:c + 1]
            e1 = "dve" if c % 2 == 0 else "pool"
            e2_ = "pool" if c % 2 == 0 else "dve"
            P.ts(W[:, c, 0:C_V], s[:, 0:C_V], gg, ALU.mult, eng=e1)
            P.ts(W[:, c, C_XW:3136], s[:, C_XW:3136], gg, ALU.mult, eng=e2_)
            P.ts(W[:, c, C_KRROT:C_KRROT + 32], s[:, 3136:3168], gn, ALU.mult, eng=e1)
            P.ts(W[:, c, C_KRROT + 32:C_KRROT + 64], s[:, 3168:3200], gg, ALU.mult, eng=e1)
            P.stt(W[:, c, C_V:C_V + 512], s[:, C_V:C_V + 512], gg, omuv, ALU.mult, ALU.mult)
            P.stt(W[:, c, C_V2:C_V2 + 512], s[:, C_V:C_V + 512], gg, muv, ALU.mult, ALU.mult)
        Wq = P.sb("Wq", [128, 2, 1024], BF16)
        for c in range(2):
            s = stg[c % 2]
            P.dma(s[:, 0:1024], wuq_d[c * 128:(c + 1) * 128, :])
            gg = pp[:, PP_GQ + c:PP_GQ + c + 1]
            gn = gqneg[:, c:c + 1]
            P.ts(Wq[:, c, 0:768], s[:, 0:768], gg, ALU.mult)
            rot_o = Wq[:, c, 768:1024].re("p (h r) -> p h r", h=4)
            rot_in = s[:, 768:1024].re("p (h r) -> p h r", h=4)
            P.ts(rot_o[:, :, 0:32], rot_in[:, :, 0:32], gn, ALU.mult)
            P.ts(rot_o[:, :, 32:64], rot_in[:, :, 32:64], gg, ALU.mult)
        Wkv = P.sb("Wkv", [128, 1024], BF16)
        s = stg[0]
        P.dma(s[:, 0:1024], wukv_d)
        P.ts(Wkv, s[:, 0:1024], pp[:, PP_GKV:PP_GKV + 1], ALU.mult)
        W2A = P.sb("W2A", [128, 512], BF16)
        s = stg[1]
        P.dma(s[:, 0:512], w2a_d)
        P.copy(W2A, s[:, 0:512])
        Wout = P.sb("Wout", [128, 8, 1024], BF16)
        for c in range(8):
            s = stg[c % 2]
            P.dma(s[:, 0:1024], wout_d[c * 128:(c + 1) * 128, :])
            P.copy(Wout[:, c, :], s[:, 0:1024], eng=("dve" if c % 2 == 0 else "pool"))

        carve_off = [0, 0]

        def carve(si, name, shape, dt):
            esz = 4 if dt in (F32, I32) else 2
            n = 1
            for d_ in shape[1:]:
                n *= d_
            nb = n * esz
            c0 = carve_off[si] // 4
            carve_off[si] += nb
            assert carve_off[si] <= 12800, (name, carve_off)
            ap = stg[si].ap[0:shape[0], c0:c0 + nb // 4]
            if dt != F32:
                ap = ap.bitcast(dt)
            if len(shape) == 3:
                ap = ap.rearrange("p (a b) -> p a b", a=shape[1])
            elif len(shape) == 4:
                ap = ap.rearrange("p (a b c) -> p a b c", a=shape[1], b=shape[2])
            return V(ap, TB(name, inherit=stg[si].tb))

        AR = carve(0, "AR", [128, 4, 2, 128], BF16)
        Bt = carve(0, "Bt", [128, 4, 128], BF16)
        Kt = carve(0, "Kt", [128, 4, 128], BF16)
        bhat = carve(0, "bhat", [128, 4, 128], BF16)
        khat = carve(0, "khat", [128, 4, 128], BF16)
        BKtok = carve(0, "BKtok", [128, 8, 128], BF16)
        BUFS = []
        for si_ in range(2):
            BUFS.append((carve(0, "NK%d" % si_, [128, 2, 4, 128], BF16),
                         carve(1, "Am%d" % si_, [128, 2, 128], BF16),
                         carve(1, "QQ%d" % si_, [128, 4, 128], BF16),
                         carve(1, "Mt%d" % si_, [128, 2, 128], BF16),
                         carve(1, "Xb%d" % si_, [128, 2, 64], BF16),
                         carve(1, "Ub%d" % si_, [128, 2, 64], BF16)))
        yrb = carve(1, "yrb", [128, 512], BF16)
        prk = carve(1, "prk", [128, 4, 128], BF16)
        lor = carve(1, "lor", [128, 128], BF16)
        vtokb = carve(1, "vtokb", [128, 512], BF16)
        PT = [carve(1, "PT0", [128, 4, 128], BF16), carve(1, "PT1", [128, 4, 128], BF16)]
        cqn = carve(1, "cqn", [128, 3, 128], BF16)
        qnT = carve(1, "qnT", [128, 4, 128], BF16)

        rot_banks = [P.ps("bank%d" % i, [128, 512]) for i in range(4)]
        bank_y = P.ps("bank_y", [128, 512])
        bank_s = P.ps("bank_s", [128, 512])
        bank_x = P.ps("bank_x", [128, 512])
        bank_t = P.ps("bank_t", [128, 1024], BF16)
        rot_i = [0]
        pool_sel = [rot_banks]
        bank_tf = V(bank_t.ap.bitcast(F32), bank_t.tb)

        def bank():
            pl = pool_sel[0]
            bkk = pl[rot_i[0] % len(pl)]
            rot_i[0] += 1
            return bkk

        KnT = P.sb("KnT", [128, 4, T], BF16)
        krT = P.sb("krT", [64, T], BF16)
        Vm = P.sb("Vm", [128, NT, 512], BF16)
        uT = P.sb("uT", [128, 8, 128], BF16)
        uTs = P.sb("uTs", [128, 8, 128], BF16)
        xt = [P.sb("xt0", [128, D]), P.sb("xt1", [128, D])]
        sc = P.sb("sc", [128, 16])
        Praw = P.sb("Praw", [128, 9, 129])
        H32 = P.sb("H32", [128, 4, 64])
        Hbf = P.sb("Hbf", [128, 4, 64], BF16)
        qrT = P.sb("qrT", [64, 4, 128], BF16)
        zsT = P.sb("zsT", [128, 8, 128], BF16)
        ycatT = P.sb("ycatT", [128, 8, 128], BF16)
        mixed = P.sb("mixed", [128, 9, 128])
        vtokf = P.sb("vtokf", [128, 512])
        atmp = P.sb("atmp", [128, 2, 128])
        dbgst = P.sb("dbgst", [128, 1024]) if dbg_d else None

        def dump(nm, v, ncols):
            if nm not in dbg_d:
                return
            P.copy(dbgst[:, 0:ncols], v)
            P.dma(dbg_d[nm], dbgst[:, 0:ncols])

        def rsqrt(out, in_, scale, eps):
            P.act(out, in_, AF.Ln, bias=eps, scale=scale)
            P.act(out, out, AF.Exp, scale=-0.5)

        def bmid(v, shape):
            return V(v.ap.unsqueeze(1).to_broadcast(list(shape)), v.tb)

        def blast(v, shape):
            return V(v.ap.unsqueeze(2).to_broadcast(list(shape)), v.tb)

        pending_f = []

        def emit_f(xtile, r0):
            bo = [bank(), bank()]
            for n in range(2):
                for c in range(8):
                    P.mm(bo[n], ycatT[:, c, :], Wout[:, c, n * 512:(n + 1) * 512], start=(c == 0), stop=(c == 7))
            P.memset(sc[:, 2:4], 0.0, eng="pool")
            P.act(g[1], bo[0], AF.Square, accum_out=sc[:, 2:3])
            P.act(g[2], bo[1], AF.Square, accum_out=sc[:, 3:4])
            P.tt(sc[:, 4:5], sc[:, 2:3], sc[:, 3:4], ALU.add)
            rsqrt(sc[:, 5:6], sc[:, 4:5], 1.0 / D, NORM_EPS)
            for n, gi in ((0, 4), (1, 8)):
                P.stt(g[gi], bo[n], sc[:, 5:6], rows[:, 1024 + n * 512:1024 + (n + 1) * 512], ALU.mult, ALU.mult)
                P.tt(xtile[:, n * 512:(n + 1) * 512], xtile[:, n * 512:(n + 1) * 512], g[gi], ALU.add, eng="pool")
            P.dma(out_d[r0:r0 + 128, :], xtile)

        tiles = [(b_, it_) for b_ in range(NBC) for it_ in range(NT)]

        def stage_a(n):
            b_, it_ = tiles[n]
            r0_ = b_ * T + it_ * 128
            xtile_ = xt[n % 2]
            P.dma(xtile_, x_d[r0_:r0_ + 128, :])
            ss = sc[:, 0:1]
            rstd = sc[:, 1:2]
            P.memset(ss, 0.0, eng="pool")
            xs = V(g[7].ap.bitcast(BF16), g[7].tb)
            P.act(xs, xtile_, AF.Square, accum_out=ss)
            rsqrt(rstd, ss, 1.0 / D, NORM_EPS)
            P.ts(xs, xtile_, rstd, ALU.mult)
            for c in range(8):
                P.tr(bank_t[:, c * 128:(c + 1) * 128], xs[:, c * 128:(c + 1) * 128], identb)
            if it_ == 0:
                P.memset(uTs[:, :, 0:1], 0.0, eng="pool")
            else:
                P.copy(uTs[:, :, 0:1], uT[:, :, 127:128], eng="pool")
            P.copy(uT, bank_t.re("p (c t) -> p c t", c=8), eng="act")
            P.copy(uTs[:, :, 1:128], uT[:, :, 0:127], eng="pool")

        ti_glob = 0
        for b in range(NBC):
            P.memset(Praw[:, :, 0:1], 0.0)
            P.memset(H32, 0.0)
            P.memset(Hbf, 0.0)
            for it in range(NT):
                r0 = b * T + it * 128
                tsl = slice(it * 128, (it + 1) * 128)
                n_tile = ti_glob
                xtile = xt[ti_glob % 2]
                ti_glob += 1
                stage_a(n_tile)
                if pending_f:
                    emit_f(*pending_f.pop())

                def proj(outv, col0, ncols, shift=False):
                    for c in range(8):
                        rhs = uTs[:, c, :] if shift else uT[:, c, :]
                        P.mm(outv, W[:, c, col0:col0 + ncols], rhs, start=(c == 0), stop=(c == 7))

                cq = g3(0, 4)[:, 0:3, :]
                sq3 = g3(1, 4)[:, 0:3, :]
                rs3 = g3(2, 4)[:, 0:3, :]
                qtmp = g3(3, 4, 64)
                qtmp2 = g3(4, 4, 64)
                sg = g3(7, 4)
                bk = bank()
                for j in range(3):
                    proj(bk[:, j * 128:(j + 1) * 128], C_CQ + j * 128, 128)
                P.copy(cq, bk[:, 0:384].re("p (j t) -> p j t", j=3), eng="act")
                P.start_seg()
                pool_sel[0] = rot_banks[0:2]
                for half in range(2):
                    bk = bank()
                    for j in range(4):
                        proj(bk[:, j * 128:(j + 1) * 128], C_Z + (half * 4 + j) * 128, 128)
                    bk3 = bk.re("p (j t) -> p j t", j=4)
                    P.act(sg, bk3, AF.Sigmoid)
                    P.tt(zsT[:, half * 4:(half + 1) * 4, :], bk3, sg, ALU.mult)
                    P.cut()
                for q_, col0 in ((0, C_R), (1, C_K)):
                    bk = bank()
                    for j in range(4):
                        proj(bk[:, j * 128:(j + 1) * 128], col0 + j * 128, 128)
                    P.copy(Praw[:, q_ * 4:(q_ + 1) * 4, 1:129], bk.re("p (j t) -> p j t", j=4),
                           eng=("act" if q_ == 0 else "dve"))
                    P.cut()
                bk = bank()
                proj(bk[:, 0:128], C_XW, 128)
                P.copy(Praw[:, 8, 1:129], bk[:, 0:128], eng="act")
                P.cut()
                bk = bank()
                for c in range(8):
                    P.mm(bk, uT[:, c, :], W[:, c, C_V:C_V + 512], start=(c == 0), stop=False)
                for c in range(8):
                    P.mm(bk, uTs[:, c, :], W[:, c, C_V2:C_V2 + 512], start=False, stop=(c == 7))
                P.copy(vtokf, bk, eng="act")
                P.copy(vtokb, vtokf, eng="pool")
                seg_b2 = P.end_seg()

                P.start_seg()
                pool_sel[0] = rot_banks[2:4]
                rope = g3(5, 4, 64)
                ropei = V(g[6].ap[0:64, 0:256].bitcast(I32).rearrange("p (a t) -> p a t", a=2), g[6].tb)
                turns, rtmp, sinT, cosT = (rope[:, i, :] for i in range(4))
                P.dma(ropei[:, 0, :], pos_d[b:b + 1, tsl].broadcast_to([64, 128]))
                P.copy(rtmp, ropei[:, 0, :])
                P.ts(turns, rtmp, pp[0:64, PP_INVF:PP_INVF + 1], ALU.mult)
                P.copy(ropei[:, 1, :], turns)
                P.copy(rtmp, ropei[:, 1, :])
                P.tt(rtmp, turns, rtmp, ALU.subtract)
                P.act(sinT, rtmp, AF.Sin, scale=2.0 * math.pi)
                P.ts(turns, turns, 0.25, ALU.add)
                P.copy(ropei[:, 1, :], turns)
                P.copy(rtmp, ropei[:, 1, :])
                P.tt(rtmp, turns, rtmp, ALU.subtract)
                P.act(cosT, rtmp, AF.Sin, scale=2.0 * math.pi)
                P.cut()
                bk = bank()
                proj(bk[0:64, 0:128], C_KR, 64)
                proj(bk[0:64, 128:256], C_KRROT, 64)
                P.tt(qtmp[:, 0, :], bk[0:64, 0:128], cosT, ALU.mult)
                P.tt(qtmp[:, 1, :], bk[0:64, 128:256], sinT, ALU.mult)
                P.tt(krT[:, tsl], qtmp[:, 0, :], qtmp[:, 1, :], ALU.add, eng="pool")
                P.cut()
                P.tt(sq3, cq, cq, ALU.mult, eng="pool")
                bk = bank()
                P.mm(bk[:, 0:128], onesf, sq3[:, 0, :], start=True, stop=False)
                P.mm(bk[:, 0:128], onesf, sq3[:, 1, :], start=False, stop=True)
                P.mm(bk[:, 128:256], onesf, sq3[:, 2, :], start=True, stop=True)
                rsqrt(rs3[:, 0, :], bk[:, 0:128], 1.0 / 256, NORM_EPS)
                rsqrt(rs3[:, 2, :], bk[:, 128:256], 1.0 / 128, NORM_EPS)
                P.tt(cqn[:, 0:2, :], cq[:, 0:2, :], bmid(rs3[:, 0, :], [128, 2, 128]), ALU.mult)
                P.tt(cqn[:, 2, :], cq[:, 2, :], rs3[:, 2, :], ALU.mult)
                P.cut()
                bk = bank()
                for h in range(4):
                    for c in range(2):
                        P.mm(bk[:, h * 128:(h + 1) * 128], Wq[:, c, h * 128:(h + 1) * 128], cqn[:, c, :],
                             start=(c == 0), stop=(c == 1))
                P.copy(qnT, bk.re("p (h t) -> p h t", h=4), eng="act")
                P.cut()
                bk = bank()
                bk2 = bank()
                for h in range(4):
                    for c in range(2):
                        P.mm(bk[0:64, h * 128:(h + 1) * 128], Wq[:, c, 512 + h * 64:512 + (h + 1) * 64], cqn[:, c, :],
                             start=(c == 0), stop=(c == 1))
                    for c in range(2):
                        P.mm(bk2[0:64, h * 128:(h + 1) * 128], Wq[:, c, 768 + h * 64:768 + (h + 1) * 64], cqn[:, c, :],
                             start=(c == 0), stop=(c == 1))
                P.tt(qtmp, bk[0:64, :].re("p (h t) -> p h t", h=4), bmid(cosT, [64, 4, 128]), ALU.mult)
                P.tt(qtmp2, bk2[0:64, :].re("p (h t) -> p h t", h=4), bmid(sinT, [64, 4, 128]), ALU.mult)
                P.tt(qrT, qtmp, qtmp2, ALU.add, eng="pool")
                P.cut()
                bk = bank()
                for h in range(4):
                    P.mm(bk[:, h * 128:(h + 1) * 128], Wkv[:, h * 128:(h + 1) * 128], cqn[:, 2, :])
                P.copy(KnT[:, :, tsl], bk.re("p (h t) -> p h t", h=4), eng="act")
                P.cut()
                bk = bank()
                P.mm(bk, cqn[:, 2, :], Wkv[:, 512:1024])
                P.copy(Vm[:, it, :], bk)
                seg_c = P.end_seg()
                pool_sel[0] = rot_banks
                P.merge(seg_b2, seg_c)

                P.start_seg()
                nj = it + 1
                pti = 0
                for h in range(4):
                    for jb in range(0, nj, 4):
                        njj = min(4, nj - jb)
                        bk = bank() if os.environ.get('NOTF') else bank_tf
                        pt = PT[pti % 2]
                        pti += 1
                        for jj in range(njj):
                            j = jb + jj
                            P.mm(bk[:, jj * 128:(jj + 1) * 128], KnT[:, h, j * 128:(j + 1) * 128], qnT[:, h, :],
                                 start=True, stop=False)
                            P.mm(bk[:, jj * 128:(jj + 1) * 128], krT[:, j * 128:(j + 1) * 128], qrT[:, h, :],
                                 start=False, stop=True)
                        P.act(pt[:, 0:njj, :], bk[:, 0:njj * 128].re("p (j t) -> p j t", j=njj), AF.Exp, scale=SM_SCALE)
                        if jb + njj == nj:
                            P.tt(pt[:, njj - 1, :], pt[:, njj - 1, :], m_iu, ALU.mult, eng="pool")
                        for jj in range(njj):
                            j = jb + jj
                            P.mm(bank_y[:, h * 128:(h + 1) * 128], Vm[:, j, h * 128:(h + 1) * 128], pt[:, jj, :],
                                 start=(j == 0), stop=(j == nj - 1))
                            P.mm(bank_s[:, h * 128:(h + 1) * 128], onesb, pt[:, jj, :],
                                 start=(j == 0), stop=(j == nj - 1))
                        P.cut()
                    P.recip(atmp[:, 0, :], bank_s[:, h * 128:(h + 1) * 128])
                    P.tt(atmp[:, 1, :], bank_y[:, h * 128:(h + 1) * 128], atmp[:, 0, :], ALU.mult)
                    P.tt(ycatT[:, h, :], atmp[:, 1, :], zsT[:, h, :], ALU.mult, eng="pool")
                    P.cut()
                seg_d = P.end_seg()
                P.start_seg()

                e2, av, kk, kmod, bb, tm1, tm2, Lc, Lx, EL = (g3(i, 4) for i in range(10))
                mu_bc = blast(pp[:, PP_MU:PP_MU + 9], [128, 9, 128])
                P.tt(mixed, Praw[:, :, 0:128], Praw[:, :, 1:129], ALU.subtract)
                P.tt(mixed, mixed, mu_bc, ALU.mult)
                P.tt(mixed, mixed, Praw[:, :, 1:129], ALU.add)
                P.copy(Praw[:, :, 0:1], Praw[:, :, 128:129], eng="pool")
                rm = mixed[:, 0:4, :]
                km = mixed[:, 4:8, :]
                P.act(lor[0:64, :], mixed[0:64, 8, :], AF.Tanh)
                P.copy(lor[64:128, :], mixed[64:128, 8, :], eng="pool")
                P.cut()
                bkw = bank()
                bka = bank()
                for hp in range(4):
                    P.mm(bkw[:, hp * 128:(hp + 1) * 128], W2A[0:64, hp * 128:(hp + 1) * 128], lor[0:64, :])
                    P.mm(bka[:, hp * 128:(hp + 1) * 128], W2A[64:128, hp * 128:(hp + 1) * 128], lor[64:128, :])

                def pbc(col):
                    return blast(pp[:, col:col + 4], [128, 4, 128])

                P.tt(e2, bkw.re("p (h t) -> p h t", h=4), pbc(PP_W0), ALU.add)
                P.tt(av, bka.re("p (h t) -> p h t", h=4), pbc(PP_A0), ALU.add)
                P.act(e2, e2, AF.Sigmoid)
                P.act(av, av, AF.Sigmoid)
                P.ts(e2, e2, math.exp(-0.5), ALU.mult)
                P.cut()
                P.tt(kk, km, pbc(PP_KK), ALU.mult)
                P.tt(tm1, kk, kk, ALU.mult)
                bk = bank()
                for hp in range(4):
                    P.mm(bk[:, hp * 128:(hp + 1) * 128], blockones, tm1[:, hp, :])
                rsqrt(tm2, bk.re("p (h t) -> p h t", h=4), 1.0, 1e-24)
                P.tt(kk, kk, tm2, ALU.mult)
                P.cut()
                P.tt(tm1, av, pbc(PP_KA), ALU.mult, eng="pool")
                P.tt(tm1, tm1, blast(omka, [128, 4, 128]), ALU.add, eng="pool")
                P.tt(kmod, km, tm1, ALU.mult, eng="pool")
                P.tt(bb, kk, av, ALU.mult, eng="pool")
                P.tt(tm1, rm, kmod, ALU.mult, eng="pool")
                P.tt(prk, tm1, pbc(PP_RK), ALU.mult, eng="pool")
                P.cut()
                for hp in range(4):
                    P.scan(Lc[:, hp, :], onesf, e2[:, hp, :], 0.0, ALU.mult, ALU.subtract)
                P.tt(Lx, Lc, e2, ALU.add)
                P.act(EL, Lc, AF.Exp)
                P.act(Lx, Lx, AF.Exp)
                P.act(Lc, Lc, AF.Exp, scale=-1.0)
                P.cut()
                gC = EL[:, :, 127:128].bc([128, 4, 128])
                P.tt(AR[:, :, 1, :], rm, EL, ALU.mult)
                P.stt(AR[:, :, 0, :], kk, -1.0, Lx, ALU.mult, ALU.mult)
                P.tt(tm1, bb, Lc, ALU.mult)
                P.tt(tm2, kmod, Lc, ALU.mult, eng="pool")
                P.copy(Bt, tm1, eng="pool")
                P.copy(Kt, tm2, eng="pool")
                P.tt(bhat, tm1, gC, ALU.mult)
                P.tt(khat, tm2, gC, ALU.mult)
                P.cut()
                for hp in range(4):
                    P.tr(bank_t[:, hp * 128:(hp + 1) * 128], bhat[:, hp, :], identb)
                    P.tr(bank_t[:, (4 + hp) * 128:(5 + hp) * 128], khat[:, hp, :], identb)
                P.copy(BKtok, bank_t.re("p (c t) -> p c t", c=8), eng="act")
                P.cut()
                msl1 = m_sl
                idb2 = bmid(identb, [128, 2, 128])

                def emit_group(hbase, S):
                    NKg, Amg, QQg, Mtg, Xg, Ug = S
                    Qg, QTg = QQg[:, 0:2, :], QQg[:, 2:4, :]
                    hp = hbase // 2
                    for hl in range(2):
                        pb = hl * 64
                        bk = bank()
                        rhs = AR[pb:pb + 64, hp, :, :]
                        P.mm(bk[:, 0:256], Bt[pb:pb + 64, hp, :], rhs)
                        P.mm(bk[:, 256:512], Kt[pb:pb + 64, hp, :], rhs)
                        P.tt(NKg[:, hl, :, :], bk.re("p (a t) -> p a t", a=4), m4, ALU.mult)
                        P.cut()
                    bke = bank()
                    bko = bank()
                    P.mm(bke[:, 0:128], AR[0:64, hp, 0, :], Bt[0:64, hp, :])
                    P.mm(bko[:, 0:128], AR[64:128, hp, 0, :], Bt[64:128, hp, :])
                    P.tt(Amg[:, 0, :], bke[:, 0:128], msl1, ALU.mult)
                    P.tt(Amg[:, 1, :], bko[:, 0:128], msl1, ALU.mult)
                    P.tt(Mtg, NKg[:, :, 0, :], idb2, ALU.add, eng="pool")
                    P.cut()
                    qc, qtc = NKg[:, :, 0, :], Amg
                    for k in range(1, 7):
                        bsq = bank()
                        for hl in range(2):
                            if k < 6:
                                P.mm(bsq[:, hl * 128:(hl + 1) * 128], qtc[:, hl, :], qc[:, hl, :])
                            P.mm(bsq[:, 256 + hl * 128:256 + (hl + 1) * 128], qc[:, hl, :], qtc[:, hl, :])
                        if k >= 2:
                            bp = bank()
                            for hl in range(2):
                                P.mm(bp[:, hl * 128:(hl + 1) * 128], qtc[:, hl, :], Mtg[:, hl, :])
                        if k < 6:
                            P.copy(QQg, bsq.re("p (h t) -> p h t", h=4), eng="act")
                        else:
                            P.copy(QQg[:, 2:4, :], bsq[:, 256:512].re("p (h t) -> p h t", h=2), eng="act")
                        if k >= 2:
                            P.tt(Mtg, bp[:, 0:256].re("p (h t) -> p h t", h=2), Mtg, ALU.add)
                        qc, qtc = Qg, QTg
                        P.cut()
                    bp = bank()
                    for hl in range(2):
                        P.mm(bp[:, hl * 128:(hl + 1) * 128], qtc[:, hl, :], Mtg[:, hl, :])
                    P.tt(Mtg, bp[:, 0:256].re("p (h t) -> p h t", h=2), Mtg, ALU.add)
                    P.cut()
                    bk = bank()
                    for hl in range(2):
                        h, pb = hbase + hl, hl * 64
                        P.mm(bk[:, hl * 64:(hl + 1) * 64], AR[pb:pb + 64, hp, 0, :], Hbf[pb:pb + 64, hp, :], start=True, stop=False)
                        P.mm(bk[:, hl * 64:(hl + 1) * 64], NKg[:, hl, 2, :], vtokb[:, h * 64:(h + 1) * 64], start=False, stop=True)
                    P.copy(Xg, bk[:, 0:128].re("p (h v) -> p h v", h=2), eng="act")
                    P.cut()
                    bk = bank()
                    for hl in range(2):
                        P.mm(bk[:, hl * 64:(hl + 1) * 64], Mtg[:, hl, :], Xg[:, hl, :])
                    P.copy(Ug, bk[:, 0:128].re("p (h v) -> p h v", h=2))
                    P.cut()
                    for hl in range(2):
                        h, pb = hbase + hl, hl * 64
                        P.mm(bank_x[:, h * 64:(h + 1) * 64], AR[pb:pb + 64, hp, 1, :], Hbf[pb:pb + 64, hp, :], start=True, stop=False)
                        P.mm(bank_x[:, h * 64:(h + 1) * 64], NKg[:, hl, 1, :], Ug[:, hl, :], start=False, stop=False)
                        P.mm(bank_x[:, h * 64:(h + 1) * 64], NKg[:, hl, 3, :], vtokb[:, h * 64:(h + 1) * 64], start=False, stop=True)
                    bk = bank()
                    for hl in range(2):
                        h = hbase + hl
                        P.mm(bk[:, hl * 64:(hl + 1) * 64], BKtok[:, hp, :], Ug[:, hl, :], start=True, stop=False)
                        P.mm(bk[:, hl * 64:(hl + 1) * 64], BKtok[:, 4 + hp, :], vtokb[:, h * 64:(h + 1) * 64], start=False, stop=True)
                    Hs = H32[:, hp, :]
                    P.tt(Hs, Hs, EL[:, hp, 127:128].bc([128, 64]), ALU.mult)
                    P.tt(Hs[0:64], bk[0:64, 0:64], Hs[0:64], ALU.add)
                    P.tt(Hs[64:128], bk[64:128, 64:128], Hs[64:128], ALU.add)
                    P.copy(Hbf[:, hp, :], Hs, eng="pool")
                    P.cut()

                for rnd in range(2):
                    P.start_seg()
                    pool_sel[0] = rot_banks[0:2]
                    emit_group(4 * rnd, BUFS[0])
                    sg0 = P.end_seg()
                    P.start_seg()
                    pool_sel[0] = rot_banks[2:4]
                    emit_group(4 * rnd + 2, BUFS[1])
                    sg1 = P.end_seg()
                    pool_sel[0] = rot_banks
                    P.merge(sg0, sg1)
                gst = sc[:, 8:16]
                yc = V(g[0].ap.rearrange("p (h v) -> p h v", h=8), g[0].tb)
                ysq = V(g[1].ap.rearrange("p (h v) -> p h v", h=8), g[1].tb)
                st1 = V(g[2].ap[:, 0:32].rearrange("p (a h) -> p a h", a=4), g[2].tb)
                y3 = bank_x.re("p (h v) -> p h v", h=8)
                P.rsum(st1[:, 0, :], y3)
                P.ts(st1[:, 0, :], st1[:, 0, :], 1.0 / 64, ALU.mult)
                P.tt(yc, y3, blast(st1[:, 0, :], [128, 8, 64]), ALU.subtract)
                P.tt(ysq, yc, yc, ALU.mult, eng="pool")
                P.rsum(st1[:, 1, :], ysq)
                rsqrt(st1[:, 2, :], st1[:, 1, :], 1.0 / 64, GN_EPS)
                P.cut()
                bkr = bank()
                for hp in range(4):
                    P.mm(bkr[:, hp * 2:hp * 2 + 2], prk[:, hp, :], headind)
                P.copy(st1[:, 3, :], bkr[:, 0:8])
                P.cut()
                P.tt(yc, yc, blast(st1[:, 2, :], [128, 8, 64]), ALU.mult)
                yc2 = yc.re("p h v -> p (h v)")
                P.tt(yc2, yc2, rows[:, 0:512], ALU.mult, eng="pool")
                P.tt(yc2, yc2, rows[:, 512:1024], ALU.add, eng="pool")
                P.tt(ysq, vtokf.re("p (h v) -> p h v", h=8), blast(st1[:, 3, :], [128, 8, 64]), ALU.mult, eng="pool")
                P.tt(yrb, yc2, ysq.re("p h v -> p (h v)"), ALU.add)
                for c in range(4):
                    P.tr(bank_t[:, c * 128:(c + 1) * 128], yrb[:, c * 128:(c + 1) * 128], identb)
                P.tt(ycatT[:, 4:8, :], bank_t[:, 0:512].re("p (c t) -> p c t", c=4), zsT[:, 4:8, :], ALU.mult)

                seg_e = P.end_seg()
                P.merge(seg_d, seg_e)
                pending_f.append((xtile, r0))

                if b == 0 and it == min(1, NT - 1):
                    dump("ycat", ycatT.re("p c t -> p (c t)"), 1024)
                    dump("yrw", yrb, 512)
                    dump("kk", kk.re("p c t -> p (c t)"), 512)
                    dump("e2", e2.re("p c t -> p (c t)"), 512)
                    dump("H", H32.re("p c t -> p (c t)"), 256)

        if pending_f:
            emit_f(*pending_f.pop())
        fw = [xt[0].tb, xt[1].tb]
        if dbgst is not None:
            fw.append(dbgst.tb)
        P.emit(final_wait=fw)
        build.stats = P.stats
    return nc


def host_params(inp):
    f = np.float32
    w_in = np.asarray(inp["w_in"][0], f)
    kr = w_in[:, 384:448]
    krrot = np.concatenate([kr[:, 32:64], kr[:, 0:32]], axis=1)
    win = np.ascontiguousarray(np.concatenate([w_in, krrot], axis=1))
    wuq = np.asarray(inp["mla_w_uq"][0], f).reshape(256, 4, 192)
    nope = wuq[:, :, 0:128].reshape(256, 512)
    rp = wuq[:, :, 128:192]
    rot = np.concatenate([rp[:, :, 32:64], rp[:, :, 0:32]], axis=2)
    wuq_l = np.ascontiguousarray(np.concatenate([nope, rp.reshape(256, 256), rot.reshape(256, 256)], axis=1))
    wukv = np.asarray(inp["mla_w_ukv"][0], f).reshape(128, 4, 256)
    wukv_l = np.ascontiguousarray(np.concatenate([wukv[:, :, 0:128].reshape(128, 512),
                                                  wukv[:, :, 128:256].reshape(128, 512)], axis=1))
    w2a = np.ascontiguousarray(np.concatenate([np.asarray(inp["rw_w2"][0], f), np.asarray(inp["rw_a2"][0], f)], axis=0))
    wout = np.ascontiguousarray(np.asarray(inp["w_out"][0], f))
    pp = np.zeros((128, NPP), f)

    def colmajor(v, n):
        return np.asarray(v, f).reshape(n, 128).T

    pp[:, PP_GPRE:PP_GPRE + 8] = colmajor(inp["norm_pre_g"][0], 8)
    pp[:, PP_GQ:PP_GQ + 2] = colmajor(inp["mla_q_norm_g"][0], 2)
    pp[:, PP_GKV:PP_GKV + 1] = colmajor(inp["mla_kv_norm_g"][0], 1)
    pp[:, PP_W0:PP_W0 + 4] = colmajor(inp["rw_w0"][0], 4)
    pp[:, PP_A0:PP_A0 + 4] = colmajor(inp["rw_a0"][0], 4)
    pp[:, PP_KK:PP_KK + 4] = colmajor(inp["rw_k_k"][0], 4)
    pp[:, PP_KA:PP_KA + 4] = colmajor(inp["rw_k_a"][0], 4)
    pp[:, PP_RK:PP_RK + 4] = colmajor(np.asarray(inp["rw_r_k"][0]).reshape(512), 4)
    invf = (10000.0 ** (-np.arange(0, 64, 2, dtype=np.float32) / 64)).astype(f)
    invf_turn = (np.concatenate([invf, invf]).astype(np.float64) / (2 * np.pi)).astype(f)
    pp[0:64, PP_INVF] = invf_turn
    mu = np.asarray(inp["rw_mu"][0], f)
    pp[:, PP_MU:PP_MU + 4] = colmajor(mu[0:512], 4)
    pp[:, PP_MU + 4:PP_MU + 8] = colmajor(mu[512:1024], 4)
    pp[:, PP_MU + 8] = mu[1536:1664]
    rows = np.concatenate([mu[1024:1536], np.asarray(inp["rw_ln_g"][0], f), np.asarray(inp["rw_ln_b"][0], f),
                           np.asarray(inp["norm_post_g"][0], f)]).reshape(1, NROWS).astype(f)
    return {"win": win, "wuq": wuq_l, "wukv": wukv_l, "w2a": w2a, "wout": wout, "pp": pp, "rows": rows}


def kernel(**inp):
    x = np.asarray(inp["x"], np.float32)
    pos = np.asarray(inp["positions"], np.int32)
    B, T, _ = x.shape
    nbc = B // N_CORES
    shared = host_params(inp)
    nc = build(nbc, T // 128)
    in_maps = []
    for c in range(N_CORES):
        m = dict(shared)
        m["x"] = np.ascontiguousarray(x[c * nbc:(c + 1) * nbc].reshape(nbc * T, D))
        m["pos"] = np.ascontiguousarray(pos[c * nbc:(c + 1) * nbc])
        in_maps.append(m)
    res = run_bass_kernel_spmd(nc, in_maps, core_ids=list(range(N_CORES)))
    out = np.concatenate([r["out"].reshape(nbc, T, D) for r in res.results], axis=0)
    return out.astype(np.float32)
```

```python
import math
import os
import numpy as np
import concourse.bass as bass
import concourse.mybir as mybir
from concourse.bass_utils import run_bass_kernel_spmd
from contextlib import ExitStack

F32 = mybir.dt.float32
BF16 = mybir.dt.bfloat16
I32 = mybir.dt.int32
AF = mybir.ActivationFunctionType
ALU = mybir.AluOpType
AX = mybir.AxisListType

SAME_ENGINE_SYNC = True
N_CORES = 8
T_FULL = 2048
D = 1024


class TB:
    __slots__ = ("name", "last_w", "readers", "dma_sem", "dma_cnt", "inherit")

    def __init__(self, name, inherit=None):
        self.name = name
        self.last_w = []
        self.readers = {}
        self.dma_sem = None
        self.dma_cnt = 0
        self.inherit = inherit


class V:
    __slots__ = ("ap", "tb")

    def __init__(self, ap, tb):
        self.ap = ap
        self.tb = tb

    def __getitem__(self, idx):
        return V(self.ap[idx], self.tb)

    def bc(self, shape):
        return V(self.ap.to_broadcast(list(shape)), self.tb)

    def re(self, s, **kw):
        return V(self.ap.rearrange(s, **kw), self.tb)


class Prog:
    ENGS = ("pe", "act", "dve", "pool", "sp")

    def __init__(self, nc, es):
        self.nc = nc
        self.es = es
        self.main = []
        self.cur = self.main
        self.stack = []
        self.ops = []
        self.signal = set()

    def sb(self, name, shape, dt=F32):
        t = self.es.enter_context(self.nc.sbuf_tensor("s_" + name, list(shape), dt))
        return V(t[:], TB(name))

    def ps(self, name, shape, dt=F32):
        t = self.es.enter_context(self.nc.psum_tensor("p_" + name, list(shape), dt))
        return V(t[:], TB(name))

    def add(self, eng, emit, reads=(), writes=(), dma_tb=None):
        rt = [r.tb if isinstance(r, V) else r for r in reads]
        wt = [w.tb if isinstance(w, V) else w for w in writes]
        self.cur.append((eng, emit, rt, wt, dma_tb))

    def start_seg(self):
        self.stack.append(self.cur)
        self.cur = []

    def end_seg(self):
        seg = self.cur
        self.cur = self.stack.pop()
        return seg

    def cut(self):
        self.cur.append(None)

    def merge(self, sa, sb):
        def units(seg):
            out, u = [], []
            for r in seg:
                if r is None:
                    if u:
                        out.append(u)
                    u = []
                else:
                    u.append(r)
            if u:
                out.append(u)
            return out
        ua, ub = units(sa), units(sb)
        na, nb = len(ua), len(ub)
        ia = ib = 0
        while ia < na or ib < nb:
            if ib >= nb or (ia < na and ia * nb <= ib * na):
                self.cur.extend(ua[ia])
                ia += 1
            else:
                self.cur.extend(ub[ib])
                ib += 1
            self.cur.append(None)

    @staticmethod
    def _touch(tb):
        if tb.inherit is not None:
            p = tb.inherit
            tb.inherit = None
            Prog._touch(p)
            tb.last_w = list(p.last_w)
            tb.readers = dict(p.readers)

    def finalize(self):
        for rec in self.main:
            if rec is None:
                continue
            (eng, emit, rt, wt, dma_tb) = rec
            idx = len(self.ops)
            deps = []
            for tb in rt:
                self._touch(tb)
                deps.extend(tb.last_w)
            for tb in wt:
                self._touch(tb)
                deps.extend(tb.last_w)
                deps.extend(tb.readers.values())
            if dma_tb is not None:
                dma_tb.dma_cnt += 1
                me = ("d", dma_tb, 16 * dma_tb.dma_cnt)
                rkey = ("d", id(dma_tb))
            else:
                me = ("e", eng, idx)
                rkey = ("e", eng)
            for tb in rt:
                tb.readers[rkey] = me
            for tb in wt:
                tb.last_w = [me]
                tb.readers = {}
            seen = set()
            d2 = []
            for d in deps:
                k = (d[0], id(d[1]) if d[0] == "d" else d[1], d[2])
                if k in seen:
                    continue
                seen.add(k)
                d2.append(d)
                if d[0] == "e" and not (d[1] == eng and (eng == "pe" or not SAME_ENGINE_SYNC)):
                    self.signal.add(d[2])
            self.ops.append((eng, emit, d2, dma_tb))

    def dma(self, out, in_, eng="sp", reads=(), writes=()):
        sbv = out if isinstance(out, V) else in_
        o = out.ap if isinstance(out, V) else out
        i = in_.ap if isinstance(in_, V) else in_
        r = list(reads) + ([in_] if isinstance(in_, V) else [])
        w = list(writes) + ([out] if isinstance(out, V) else [])
        return self.add(eng, lambda e: e.dma_start(out=o, in_=i), r, w, dma_tb=sbv.tb)

    def mm(self, out, lhsT, rhs, start=True, stop=True):
        return self.add("pe", lambda e: e.matmul(out.ap, lhsT.ap, rhs.ap, start=start, stop=stop),
                        [lhsT, rhs], [out])

    def tr(self, out, in_, ident):
        return self.add("pe", lambda e: e.transpose(out.ap, in_.ap, ident.ap), [in_, ident], [out])

    def act(self, out, in_, func, bias=None, scale=1.0, accum_out=None):
        reads = [in_]
        kw = {}
        if isinstance(bias, V):
            reads.append(bias)
            kw["bias"] = bias.ap
        elif bias is not None:
            kw["bias"] = bias
        if isinstance(scale, V):
            reads.append(scale)
            kw["scale"] = scale.ap
        else:
            kw["scale"] = scale
        writes = [out]
        if accum_out is not None:
            writes.append(accum_out)
            kw["accum_out"] = accum_out.ap
        return self.add("act", lambda e: e.activation(out.ap, in_.ap, func, **kw), reads, writes)

    def tt(self, out, a, b, op, eng="dve"):
        return self.add(eng, lambda e: e.tensor_tensor(out.ap, a.ap, b.ap, op), [a, b], [out])

    def ts(self, out, a, s1, op0, s2=None, op1=None, eng="dve"):
        reads = [a]
        x1, x2 = s1, s2
        if isinstance(s1, V):
            reads.append(s1)
            x1 = s1.ap
        if isinstance(s2, V):
            reads.append(s2)
            x2 = s2.ap
        kw = {}
        if op1 is not None:
            kw["op1"] = op1
        return self.add(eng, lambda e: e.tensor_scalar(out.ap, a.ap, x1, x2, op0, **kw), reads, [out])

    def stt(self, out, a, s, b, op0, op1, eng="dve"):
        reads = [a, b]
        x = s
        if isinstance(s, V):
            reads.append(s)
            x = s.ap
        return self.add(eng, lambda e: e.scalar_tensor_tensor(out.ap, a.ap, x, b.ap, op0, op1), reads, [out])

    def copy(self, out, in_, eng="dve"):
        if eng == "act":
            return self.add("act", lambda e: e.copy(out.ap, in_.ap), [in_], [out])
        return self.add(eng, lambda e: e.tensor_copy(out.ap, in_.ap), [in_], [out])

    def memset(self, out, val, eng="dve"):
        return self.add(eng, lambda e: e.memset(out.ap, val), [], [out])

    def recip(self, out, in_):
        return self.add("dve", lambda e: e.reciprocal(out.ap, in_.ap), [in_], [out])

    def rsum(self, out, in_, eng="dve"):
        return self.add(eng, lambda e: e.tensor_reduce(out.ap, in_.ap, AX.X, ALU.add), [in_], [out])

    def scan(self, out, d0, d1, init, op0, op1):
        return self.add("dve", lambda e: e.tensor_tensor_scan(out.ap, d0.ap, d1.ap, init, op0, op1), [d0, d1], [out])

    def aselect(self, out, in_, pattern, cmp, fill, base, cm):
        return self.add("pool", lambda e: e.affine_select(out.ap, in_.ap, pattern, cmp, fill, base=base,
                                                          channel_multiplier=cm), [in_], [out])

    def emit(self, final_wait=()):
        nc, es = self.nc, self.es
        self.finalize()
        ordn = {}
        cnt = {e: 0 for e in self.ENGS}
        for i, (eng, _, _, dma_tb) in enumerate(self.ops):
            if dma_tb is None and i in self.signal:
                cnt[eng] += 1
                ordn[i] = cnt[eng]
        esem = {e: es.enter_context(nc.semaphore("sem_" + e)) for e in self.ENGS}
        for (eng, _, _, dma_tb) in self.ops:
            if dma_tb is not None and dma_tb.dma_sem is None:
                dma_tb.dma_sem = es.enter_context(nc.semaphore("dsem_%s" % dma_tb.name))
        per_eng = {e: [] for e in self.ENGS}
        for i, op in enumerate(self.ops):
            per_eng[op[0]].append(i)
        self.stats = {e: [len(per_eng[e]), cnt[e], 0] for e in self.ENGS}
        block = es.enter_context(nc.Block())
        ops, stats = self.ops, self.stats

        def run(engname, e):
            waited = {}
            for i in per_eng[engname]:
                _, emit, deps, dma_tb = ops[i]
                need = {}
                for d in deps:
                    if d[0] == "e":
                        if d[1] == engname and (engname == "pe" or not SAME_ENGINE_SYNC):
                            continue
                        sem, val, key = esem[d[1]], ordn[d[2]], "e" + d[1]
                    else:
                        sem, val, key = d[1].dma_sem, d[2], id(d[1])
                    if waited.get(key, 0) >= val:
                        continue
                    if key not in need or need[key][1] < val:
                        need[key] = (sem, val)
                for key, (sem, val) in need.items():
                    e.wait_ge(sem, val)
                    waited[key] = val
                    stats[engname][2] += 1
                ins = emit(e)
                if dma_tb is not None:
                    ins.then_inc(dma_tb.dma_sem, 16)
                elif i in ordn:
                    ins.then_inc(esem[engname], 1)
            if engname == "sp":
                for tb in final_wait:
                    if tb.dma_sem is not None:
                        e.wait_ge(tb.dma_sem, 16 * tb.dma_cnt)

        @block.tensor
        def _(e):
            run("pe", e)

        @block.scalar
        def _(e):
            run("act", e)

        @block.vector
        def _(e):
            run("dve", e)

        @block.gpsimd
        def _(e):
            run("pool", e)

        @block.sync
        def _(e):
            run("sp", e)


WC = 3136 + 64 + 512
C_CQ, C_CKV, C_KR, C_R, C_K, C_V, C_XW, C_Z = 0, 256, 384, 448, 960, 1472, 1984, 2112
C_KRROT, C_V2 = 3136, 3200
PP_GPRE, PP_GQ, PP_GKV, PP_W0, PP_A0, PP_KK, PP_KA, PP_RK, PP_INVF, PP_MU = 0, 8, 10, 11, 15, 19, 23, 27, 31, 32
NPP = 41
NROWS = 2560
GN_EPS = 64e-5
NORM_EPS = 1e-6
SM_SCALE = 192.0 ** -0.5


def build(NBC, NT, dbg_names=(), stop=None):
    nc = bass.Bass("TRN2", target_bir_lowering=False)

    def din(name, shape, dt=F32):
        return nc.dram_tensor(name, list(shape), dt, kind="ExternalInput").ap()

    T = NT * 128
    x_d = din("x", [NBC * T, D])
    pos_d = din("pos", [NBC, T], I32)
    win_d = din("win", [D, 3200])
    wuq_d = din("wuq", [256, 1024])
    wukv_d = din("wukv", [128, 1024])
    w2a_d = din("w2a", [128, 512])
    wout_d = din("wout", [D, D])
    pp_d = din("pp", [128, NPP])
    rows_d = din("rows", [1, NROWS])
    out_d = nc.dram_tensor("out", [NBC * T, D], F32, kind="ExternalOutput").ap()
    dbg_d = {}
    for (nm, shape) in dbg_names:
        dbg_d[nm] = nc.dram_tensor("dbg_" + nm, list(shape), F32, kind="ExternalOutput").ap()

    with ExitStack() as es:
        P = Prog(nc, es)
        ident = P.sb("ident", [128, 128])
        identb = P.sb("identb", [128, 128], BF16)
        onesf = P.sb("onesf", [128, 128])
        onesb = P.sb("onesb", [128, 128], BF16)
        blockones = P.sb("blockones", [128, 128])
        headind = P.sb("headind", [128, 2], BF16)
        m_su = P.sb("m_su", [128, 128], BF16)
        m_iu = P.sb("m_iu", [128, 128], BF16)
        m_sl = P.sb("m_sl", [128, 128], BF16)
        mask4 = P.sb("mask4", [128, 4, 128], BF16)
        P.memset(ident, 0.0, eng="pool")
        P.aselect(ident, ident, [[-1, 128]], ALU.not_equal, 1.0, 0, 1)
        P.copy(identb, ident)
        P.memset(onesf, 1.0)
        P.memset(onesb, 1.0)
        P.memset(blockones, 0.0)
        P.memset(blockones[0:64, 0:64], 1.0)
        P.memset(blockones[64:128, 64:128], 1.0)
        P.memset(headind, 0.0)
        P.memset(headind[0:64, 0:1], 1.0)
        P.memset(headind[64:128, 1:2], 1.0)
        for m in (m_su, m_iu, m_sl):
            P.memset(m, 1.0, eng="pool")
        P.aselect(m_su, m_su, [[1, 128]], ALU.is_gt, 0.0, 0, -1)
        P.aselect(m_iu, m_iu, [[1, 128]], ALU.is_ge, 0.0, 0, -1)
        P.aselect(m_sl, m_sl, [[-1, 128]], ALU.is_gt, 0.0, 0, 1)
        P.copy(mask4[:, 0, :], m_su)
        P.copy(mask4[:, 1, :], m_iu)
        P.copy(mask4[:, 2, :], m_su)
        P.copy(mask4[:, 3, :], m_iu)
        m4 = mask4

        pp = P.sb("pp", [128, NPP])
        P.dma(pp, pp_d)
        rows = P.sb("rows", [128, 2048])
        P.dma(rows, rows_d[:, 512:2560].broadcast_to([128, 2048]))
        der = P.sb("der", [128, 32])
        gneg = der[:, 0:8]
        gqneg = der[:, 8:10]
        omka = der[:, 10:14]
        P.ts(gneg, pp[:, PP_GPRE:PP_GPRE + 8], -1.0, ALU.mult)
        P.ts(gqneg, pp[:, PP_GQ:PP_GQ + 2], -1.0, ALU.mult)
        P.ts(omka, pp[:, PP_KA:PP_KA + 4], -1.0, ALU.mult, 1.0, ALU.add)
        Gt = es.enter_context(nc.sbuf_tensor("s_G", [128, 10, 512], F32))
        g = [V(Gt[:][:, i, :], TB("G%d" % i)) for i in range(10)]

        def g3(i, a, parts=128):
            return V(g[i].ap[0:parts, :].rearrange("p (a t) -> p a t", a=a), g[i].tb)

        muv, omuv = g[8], g[9]
        P.dma(muv, rows_d[:, 0:512].broadcast_to([128, 512]))
        P.ts(omuv, muv, -1.0, ALU.mult, 1.0, ALU.add)

        W = P.sb("W", [128, 8, WC], BF16)
        stg = [P.sb("stg0", [128, 3200]), P.sb("stg1", [128, 3200])]
        for c in range(8):
            s = stg[c % 2]
            P.dma(s, win_d[c * 128:(c + 1) * 128, :])
            gg = pp[:, PP_GPRE + c:PP_GPRE + c + 1]
            gn = gneg[:, c:c + 1]
            e1 = "dve" if c % 2 == 0 else "pool"
            e2_ = "pool" if c % 2 == 0 else "dve"
            P.ts(W[:, c, 0:C_V], s[:, 0:C_V], gg, ALU.mult, eng=e1)
            P.ts(W[:, c, C_XW:3136], s[:, C_XW:3136], gg, ALU.mult, eng=e2_)
            P.ts(W[:, c, C_KRROT:C_KRROT + 32], s[:, 3136:3168], gn, ALU.mult, eng=e1)
            P.ts(W[:, c, C_KRROT + 32:C_KRROT + 64], s[:, 3168:3200], gg, ALU.mult, eng=e1)
            P.stt(W[:, c, C_V:C_V + 512], s[:, C_V:C_V + 512], gg, omuv, ALU.mult, ALU.mult)
            P.stt(W[:, c, C_V2:C_V2 + 512], s[:, C_V:C_V + 512], gg, muv, ALU.mult, ALU.mult)
        Wq = P.sb("Wq", [128, 2, 1024], BF16)
        for c in range(2):
            s = stg[c % 2]
            P.dma(s[:, 0:1024], wuq_d[c * 128:(c + 1) * 128, :])
            gg = pp[:, PP_GQ + c:PP_GQ + c + 1]
            gn = gqneg[:, c:c + 1]
            P.ts(Wq[:, c, 0:768], s[:, 0:768], gg, ALU.mult)
            rot_o = Wq[:, c, 768:1024].re("p (h r) -> p h r", h=4)
            rot_in = s[:, 768:1024].re("p (h r) -> p h r", h=4)
            P.ts(rot_o[:, :, 0:32], rot_in[:, :, 0:32], gn, ALU.mult)
            P.ts(rot_o[:, :, 32:64], rot_in[:, :, 32:64], gg, ALU.mult)
        Wkv = P.sb("Wkv", [128, 1024], BF16)
        s = stg[0]
        P.dma(s[:, 0:1024], wukv_d)
        P.ts(Wkv, s[:, 0:1024], pp[:, PP_GKV:PP_GKV + 1], ALU.mult)
        W2A = P.sb("W2A", [128, 512], BF16)
        P.dma(W2A, w2a_d, eng="pool")
        Wout = P.sb("Wout", [128, 8, 1024], BF16)
        for c in range(8):
            P.dma(Wout[:, c, :], wout_d[c * 128:(c + 1) * 128, :], eng="pool")

        carve_off = [0, 0]

        def carve(si, name, shape, dt):
            esz = 4 if dt in (F32, I32) else 2
            n = 1
            for d_ in shape[1:]:
                n *= d_
            nb = n * esz
            c0 = carve_off[si] // 4
            carve_off[si] += nb
            assert carve_off[si] <= 12800, (name, carve_off)
            ap = stg[si].ap[0:shape[0], c0:c0 + nb // 4]
            if dt != F32:
                ap = ap.bitcast(dt)
            if len(shape) == 3:
                ap = ap.rearrange("p (a b) -> p a b", a=shape[1])
            elif len(shape) == 4:
                ap = ap.rearrange("p (a b c) -> p a b c", a=shape[1], b=shape[2])
            return V(ap, TB(name, inherit=stg[si].tb))

        AR = carve(0, "AR", [128, 4, 2, 128], BF16)
        Bt = carve(0, "Bt", [128, 4, 128], BF16)
        Kt = carve(0, "Kt", [128, 4, 128], BF16)
        bhat = carve(0, "bhat", [128, 4, 128], BF16)
        khat = carve(0, "khat", [128, 4, 128], BF16)
        BKtok = carve(0, "BKtok", [128, 8, 128], BF16)
        BUFS = []
        for si_ in range(2):
            BUFS.append((carve(0, "NK%d" % si_, [128, 2, 4, 128], BF16),
                         carve(1, "Am%d" % si_, [128, 2, 128], BF16),
                         carve(1, "QQ%d" % si_, [128, 4, 128], BF16),
                         carve(1, "Mt%d" % si_, [128, 2, 128], BF16),
                         carve(1, "Xb%d" % si_, [128, 2, 64], BF16),
                         carve(1, "Ub%d" % si_, [128, 2, 64], BF16)))
        yrb = carve(1, "yrb", [128, 512], BF16)
        prk = carve(1, "prk", [128, 4, 128], BF16)
        lor = carve(1, "lor", [128, 128], BF16)
        vtokb = carve(1, "vtokb", [128, 512], BF16)
        PT = [carve(1, "PT0", [128, 4, 128], BF16), carve(1, "PT1", [128, 4, 128], BF16)]
        cqn = carve(1, "cqn", [128, 3, 128], BF16)
        qnT = carve(1, "qnT", [128, 4, 128], BF16)

        rot_banks = [P.ps("bank%d" % i, [128, 512]) for i in range(4)]
        bank_y = P.ps("bank_y", [128, 512])
        bank_s = P.ps("bank_s", [128, 512])
        bank_x = P.ps("bank_x", [128, 512])
        bank_t = P.ps("bank_t", [128, 1024], BF16)
        rot_i = [0]
        pool_sel = [rot_banks]
        bank_tf = V(bank_t.ap.bitcast(F32), bank_t.tb)

        def bank():
            pl = pool_sel[0]
            bkk = pl[rot_i[0] % len(pl)]
            rot_i[0] += 1
            return bkk

        KnT = P.sb("KnT", [128, 4, T], BF16)
        krT = P.sb("krT", [64, T], BF16)
        Vm = P.sb("Vm", [128, NT, 512], BF16)
        uT = P.sb("uT", [128, 8, 128], BF16)
        uTs = P.sb("uTs", [128, 8, 128], BF16)
        xt = [P.sb("xt0", [128, D]), P.sb("xt1", [128, D])]
        sc = P.sb("sc", [128, 16])
        Praw = P.sb("Praw", [128, 9, 129])
        H32 = P.sb("H32", [128, 4, 64])
        Hbf = P.sb("Hbf", [128, 4, 64], BF16)
        qrT = P.sb("qrT", [64, 4, 128], BF16)
        zsT = P.sb("zsT", [128, 8, 128], BF16)
        ycatT = P.sb("ycatT", [128, 8, 128], BF16)
        mixed = P.sb("mixed", [128, 9, 128])
        vtokf = P.sb("vtokf", [128, 512])
        atmp = P.sb("atmp", [128, 2, 128])
        ropei = P.sb("ropei", [64, 2, 128], I32)
        dbgst = P.sb("dbgst", [128, 1024]) if dbg_d else None

        def dump(nm, v, ncols):
            if nm not in dbg_d:
                return
            P.copy(dbgst[:, 0:ncols], v)
            P.dma(dbg_d[nm], dbgst[:, 0:ncols])

        def rsqrt(out, in_, scale, eps):
            P.act(out, in_, AF.Ln, bias=eps, scale=scale)
            P.act(out, out, AF.Exp, scale=-0.5)

        def bmid(v, shape):
            return V(v.ap.unsqueeze(1).to_broadcast(list(shape)), v.tb)

        def blast(v, shape):
            return V(v.ap.unsqueeze(2).to_broadcast(list(shape)), v.tb)

        pending_f = []

        def emit_f(xtile, r0):
            bo = [bank(), bank()]
            for n in range(2):
                for c in range(8):
                    P.mm(bo[n], ycatT[:, c, :], Wout[:, c, n * 512:(n + 1) * 512], start=(c == 0), stop=(c == 7))
            P.memset(sc[:, 2:4], 0.0, eng="pool")
            P.act(g[1], bo[0], AF.Square, accum_out=sc[:, 2:3])
            P.act(g[2], bo[1], AF.Square, accum_out=sc[:, 3:4])
            P.tt(sc[:, 4:5], sc[:, 2:3], sc[:, 3:4], ALU.add)
            rsqrt(sc[:, 5:6], sc[:, 4:5], 1.0 / D, NORM_EPS)
            for n, gi in ((0, 4), (1, 8)):
                P.stt(g[gi], bo[n], sc[:, 5:6], rows[:, 1024 + n * 512:1024 + (n + 1) * 512], ALU.mult, ALU.mult)
                P.tt(xtile[:, n * 512:(n + 1) * 512], xtile[:, n * 512:(n + 1) * 512], g[gi], ALU.add, eng="pool")
            P.dma(out_d[r0:r0 + 128, :], xtile)

        tiles = [(b_, it_) for b_ in range(NBC) for it_ in range(NT)]

        def stage_a(n):
            b_, it_ = tiles[n]
            r0_ = b_ * T + it_ * 128
            xtile_ = xt[n % 2]
            P.dma(xtile_, x_d[r0_:r0_ + 128, :])
            ss = sc[:, 0:1]
            rstd = sc[:, 1:2]
            P.memset(ss, 0.0, eng="pool")
            xs = V(g[7].ap.bitcast(BF16), g[7].tb)
            P.act(xs, xtile_, AF.Square, accum_out=ss)
            rsqrt(rstd, ss, 1.0 / D, NORM_EPS)
            P.ts(xs, xtile_, rstd, ALU.mult)
            for c in range(8):
                P.tr(bank_t[:, c * 128:(c + 1) * 128], xs[:, c * 128:(c + 1) * 128], identb)
            if it_ == 0:
                P.memset(uTs[:, :, 0:1], 0.0, eng="pool")
            else:
                P.copy(uTs[:, :, 0:1], uT[:, :, 127:128], eng="pool")
            P.copy(uT, bank_t.re("p (c t) -> p c t", c=8), eng="act")
            P.copy(uTs[:, :, 1:128], uT[:, :, 0:127], eng="pool")

        ti_glob = 0
        for b in range(NBC):
            P.memset(Praw[:, :, 0:1], 0.0)
            P.memset(H32, 0.0)
            P.memset(Hbf, 0.0)
            for it in range(NT):
                r0 = b * T + it * 128
                tsl = slice(it * 128, (it + 1) * 128)
                n_tile = ti_glob
                xtile = xt[ti_glob % 2]
                ti_glob += 1
                stage_a(n_tile)
                if pending_f:
                    emit_f(*pending_f.pop())

                def proj(outv, col0, ncols, shift=False):
                    for c in range(8):
                        rhs = uTs[:, c, :] if shift else uT[:, c, :]
                        P.mm(outv, W[:, c, col0:col0 + ncols], rhs, start=(c == 0), stop=(c == 7))

                cq = g3(0, 4)[:, 0:3, :]
                sq3 = g3(1, 4)[:, 0:3, :]
                rs3 = g3(2, 4)[:, 0:3, :]
                qtmp = g3(3, 4, 64)
                qtmp2 = g3(4, 4, 64)
                sg = g3(7, 4)
                bk = bank()
                for j in range(3):
                    proj(bk[:, j * 128:(j + 1) * 128], C_CQ + j * 128, 128)
                P.copy(cq, bk[:, 0:384].re("p (j t) -> p j t", j=3), eng="act")
                P.start_seg()
                pool_sel[0] = rot_banks[0:2]
                for half in range(2):
                    bk = bank()
                    for j in range(4):
                        proj(bk[:, j * 128:(j + 1) * 128], C_Z + (half * 4 + j) * 128, 128)
                    bk3 = bk.re("p (j t) -> p j t", j=4)
                    P.act(sg, bk3, AF.Sigmoid)
                    P.tt(zsT[:, half * 4:(half + 1) * 4, :], bk3, sg, ALU.mult)
                    P.cut()
                for q_, col0 in ((0, C_R), (1, C_K)):
                    bk = bank()
                    for j in range(4):
                        proj(bk[:, j * 128:(j + 1) * 128], col0 + j * 128, 128)
                    P.copy(Praw[:, q_ * 4:(q_ + 1) * 4, 1:129], bk.re("p (j t) -> p j t", j=4),
                           eng=("act" if q_ == 0 else "dve"))
                    P.cut()
                bk = bank()
                proj(bk[:, 0:128], C_XW, 128)
                P.copy(Praw[:, 8, 1:129], bk[:, 0:128], eng="act")
                P.cut()
                bk = bank()
                for c in range(8):
                    P.mm(bk, uT[:, c, :], W[:, c, C_V:C_V + 512], start=(c == 0), stop=False)
                for c in range(8):
                    P.mm(bk, uTs[:, c, :], W[:, c, C_V2:C_V2 + 512], start=False, stop=(c == 7))
                P.copy(vtokf, bk, eng="act")
                P.copy(vtokb, vtokf, eng="pool")
                seg_b2 = P.end_seg()

                P.start_seg()
                pool_sel[0] = rot_banks[2:4]
                rope = g3(5, 4, 64)
                turns, rtmp, sinT, cosT = (rope[:, i, :] for i in range(4))
                P.dma(ropei[:, 0, :], pos_d[b:b + 1, tsl].broadcast_to([64, 128]))
                P.copy(rtmp, ropei[:, 0, :])
                P.ts(turns, rtmp, pp[0:64, PP_INVF:PP_INVF + 1], ALU.mult)
                P.copy(ropei[:, 1, :], turns)
                P.copy(rtmp, ropei[:, 1, :])
                P.tt(rtmp, turns, rtmp, ALU.subtract)
                P.act(sinT, rtmp, AF.Sin, scale=2.0 * math.pi)
                P.ts(turns, turns, 0.25, ALU.add)
                P.copy(ropei[:, 1, :], turns)
                P.copy(rtmp, ropei[:, 1, :])
                P.tt(rtmp, turns, rtmp, ALU.subtract)
                P.act(cosT, rtmp, AF.Sin, scale=2.0 * math.pi)
                P.cut()
                bk = bank()
                proj(bk[0:64, 0:128], C_KR, 64)
                proj(bk[0:64, 128:256], C_KRROT, 64)
                P.tt(qtmp[:, 0, :], bk[0:64, 0:128], cosT, ALU.mult)
                P.tt(qtmp[:, 1, :], bk[0:64, 128:256], sinT, ALU.mult)
                P.tt(krT[:, tsl], qtmp[:, 0, :], qtmp[:, 1, :], ALU.add, eng="pool")
                P.cut()
                P.tt(sq3, cq, cq, ALU.mult, eng="pool")
                bk = bank()
                P.mm(bk[:, 0:128], onesf, sq3[:, 0, :], start=True, stop=False)
                P.mm(bk[:, 0:128], onesf, sq3[:, 1, :], start=False, stop=True)
                P.mm(bk[:, 128:256], onesf, sq3[:, 2, :], start=True, stop=True)
                rsqrt(rs3[:, 0, :], bk[:, 0:128], 1.0 / 256, NORM_EPS)
                rsqrt(rs3[:, 2, :], bk[:, 128:256], 1.0 / 128, NORM_EPS)
                P.tt(cqn[:, 0:2, :], cq[:, 0:2, :], bmid(rs3[:, 0, :], [128, 2, 128]), ALU.mult)
                P.tt(cqn[:, 2, :], cq[:, 2, :], rs3[:, 2, :], ALU.mult)
                P.cut()
                bk = bank()
                for h in range(4):
                    for c in range(2):
                        P.mm(bk[:, h * 128:(h + 1) * 128], Wq[:, c, h * 128:(h + 1) * 128], cqn[:, c, :],
                             start=(c == 0), stop=(c == 1))
                P.copy(qnT, bk.re("p (h t) -> p h t", h=4), eng="act")
                P.cut()
                bk = bank()
                bk2 = bank()
                for h in range(4):
                    for c in range(2):
                        P.mm(bk[0:64, h * 128:(h + 1) * 128], Wq[:, c, 512 + h * 64:512 + (h + 1) * 64], cqn[:, c, :],
                             start=(c == 0), stop=(c == 1))
                    for c in range(2):
                        P.mm(bk2[0:64, h * 128:(h + 1) * 128], Wq[:, c, 768 + h * 64:768 + (h + 1) * 64], cqn[:, c, :],
                             start=(c == 0), stop=(c == 1))
                P.tt(qtmp, bk[0:64, :].re("p (h t) -> p h t", h=4), bmid(cosT, [64, 4, 128]), ALU.mult)
                P.tt(qtmp2, bk2[0:64, :].re("p (h t) -> p h t", h=4), bmid(sinT, [64, 4, 128]), ALU.mult)
                P.tt(qrT, qtmp, qtmp2, ALU.add, eng="pool")
                P.cut()
                bk = bank()
                for h in range(4):
                    P.mm(bk[:, h * 128:(h + 1) * 128], Wkv[:, h * 128:(h + 1) * 128], cqn[:, 2, :])
                P.copy(KnT[:, :, tsl], bk.re("p (h t) -> p h t", h=4), eng="act")
                P.cut()
                bk = bank()
                P.mm(bk, cqn[:, 2, :], Wkv[:, 512:1024])
                P.copy(Vm[:, it, :], bk)
                seg_c = P.end_seg()
                pool_sel[0] = rot_banks
                P.merge(seg_b2, seg_c)

                P.start_seg()
                nj = it + 1
                pti = 0
                for h in range(4):
                    for jb in range(0, nj, 4):
                        njj = min(4, nj - jb)
                        bk = bank_tf
                        pt = PT[pti % 2]
                        pti += 1
                        for jj in range(njj):
                            j = jb + jj
                            P.mm(bk[:, jj * 128:(jj + 1) * 128], KnT[:, h, j * 128:(j + 1) * 128], qnT[:, h, :],
                                 start=True, stop=False)
                            P.mm(bk[:, jj * 128:(jj + 1) * 128], krT[:, j * 128:(j + 1) * 128], qrT[:, h, :],
                                 start=False, stop=True)
                        P.act(pt[:, 0:njj, :], bk[:, 0:njj * 128].re("p (j t) -> p j t", j=njj), AF.Exp, scale=SM_SCALE)
                        if jb + njj == nj:
                            P.tt(pt[:, njj - 1, :], pt[:, njj - 1, :], m_iu, ALU.mult, eng="pool")
                        for jj in range(njj):
                            j = jb + jj
                            P.mm(bank_y[:, h * 128:(h + 1) * 128], Vm[:, j, h * 128:(h + 1) * 128], pt[:, jj, :],
                                 start=(j == 0), stop=(j == nj - 1))
                            P.mm(bank_s[:, h * 128:(h + 1) * 128], onesb, pt[:, jj, :],
                                 start=(j == 0), stop=(j == nj - 1))
                        P.cut()
                    P.recip(atmp[:, 0, :], bank_s[:, h * 128:(h + 1) * 128])
                    P.tt(atmp[:, 1, :], bank_y[:, h * 128:(h + 1) * 128], atmp[:, 0, :], ALU.mult)
                    P.tt(ycatT[:, h, :], atmp[:, 1, :], zsT[:, h, :], ALU.mult, eng="pool")
                    P.cut()
                seg_d = P.end_seg()
                P.start_seg()

                e2, av, kk, kmod, bb, tm1, tm2, Lc, Lx, EL = (g3(i, 4) for i in range(10))
                mu_bc = blast(pp[:, PP_MU:PP_MU + 9], [128, 9, 128])
                P.tt(mixed, Praw[:, :, 0:128], Praw[:, :, 1:129], ALU.subtract)
                P.tt(mixed, mixed, mu_bc, ALU.mult)
                P.tt(mixed, mixed, Praw[:, :, 1:129], ALU.add)
                P.copy(Praw[:, :, 0:1], Praw[:, :, 128:129], eng="pool")
                rm = mixed[:, 0:4, :]
                km = mixed[:, 4:8, :]
                P.act(lor[0:64, :], mixed[0:64, 8, :], AF.Tanh)
                P.copy(lor[64:128, :], mixed[64:128, 8, :], eng="pool")
                P.cut()
                bkw = bank()
                bka = bank()
                for hp in range(4):
                    P.mm(bkw[:, hp * 128:(hp + 1) * 128], W2A[0:64, hp * 128:(hp + 1) * 128], lor[0:64, :])
                    P.mm(bka[:, hp * 128:(hp + 1) * 128], W2A[64:128, hp * 128:(hp + 1) * 128], lor[64:128, :])

                def pbc(col):
                    return blast(pp[:, col:col + 4], [128, 4, 128])

                P.tt(e2, bkw.re("p (h t) -> p h t", h=4), pbc(PP_W0), ALU.add)
                P.tt(av, bka.re("p (h t) -> p h t", h=4), pbc(PP_A0), ALU.add)
                P.act(e2, e2, AF.Sigmoid)
                P.act(av, av, AF.Sigmoid)
                P.ts(e2, e2, math.exp(-0.5), ALU.mult)
                P.cut()
                P.tt(kk, km, pbc(PP_KK), ALU.mult)
                P.tt(tm1, kk, kk, ALU.mult)
                bk = bank()
                for hp in range(4):
                    P.mm(bk[:, hp * 128:(hp + 1) * 128], blockones, tm1[:, hp, :])
                rsqrt(tm2, bk.re("p (h t) -> p h t", h=4), 1.0, 1e-24)
                P.tt(kk, kk, tm2, ALU.mult)
                P.cut()
                P.tt(tm1, av, pbc(PP_KA), ALU.mult, eng="pool")
                P.tt(tm1, tm1, blast(omka, [128, 4, 128]), ALU.add, eng="pool")
                P.tt(kmod, km, tm1, ALU.mult, eng="pool")
                P.tt(bb, kk, av, ALU.mult, eng="pool")
                P.tt(tm1, rm, kmod, ALU.mult, eng="pool")
                P.tt(prk, tm1, pbc(PP_RK), ALU.mult, eng="pool")
                P.cut()
                for hp in range(4):
                    P.scan(Lc[:, hp, :], onesf, e2[:, hp, :], 0.0, ALU.mult, ALU.subtract)
                P.tt(Lx, Lc, e2, ALU.add)
                P.act(EL, Lc, AF.Exp)
                P.act(Lx, Lx, AF.Exp)
                P.act(Lc, Lc, AF.Exp, scale=-1.0)
                P.cut()
                gC = EL[:, :, 127:128].bc([128, 4, 128])
                P.tt(AR[:, :, 1, :], rm, EL, ALU.mult)
                P.stt(AR[:, :, 0, :], kk, -1.0, Lx, ALU.mult, ALU.mult)
                P.tt(tm1, bb, Lc, ALU.mult)
                P.tt(tm2, kmod, Lc, ALU.mult, eng="pool")
                P.copy(Bt, tm1, eng="pool")
                P.copy(Kt, tm2, eng="pool")
                P.tt(bhat, tm1, gC, ALU.mult)
                P.tt(khat, tm2, gC, ALU.mult)
                P.cut()
                for hp in range(4):
                    P.tr(bank_t[:, hp * 128:(hp + 1) * 128], bhat[:, hp, :], identb)
                    P.tr(bank_t[:, (4 + hp) * 128:(5 + hp) * 128], khat[:, hp, :], identb)
                P.copy(BKtok, bank_t.re("p (c t) -> p c t", c=8), eng="act")
                P.cut()
                msl1 = m_sl
                idb2 = bmid(identb, [128, 2, 128])

                def emit_group(hbase, S):
                    NKg, Amg, QQg, Mtg, Xg, Ug = S
                    Qg, QTg = QQg[:, 0:2, :], QQg[:, 2:4, :]
                    hp = hbase // 2
                    for hl in range(2):
                        pb = hl * 64
                        bk = bank()
                        rhs = AR[pb:pb + 64, hp, :, :]
                        P.mm(bk[:, 0:256], Bt[pb:pb + 64, hp, :], rhs)
                        P.mm(bk[:, 256:512], Kt[pb:pb + 64, hp, :], rhs)
                        P.tt(NKg[:, hl, :, :], bk.re("p (a t) -> p a t", a=4), m4, ALU.mult)
                        P.cut()
                    bke = bank()
                    bko = bank()
                    P.mm(bke[:, 0:128], AR[0:64, hp, 0, :], Bt[0:64, hp, :])
                    P.mm(bko[:, 0:128], AR[64:128, hp, 0, :], Bt[64:128, hp, :])
                    P.tt(Amg[:, 0, :], bke[:, 0:128], msl1, ALU.mult)
                    P.tt(Amg[:, 1, :], bko[:, 0:128], msl1, ALU.mult)
                    P.tt(Mtg, NKg[:, :, 0, :], idb2, ALU.add, eng="pool")
                    P.cut()
                    qc, qtc = NKg[:, :, 0, :], Amg
                    for k in range(1, 7):
                        bsq = bank()
                        for hl in range(2):
                            if k < 6:
                                P.mm(bsq[:, hl * 128:(hl + 1) * 128], qtc[:, hl, :], qc[:, hl, :])
                            P.mm(bsq[:, 256 + hl * 128:256 + (hl + 1) * 128], qc[:, hl, :], qtc[:, hl, :])
                        if k >= 2:
                            bp = bank()
                            for hl in range(2):
                                P.mm(bp[:, hl * 128:(hl + 1) * 128], qtc[:, hl, :], Mtg[:, hl, :])
                        if k < 6:
                            P.copy(QQg, bsq.re("p (h t) -> p h t", h=4), eng="act")
                        else:
                            P.copy(QQg[:, 2:4, :], bsq[:, 256:512].re("p (h t) -> p h t", h=2), eng="act")
                        if k >= 2:
                            P.tt(Mtg, bp[:, 0:256].re("p (h t) -> p h t", h=2), Mtg, ALU.add)
                        qc, qtc = Qg, QTg
                        P.cut()
                    bp = bank()
                    for hl in range(2):
                        P.mm(bp[:, hl * 128:(hl + 1) * 128], qtc[:, hl, :], Mtg[:, hl, :])
                    P.tt(Mtg, bp[:, 0:256].re("p (h t) -> p h t", h=2), Mtg, ALU.add)
                    P.cut()
                    bk = bank()
                    for hl in range(2):
                        h, pb = hbase + hl, hl * 64
                        P.mm(bk[:, hl * 64:(hl + 1) * 64], AR[pb:pb + 64, hp, 0, :], Hbf[pb:pb + 64, hp, :], start=True, stop=False)
                        P.mm(bk[:, hl * 64:(hl + 1) * 64], NKg[:, hl, 2, :], vtokb[:, h * 64:(h + 1) * 64], start=False, stop=True)
                    P.copy(Xg, bk[:, 0:128].re("p (h v) -> p h v", h=2), eng="act")
                    P.cut()
                    bk = bank()
                    for hl in range(2):
                        P.mm(bk[:, hl * 64:(hl + 1) * 64], Mtg[:, hl, :], Xg[:, hl, :])
                    P.copy(Ug, bk[:, 0:128].re("p (h v) -> p h v", h=2))
                    P.cut()
                    for hl in range(2):
                        h, pb = hbase + hl, hl * 64
                        P.mm(bank_x[:, h * 64:(h + 1) * 64], AR[pb:pb + 64, hp, 1, :], Hbf[pb:pb + 64, hp, :], start=True, stop=False)
                        P.mm(bank_x[:, h * 64:(h + 1) * 64], NKg[:, hl, 1, :], Ug[:, hl, :], start=False, stop=False)
                        P.mm(bank_x[:, h * 64:(h + 1) * 64], NKg[:, hl, 3, :], vtokb[:, h * 64:(h + 1) * 64], start=False, stop=True)
                    bk = bank()
                    for hl in range(2):
                        h = hbase + hl
                        P.mm(bk[:, hl * 64:(hl + 1) * 64], BKtok[:, hp, :], Ug[:, hl, :], start=True, stop=False)
                        P.mm(bk[:, hl * 64:(hl + 1) * 64], BKtok[:, 4 + hp, :], vtokb[:, h * 64:(h + 1) * 64], start=False, stop=True)
                    Hs = H32[:, hp, :]
                    P.tt(Hs, Hs, EL[:, hp, 127:128].bc([128, 64]), ALU.mult)
                    P.tt(Hs[0:64], bk[0:64, 0:64], Hs[0:64], ALU.add)
                    P.tt(Hs[64:128], bk[64:128, 64:128], Hs[64:128], ALU.add)
                    P.copy(Hbf[:, hp, :], Hs, eng="pool")
                    P.cut()

                for rnd in range(2):
                    P.start_seg()
                    pool_sel[0] = rot_banks[0:2]
                    emit_group(4 * rnd, BUFS[0])
                    sg0 = P.end_seg()
                    P.start_seg()
                    pool_sel[0] = rot_banks[2:4]
                    emit_group(4 * rnd + 2, BUFS[1])
                    sg1 = P.end_seg()
                    pool_sel[0] = rot_banks
                    P.merge(sg0, sg1)
                gst = sc[:, 8:16]
                yc = V(g[0].ap.rearrange("p (h v) -> p h v", h=8), g[0].tb)
                ysq = V(g[1].ap.rearrange("p (h v) -> p h v", h=8), g[1].tb)
                st1 = V(g[2].ap[:, 0:32].rearrange("p (a h) -> p a h", a=4), g[2].tb)
                y3 = bank_x.re("p (h v) -> p h v", h=8)
                P.rsum(st1[:, 0, :], y3)
                P.ts(st1[:, 0, :], st1[:, 0, :], 1.0 / 64, ALU.mult)
                P.tt(yc, y3, blast(st1[:, 0, :], [128, 8, 64]), ALU.subtract)
                P.tt(ysq, yc, yc, ALU.mult)
                P.rsum(st1[:, 1, :], ysq)
                rsqrt(st1[:, 2, :], st1[:, 1, :], 1.0 / 64, GN_EPS)
                P.cut()
                bkr = bank()
                for hp in range(4):
                    P.mm(bkr[:, hp * 2:hp * 2 + 2], prk[:, hp, :], headind)
                P.copy(st1[:, 3, :], bkr[:, 0:8])
                P.cut()
                P.tt(yc, yc, blast(st1[:, 2, :], [128, 8, 64]), ALU.mult)
                yc2 = yc.re("p h v -> p (h v)")
                P.tt(yc2, yc2, rows[:, 0:512], ALU.mult)
                P.tt(yc2, yc2, rows[:, 512:1024], ALU.add, eng="pool")
                P.tt(ysq, vtokf.re("p (h v) -> p h v", h=8), blast(st1[:, 3, :], [128, 8, 64]), ALU.mult, eng="pool")
                P.tt(yrb, yc2, ysq.re("p h v -> p (h v)"), ALU.add)
                for c in range(4):
                    P.tr(bank_t[:, c * 128:(c + 1) * 128], yrb[:, c * 128:(c + 1) * 128], identb)
                P.tt(ycatT[:, 4:8, :], bank_t[:, 0:512].re("p (c t) -> p c t", c=4), zsT[:, 4:8, :], ALU.mult)

                seg_e = P.end_seg()
                P.merge(seg_d, seg_e)
                pending_f.append((xtile, r0))

                if b == 0 and it == min(1, NT - 1):
                    dump("ycat", ycatT.re("p c t -> p (c t)"), 1024)
                    dump("yrw", yrb, 512)
                    dump("kk", kk.re("p c t -> p (c t)"), 512)
                    dump("e2", e2.re("p c t -> p (c t)"), 512)
                    dump("H", H32.re("p c t -> p (c t)"), 256)

        if pending_f:
            emit_f(*pending_f.pop())
        fw = [xt[0].tb, xt[1].tb]
        if dbgst is not None:
            fw.append(dbgst.tb)
        P.emit(final_wait=fw)
        build.stats = P.stats
    return nc


def host_params(inp):
    f = np.float32
    w_in = np.asarray(inp["w_in"][0], f)
    kr = w_in[:, 384:448]
    krrot = np.concatenate([kr[:, 32:64], kr[:, 0:32]], axis=1)
    win = np.ascontiguousarray(np.concatenate([w_in, krrot], axis=1))
    wuq = np.asarray(inp["mla_w_uq"][0], f).reshape(256, 4, 192)
    nope = wuq[:, :, 0:128].reshape(256, 512)
    rp = wuq[:, :, 128:192]
    rot = np.concatenate([rp[:, :, 32:64], rp[:, :, 0:32]], axis=2)
    wuq_l = np.ascontiguousarray(np.concatenate([nope, rp.reshape(256, 256), rot.reshape(256, 256)], axis=1))
    wukv = np.asarray(inp["mla_w_ukv"][0], f).reshape(128, 4, 256)
    wukv_l = np.ascontiguousarray(np.concatenate([wukv[:, :, 0:128].reshape(128, 512),
                                                  wukv[:, :, 128:256].reshape(128, 512)], axis=1))
    w2a = np.ascontiguousarray(np.concatenate([np.asarray(inp["rw_w2"][0], f), np.asarray(inp["rw_a2"][0], f)], axis=0))
    wout = np.ascontiguousarray(np.asarray(inp["w_out"][0], f))
    pp = np.zeros((128, NPP), f)

    def colmajor(v, n):
        return np.asarray(v, f).reshape(n, 128).T

    pp[:, PP_GPRE:PP_GPRE + 8] = colmajor(inp["norm_pre_g"][0], 8)
    pp[:, PP_GQ:PP_GQ + 2] = colmajor(inp["mla_q_norm_g"][0], 2)
    pp[:, PP_GKV:PP_GKV + 1] = colmajor(inp["mla_kv_norm_g"][0], 1)
    pp[:, PP_W0:PP_W0 + 4] = colmajor(inp["rw_w0"][0], 4)
    pp[:, PP_A0:PP_A0 + 4] = colmajor(inp["rw_a0"][0], 4)
    pp[:, PP_KK:PP_KK + 4] = colmajor(inp["rw_k_k"][0], 4)
    pp[:, PP_KA:PP_KA + 4] = colmajor(inp["rw_k_a"][0], 4)
    pp[:, PP_RK:PP_RK + 4] = colmajor(np.asarray(inp["rw_r_k"][0]).reshape(512), 4)
    invf = (10000.0 ** (-np.arange(0, 64, 2, dtype=np.float32) / 64)).astype(f)
    invf_turn = (np.concatenate([invf, invf]).astype(np.float64) / (2 * np.pi)).astype(f)
    pp[0:64, PP_INVF] = invf_turn
    mu = np.asarray(inp["rw_mu"][0], f)
    pp[:, PP_MU:PP_MU + 4] = colmajor(mu[0:512], 4)
    pp[:, PP_MU + 4:PP_MU + 8] = colmajor(mu[512:1024], 4)
    pp[:, PP_MU + 8] = mu[1536:1664]
    rows = np.concatenate([mu[1024:1536], np.asarray(inp["rw_ln_g"][0], f), np.asarray(inp["rw_ln_b"][0], f),
                           np.asarray(inp["norm_post_g"][0], f)]).reshape(1, NROWS).astype(f)
    return {"win": win, "wuq": wuq_l, "wukv": wukv_l, "w2a": w2a, "wout": wout, "pp": pp, "rows": rows}


def kernel(**inp):
    x = np.asarray(inp["x"], np.float32)
    pos = np.asarray(inp["positions"], np.int32)
    B, T, _ = x.shape
    nbc = B // N_CORES
    shared = host_params(inp)
    nc = build(nbc, T // 128)
    in_maps = []
    for c in range(N_CORES):
        m = dict(shared)
        m["x"] = np.ascontiguousarray(x[c * nbc:(c + 1) * nbc].reshape(nbc * T, D))
        m["pos"] = np.ascontiguousarray(pos[c * nbc:(c + 1) * nbc])
        in_maps.append(m)
    res = run_bass_kernel_spmd(nc, in_maps, core_ids=list(range(N_CORES)))
    out = np.concatenate([r["out"].reshape(nbc, T, D) for r in res.results], axis=0)
    return out.astype(np.float32)
```

```python
import math
import os
import numpy as np
import concourse.bass as bass
import concourse.mybir as mybir
from concourse.bass_utils import run_bass_kernel_spmd
from contextlib import ExitStack

F32 = mybir.dt.float32
BF16 = mybir.dt.bfloat16
I32 = mybir.dt.int32
AF = mybir.ActivationFunctionType
ALU = mybir.AluOpType
AX = mybir.AxisListType

SAME_ENGINE_SYNC = True
N_CORES = 8
T_FULL = 2048
D = 1024


class TB:
    __slots__ = ("name", "last_w", "readers", "dma_sem", "dma_cnt", "inherit")

    def __init__(self, name, inherit=None):
        self.name = name
        self.last_w = []
        self.readers = {}
        self.dma_sem = None
        self.dma_cnt = 0
        self.inherit = inherit


class V:
    __slots__ = ("ap", "tb")

    def __init__(self, ap, tb):
        self.ap = ap
        self.tb = tb

    def __getitem__(self, idx):
        return V(self.ap[idx], self.tb)

    def bc(self, shape):
        return V(self.ap.to_broadcast(list(shape)), self.tb)

    def re(self, s, **kw):
        return V(self.ap.rearrange(s, **kw), self.tb)


class Prog:
    ENGS = ("pe", "act", "dve", "pool", "sp")

    def __init__(self, nc, es):
        self.nc = nc
        self.es = es
        self.main = []
        self.cur = self.main
        self.stack = []
        self.ops = []
        self.signal = set()

    def sb(self, name, shape, dt=F32):
        t = self.es.enter_context(self.nc.sbuf_tensor("s_" + name, list(shape), dt))
        return V(t[:], TB(name))

    def ps(self, name, shape, dt=F32):
        t = self.es.enter_context(self.nc.psum_tensor("p_" + name, list(shape), dt))
        return V(t[:], TB(name))

    def add(self, eng, emit, reads=(), writes=(), dma_tb=None):
        rt = [r.tb if isinstance(r, V) else r for r in reads]
        wt = [w.tb if isinstance(w, V) else w for w in writes]
        self.cur.append((eng, emit, rt, wt, dma_tb))

    def start_seg(self):
        self.stack.append(self.cur)
        self.cur = []

    def end_seg(self):
        seg = self.cur
        self.cur = self.stack.pop()
        return seg

    def cut(self):
        self.cur.append(None)

    def merge(self, sa, sb):
        def units(seg):
            out, u = [], []
            for r in seg:
                if r is None:
                    if u:
                        out.append(u)
                    u = []
                else:
                    u.append(r)
            if u:
                out.append(u)
            return out
        ua, ub = units(sa), units(sb)
        na, nb = len(ua), len(ub)
        ia = ib = 0
        while ia < na or ib < nb:
            if ib >= nb or (ia < na and ia * nb <= ib * na):
                self.cur.extend(ua[ia])
                ia += 1
            else:
                self.cur.extend(ub[ib])
                ib += 1
            self.cur.append(None)

    @staticmethod
    def _touch(tb):
        if tb.inherit is not None:
            p = tb.inherit
            tb.inherit = None
            Prog._touch(p)
            tb.last_w = list(p.last_w)
            tb.readers = dict(p.readers)

    def finalize(self):
        for rec in self.main:
            if rec is None:
                continue
            (eng, emit, rt, wt, dma_tb) = rec
            idx = len(self.ops)
            deps = []
            for tb in rt:
                self._touch(tb)
                deps.extend(tb.last_w)
            for tb in wt:
                self._touch(tb)
                deps.extend(tb.last_w)
                deps.extend(tb.readers.values())
            if dma_tb is not None:
                dma_tb.dma_cnt += 1
                me = ("d", dma_tb, 16 * dma_tb.dma_cnt)
                rkey = ("d", id(dma_tb))
            else:
                me = ("e", eng, idx)
                rkey = ("e", eng)
            for tb in rt:
                tb.readers[rkey] = me
            for tb in wt:
                tb.last_w = [me]
                tb.readers = {}
            seen = set()
            d2 = []
            for d in deps:
                k = (d[0], id(d[1]) if d[0] == "d" else d[1], d[2])
                if k in seen:
                    continue
                seen.add(k)
                d2.append(d)
                if d[0] == "e" and not (d[1] == eng and (eng == "pe" or not SAME_ENGINE_SYNC)):
                    self.signal.add(d[2])
            self.ops.append((eng, emit, d2, dma_tb))

    def dma(self, out, in_, eng="sp", reads=(), writes=()):
        sbv = out if isinstance(out, V) else in_
        o = out.ap if isinstance(out, V) else out
        i = in_.ap if isinstance(in_, V) else in_
        r = list(reads) + ([in_] if isinstance(in_, V) else [])
        w = list(writes) + ([out] if isinstance(out, V) else [])
        return self.add(eng, lambda e: e.dma_start(out=o, in_=i), r, w, dma_tb=sbv.tb)

    def mm(self, out, lhsT, rhs, start=True, stop=True):
        return self.add("pe", lambda e: e.matmul(out.ap, lhsT.ap, rhs.ap, start=start, stop=stop),
                        [lhsT, rhs], [out])

    def tr(self, out, in_, ident):
        return self.add("pe", lambda e: e.transpose(out.ap, in_.ap, ident.ap), [in_, ident], [out])

    def act(self, out, in_, func, bias=None, scale=1.0, accum_out=None):
        reads = [in_]
        kw = {}
        if isinstance(bias, V):
            reads.append(bias)
            kw["bias"] = bias.ap
        elif bias is not None:
            kw["bias"] = bias
        if isinstance(scale, V):
            reads.append(scale)
            kw["scale"] = scale.ap
        else:
            kw["scale"] = scale
        writes = [out]
        if accum_out is not None:
            writes.append(accum_out)
            kw["accum_out"] = accum_out.ap
        return self.add("act", lambda e: e.activation(out.ap, in_.ap, func, **kw), reads, writes)

    def tt(self, out, a, b, op, eng="dve"):
        return self.add(eng, lambda e: e.tensor_tensor(out.ap, a.ap, b.ap, op), [a, b], [out])

    def ts(self, out, a, s1, op0, s2=None, op1=None, eng="dve"):
        reads = [a]
        x1, x2 = s1, s2
        if isinstance(s1, V):
            reads.append(s1)
            x1 = s1.ap
        if isinstance(s2, V):
            reads.append(s2)
            x2 = s2.ap
        kw = {}
        if op1 is not None:
            kw["op1"] = op1
        return self.add(eng, lambda e: e.tensor_scalar(out.ap, a.ap, x1, x2, op0, **kw), reads, [out])

    def stt(self, out, a, s, b, op0, op1, eng="dve"):
        reads = [a, b]
        x = s
        if isinstance(s, V):
            reads.append(s)
            x = s.ap
        return self.add(eng, lambda e: e.scalar_tensor_tensor(out.ap, a.ap, x, b.ap, op0, op1), reads, [out])

    def copy(self, out, in_, eng="dve"):
        if eng == "act":
            return self.add("act", lambda e: e.copy(out.ap, in_.ap), [in_], [out])
        return self.add(eng, lambda e: e.tensor_copy(out.ap, in_.ap), [in_], [out])

    def memset(self, out, val, eng="dve"):
        return self.add(eng, lambda e: e.memset(out.ap, val), [], [out])

    def recip(self, out, in_):
        return self.add("dve", lambda e: e.reciprocal(out.ap, in_.ap), [in_], [out])

    def rsum(self, out, in_, eng="dve"):
        return self.add(eng, lambda e: e.tensor_reduce(out.ap, in_.ap, AX.X, ALU.add), [in_], [out])

    def scan(self, out, d0, d1, init, op0, op1):
        return self.add("dve", lambda e: e.tensor_tensor_scan(out.ap, d0.ap, d1.ap, init, op0, op1), [d0, d1], [out])

    def aselect(self, out, in_, pattern, cmp, fill, base, cm):
        return self.add("pool", lambda e: e.affine_select(out.ap, in_.ap, pattern, cmp, fill, base=base,
                                                          channel_multiplier=cm), [in_], [out])

    def emit(self, final_wait=()):
        nc, es = self.nc, self.es
        self.finalize()
        ordn = {}
        cnt = {e: 0 for e in self.ENGS}
        for i, (eng, _, _, dma_tb) in enumerate(self.ops):
            if dma_tb is None and i in self.signal:
                cnt[eng] += 1
                ordn[i] = cnt[eng]
        esem = {e: es.enter_context(nc.semaphore("sem_" + e)) for e in self.ENGS}
        for (eng, _, _, dma_tb) in self.ops:
            if dma_tb is not None and dma_tb.dma_sem is None:
                dma_tb.dma_sem = es.enter_context(nc.semaphore("dsem_%s" % dma_tb.name))
        per_eng = {e: [] for e in self.ENGS}
        for i, op in enumerate(self.ops):
            per_eng[op[0]].append(i)
        self.stats = {e: [len(per_eng[e]), cnt[e], 0] for e in self.ENGS}
        block = es.enter_context(nc.Block())
        ops, stats = self.ops, self.stats

        def run(engname, e):
            waited = {}
            for i in per_eng[engname]:
                _, emit, deps, dma_tb = ops[i]
                need = {}
                for d in deps:
                    if d[0] == "e":
                        if d[1] == engname and (engname == "pe" or not SAME_ENGINE_SYNC):
                            continue
                        sem, val, key = esem[d[1]], ordn[d[2]], "e" + d[1]
                    else:
                        sem, val, key = d[1].dma_sem, d[2], id(d[1])
                    if waited.get(key, 0) >= val:
                        continue
                    if key not in need or need[key][1] < val:
                        need[key] = (sem, val)
                for key, (sem, val) in need.items():
                    e.wait_ge(sem, val)
                    waited[key] = val
                    stats[engname][2] += 1
                ins = emit(e)
                if dma_tb is not None:
                    ins.then_inc(dma_tb.dma_sem, 16)
                elif i in ordn:
                    ins.then_inc(esem[engname], 1)
            if engname == "sp":
                for tb in final_wait:
                    if tb.dma_sem is not None:
                        e.wait_ge(tb.dma_sem, 16 * tb.dma_cnt)

        @block.tensor
        def _(e):
            run("pe", e)

        @block.scalar
        def _(e):
            run("act", e)

        @block.vector
        def _(e):
            run("dve", e)

        @block.gpsimd
        def _(e):
            run("pool", e)

        @block.sync
        def _(e):
            run("sp", e)


WC = 3136 + 64 + 512
C_CQ, C_CKV, C_KR, C_R, C_K, C_V, C_XW, C_Z = 0, 256, 384, 448, 960, 1472, 1984, 2112
C_KRROT, C_V2 = 3136, 3200
PP_GPRE, PP_GQ, PP_GKV, PP_W0, PP_A0, PP_KK, PP_KA, PP_RK, PP_INVF, PP_MU = 0, 8, 10, 11, 15, 19, 23, 27, 31, 32
NPP = 41
NROWS = 2560
GN_EPS = 64e-5
NORM_EPS = 1e-6
SM_SCALE = 192.0 ** -0.5


def build(NBC, NT, dbg_names=(), stop=None):
    nc = bass.Bass("TRN2", target_bir_lowering=False)

    def din(name, shape, dt=F32):
        return nc.dram_tensor(name, list(shape), dt, kind="ExternalInput").ap()

    T = NT * 128
    x_d = din("x", [NBC * T, D])
    pos_d = din("pos", [NBC, T], I32)
    win_d = din("win", [D, 3200])
    wuq_d = din("wuq", [256, 1024])
    wukv_d = din("wukv", [128, 1024])
    w2a_d = din("w2a", [128, 512])
    wout_d = din("wout", [D, D])
    pp_d = din("pp", [128, NPP])
    rows_d = din("rows", [1, NROWS])
    out_d = nc.dram_tensor("out", [NBC * T, D], F32, kind="ExternalOutput").ap()
    dbg_d = {}
    for (nm, shape) in dbg_names:
        dbg_d[nm] = nc.dram_tensor("dbg_" + nm, list(shape), F32, kind="ExternalOutput").ap()

    with ExitStack() as es:
        P = Prog(nc, es)
        ident = P.sb("ident", [128, 128])
        identb = P.sb("identb", [128, 128], BF16)
        onesf = P.sb("onesf", [128, 128])
        onesb = P.sb("onesb", [128, 128], BF16)
        blockones = P.sb("blockones", [128, 128])
        headind = P.sb("headind", [128, 2], BF16)
        m_su = P.sb("m_su", [128, 128], BF16)
        m_iu = P.sb("m_iu", [128, 128], BF16)
        m_sl = P.sb("m_sl", [128, 128], BF16)
        mask4 = P.sb("mask4", [128, 4, 128], BF16)
        P.memset(ident, 0.0, eng="pool")
        P.aselect(ident, ident, [[-1, 128]], ALU.not_equal, 1.0, 0, 1)
        P.copy(identb, ident)
        P.memset(onesf, 1.0)
        P.memset(onesb, 1.0)
        P.memset(blockones, 0.0)
        P.memset(blockones[0:64, 0:64], 1.0)
        P.memset(blockones[64:128, 64:128], 1.0)
        P.memset(headind, 0.0)
        P.memset(headind[0:64, 0:1], 1.0)
        P.memset(headind[64:128, 1:2], 1.0)
        for m in (m_su, m_iu, m_sl):
            P.memset(m, 1.0, eng="pool")
        P.aselect(m_su, m_su, [[1, 128]], ALU.is_gt, 0.0, 0, -1)
        P.aselect(m_iu, m_iu, [[1, 128]], ALU.is_ge, 0.0, 0, -1)
        P.aselect(m_sl, m_sl, [[-1, 128]], ALU.is_gt, 0.0, 0, 1)
        P.copy(mask4[:, 0, :], m_su)
        P.copy(mask4[:, 1, :], m_iu)
        P.copy(mask4[:, 2, :], m_su)
        P.copy(mask4[:, 3, :], m_iu)
        m4 = mask4

        pp = P.sb("pp", [128, NPP])
        P.dma(pp, pp_d)
        rows = P.sb("rows", [128, 2048])
        P.dma(rows, rows_d[:, 512:2560].broadcast_to([128, 2048]))
        der = P.sb("der", [128, 32])
        gneg = der[:, 0:8]
        gqneg = der[:, 8:10]
        omka = der[:, 10:14]
        P.ts(gneg, pp[:, PP_GPRE:PP_GPRE + 8], -1.0, ALU.mult)
        P.ts(gqneg, pp[:, PP_GQ:PP_GQ + 2], -1.0, ALU.mult)
        P.ts(omka, pp[:, PP_KA:PP_KA + 4], -1.0, ALU.mult, 1.0, ALU.add)
        Gt = es.enter_context(nc.sbuf_tensor("s_G", [128, 10, 512], F32))
        g = [V(Gt[:][:, i, :], TB("G%d" % i)) for i in range(10)]

        def g3(i, a, parts=128):
            return V(g[i].ap[0:parts, :].rearrange("p (a t) -> p a t", a=a), g[i].tb)

        muv, omuv = g[8], g[9]
        P.dma(muv, rows_d[:, 0:512].broadcast_to([128, 512]))
        P.ts(omuv, muv, -1.0, ALU.mult, 1.0, ALU.add)

        W = P.sb("W", [128, 8, WC], BF16)
        stg = [P.sb("stg0", [128, 3200]), P.sb("stg1", [128, 3200])]
        for c in range(8):
            s = stg[c % 2]
            P.dma(s, win_d[c * 128:(c + 1) * 128, :])
            gg = pp[:, PP_GPRE + c:PP_GPRE + c + 1]
            gn = gneg[:, c:c + 1]
            e1 = "dve"
            e2_ = "pool" if c % 2 == 0 else "dve"
            P.ts(W[:, c, 0:C_V], s[:, 0:C_V], gg, ALU.mult, eng=e1)
            P.act(W[:, c, C_XW:3136], s[:, C_XW:3136], AF.Copy, scale=gg)
            P.ts(W[:, c, C_KRROT:C_KRROT + 32], s[:, 3136:3168], gn, ALU.mult, eng=e1)
            P.ts(W[:, c, C_KRROT + 32:C_KRROT + 64], s[:, 3168:3200], gg, ALU.mult, eng=e1)
            P.stt(W[:, c, C_V:C_V + 512], s[:, C_V:C_V + 512], gg, omuv, ALU.mult, ALU.mult)
            P.stt(W[:, c, C_V2:C_V2 + 512], s[:, C_V:C_V + 512], gg, muv, ALU.mult, ALU.mult)
        Wq = P.sb("Wq", [128, 2, 1024], BF16)
        for c in range(2):
            s = stg[c % 2]
            P.dma(s[:, 0:1024], wuq_d[c * 128:(c + 1) * 128, :])
            gg = pp[:, PP_GQ + c:PP_GQ + c + 1]
            gn = gqneg[:, c:c + 1]
            P.ts(Wq[:, c, 0:768], s[:, 0:768], gg, ALU.mult)
            rot_o = Wq[:, c, 768:1024].re("p (h r) -> p h r", h=4)
            rot_in = s[:, 768:1024].re("p (h r) -> p h r", h=4)
            P.ts(rot_o[:, :, 0:32], rot_in[:, :, 0:32], gn, ALU.mult)
            P.ts(rot_o[:, :, 32:64], rot_in[:, :, 32:64], gg, ALU.mult)
        Wkv = P.sb("Wkv", [128, 1024], BF16)
        s = stg[0]
        P.dma(s[:, 0:1024], wukv_d)
        P.ts(Wkv, s[:, 0:1024], pp[:, PP_GKV:PP_GKV + 1], ALU.mult)
        W2A = P.sb("W2A", [128, 512], BF16)
        P.dma(W2A, w2a_d, eng="pool")
        Wout = P.sb("Wout", [128, 8, 1024], BF16)
        for c in range(8):
            P.dma(Wout[:, c, :], wout_d[c * 128:(c + 1) * 128, :], eng="pool")

        carve_off = [0, 0]

        def carve(si, name, shape, dt):
            esz = 4 if dt in (F32, I32) else 2
            n = 1
            for d_ in shape[1:]:
                n *= d_
            nb = n * esz
            c0 = carve_off[si] // 4
            carve_off[si] += nb
            assert carve_off[si] <= 12800, (name, carve_off)
            ap = stg[si].ap[0:shape[0], c0:c0 + nb // 4]
            if dt != F32:
                ap = ap.bitcast(dt)
            if len(shape) == 3:
                ap = ap.rearrange("p (a b) -> p a b", a=shape[1])
            elif len(shape) == 4:
                ap = ap.rearrange("p (a b c) -> p a b c", a=shape[1], b=shape[2])
            return V(ap, TB(name, inherit=stg[si].tb))

        AR = carve(0, "AR", [128, 4, 2, 128], BF16)
        Bt = carve(0, "Bt", [128, 4, 128], BF16)
        Kt = carve(0, "Kt", [128, 4, 128], BF16)
        bhat = carve(0, "bhat", [128, 4, 128], BF16)
        khat = carve(0, "khat", [128, 4, 128], BF16)
        BKtok = carve(0, "BKtok", [128, 8, 128], BF16)
        BUFS = []
        for si_ in range(2):
            BUFS.append((carve(0, "NK%d" % si_, [128, 2, 4, 128], BF16),
                         carve(1, "Am%d" % si_, [128, 2, 128], BF16),
                         carve(1, "QQ%d" % si_, [128, 4, 128], BF16),
                         carve(1, "Mt%d" % si_, [128, 2, 128], BF16),
                         carve(1, "Xb%d" % si_, [128, 2, 64], BF16),
                         carve(1, "Ub%d" % si_, [128, 2, 64], BF16)))
        yrb = carve(1, "yrb", [128, 512], BF16)
        prk = carve(1, "prk", [128, 4, 128], BF16)
        lor = carve(1, "lor", [128, 128], BF16)
        vtokb = carve(1, "vtokb", [128, 512], BF16)
        PT = [carve(1, "PT0", [128, 4, 128], BF16), carve(1, "PT1", [128, 4, 128], BF16)]
        cqn = carve(1, "cqn", [128, 3, 128], BF16)
        qnT = carve(1, "qnT", [128, 4, 128], BF16)

        rot_banks = [P.ps("bank%d" % i, [128, 512]) for i in range(4)]
        bank_y = P.ps("bank_y", [128, 512])
        bank_s = P.ps("bank_s", [128, 512])
        bank_x = P.ps("bank_x", [128, 512])
        bank_t = P.ps("bank_t", [128, 1024], BF16)
        rot_i = [0]
        pool_sel = [rot_banks]
        bank_tf = V(bank_t.ap.bitcast(F32), bank_t.tb)

        def bank():
            pl = pool_sel[0]
            bkk = pl[rot_i[0] % len(pl)]
            rot_i[0] += 1
            return bkk

        KnT = P.sb("KnT", [128, 4, T], BF16)
        krT = P.sb("krT", [64, T], BF16)
        Vm = P.sb("Vm", [128, NT, 512], BF16)
        uT = P.sb("uT", [128, 8, 128], BF16)
        uTs = P.sb("uTs", [128, 8, 128], BF16)
        xt = [P.sb("xt0", [128, D]), P.sb("xt1", [128, D])]
        sc = P.sb("sc", [128, 16])
        Praw = P.sb("Praw", [128, 9, 129])
        H32 = P.sb("H32", [128, 4, 64])
        Hbf = P.sb("Hbf", [128, 4, 64], BF16)
        qrT = P.sb("qrT", [64, 4, 128], BF16)
        zsT = P.sb("zsT", [128, 8, 128], BF16)
        ycatT = P.sb("ycatT", [128, 8, 128], BF16)
        mixed = P.sb("mixed", [128, 9, 128])
        vtokf = P.sb("vtokf", [128, 512])
        atmp = P.sb("atmp", [128, 2, 128])
        ropei = P.sb("ropei", [64, 2, 128], I32)
        dbgst = P.sb("dbgst", [128, 1024]) if dbg_d else None

        def dump(nm, v, ncols):
            if nm not in dbg_d:
                return
            P.copy(dbgst[:, 0:ncols], v)
            P.dma(dbg_d[nm], dbgst[:, 0:ncols])

        def rsqrt(out, in_, scale, eps):
            P.act(out, in_, AF.Ln, bias=eps, scale=scale)
            P.act(out, out, AF.Exp, scale=-0.5)

        def bmid(v, shape):
            return V(v.ap.unsqueeze(1).to_broadcast(list(shape)), v.tb)

        def blast(v, shape):
            return V(v.ap.unsqueeze(2).to_broadcast(list(shape)), v.tb)

        pending_f = []

        def emit_f(xtile, r0):
            bo = [bank(), bank()]
            for n in range(2):
                for c in range(8):
                    P.mm(bo[n], ycatT[:, c, :], Wout[:, c, n * 512:(n + 1) * 512], start=(c == 0), stop=(c == 7))
            P.memset(sc[:, 2:4], 0.0, eng="pool")
            P.act(g[1], bo[0], AF.Square, accum_out=sc[:, 2:3])
            P.act(g[2], bo[1], AF.Square, accum_out=sc[:, 3:4])
            P.tt(sc[:, 4:5], sc[:, 2:3], sc[:, 3:4], ALU.add)
            rsqrt(sc[:, 5:6], sc[:, 4:5], 1.0 / D, NORM_EPS)
            for n, gi in ((0, 4), (1, 8)):
                P.stt(g[gi], bo[n], sc[:, 5:6], rows[:, 1024 + n * 512:1024 + (n + 1) * 512], ALU.mult, ALU.mult)
                P.tt(xtile[:, n * 512:(n + 1) * 512], xtile[:, n * 512:(n + 1) * 512], g[gi], ALU.add, eng="pool")
            P.dma(out_d[r0:r0 + 128, :], xtile)

        tiles = [(b_, it_) for b_ in range(NBC) for it_ in range(NT)]

        def stage_a(n):
            b_, it_ = tiles[n]
            r0_ = b_ * T + it_ * 128
            xtile_ = xt[n % 2]
            P.dma(xtile_, x_d[r0_:r0_ + 128, :])
            ss = sc[:, 0:1]
            rstd = sc[:, 1:2]
            P.memset(ss, 0.0, eng="pool")
            xs = V(g[7].ap.bitcast(BF16), g[7].tb)
            P.act(xs, xtile_, AF.Square, accum_out=ss)
            rsqrt(rstd, ss, 1.0 / D, NORM_EPS)
            P.ts(xs, xtile_, rstd, ALU.mult)
            for c in range(8):
                P.tr(bank_t[:, c * 128:(c + 1) * 128], xs[:, c * 128:(c + 1) * 128], identb)
            if it_ == 0:
                P.memset(uTs[:, :, 0:1], 0.0, eng="pool")
            else:
                P.copy(uTs[:, :, 0:1], uT[:, :, 127:128], eng="pool")
            P.copy(uT, bank_t.re("p (c t) -> p c t", c=8), eng="act")
            P.copy(uTs[:, :, 1:128], uT[:, :, 0:127], eng="pool")

        ti_glob = 0
        for b in range(NBC):
            P.memset(Praw[:, :, 0:1], 0.0)
            P.memset(H32, 0.0)
            P.memset(Hbf, 0.0)
            for it in range(NT):
                r0 = b * T + it * 128
                tsl = slice(it * 128, (it + 1) * 128)
                n_tile = ti_glob
                xtile = xt[ti_glob % 2]
                ti_glob += 1
                stage_a(n_tile)
                if pending_f:
                    emit_f(*pending_f.pop())

                def proj(outv, col0, ncols, shift=False):
                    for c in range(8):
                        rhs = uTs[:, c, :] if shift else uT[:, c, :]
                        P.mm(outv, W[:, c, col0:col0 + ncols], rhs, start=(c == 0), stop=(c == 7))

                cq = g3(0, 4)[:, 0:3, :]
                sq3 = g3(1, 4)[:, 0:3, :]
                rs3 = g3(2, 4)[:, 0:3, :]
                qtmp = g3(3, 4, 64)
                qtmp2 = g3(4, 4, 64)
                sg = g3(7, 4)
                bk = bank()
                for j in range(3):
                    proj(bk[:, j * 128:(j + 1) * 128], C_CQ + j * 128, 128)
                P.copy(cq, bk[:, 0:384].re("p (j t) -> p j t", j=3), eng="act")
                P.start_seg()
                pool_sel[0] = rot_banks[0:2]
                for half in range(2):
                    bk = bank()
                    for j in range(4):
                        proj(bk[:, j * 128:(j + 1) * 128], C_Z + (half * 4 + j) * 128, 128)
                    bk3 = bk.re("p (j t) -> p j t", j=4)
                    P.act(sg, bk3, AF.Sigmoid)
                    P.tt(zsT[:, half * 4:(half + 1) * 4, :], bk3, sg, ALU.mult)
                    P.cut()
                for q_, col0 in ((0, C_R), (1, C_K)):
                    bk = bank()
                    for j in range(4):
                        proj(bk[:, j * 128:(j + 1) * 128], col0 + j * 128, 128)
                    P.copy(Praw[:, q_ * 4:(q_ + 1) * 4, 1:129], bk.re("p (j t) -> p j t", j=4),
                           eng=("act" if q_ == 0 else "dve"))
                    P.cut()
                bk = bank()
                proj(bk[:, 0:128], C_XW, 128)
                P.copy(Praw[:, 8, 1:129], bk[:, 0:128], eng="act")
                P.cut()
                bk = bank()
                for c in range(8):
                    P.mm(bk, uT[:, c, :], W[:, c, C_V:C_V + 512], start=(c == 0), stop=False)
                for c in range(8):
                    P.mm(bk, uTs[:, c, :], W[:, c, C_V2:C_V2 + 512], start=False, stop=(c == 7))
                P.copy(vtokf, bk, eng="act")
                P.copy(vtokb, vtokf, eng="pool")
                seg_b2 = P.end_seg()

                P.start_seg()
                pool_sel[0] = rot_banks[2:4]
                rope = g3(5, 4, 64)
                turns, rtmp, sinT, cosT = (rope[:, i, :] for i in range(4))
                P.dma(ropei[:, 0, :], pos_d[b:b + 1, tsl].broadcast_to([64, 128]))
                P.copy(rtmp, ropei[:, 0, :])
                P.ts(turns, rtmp, pp[0:64, PP_INVF:PP_INVF + 1], ALU.mult)
                P.copy(ropei[:, 1, :], turns)
                P.copy(rtmp, ropei[:, 1, :])
                P.tt(rtmp, turns, rtmp, ALU.subtract)
                P.act(sinT, rtmp, AF.Sin, scale=2.0 * math.pi)
                P.ts(turns, turns, 0.25, ALU.add)
                P.copy(ropei[:, 1, :], turns)
                P.copy(rtmp, ropei[:, 1, :])
                P.tt(rtmp, turns, rtmp, ALU.subtract)
                P.act(cosT, rtmp, AF.Sin, scale=2.0 * math.pi)
                P.cut()
                bk = bank()
                proj(bk[0:64, 0:128], C_KR, 64)
                proj(bk[0:64, 128:256], C_KRROT, 64)
                P.tt(qtmp[:, 0, :], bk[0:64, 0:128], cosT, ALU.mult)
                P.tt(qtmp[:, 1, :], bk[0:64, 128:256], sinT, ALU.mult)
                P.tt(krT[:, tsl], qtmp[:, 0, :], qtmp[:, 1, :], ALU.add, eng="pool")
                P.cut()
                P.tt(sq3, cq, cq, ALU.mult, eng="pool")
                bk = bank()
                P.mm(bk[:, 0:128], onesf, sq3[:, 0, :], start=True, stop=False)
                P.mm(bk[:, 0:128], onesf, sq3[:, 1, :], start=False, stop=True)
                P.mm(bk[:, 128:256], onesf, sq3[:, 2, :], start=True, stop=True)
                rsqrt(rs3[:, 0, :], bk[:, 0:128], 1.0 / 256, NORM_EPS)
                rsqrt(rs3[:, 2, :], bk[:, 128:256], 1.0 / 128, NORM_EPS)
                P.tt(cqn[:, 0:2, :], cq[:, 0:2, :], bmid(rs3[:, 0, :], [128, 2, 128]), ALU.mult)
                P.tt(cqn[:, 2, :], cq[:, 2, :], rs3[:, 2, :], ALU.mult)
                P.cut()
                bk = bank()
                for h in range(4):
                    for c in range(2):
                        P.mm(bk[:, h * 128:(h + 1) * 128], Wq[:, c, h * 128:(h + 1) * 128], cqn[:, c, :],
                             start=(c == 0), stop=(c == 1))
                P.copy(qnT, bk.re("p (h t) -> p h t", h=4), eng="act")
                P.cut()
                bk = bank()
                bk2 = bank()
                for h in range(4):
                    for c in range(2):
                        P.mm(bk[0:64, h * 128:(h + 1) * 128], Wq[:, c, 512 + h * 64:512 + (h + 1) * 64], cqn[:, c, :],
                             start=(c == 0), stop=(c == 1))
                    for c in range(2):
                        P.mm(bk2[0:64, h * 128:(h + 1) * 128], Wq[:, c, 768 + h * 64:768 + (h + 1) * 64], cqn[:, c, :],
                             start=(c == 0), stop=(c == 1))
                P.tt(qtmp, bk[0:64, :].re("p (h t) -> p h t", h=4), bmid(cosT, [64, 4, 128]), ALU.mult)
                P.tt(qtmp2, bk2[0:64, :].re("p (h t) -> p h t", h=4), bmid(sinT, [64, 4, 128]), ALU.mult)
                P.tt(qrT, qtmp, qtmp2, ALU.add, eng="pool")
                P.cut()
                bk = bank()
                for h in range(4):
                    P.mm(bk[:, h * 128:(h + 1) * 128], Wkv[:, h * 128:(h + 1) * 128], cqn[:, 2, :])
                P.copy(KnT[:, :, tsl], bk.re("p (h t) -> p h t", h=4), eng="act")
                P.cut()
                bk = bank()
                P.mm(bk, cqn[:, 2, :], Wkv[:, 512:1024])
                P.copy(Vm[:, it, :], bk)
                seg_c = P.end_seg()
                pool_sel[0] = rot_banks
                P.merge(seg_b2, seg_c)

                P.start_seg()
                nj = it + 1
                pti = 0
                for h in range(4):
                    for jb in range(0, nj, 4):
                        njj = min(4, nj - jb)
                        bk = bank_tf
                        pt = PT[pti % 2]
                        pti += 1
                        for jj in range(njj):
                            j = jb + jj
                            P.mm(bk[:, jj * 128:(jj + 1) * 128], KnT[:, h, j * 128:(j + 1) * 128], qnT[:, h, :],
                                 start=True, stop=False)
                            P.mm(bk[:, jj * 128:(jj + 1) * 128], krT[:, j * 128:(j + 1) * 128], qrT[:, h, :],
                                 start=False, stop=True)
                        P.act(pt[:, 0:njj, :], bk[:, 0:njj * 128].re("p (j t) -> p j t", j=njj), AF.Exp, scale=SM_SCALE)
                        if jb + njj == nj:
                            P.tt(pt[:, njj - 1, :], pt[:, njj - 1, :], m_iu, ALU.mult, eng="pool")
                        for jj in range(njj):
                            j = jb + jj
                            P.mm(bank_y[:, h * 128:(h + 1) * 128], Vm[:, j, h * 128:(h + 1) * 128], pt[:, jj, :],
                                 start=(j == 0), stop=(j == nj - 1))
                            P.mm(bank_s[:, h * 128:(h + 1) * 128], onesb, pt[:, jj, :],
                                 start=(j == 0), stop=(j == nj - 1))
                        P.cut()
                    P.recip(atmp[:, 0, :], bank_s[:, h * 128:(h + 1) * 128])
                    P.tt(atmp[:, 1, :], bank_y[:, h * 128:(h + 1) * 128], atmp[:, 0, :], ALU.mult)
                    P.tt(ycatT[:, h, :], atmp[:, 1, :], zsT[:, h, :], ALU.mult, eng="pool")
                    P.cut()
                seg_d = P.end_seg()
                P.start_seg()

                e2, av, kk, kmod, bb, tm1, tm2, Lc, Lx, EL = (g3(i, 4) for i in range(10))
                mu_bc = blast(pp[:, PP_MU:PP_MU + 9], [128, 9, 128])
                P.tt(mixed, Praw[:, :, 0:128], Praw[:, :, 1:129], ALU.subtract)
                P.tt(mixed, mixed, mu_bc, ALU.mult)
                P.tt(mixed, mixed, Praw[:, :, 1:129], ALU.add)
                P.copy(Praw[:, :, 0:1], Praw[:, :, 128:129], eng="pool")
                rm = mixed[:, 0:4, :]
                km = mixed[:, 4:8, :]
                P.act(lor[0:64, :], mixed[0:64, 8, :], AF.Tanh)
                P.copy(lor[64:128, :], mixed[64:128, 8, :], eng="pool")
                P.cut()
                bkw = bank()
                bka = bank()
                for hp in range(4):
                    P.mm(bkw[:, hp * 128:(hp + 1) * 128], W2A[0:64, hp * 128:(hp + 1) * 128], lor[0:64, :])
                    P.mm(bka[:, hp * 128:(hp + 1) * 128], W2A[64:128, hp * 128:(hp + 1) * 128], lor[64:128, :])

                def pbc(col):
                    return blast(pp[:, col:col + 4], [128, 4, 128])

                P.tt(e2, bkw.re("p (h t) -> p h t", h=4), pbc(PP_W0), ALU.add)
                P.tt(av, bka.re("p (h t) -> p h t", h=4), pbc(PP_A0), ALU.add)
                P.act(e2, e2, AF.Sigmoid)
                P.act(av, av, AF.Sigmoid)
                P.ts(e2, e2, math.exp(-0.5), ALU.mult)
                P.cut()
                P.tt(kk, km, pbc(PP_KK), ALU.mult)
                P.tt(tm1, kk, kk, ALU.mult)
                bk = bank()
                for hp in range(4):
                    P.mm(bk[:, hp * 128:(hp + 1) * 128], blockones, tm1[:, hp, :])
                rsqrt(tm2, bk.re("p (h t) -> p h t", h=4), 1.0, 1e-24)
                P.tt(kk, kk, tm2, ALU.mult)
                P.cut()
                P.tt(tm1, av, pbc(PP_KA), ALU.mult, eng="pool")
                P.tt(tm1, tm1, blast(omka, [128, 4, 128]), ALU.add, eng="pool")
                P.tt(kmod, km, tm1, ALU.mult, eng="pool")
                P.tt(bb, kk, av, ALU.mult, eng="pool")
                P.tt(tm1, rm, kmod, ALU.mult, eng="pool")
                P.tt(prk, tm1, pbc(PP_RK), ALU.mult, eng="pool")
                P.cut()
                for hp in range(4):
                    P.scan(Lc[:, hp, :], onesf, e2[:, hp, :], 0.0, ALU.mult, ALU.subtract)
                P.tt(Lx, Lc, e2, ALU.add)
                P.act(EL, Lc, AF.Exp)
                P.act(Lx, Lx, AF.Exp)
                P.act(Lc, Lc, AF.Exp, scale=-1.0)
                P.cut()
                gC = EL[:, :, 127:128].bc([128, 4, 128])
                P.tt(AR[:, :, 1, :], rm, EL, ALU.mult)
                P.stt(AR[:, :, 0, :], kk, -1.0, Lx, ALU.mult, ALU.mult)
                P.tt(tm1, bb, Lc, ALU.mult)
                P.tt(tm2, kmod, Lc, ALU.mult, eng="pool")
                P.copy(Bt, tm1, eng="pool")
                P.copy(Kt, tm2, eng="pool")
                P.tt(bhat, tm1, gC, ALU.mult)
                P.tt(khat, tm2, gC, ALU.mult)
                P.cut()
                for hp in range(4):
                    P.tr(bank_t[:, hp * 128:(hp + 1) * 128], bhat[:, hp, :], identb)
                    P.tr(bank_t[:, (4 + hp) * 128:(5 + hp) * 128], khat[:, hp, :], identb)
                P.copy(BKtok, bank_t.re("p (c t) -> p c t", c=8), eng="act")
                P.cut()
                msl1 = m_sl
                idb2 = bmid(identb, [128, 2, 128])

                def emit_group(hbase, S):
                    NKg, Amg, QQg, Mtg, Xg, Ug = S
                    Qg, QTg = QQg[:, 0:2, :], QQg[:, 2:4, :]
                    hp = hbase // 2
                    for hl in range(2):
                        pb = hl * 64
                        bk = bank()
                        rhs = AR[pb:pb + 64, hp, :, :]
                        P.mm(bk[:, 0:256], Bt[pb:pb + 64, hp, :], rhs)
                        P.mm(bk[:, 256:512], Kt[pb:pb + 64, hp, :], rhs)
                        P.tt(NKg[:, hl, :, :], bk.re("p (a t) -> p a t", a=4), m4, ALU.mult)
                        P.cut()
                    bke = bank()
                    bko = bank()
                    P.mm(bke[:, 0:128], AR[0:64, hp, 0, :], Bt[0:64, hp, :])
                    P.mm(bko[:, 0:128], AR[64:128, hp, 0, :], Bt[64:128, hp, :])
                    P.tt(Amg[:, 0, :], bke[:, 0:128], msl1, ALU.mult)
                    P.tt(Amg[:, 1, :], bko[:, 0:128], msl1, ALU.mult)
                    P.tt(Mtg, NKg[:, :, 0, :], idb2, ALU.add, eng="pool")
                    P.cut()
                    qc, qtc = NKg[:, :, 0, :], Amg
                    for k in range(1, 7):
                        bsq = bank()
                        for hl in range(2):
                            if k < 6:
                                P.mm(bsq[:, hl * 128:(hl + 1) * 128], qtc[:, hl, :], qc[:, hl, :])
                            P.mm(bsq[:, 256 + hl * 128:256 + (hl + 1) * 128], qc[:, hl, :], qtc[:, hl, :])
                        if k >= 2:
                            bp = bank()
                            for hl in range(2):
                                P.mm(bp[:, hl * 128:(hl + 1) * 128], qtc[:, hl, :], Mtg[:, hl, :])
                        if k < 6:
                            P.copy(QQg, bsq.re("p (h t) -> p h t", h=4), eng="act")
                        else:
                            P.copy(QQg[:, 2:4, :], bsq[:, 256:512].re("p (h t) -> p h t", h=2), eng="act")
                        if k >= 2:
                            P.tt(Mtg, bp[:, 0:256].re("p (h t) -> p h t", h=2), Mtg, ALU.add)
                        qc, qtc = Qg, QTg
                        P.cut()
                    bp = bank()
                    for hl in range(2):
                        P.mm(bp[:, hl * 128:(hl + 1) * 128], qtc[:, hl, :], Mtg[:, hl, :])
                    P.tt(Mtg, bp[:, 0:256].re("p (h t) -> p h t", h=2), Mtg, ALU.add)
                    P.cut()
                    bk = bank()
                    for hl in range(2):
                        h, pb = hbase + hl, hl * 64
                        P.mm(bk[:, hl * 64:(hl + 1) * 64], AR[pb:pb + 64, hp, 0, :], Hbf[pb:pb + 64, hp, :], start=True, stop=False)
                        P.mm(bk[:, hl * 64:(hl + 1) * 64], NKg[:, hl, 2, :], vtokb[:, h * 64:(h + 1) * 64], start=False, stop=True)
                    P.copy(Xg, bk[:, 0:128].re("p (h v) -> p h v", h=2), eng="act")
                    P.cut()
                    bk = bank()
                    for hl in range(2):
                        P.mm(bk[:, hl * 64:(hl + 1) * 64], Mtg[:, hl, :], Xg[:, hl, :])
                    P.copy(Ug, bk[:, 0:128].re("p (h v) -> p h v", h=2))
                    P.cut()
                    for hl in range(2):
                        h, pb = hbase + hl, hl * 64
                        P.mm(bank_x[:, h * 64:(h + 1) * 64], AR[pb:pb + 64, hp, 1, :], Hbf[pb:pb + 64, hp, :], start=True, stop=False)
                        P.mm(bank_x[:, h * 64:(h + 1) * 64], NKg[:, hl, 1, :], Ug[:, hl, :], start=False, stop=False)
                        P.mm(bank_x[:, h * 64:(h + 1) * 64], NKg[:, hl, 3, :], vtokb[:, h * 64:(h + 1) * 64], start=False, stop=True)
                    bk = bank()
                    for hl in range(2):
                        h = hbase + hl
                        P.mm(bk[:, hl * 64:(hl + 1) * 64], BKtok[:, hp, :], Ug[:, hl, :], start=True, stop=False)
                        P.mm(bk[:, hl * 64:(hl + 1) * 64], BKtok[:, 4 + hp, :], vtokb[:, h * 64:(h + 1) * 64], start=False, stop=True)
                    Hs = H32[:, hp, :]
                    P.tt(Hs, Hs, EL[:, hp, 127:128].bc([128, 64]), ALU.mult)
                    P.tt(Hs[0:64], bk[0:64, 0:64], Hs[0:64], ALU.add)
                    P.tt(Hs[64:128], bk[64:128, 64:128], Hs[64:128], ALU.add)
                    P.copy(Hbf[:, hp, :], Hs, eng="pool")
                    P.cut()

                for rnd in range(2):
                    P.start_seg()
                    pool_sel[0] = rot_banks[0:2]
                    emit_group(4 * rnd, BUFS[0])
                    sg0 = P.end_seg()
                    P.start_seg()
                    pool_sel[0] = rot_banks[2:4]
                    emit_group(4 * rnd + 2, BUFS[1])
                    sg1 = P.end_seg()
                    pool_sel[0] = rot_banks
                    P.merge(sg0, sg1)
                gst = sc[:, 8:16]
                yc = V(g[0].ap.rearrange("p (h v) -> p h v", h=8), g[0].tb)
                ysq = V(g[1].ap.rearrange("p (h v) -> p h v", h=8), g[1].tb)
                st1 = V(g[2].ap[:, 0:32].rearrange("p (a h) -> p a h", a=4), g[2].tb)
                y3 = bank_x.re("p (h v) -> p h v", h=8)
                P.rsum(st1[:, 0, :], y3)
                P.ts(st1[:, 0, :], st1[:, 0, :], 1.0 / 64, ALU.mult)
                P.tt(yc, y3, blast(st1[:, 0, :], [128, 8, 64]), ALU.subtract)
                P.tt(ysq, yc, yc, ALU.mult)
                P.rsum(st1[:, 1, :], ysq)
                rsqrt(st1[:, 2, :], st1[:, 1, :], 1.0 / 64, GN_EPS)
                P.cut()
                bkr = bank()
                for hp in range(4):
                    P.mm(bkr[:, hp * 2:hp * 2 + 2], prk[:, hp, :], headind)
                P.copy(st1[:, 3, :], bkr[:, 0:8])
                P.cut()
                P.tt(yc, yc, blast(st1[:, 2, :], [128, 8, 64]), ALU.mult)
                yc2 = yc.re("p h v -> p (h v)")
                P.tt(yc2, yc2, rows[:, 0:512], ALU.mult)
                P.tt(yc2, yc2, rows[:, 512:1024], ALU.add, eng="pool")
                P.tt(ysq, vtokf.re("p (h v) -> p h v", h=8), blast(st1[:, 3, :], [128, 8, 64]), ALU.mult, eng="pool")
                P.tt(yrb, yc2, ysq.re("p h v -> p (h v)"), ALU.add)
                for c in range(4):
                    P.tr(bank_t[:, c * 128:(c + 1) * 128], yrb[:, c * 128:(c + 1) * 128], identb)
                P.tt(ycatT[:, 4:8, :], bank_t[:, 0:512].re("p (c t) -> p c t", c=4), zsT[:, 4:8, :], ALU.mult)

                seg_e = P.end_seg()
                P.merge(seg_d, seg_e)
                pending_f.append((xtile, r0))

                if b == 0 and it == min(1, NT - 1):
                    dump("ycat", ycatT.re("p c t -> p (c t)"), 1024)
                    dump("yrw", yrb, 512)
                    dump("kk", kk.re("p c t -> p (c t)"), 512)
                    dump("e2", e2.re("p c t -> p (c t)"), 512)
                    dump("H", H32.re("p c t -> p (c t)"), 256)

        if pending_f:
            emit_f(*pending_f.pop())
        fw = [xt[0].tb, xt[1].tb]
        if dbgst is not None:
            fw.append(dbgst.tb)
        P.emit(final_wait=fw)
        build.stats = P.stats
    return nc


def host_params(inp):
    f = np.float32
    w_in = np.asarray(inp["w_in"][0], f)
    kr = w_in[:, 384:448]
    krrot = np.concatenate([kr[:, 32:64], kr[:, 0:32]], axis=1)
    win = np.ascontiguousarray(np.concatenate([w_in, krrot], axis=1))
    wuq = np.asarray(inp["mla_w_uq"][0], f).reshape(256, 4, 192)
    nope = wuq[:, :, 0:128].reshape(256, 512)
    rp = wuq[:, :, 128:192]
    rot = np.concatenate([rp[:, :, 32:64], rp[:, :, 0:32]], axis=2)
    wuq_l = np.ascontiguousarray(np.concatenate([nope, rp.reshape(256, 256), rot.reshape(256, 256)], axis=1))
    wukv = np.asarray(inp["mla_w_ukv"][0], f).reshape(128, 4, 256)
    wukv_l = np.ascontiguousarray(np.concatenate([wukv[:, :, 0:128].reshape(128, 512),
                                                  wukv[:, :, 128:256].reshape(128, 512)], axis=1))
    w2a = np.ascontiguousarray(np.concatenate([np.asarray(inp["rw_w2"][0], f), np.asarray(inp["rw_a2"][0], f)], axis=0))
    wout = np.ascontiguousarray(np.asarray(inp["w_out"][0], f))
    pp = np.zeros((128, NPP), f)

    def colmajor(v, n):
        return np.asarray(v, f).reshape(n, 128).T

    pp[:, PP_GPRE:PP_GPRE + 8] = colmajor(inp["norm_pre_g"][0], 8)
    pp[:, PP_GQ:PP_GQ + 2] = colmajor(inp["mla_q_norm_g"][0], 2)
    pp[:, PP_GKV:PP_GKV + 1] = colmajor(inp["mla_kv_norm_g"][0], 1)
    pp[:, PP_W0:PP_W0 + 4] = colmajor(inp["rw_w0"][0], 4)
    pp[:, PP_A0:PP_A0 + 4] = colmajor(inp["rw_a0"][0], 4)
    pp[:, PP_KK:PP_KK + 4] = colmajor(inp["rw_k_k"][0], 4)
    pp[:, PP_KA:PP_KA + 4] = colmajor(inp["rw_k_a"][0], 4)
    pp[:, PP_RK:PP_RK + 4] = colmajor(np.asarray(inp["rw_r_k"][0]).reshape(512), 4)
    invf = (10000.0 ** (-np.arange(0, 64, 2, dtype=np.float32) / 64)).astype(f)
    invf_turn = (np.concatenate([invf, invf]).astype(np.float64) / (2 * np.pi)).astype(f)
    pp[0:64, PP_INVF] = invf_turn
    mu = np.asarray(inp["rw_mu"][0], f)
    pp[:, PP_MU:PP_MU + 4] = colmajor(mu[0:512], 4)
    pp[:, PP_MU + 4:PP_MU + 8] = colmajor(mu[512:1024], 4)
    pp[:, PP_MU + 8] = mu[1536:1664]
    rows = np.concatenate([mu[1024:1536], np.asarray(inp["rw_ln_g"][0], f), np.asarray(inp["rw_ln_b"][0], f),
                           np.asarray(inp["norm_post_g"][0], f)]).reshape(1, NROWS).astype(f)
    return {"win": win, "wuq": wuq_l, "wukv": wukv_l, "w2a": w2a, "wout": wout, "pp": pp, "rows": rows}


def kernel(**inp):
    x = np.asarray(inp["x"], np.float32)
    pos = np.asarray(inp["positions"], np.int32)
    B, T, _ = x.shape
    nbc = B // N_CORES
    shared = host_params(inp)
    nc = build(nbc, T // 128)
    in_maps = []
    for c in range(N_CORES):
        m = dict(shared)
        m["x"] = np.ascontiguousarray(x[c * nbc:(c + 1) * nbc].reshape(nbc * T, D))
        m["pos"] = np.ascontiguousarray(pos[c * nbc:(c + 1) * nbc])
        in_maps.append(m)
    res = run_bass_kernel_spmd(nc, in_maps, core_ids=list(range(N_CORES)))
    out = np.concatenate([r["out"].reshape(nbc, T, D) for r in res.results], axis=0)
    return out.astype(np.float32)
```

```python
import math
import os
import numpy as np
import concourse.bass as bass
import concourse.mybir as mybir
from concourse.bass_utils import run_bass_kernel_spmd
from contextlib import ExitStack

F32 = mybir.dt.float32
BF16 = mybir.dt.bfloat16
I32 = mybir.dt.int32
AF = mybir.ActivationFunctionType
ALU = mybir.AluOpType
AX = mybir.AxisListType

SAME_ENGINE_SYNC = True
N_CORES = 8
T_FULL = 2048
D = 1024


class TB:
    __slots__ = ("name", "last_w", "readers", "dma_sem", "dma_cnt", "inherit")

    def __init__(self, name, inherit=None):
        self.name = name
        self.last_w = []
        self.readers = {}
        self.dma_sem = None
        self.dma_cnt = 0
        self.inherit = inherit


class V:
    __slots__ = ("ap", "tb")

    def __init__(self, ap, tb):
        self.ap = ap
        self.tb = tb

    def __getitem__(self, idx):
        return V(self.ap[idx], self.tb)

    def bc(self, shape):
        return V(self.ap.to_broadcast(list(shape)), self.tb)

    def re(self, s, **kw):
        return V(self.ap.rearrange(s, **kw), self.tb)


class Prog:
    ENGS = ("pe", "act", "dve", "pool", "sp")

    def __init__(self, nc, es):
        self.nc = nc
        self.es = es
        self.main = []
        self.cur = self.main
        self.stack = []
        self.ops = []
        self.signal = set()

    def sb(self, name, shape, dt=F32):
        t = self.es.enter_context(self.nc.sbuf_tensor("s_" + name, list(shape), dt))
        return V(t[:], TB(name))

    def ps(self, name, shape, dt=F32):
        t = self.es.enter_context(self.nc.psum_tensor("p_" + name, list(shape), dt))
        return V(t[:], TB(name))

    def add(self, eng, emit, reads=(), writes=(), dma_tb=None):
        rt = [r.tb if isinstance(r, V) else r for r in reads]
        wt = [w.tb if isinstance(w, V) else w for w in writes]
        self.cur.append((eng, emit, rt, wt, dma_tb))

    def start_seg(self):
        self.stack.append(self.cur)
        self.cur = []

    def end_seg(self):
        seg = self.cur
        self.cur = self.stack.pop()
        return seg

    def cut(self):
        self.cur.append(None)

    def merge(self, sa, sb):
        def units(seg):
            out, u = [], []
            for r in seg:
                if r is None:
                    if u:
                        out.append(u)
                    u = []
                else:
                    u.append(r)
            if u:
                out.append(u)
            return out
        ua, ub = units(sa), units(sb)
        na, nb = len(ua), len(ub)
        ia = ib = 0
        while ia < na or ib < nb:
            if ib >= nb or (ia < na and ia * nb <= ib * na):
                self.cur.extend(ua[ia])
                ia += 1
            else:
                self.cur.extend(ub[ib])
                ib += 1
            self.cur.append(None)

    @staticmethod
    def _touch(tb):
        if tb.inherit is not None:
            p = tb.inherit
            tb.inherit = None
            Prog._touch(p)
            tb.last_w = list(p.last_w)
            tb.readers = dict(p.readers)

    def finalize(self):
        for rec in self.main:
            if rec is None:
                continue
            (eng, emit, rt, wt, dma_tb) = rec
            idx = len(self.ops)
            deps = []
            for tb in rt:
                self._touch(tb)
                deps.extend(tb.last_w)
            for tb in wt:
                self._touch(tb)
                deps.extend(tb.last_w)
                deps.extend(tb.readers.values())
            if dma_tb is not None:
                dma_tb.dma_cnt += 1
                me = ("d", dma_tb, 16 * dma_tb.dma_cnt)
                rkey = ("d", id(dma_tb))
            else:
                me = ("e", eng, idx)
                rkey = ("e", eng)
            for tb in rt:
                tb.readers[rkey] = me
            for tb in wt:
                tb.last_w = [me]
                tb.readers = {}
            seen = set()
            d2 = []
            for d in deps:
                k = (d[0], id(d[1]) if d[0] == "d" else d[1], d[2])
                if k in seen:
                    continue
                seen.add(k)
                d2.append(d)
                if d[0] == "e" and not (d[1] == eng and (eng == "pe" or not SAME_ENGINE_SYNC)):
                    self.signal.add(d[2])
            self.ops.append((eng, emit, d2, dma_tb))

    def dma(self, out, in_, eng="sp", reads=(), writes=()):
        sbv = out if isinstance(out, V) else in_
        o = out.ap if isinstance(out, V) else out
        i = in_.ap if isinstance(in_, V) else in_
        r = list(reads) + ([in_] if isinstance(in_, V) else [])
        w = list(writes) + ([out] if isinstance(out, V) else [])
        return self.add(eng, lambda e: e.dma_start(out=o, in_=i), r, w, dma_tb=sbv.tb)

    def mm(self, out, lhsT, rhs, start=True, stop=True):
        return self.add("pe", lambda e: e.matmul(out.ap, lhsT.ap, rhs.ap, start=start, stop=stop),
                        [lhsT, rhs], [out])

    def tr(self, out, in_, ident):
        return self.add("pe", lambda e: e.transpose(out.ap, in_.ap, ident.ap), [in_, ident], [out])

    def act(self, out, in_, func, bias=None, scale=1.0, accum_out=None):
        reads = [in_]
        kw = {}
        if isinstance(bias, V):
            reads.append(bias)
            kw["bias"] = bias.ap
        elif bias is not None:
            kw["bias"] = bias
        if isinstance(scale, V):
            reads.append(scale)
            kw["scale"] = scale.ap
        else:
            kw["scale"] = scale
        writes = [out]
        if accum_out is not None:
            writes.append(accum_out)
            kw["accum_out"] = accum_out.ap
        return self.add("act", lambda e: e.activation(out.ap, in_.ap, func, **kw), reads, writes)

    def tt(self, out, a, b, op, eng="dve"):
        return self.add(eng, lambda e: e.tensor_tensor(out.ap, a.ap, b.ap, op), [a, b], [out])

    def ts(self, out, a, s1, op0, s2=None, op1=None, eng="dve"):
        reads = [a]
        x1, x2 = s1, s2
        if isinstance(s1, V):
            reads.append(s1)
            x1 = s1.ap
        if isinstance(s2, V):
            reads.append(s2)
            x2 = s2.ap
        kw = {}
        if op1 is not None:
            kw["op1"] = op1
        return self.add(eng, lambda e: e.tensor_scalar(out.ap, a.ap, x1, x2, op0, **kw), reads, [out])

    def stt(self, out, a, s, b, op0, op1, eng="dve"):
        reads = [a, b]
        x = s
        if isinstance(s, V):
            reads.append(s)
            x = s.ap
        return self.add(eng, lambda e: e.scalar_tensor_tensor(out.ap, a.ap, x, b.ap, op0, op1), reads, [out])

    def copy(self, out, in_, eng="dve"):
        if eng == "act":
            return self.add("act", lambda e: e.copy(out.ap, in_.ap), [in_], [out])
        return self.add(eng, lambda e: e.tensor_copy(out.ap, in_.ap), [in_], [out])

    def memset(self, out, val, eng="dve"):
        return self.add(eng, lambda e: e.memset(out.ap, val), [], [out])

    def recip(self, out, in_):
        return self.add("dve", lambda e: e.reciprocal(out.ap, in_.ap), [in_], [out])

    def rsum(self, out, in_, eng="dve"):
        return self.add(eng, lambda e: e.tensor_reduce(out.ap, in_.ap, AX.X, ALU.add), [in_], [out])

    def scan(self, out, d0, d1, init, op0, op1):
        return self.add("dve", lambda e: e.tensor_tensor_scan(out.ap, d0.ap, d1.ap, init, op0, op1), [d0, d1], [out])

    def aselect(self, out, in_, pattern, cmp, fill, base, cm):
        return self.add("pool", lambda e: e.affine_select(out.ap, in_.ap, pattern, cmp, fill, base=base,
                                                          channel_multiplier=cm), [in_], [out])

    def emit(self, final_wait=()):
        nc, es = self.nc, self.es
        self.finalize()
        ordn = {}
        cnt = {e: 0 for e in self.ENGS}
        for i, (eng, _, _, dma_tb) in enumerate(self.ops):
            if dma_tb is None and i in self.signal:
                cnt[eng] += 1
                ordn[i] = cnt[eng]
        esem = {e: es.enter_context(nc.semaphore("sem_" + e)) for e in self.ENGS}
        for (eng, _, _, dma_tb) in self.ops:
            if dma_tb is not None and dma_tb.dma_sem is None:
                dma_tb.dma_sem = es.enter_context(nc.semaphore("dsem_%s" % dma_tb.name))
        per_eng = {e: [] for e in self.ENGS}
        for i, op in enumerate(self.ops):
            per_eng[op[0]].append(i)
        self.stats = {e: [len(per_eng[e]), cnt[e], 0] for e in self.ENGS}
        block = es.enter_context(nc.Block())
        ops, stats = self.ops, self.stats

        def run(engname, e):
            waited = {}
            for i in per_eng[engname]:
                _, emit, deps, dma_tb = ops[i]
                need = {}
                for d in deps:
                    if d[0] == "e":
                        if d[1] == engname and (engname == "pe" or not SAME_ENGINE_SYNC):
                            continue
                        sem, val, key = esem[d[1]], ordn[d[2]], "e" + d[1]
                    else:
                        sem, val, key = d[1].dma_sem, d[2], id(d[1])
                    if waited.get(key, 0) >= val:
                        continue
                    if key not in need or need[key][1] < val:
                        need[key] = (sem, val)
                for key, (sem, val) in need.items():
                    e.wait_ge(sem, val)
                    waited[key] = val
                    stats[engname][2] += 1
                ins = emit(e)
                if dma_tb is not None:
                    ins.then_inc(dma_tb.dma_sem, 16)
                elif i in ordn:
                    ins.then_inc(esem[engname], 1)
            if engname == "sp":
                for tb in final_wait:
                    if tb.dma_sem is not None:
                        e.wait_ge(tb.dma_sem, 16 * tb.dma_cnt)

        @block.tensor
        def _(e):
            run("pe", e)

        @block.scalar
        def _(e):
            run("act", e)

        @block.vector
        def _(e):
            run("dve", e)

        @block.gpsimd
        def _(e):
            run("pool", e)

        @block.sync
        def _(e):
            run("sp", e)


WC = 3136 + 64 + 512
C_CQ, C_CKV, C_KR, C_R, C_K, C_V, C_XW, C_Z = 0, 256, 384, 448, 960, 1472, 1984, 2112
C_KRROT, C_V2 = 3136, 3200
PP_GPRE, PP_GQ, PP_GKV, PP_W0, PP_A0, PP_KK, PP_KA, PP_RK, PP_INVF, PP_MU = 0, 8, 10, 11, 15, 19, 23, 27, 31, 32
NPP = 41
NROWS = 2560
GN_EPS = 64e-5
NORM_EPS = 1e-6
SM_SCALE = 192.0 ** -0.5


def build(NBC, NT, dbg_names=(), stop=None):
    nc = bass.Bass("TRN2", target_bir_lowering=False)

    def din(name, shape, dt=F32):
        return nc.dram_tensor(name, list(shape), dt, kind="ExternalInput").ap()

    T = NT * 128
    x_d = din("x", [NBC * T, D])
    pos_d = din("pos", [NBC, T], I32)
    win_d = din("win", [D, 3200])
    wuq_d = din("wuq", [256, 1024])
    wukv_d = din("wukv", [128, 1024])
    w2a_d = din("w2a", [128, 512])
    wout_d = din("wout", [D, D])
    pp_d = din("pp", [128, NPP])
    rows_d = din("rows", [1, NROWS])
    out_d = nc.dram_tensor("out", [NBC * T, D], F32, kind="ExternalOutput").ap()
    dbg_d = {}
    for (nm, shape) in dbg_names:
        dbg_d[nm] = nc.dram_tensor("dbg_" + nm, list(shape), F32, kind="ExternalOutput").ap()

    with ExitStack() as es:
        P = Prog(nc, es)
        ident = P.sb("ident", [128, 128])
        identb = P.sb("identb", [128, 128], BF16)
        onesf = P.sb("onesf", [128, 128])
        onesb = P.sb("onesb", [128, 128], BF16)
        blockones = P.sb("blockones", [128, 128])
        headind = P.sb("headind", [128, 2], BF16)
        m_su = P.sb("m_su", [128, 128], BF16)
        m_iu = P.sb("m_iu", [128, 128], BF16)
        m_sl = P.sb("m_sl", [128, 128], BF16)
        mask4 = P.sb("mask4", [128, 4, 128], BF16)
        P.memset(ident, 0.0, eng="pool")
        P.aselect(ident, ident, [[-1, 128]], ALU.not_equal, 1.0, 0, 1)
        P.copy(identb, ident)
        P.memset(onesf, 1.0)
        P.memset(onesb, 1.0)
        P.memset(blockones, 0.0)
        P.memset(blockones[0:64, 0:64], 1.0)
        P.memset(blockones[64:128, 64:128], 1.0)
        P.memset(headind, 0.0)
        P.memset(headind[0:64, 0:1], 1.0)
        P.memset(headind[64:128, 1:2], 1.0)
        for m in (m_su, m_iu, m_sl):
            P.memset(m, 1.0, eng="pool")
        P.aselect(m_su, m_su, [[1, 128]], ALU.is_gt, 0.0, 0, -1)
        P.aselect(m_iu, m_iu, [[1, 128]], ALU.is_ge, 0.0, 0, -1)
        P.aselect(m_sl, m_sl, [[-1, 128]], ALU.is_gt, 0.0, 0, 1)
        P.copy(mask4[:, 0, :], m_su)
        P.copy(mask4[:, 1, :], m_iu)
        P.copy(mask4[:, 2, :], m_su)
        P.copy(mask4[:, 3, :], m_iu)
        m4 = mask4

        pp = P.sb("pp", [128, NPP])
        P.dma(pp, pp_d)
        rows = P.sb("rows", [128, 2048])
        P.dma(rows, rows_d[:, 512:2560].broadcast_to([128, 2048]))
        der = P.sb("der", [128, 32])
        gneg = der[:, 0:8]
        gqneg = der[:, 8:10]
        omka = der[:, 10:14]
        P.ts(gneg, pp[:, PP_GPRE:PP_GPRE + 8], -1.0, ALU.mult)
        P.ts(gqneg, pp[:, PP_GQ:PP_GQ + 2], -1.0, ALU.mult)
        P.ts(omka, pp[:, PP_KA:PP_KA + 4], -1.0, ALU.mult, 1.0, ALU.add)
        Gt = es.enter_context(nc.sbuf_tensor("s_G", [128, 10, 512], F32))
        g = [V(Gt[:][:, i, :], TB("G%d" % i)) for i in range(10)]

        def g3(i, a, parts=128):
            return V(g[i].ap[0:parts, :].rearrange("p (a t) -> p a t", a=a), g[i].tb)

        muv, omuv = g[8], g[9]
        P.dma(muv, rows_d[:, 0:512].broadcast_to([128, 512]))
        P.ts(omuv, muv, -1.0, ALU.mult, 1.0, ALU.add)

        W = P.sb("W", [128, 8, WC], BF16)
        stg = [P.sb("stg0", [128, 3200]), P.sb("stg1", [128, 3200])]
        for c in range(8):
            s = stg[c % 2]
            P.dma(s, win_d[c * 128:(c + 1) * 128, :])
            gg = pp[:, PP_GPRE + c:PP_GPRE + c + 1]
            gn = gneg[:, c:c + 1]
            e1 = "dve"
            e2_ = "pool" if c % 2 == 0 else "dve"
            P.ts(W[:, c, 0:C_V], s[:, 0:C_V], gg, ALU.mult, eng=e1)
            P.act(W[:, c, C_XW:3136], s[:, C_XW:3136], AF.Copy, scale=gg)
            P.ts(W[:, c, C_KRROT:C_KRROT + 32], s[:, 3136:3168], gn, ALU.mult, eng=e1)
            P.ts(W[:, c, C_KRROT + 32:C_KRROT + 64], s[:, 3168:3200], gg, ALU.mult, eng=e1)
            P.stt(W[:, c, C_V:C_V + 512], s[:, C_V:C_V + 512], gg, omuv, ALU.mult, ALU.mult)
            P.stt(W[:, c, C_V2:C_V2 + 512], s[:, C_V:C_V + 512], gg, muv, ALU.mult, ALU.mult)
        Wq = P.sb("Wq", [128, 2, 1024], BF16)
        for c in range(2):
            s = stg[c % 2]
            P.dma(s[:, 0:1024], wuq_d[c * 128:(c + 1) * 128, :])
            gg = pp[:, PP_GQ + c:PP_GQ + c + 1]
            gn = gqneg[:, c:c + 1]
            P.ts(Wq[:, c, 0:768], s[:, 0:768], gg, ALU.mult)
            rot_o = Wq[:, c, 768:1024].re("p (h r) -> p h r", h=4)
            rot_in = s[:, 768:1024].re("p (h r) -> p h r", h=4)
            P.ts(rot_o[:, :, 0:32], rot_in[:, :, 0:32], gn, ALU.mult)
            P.ts(rot_o[:, :, 32:64], rot_in[:, :, 32:64], gg, ALU.mult)
        Wkv = P.sb("Wkv", [128, 1024], BF16)
        s = stg[0]
        P.dma(s[:, 0:1024], wukv_d)
        P.ts(Wkv, s[:, 0:1024], pp[:, PP_GKV:PP_GKV + 1], ALU.mult)
        W2A = P.sb("W2A", [128, 512], BF16)
        P.dma(W2A, w2a_d, eng="pool")
        Wout = P.sb("Wout", [128, 8, 1024], BF16)
        for c in range(8):
            P.dma(Wout[:, c, :], wout_d[c * 128:(c + 1) * 128, :], eng="pool")

        carve_off = [0, 0]

        def carve(si, name, shape, dt):
            esz = 4 if dt in (F32, I32) else 2
            n = 1
            for d_ in shape[1:]:
                n *= d_
            nb = n * esz
            c0 = carve_off[si] // 4
            carve_off[si] += nb
            assert carve_off[si] <= 12800, (name, carve_off)
            ap = stg[si].ap[0:shape[0], c0:c0 + nb // 4]
            if dt != F32:
                ap = ap.bitcast(dt)
            if len(shape) == 3:
                ap = ap.rearrange("p (a b) -> p a b", a=shape[1])
            elif len(shape) == 4:
                ap = ap.rearrange("p (a b c) -> p a b c", a=shape[1], b=shape[2])
            return V(ap, TB(name, inherit=stg[si].tb))

        AR = carve(0, "AR", [128, 4, 2, 128], BF16)
        Bt = carve(0, "Bt", [128, 4, 128], BF16)
        Kt = carve(0, "Kt", [128, 4, 128], BF16)
        bhat = carve(0, "bhat", [128, 4, 128], BF16)
        khat = carve(0, "khat", [128, 4, 128], BF16)
        BKtok = carve(0, "BKtok", [128, 8, 128], BF16)
        BUFS = []
        for si_ in range(2):
            BUFS.append((carve(0, "NK%d" % si_, [128, 2, 4, 128], BF16),
                         carve(1, "Am%d" % si_, [128, 2, 128], BF16),
                         carve(1, "QQ%d" % si_, [128, 4, 128], BF16),
                         carve(1, "Mt%d" % si_, [128, 2, 128], BF16),
                         carve(1, "Xb%d" % si_, [128, 2, 64], BF16),
                         carve(1, "Ub%d" % si_, [128, 2, 64], BF16)))
        yrb = carve(1, "yrb", [128, 512], BF16)
        prk = carve(1, "prk", [128, 4, 128], BF16)
        lor = carve(1, "lor", [128, 128], BF16)
        vtokb = carve(1, "vtokb", [128, 512], BF16)
        PT = [carve(1, "PT0", [128, 4, 128], BF16), carve(1, "PT1", [128, 4, 128], BF16)]
        cqn = carve(1, "cqn", [128, 3, 128], BF16)
        qnT = carve(1, "qnT", [128, 4, 128], BF16)

        rot_banks = [P.ps("bank%d" % i, [128, 512]) for i in range(4)]
        bank_y = P.ps("bank_y", [128, 512])
        bank_s = P.ps("bank_s", [128, 512])
        bank_x = P.ps("bank_x", [128, 512])
        bank_t = P.ps("bank_t", [128, 1024], BF16)
        rot_i = [0]
        pool_sel = [rot_banks]
        bank_tf = V(bank_t.ap.bitcast(F32), bank_t.tb)

        def bank():
            pl = pool_sel[0]
            bkk = pl[rot_i[0] % len(pl)]
            rot_i[0] += 1
            return bkk

        KnT = P.sb("KnT", [128, 4, T], BF16)
        krT = P.sb("krT", [64, T], BF16)
        Vm = P.sb("Vm", [128, NT, 512], BF16)
        uT = P.sb("uT", [128, 8, 128], BF16)
        uTs = P.sb("uTs", [128, 8, 128], BF16)
        xt = [P.sb("xt0", [128, D]), P.sb("xt1", [128, D])]
        sc = P.sb("sc", [128, 16])
        Praw = P.sb("Praw", [128, 9, 129])
        H32 = P.sb("H32", [128, 4, 64])
        Hbf = P.sb("Hbf", [128, 4, 64], BF16)
        qrT = P.sb("qrT", [64, 4, 128], BF16)
        zsT = P.sb("zsT", [128, 8, 128], BF16)
        ycatT = P.sb("ycatT", [128, 8, 128], BF16)
        mixed = P.sb("mixed", [128, 9, 128])
        vtokf = P.sb("vtokf", [128, 512])
        atmp = P.sb("atmp", [128, 2, 128])
        ropei = P.sb("ropei", [64, 2, 128], I32)
        dbgst = P.sb("dbgst", [128, 1024]) if dbg_d else None

        def dump(nm, v, ncols):
            if nm not in dbg_d:
                return
            P.copy(dbgst[:, 0:ncols], v)
            P.dma(dbg_d[nm], dbgst[:, 0:ncols])

        def rsqrt(out, in_, scale, eps):
            P.act(out, in_, AF.Ln, bias=eps, scale=scale)
            P.act(out, out, AF.Exp, scale=-0.5)

        def bmid(v, shape):
            return V(v.ap.unsqueeze(1).to_broadcast(list(shape)), v.tb)

        def blast(v, shape):
            return V(v.ap.unsqueeze(2).to_broadcast(list(shape)), v.tb)

        pending_f = []

        def emit_f(xtile, r0):
            bo = [bank(), bank()]
            for n in range(2):
                for c in range(8):
                    P.mm(bo[n], ycatT[:, c, :], Wout[:, c, n * 512:(n + 1) * 512], start=(c == 0), stop=(c == 7))
            P.memset(sc[:, 2:4], 0.0, eng="pool")
            P.act(g[1], bo[0], AF.Square, accum_out=sc[:, 2:3])
            P.act(g[2], bo[1], AF.Square, accum_out=sc[:, 3:4])
            P.tt(sc[:, 4:5], sc[:, 2:3], sc[:, 3:4], ALU.add)
            rsqrt(sc[:, 5:6], sc[:, 4:5], 1.0 / D, NORM_EPS)
            for n, gi in ((0, 4), (1, 8)):
                P.stt(g[gi], bo[n], sc[:, 5:6], rows[:, 1024 + n * 512:1024 + (n + 1) * 512], ALU.mult, ALU.mult)
                P.tt(xtile[:, n * 512:(n + 1) * 512], xtile[:, n * 512:(n + 1) * 512], g[gi], ALU.add)
            P.dma(out_d[r0:r0 + 128, :], xtile)

        tiles = [(b_, it_) for b_ in range(NBC) for it_ in range(NT)]

        def stage_a(n):
            b_, it_ = tiles[n]
            r0_ = b_ * T + it_ * 128
            xtile_ = xt[n % 2]
            P.dma(xtile_, x_d[r0_:r0_ + 128, :])
            ss = sc[:, 0:1]
            rstd = sc[:, 1:2]
            P.memset(ss, 0.0, eng="pool")
            xs = V(g[7].ap.bitcast(BF16), g[7].tb)
            P.act(xs, xtile_, AF.Square, accum_out=ss)
            rsqrt(rstd, ss, 1.0 / D, NORM_EPS)
            P.ts(xs, xtile_, rstd, ALU.mult)
            for c in range(8):
                P.tr(bank_t[:, c * 128:(c + 1) * 128], xs[:, c * 128:(c + 1) * 128], identb)
            if it_ == 0:
                P.memset(uTs[:, :, 0:1], 0.0, eng="pool")
            else:
                P.copy(uTs[:, :, 0:1], uT[:, :, 127:128], eng="pool")
            P.copy(uT, bank_t.re("p (c t) -> p c t", c=8), eng="act")
            P.copy(uTs[:, :, 1:128], uT[:, :, 0:127], eng="pool")

        ti_glob = 0
        for b in range(NBC):
            P.memset(Praw[:, :, 0:1], 0.0)
            P.memset(H32, 0.0)
            P.memset(Hbf, 0.0)
            for it in range(NT):
                r0 = b * T + it * 128
                tsl = slice(it * 128, (it + 1) * 128)
                n_tile = ti_glob
                xtile = xt[ti_glob % 2]
                ti_glob += 1
                stage_a(n_tile)
                if pending_f:
                    emit_f(*pending_f.pop())

                def proj(outv, col0, ncols, shift=False):
                    for c in range(8):
                        rhs = uTs[:, c, :] if shift else uT[:, c, :]
                        P.mm(outv, W[:, c, col0:col0 + ncols], rhs, start=(c == 0), stop=(c == 7))

                cq = g3(0, 4)[:, 0:3, :]
                sq3 = g3(1, 4)[:, 0:3, :]
                rs3 = g3(2, 4)[:, 0:3, :]
                qtmp = g3(3, 4, 64)
                qtmp2 = g3(4, 4, 64)
                sg = g3(7, 4)
                bk = bank()
                for j in range(3):
                    proj(bk[:, j * 128:(j + 1) * 128], C_CQ + j * 128, 128)
                P.copy(cq, bk[:, 0:384].re("p (j t) -> p j t", j=3), eng="act")
                P.start_seg()
                pool_sel[0] = rot_banks[0:2]
                for half in range(2):
                    bk = bank()
                    for j in range(4):
                        proj(bk[:, j * 128:(j + 1) * 128], C_Z + (half * 4 + j) * 128, 128)
                    bk3 = bk.re("p (j t) -> p j t", j=4)
                    P.act(sg, bk3, AF.Sigmoid)
                    P.tt(zsT[:, half * 4:(half + 1) * 4, :], bk3, sg, ALU.mult)
                    P.cut()
                for q_, col0 in ((0, C_R), (1, C_K)):
                    bk = bank()
                    for j in range(4):
                        proj(bk[:, j * 128:(j + 1) * 128], col0 + j * 128, 128)
                    P.copy(Praw[:, q_ * 4:(q_ + 1) * 4, 1:129], bk.re("p (j t) -> p j t", j=4),
                           eng=("act" if q_ == 0 else "dve"))
                    P.cut()
                bk = bank()
                proj(bk[:, 0:128], C_XW, 128)
                P.copy(Praw[:, 8, 1:129], bk[:, 0:128], eng="act")
                P.cut()
                bk = bank()
                for c in range(8):
                    P.mm(bk, uT[:, c, :], W[:, c, C_V:C_V + 512], start=(c == 0), stop=False)
                for c in range(8):
                    P.mm(bk, uTs[:, c, :], W[:, c, C_V2:C_V2 + 512], start=False, stop=(c == 7))
                P.copy(vtokf, bk, eng="act")
                P.copy(vtokb, vtokf, eng="pool")
                seg_b2 = P.end_seg()

                P.start_seg()
                pool_sel[0] = rot_banks[2:4]
                rope = g3(5, 4, 64)
                turns, rtmp, sinT, cosT = (rope[:, i, :] for i in range(4))
                P.dma(ropei[:, 0, :], pos_d[b:b + 1, tsl].broadcast_to([64, 128]))
                P.copy(rtmp, ropei[:, 0, :])
                P.ts(turns, rtmp, pp[0:64, PP_INVF:PP_INVF + 1], ALU.mult)
                P.copy(ropei[:, 1, :], turns)
                P.copy(rtmp, ropei[:, 1, :])
                P.tt(rtmp, turns, rtmp, ALU.subtract)
                P.act(sinT, rtmp, AF.Sin, scale=2.0 * math.pi)
                P.ts(turns, turns, 0.25, ALU.add)
                P.copy(ropei[:, 1, :], turns)
                P.copy(rtmp, ropei[:, 1, :])
                P.tt(rtmp, turns, rtmp, ALU.subtract)
                P.act(cosT, rtmp, AF.Sin, scale=2.0 * math.pi)
                P.cut()
                bk = bank()
                proj(bk[0:64, 0:128], C_KR, 64)
                proj(bk[0:64, 128:256], C_KRROT, 64)
                P.tt(qtmp[:, 0, :], bk[0:64, 0:128], cosT, ALU.mult)
                P.tt(qtmp[:, 1, :], bk[0:64, 128:256], sinT, ALU.mult)
                P.tt(krT[:, tsl], qtmp[:, 0, :], qtmp[:, 1, :], ALU.add, eng="pool")
                P.cut()
                P.tt(sq3, cq, cq, ALU.mult, eng="pool")
                bk = bank()
                P.mm(bk[:, 0:128], onesf, sq3[:, 0, :], start=True, stop=False)
                P.mm(bk[:, 0:128], onesf, sq3[:, 1, :], start=False, stop=True)
                P.mm(bk[:, 128:256], onesf, sq3[:, 2, :], start=True, stop=True)
                rsqrt(rs3[:, 0, :], bk[:, 0:128], 1.0 / 256, NORM_EPS)
                rsqrt(rs3[:, 2, :], bk[:, 128:256], 1.0 / 128, NORM_EPS)
                P.tt(cqn[:, 0:2, :], cq[:, 0:2, :], bmid(rs3[:, 0, :], [128, 2, 128]), ALU.mult)
                P.tt(cqn[:, 2, :], cq[:, 2, :], rs3[:, 2, :], ALU.mult)
                P.cut()
                bk = bank()
                for h in range(4):
                    for c in range(2):
                        P.mm(bk[:, h * 128:(h + 1) * 128], Wq[:, c, h * 128:(h + 1) * 128], cqn[:, c, :],
                             start=(c == 0), stop=(c == 1))
                P.copy(qnT, bk.re("p (h t) -> p h t", h=4), eng="act")
                P.cut()
                bk = bank()
                bk2 = bank()
                for h in range(4):
                    for c in range(2):
                        P.mm(bk[0:64, h * 128:(h + 1) * 128], Wq[:, c, 512 + h * 64:512 + (h + 1) * 64], cqn[:, c, :],
                             start=(c == 0), stop=(c == 1))
                    for c in range(2):
                        P.mm(bk2[0:64, h * 128:(h + 1) * 128], Wq[:, c, 768 + h * 64:768 + (h + 1) * 64], cqn[:, c, :],
                             start=(c == 0), stop=(c == 1))
                P.tt(qtmp, bk[0:64, :].re("p (h t) -> p h t", h=4), bmid(cosT, [64, 4, 128]), ALU.mult)
                P.tt(qtmp2, bk2[0:64, :].re("p (h t) -> p h t", h=4), bmid(sinT, [64, 4, 128]), ALU.mult)
                P.tt(qrT, qtmp, qtmp2, ALU.add, eng="pool")
                P.cut()
                bk = bank()
                for h in range(4):
                    P.mm(bk[:, h * 128:(h + 1) * 128], Wkv[:, h * 128:(h + 1) * 128], cqn[:, 2, :])
                P.copy(KnT[:, :, tsl], bk.re("p (h t) -> p h t", h=4), eng="act")
                P.cut()
                bk = bank()
                P.mm(bk, cqn[:, 2, :], Wkv[:, 512:1024])
                P.copy(Vm[:, it, :], bk)
                seg_c = P.end_seg()
                pool_sel[0] = rot_banks
                P.merge(seg_b2, seg_c)

                P.start_seg()
                nj = it + 1
                pti = 0
                for h in range(4):
                    for jb in range(0, nj, 4):
                        njj = min(4, nj - jb)
                        bk = bank_tf
                        pt = PT[pti % 2]
                        pti += 1
                        for jj in range(njj):
                            j = jb + jj
                            P.mm(bk[:, jj * 128:(jj + 1) * 128], KnT[:, h, j * 128:(j + 1) * 128], qnT[:, h, :],
                                 start=True, stop=False)
                            P.mm(bk[:, jj * 128:(jj + 1) * 128], krT[:, j * 128:(j + 1) * 128], qrT[:, h, :],
                                 start=False, stop=True)
                        P.act(pt[:, 0:njj, :], bk[:, 0:njj * 128].re("p (j t) -> p j t", j=njj), AF.Exp, scale=SM_SCALE)
                        if jb + njj == nj:
                            P.tt(pt[:, njj - 1, :], pt[:, njj - 1, :], m_iu, ALU.mult, eng="pool")
                        for jj in range(njj):
                            j = jb + jj
                            P.mm(bank_y[:, h * 128:(h + 1) * 128], Vm[:, j, h * 128:(h + 1) * 128], pt[:, jj, :],
                                 start=(j == 0), stop=(j == nj - 1))
                            P.mm(bank_s[:, h * 128:(h + 1) * 128], onesb, pt[:, jj, :],
                                 start=(j == 0), stop=(j == nj - 1))
                        P.cut()
                    P.recip(atmp[:, 0, :], bank_s[:, h * 128:(h + 1) * 128])
                    P.tt(atmp[:, 1, :], bank_y[:, h * 128:(h + 1) * 128], atmp[:, 0, :], ALU.mult)
                    P.tt(ycatT[:, h, :], atmp[:, 1, :], zsT[:, h, :], ALU.mult, eng="pool")
                    P.cut()
                seg_d = P.end_seg()
                P.start_seg()

                e2, av, kk, kmod, bb, tm1, tm2, Lc, Lx, EL = (g3(i, 4) for i in range(10))
                mu_bc = blast(pp[:, PP_MU:PP_MU + 9], [128, 9, 128])
                P.tt(mixed, Praw[:, :, 0:128], Praw[:, :, 1:129], ALU.subtract)
                P.tt(mixed, mixed, mu_bc, ALU.mult)
                P.tt(mixed, mixed, Praw[:, :, 1:129], ALU.add)
                P.copy(Praw[:, :, 0:1], Praw[:, :, 128:129], eng="pool")
                rm = mixed[:, 0:4, :]
                km = mixed[:, 4:8, :]
                P.act(lor[0:64, :], mixed[0:64, 8, :], AF.Tanh)
                P.copy(lor[64:128, :], mixed[64:128, 8, :], eng="pool")
                P.cut()
                bkw = bank()
                bka = bank()
                for hp in range(4):
                    P.mm(bkw[:, hp * 128:(hp + 1) * 128], W2A[0:64, hp * 128:(hp + 1) * 128], lor[0:64, :])
                    P.mm(bka[:, hp * 128:(hp + 1) * 128], W2A[64:128, hp * 128:(hp + 1) * 128], lor[64:128, :])

                def pbc(col):
                    return blast(pp[:, col:col + 4], [128, 4, 128])

                P.tt(e2, bkw.re("p (h t) -> p h t", h=4), pbc(PP_W0), ALU.add)
                P.tt(av, bka.re("p (h t) -> p h t", h=4), pbc(PP_A0), ALU.add)
                P.act(e2, e2, AF.Sigmoid)
                P.act(av, av, AF.Sigmoid)
                P.ts(e2, e2, math.exp(-0.5), ALU.mult)
                P.cut()
                P.tt(kk, km, pbc(PP_KK), ALU.mult)
                P.tt(tm1, kk, kk, ALU.mult)
                bk = bank()
                for hp in range(4):
                    P.mm(bk[:, hp * 128:(hp + 1) * 128], blockones, tm1[:, hp, :])
                rsqrt(tm2, bk.re("p (h t) -> p h t", h=4), 1.0, 1e-24)
                P.tt(kk, kk, tm2, ALU.mult)
                P.cut()
                P.tt(tm1, av, pbc(PP_KA), ALU.mult, eng="pool")
                P.tt(tm1, tm1, blast(omka, [128, 4, 128]), ALU.add, eng="pool")
                P.tt(kmod, km, tm1, ALU.mult, eng="pool")
                P.tt(bb, kk, av, ALU.mult, eng="pool")
                P.tt(tm1, rm, kmod, ALU.mult, eng="pool")
                P.tt(prk, tm1, pbc(PP_RK), ALU.mult, eng="pool")
                P.cut()
                for hp in range(4):
                    P.scan(Lc[:, hp, :], onesf, e2[:, hp, :], 0.0, ALU.mult, ALU.subtract)
                P.tt(Lx, Lc, e2, ALU.add)
                P.act(EL, Lc, AF.Exp)
                P.act(Lx, Lx, AF.Exp)
                P.act(Lc, Lc, AF.Exp, scale=-1.0)
                P.cut()
                gC = EL[:, :, 127:128].bc([128, 4, 128])
                P.tt(AR[:, :, 1, :], rm, EL, ALU.mult)
                P.stt(AR[:, :, 0, :], kk, -1.0, Lx, ALU.mult, ALU.mult)
                P.tt(tm1, bb, Lc, ALU.mult)
                P.tt(tm2, kmod, Lc, ALU.mult, eng="pool")
                P.copy(Bt, tm1, eng="pool")
                P.copy(Kt, tm2, eng="pool")
                P.tt(bhat, tm1, gC, ALU.mult)
                P.tt(khat, tm2, gC, ALU.mult)
                P.cut()
                for hp in range(4):
                    P.tr(bank_t[:, hp * 128:(hp + 1) * 128], bhat[:, hp, :], identb)
                    P.tr(bank_t[:, (4 + hp) * 128:(5 + hp) * 128], khat[:, hp, :], identb)
                P.copy(BKtok, bank_t.re("p (c t) -> p c t", c=8), eng="act")
                P.cut()
                msl1 = m_sl
                idb2 = bmid(identb, [128, 2, 128])

                def emit_group(hbase, S):
                    NKg, Amg, QQg, Mtg, Xg, Ug = S
                    Qg, QTg = QQg[:, 0:2, :], QQg[:, 2:4, :]
                    hp = hbase // 2
                    for hl in range(2):
                        pb = hl * 64
                        bk = bank()
                        rhs = AR[pb:pb + 64, hp, :, :]
                        P.mm(bk[:, 0:256], Bt[pb:pb + 64, hp, :], rhs)
                        P.mm(bk[:, 256:512], Kt[pb:pb + 64, hp, :], rhs)
                        P.tt(NKg[:, hl, :, :], bk.re("p (a t) -> p a t", a=4), m4, ALU.mult)
                        P.cut()
                    bke = bank()
                    bko = bank()
                    P.mm(bke[:, 0:128], AR[0:64, hp, 0, :], Bt[0:64, hp, :])
                    P.mm(bko[:, 0:128], AR[64:128, hp, 0, :], Bt[64:128, hp, :])
                    P.tt(Amg[:, 0, :], bke[:, 0:128], msl1, ALU.mult)
                    P.tt(Amg[:, 1, :], bko[:, 0:128], msl1, ALU.mult)
                    P.tt(Mtg, NKg[:, :, 0, :], idb2, ALU.add, eng="pool")
                    P.cut()
                    qc, qtc = NKg[:, :, 0, :], Amg
                    for k in range(1, 7):
                        bsq = bank()
                        for hl in range(2):
                            if k < 6:
                                P.mm(bsq[:, hl * 128:(hl + 1) * 128], qtc[:, hl, :], qc[:, hl, :])
                            P.mm(bsq[:, 256 + hl * 128:256 + (hl + 1) * 128], qc[:, hl, :], qtc[:, hl, :])
                        if k >= 2:
                            bp = bank()
                            for hl in range(2):
                                P.mm(bp[:, hl * 128:(hl + 1) * 128], qtc[:, hl, :], Mtg[:, hl, :])
                        if k < 6:
                            P.copy(QQg, bsq.re("p (h t) -> p h t", h=4), eng="act")
                        else:
                            P.copy(QQg[:, 2:4, :], bsq[:, 256:512].re("p (h t) -> p h t", h=2), eng="act")
                        if k >= 2:
                            P.tt(Mtg, bp[:, 0:256].re("p (h t) -> p h t", h=2), Mtg, ALU.add)
                        qc, qtc = Qg, QTg
                        P.cut()
                    bp = bank()
                    for hl in range(2):
                        P.mm(bp[:, hl * 128:(hl + 1) * 128], qtc[:, hl, :], Mtg[:, hl, :])
                    P.tt(Mtg, bp[:, 0:256].re("p (h t) -> p h t", h=2), Mtg, ALU.add)
                    P.cut()
                    bk = bank()
                    for hl in range(2):
                        h, pb = hbase + hl, hl * 64
                        P.mm(bk[:, hl * 64:(hl + 1) * 64], AR[pb:pb + 64, hp, 0, :], Hbf[pb:pb + 64, hp, :], start=True, stop=False)
                        P.mm(bk[:, hl * 64:(hl + 1) * 64], NKg[:, hl, 2, :], vtokb[:, h * 64:(h + 1) * 64], start=False, stop=True)
                    P.copy(Xg, bk[:, 0:128].re("p (h v) -> p h v", h=2), eng="act")
                    P.cut()
                    bk = bank()
                    for hl in range(2):
                        P.mm(bk[:, hl * 64:(hl + 1) * 64], Mtg[:, hl, :], Xg[:, hl, :])
                    P.copy(Ug, bk[:, 0:128].re("p (h v) -> p h v", h=2))
                    P.cut()
                    for hl in range(2):
                        h, pb = hbase + hl, hl * 64
                        P.mm(bank_x[:, h * 64:(h + 1) * 64], AR[pb:pb + 64, hp, 1, :], Hbf[pb:pb + 64, hp, :], start=True, stop=False)
                        P.mm(bank_x[:, h * 64:(h + 1) * 64], NKg[:, hl, 1, :], Ug[:, hl, :], start=False, stop=False)
                        P.mm(bank_x[:, h * 64:(h + 1) * 64], NKg[:, hl, 3, :], vtokb[:, h * 64:(h + 1) * 64], start=False, stop=True)
                    bk = bank()
                    for hl in range(2):
                        h = hbase + hl
                        P.mm(bk[:, hl * 64:(hl + 1) * 64], BKtok[:, hp, :], Ug[:, hl, :], start=True, stop=False)
                        P.mm(bk[:, hl * 64:(hl + 1) * 64], BKtok[:, 4 + hp, :], vtokb[:, h * 64:(h + 1) * 64], start=False, stop=True)
                    Hs = H32[:, hp, :]
                    P.tt(Hs, Hs, EL[:, hp, 127:128].bc([128, 64]), ALU.mult)
                    P.tt(Hs[0:64], bk[0:64, 0:64], Hs[0:64], ALU.add)
                    P.tt(Hs[64:128], bk[64:128, 64:128], Hs[64:128], ALU.add)
                    P.copy(Hbf[:, hp, :], Hs, eng="pool")
                    P.cut()

                for rnd in range(2):
                    P.start_seg()
                    pool_sel[0] = rot_banks[0:2]
                    emit_group(4 * rnd, BUFS[0])
                    sg0 = P.end_seg()
                    P.start_seg()
                    pool_sel[0] = rot_banks[2:4]
                    emit_group(4 * rnd + 2, BUFS[1])
                    sg1 = P.end_seg()
                    pool_sel[0] = rot_banks
                    P.merge(sg0, sg1)
                gst = sc[:, 8:16]
                yc = V(g[0].ap.rearrange("p (h v) -> p h v", h=8), g[0].tb)
                ysq = V(g[1].ap.rearrange("p (h v) -> p h v", h=8), g[1].tb)
                st1 = V(g[2].ap[:, 0:32].rearrange("p (a h) -> p a h", a=4), g[2].tb)
                y3 = bank_x.re("p (h v) -> p h v", h=8)
                P.rsum(st1[:, 0, :], y3)
                P.ts(st1[:, 0, :], st1[:, 0, :], 1.0 / 64, ALU.mult)
                P.tt(yc, y3, blast(st1[:, 0, :], [128, 8, 64]), ALU.subtract)
                P.tt(ysq, yc, yc, ALU.mult)
                P.rsum(st1[:, 1, :], ysq)
                rsqrt(st1[:, 2, :], st1[:, 1, :], 1.0 / 64, GN_EPS)
                P.cut()
                bkr = bank()
                for hp in range(4):
                    P.mm(bkr[:, hp * 2:hp * 2 + 2], prk[:, hp, :], headind)
                P.copy(st1[:, 3, :], bkr[:, 0:8])
                P.cut()
                P.tt(yc, yc, blast(st1[:, 2, :], [128, 8, 64]), ALU.mult)
                yc2 = yc.re("p h v -> p (h v)")
                P.tt(yc2, yc2, rows[:, 0:512], ALU.mult)
                P.tt(yc2, yc2, rows[:, 512:1024], ALU.add)
                P.tt(ysq, vtokf.re("p (h v) -> p h v", h=8), blast(st1[:, 3, :], [128, 8, 64]), ALU.mult, eng="pool")
                P.tt(yrb, yc2, ysq.re("p h v -> p (h v)"), ALU.add)
                for c in range(4):
                    P.tr(bank_t[:, c * 128:(c + 1) * 128], yrb[:, c * 128:(c + 1) * 128], identb)
                P.tt(ycatT[:, 4:8, :], bank_t[:, 0:512].re("p (c t) -> p c t", c=4), zsT[:, 4:8, :], ALU.mult)

                seg_e = P.end_seg()
                P.merge(seg_d, seg_e)
                pending_f.append((xtile, r0))

                if b == 0 and it == min(1, NT - 1):
                    dump("ycat", ycatT.re("p c t -> p (c t)"), 1024)
                    dump("yrw", yrb, 512)
                    dump("kk", kk.re("p c t -> p (c t)"), 512)
                    dump("e2", e2.re("p c t -> p (c t)"), 512)
                    dump("H", H32.re("p c t -> p (c t)"), 256)

        if pending_f:
            emit_f(*pending_f.pop())
        fw = [xt[0].tb, xt[1].tb]
        if dbgst is not None:
            fw.append(dbgst.tb)
        P.emit(final_wait=fw)
        build.stats = P.stats
    return nc


def host_params(inp):
    f = np.float32
    w_in = np.asarray(inp["w_in"][0], f)
    kr = w_in[:, 384:448]
    krrot = np.concatenate([kr[:, 32:64], kr[:, 0:32]], axis=1)
    win = np.ascontiguousarray(np.concatenate([w_in, krrot], axis=1))
    wuq = np.asarray(inp["mla_w_uq"][0], f).reshape(256, 4, 192)
    nope = wuq[:, :, 0:128].reshape(256, 512)
    rp = wuq[:, :, 128:192]
    rot = np.concatenate([rp[:, :, 32:64], rp[:, :, 0:32]], axis=2)
    wuq_l = np.ascontiguousarray(np.concatenate([nope, rp.reshape(256, 256), rot.reshape(256, 256)], axis=1))
    wukv = np.asarray(inp["mla_w_ukv"][0], f).reshape(128, 4, 256)
    wukv_l = np.ascontiguousarray(np.concatenate([wukv[:, :, 0:128].reshape(128, 512),
                                                  wukv[:, :, 128:256].reshape(128, 512)], axis=1))
    w2a = np.ascontiguousarray(np.concatenate([np.asarray(inp["rw_w2"][0], f), np.asarray(inp["rw_a2"][0], f)], axis=0))
    wout = np.ascontiguousarray(np.asarray(inp["w_out"][0], f))
    pp = np.zeros((128, NPP), f)

    def colmajor(v, n):
        return np.asarray(v, f).reshape(n, 128).T

    pp[:, PP_GPRE:PP_GPRE + 8] = colmajor(inp["norm_pre_g"][0], 8)
    pp[:, PP_GQ:PP_GQ + 2] = colmajor(inp["mla_q_norm_g"][0], 2)
    pp[:, PP_GKV:PP_GKV + 1] = colmajor(inp["mla_kv_norm_g"][0], 1)
    pp[:, PP_W0:PP_W0 + 4] = colmajor(inp["rw_w0"][0], 4)
    pp[:, PP_A0:PP_A0 + 4] = colmajor(inp["rw_a0"][0], 4)
    pp[:, PP_KK:PP_KK + 4] = colmajor(inp["rw_k_k"][0], 4)
    pp[:, PP_KA:PP_KA + 4] = colmajor(inp["rw_k_a"][0], 4)
    pp[:, PP_RK:PP_RK + 4] = colmajor(np.asarray(inp["rw_r_k"][0]).reshape(512), 4)
    invf = (10000.0 ** (-np.arange(0, 64, 2, dtype=np.float32) / 64)).astype(f)
    invf_turn = (np.concatenate([invf, invf]).astype(np.float64) / (2 * np.pi)).astype(f)
    pp[0:64, PP_INVF] = invf_turn
    mu = np.asarray(inp["rw_mu"][0], f)
    pp[:, PP_MU:PP_MU + 4] = colmajor(mu[0:512], 4)
    pp[:, PP_MU + 4:PP_MU + 8] = colmajor(mu[512:1024], 4)
    pp[:, PP_MU + 8] = mu[1536:1664]
    rows = np.concatenate([mu[1024:1536], np.asarray(inp["rw_ln_g"][0], f), np.asarray(inp["rw_ln_b"][0], f),
                           np.asarray(inp["norm_post_g"][0], f)]).reshape(1, NROWS).astype(f)
    return {"win": win, "wuq": wuq_l, "wukv": wukv_l, "w2a": w2a, "wout": wout, "pp": pp, "rows": rows}


def kernel(**inp):
    x = np.asarray(inp["x"], np.float32)
    pos = np.asarray(inp["positions"], np.int32)
    B, T, _ = x.shape
    nbc = B // N_CORES
    shared = host_params(inp)
    nc = build(nbc, T // 128)
    in_maps = []
    for c in range(N_CORES):
        m = dict(shared)
        m["x"] = np.ascontiguousarray(x[c * nbc:(c + 1) * nbc].reshape(nbc * T, D))
        m["pos"] = np.ascontiguousarray(pos[c * nbc:(c + 1) * nbc])
        in_maps.append(m)
    res = run_bass_kernel_spmd(nc, in_maps, core_ids=list(range(N_CORES)))
    out = np.concatenate([r["out"].reshape(nbc, T, D) for r in res.results], axis=0)
    return out.astype(np.float32)
```
